# Optimizing a Trainium2 kernel written in Bass

```python
import jax, jax.numpy as jnp
from jax import lax
import numpy as np

D_MODEL = 1024
BATCH = 8
SEQ = 4096
DEPTH = 4

HEAD_DIM = 64
CONV_CH = 256
CONV_WIDTH = 31
NSA_HEADS = 8
NSA_KV_HEADS = 2
NSA_GROUP = NSA_HEADS // NSA_KV_HEADS
NSA_DIM = NSA_HEADS * HEAD_DIM
NSA_KV_DIM = NSA_KV_HEADS * HEAD_DIM
CMP_BLOCK = 32
CMP_STRIDE = 16
SEL_BLOCK = 64
SEL_TOP = 16
WIN = 512
FORCE_SCORE = 1e4
DIL_PAIRS = ((128, 1), (512, 4), (2048, 16))
DIL_HEADS = 4
DIL_GROUP_DIM = DIL_HEADS * HEAD_DIM
D_FF = 2816
FFN_CONV_WIDTH = 3
ROPE_THETA = 10000.0
NORM_EPS = 1e-6
Q_BLOCK = 128
ATTN_SCALE = HEAD_DIM ** -0.5
NEG_INF = -1e30
MIX_DIM = CONV_CH + NSA_DIM + DIL_GROUP_DIM
IN_WIDTHS = (CONV_CH, CONV_CH, NSA_DIM, NSA_KV_DIM, NSA_KV_DIM, NSA_KV_DIM, NSA_KV_DIM, NSA_KV_DIM, NSA_KV_DIM, 3 * NSA_HEADS, 3 * DIL_GROUP_DIM, 3 * DIL_GROUP_DIM, 3 * DIL_GROUP_DIM)
IN_DIM = 2 * CONV_CH + NSA_DIM + 6 * NSA_KV_DIM + 3 * NSA_HEADS + 9 * DIL_GROUP_DIM

kernel_name = "hymba_conv_nsa_dilated_hybrid"


def rms_norm(x, g):
    xf = x.astype(jnp.float32)
    y = xf * lax.rsqrt(jnp.mean(xf * xf, axis=-1, keepdims=True) + NORM_EPS)
    return (y * g.astype(jnp.float32)).astype(x.dtype)


def layer_norm(x, g, b):
    xf = x.astype(jnp.float32)
    mu = jnp.mean(xf, axis=-1, keepdims=True)
    var = jnp.mean(jnp.square(xf - mu), axis=-1, keepdims=True)
    y = (xf - mu) * lax.rsqrt(var + NORM_EPS)
    return (y * g.astype(jnp.float32) + b.astype(jnp.float32)).astype(x.dtype)


def split_heads(t, n):
    b, s, _ = t.shape
    return t.reshape(b, s, n, HEAD_DIM).transpose(0, 2, 1, 3)


def rope(x, pos):
    half = HEAD_DIM // 2
    inv_freq = jnp.power(ROPE_THETA, -jnp.arange(half, dtype=jnp.float32) / half)
    ang = pos.astype(jnp.float32)[:, None, :, None] * inv_freq
    cos, sin = jnp.cos(ang), jnp.sin(ang)
    xf = x.astype(jnp.float32)
    x1, x2 = xf[..., :half], xf[..., half:]
    return jnp.concatenate([x1 * cos - x2 * sin, x2 * cos + x1 * sin], axis=-1).astype(x.dtype)


def causal_dwconv(x, w, b):
    k = w.shape[0]
    y = lax.conv_general_dilated(x, w[:, None, :], window_strides=(1,), padding=[(k - 1, 0)],
                                 dimension_numbers=("NWC", "WIO", "NWC"),
                                 feature_group_count=x.shape[-1])
    return y + b


def masked_softmax(s, mask):
    s = jnp.where(mask, s, NEG_INF)
    m = jnp.max(s, axis=-1, keepdims=True)
    e = jnp.where(mask, jnp.exp(s - m), 0.0)
    d = jnp.sum(e, axis=-1, keepdims=True)
    d = jnp.where(d > 0, d, 1.0)
    return e / d, (m + jnp.log(d))[..., 0]


def banded_attention(q, k, v, max_dist, block):
    b, hk, g, L, dh = q.shape
    blk = min(block, L)
    nb = -(-L // blk)
    lp = nb * blk
    n_prev = -(-max_dist // blk)
    span = (n_prev + 1) * blk
    qb = jnp.pad(q, ((0, 0), (0, 0), (0, 0), (0, lp - L), (0, 0))).reshape(b, hk, g, nb, blk, dh)
    qb = jnp.moveaxis(qb, 3, 0)
    kp = jnp.pad(k, ((0, 0), (0, 0), (n_prev * blk, lp - L), (0, 0)))
    vp = jnp.pad(v, ((0, 0), (0, 0), (n_prev * blk, lp - L), (0, 0)))
    qi = jnp.arange(blk)[:, None]
    kj = jnp.arange(span)[None, :]
    dist = qi + n_prev * blk - kj
    band = (dist >= 0) & (dist <= max_dist)

    def one_block(args):
        qc, n = args
        kc = lax.dynamic_slice_in_dim(kp, n * blk, span, axis=2)
        vc = lax.dynamic_slice_in_dim(vp, n * blk, span, axis=2)
        s = jnp.einsum("bhgqd,bhkd->bhgqk", qc, kc, preferred_element_type=jnp.float32) * ATTN_SCALE
        mask = band & ((n - n_prev) * blk + kj >= 0)
        p, lse = masked_softmax(s, mask)
        return jnp.einsum("bhgqk,bhkd->bhgqd", p.astype(vc.dtype), vc), lse

    o, lse = lax.map(one_block, (qb, jnp.arange(nb)))
    o = jnp.moveaxis(o, 0, 3).reshape(b, hk, g, lp, dh)[..., :L, :]
    lse = jnp.moveaxis(lse, 0, 3).reshape(b, hk, g, lp)[..., :L]
    return o, lse


def conv_module(val, gate, dw, dw_b, ln_g, ln_b, pw, pw_b):
    a = val * jax.nn.sigmoid(gate)
    a = causal_dwconv(a, dw, dw_b)
    a = jax.nn.silu(layer_norm(a, ln_g, ln_b))
    return a @ pw + pw_b


def nsa_mixer(q, kc, vc, ks, vs, kw, vw, gates, pos, qn, kn,
              ck_pos, ck_w1, ck_w2, cv_pos, cv_w1, cv_w2):
    b, S, _ = q.shape
    t = jnp.arange(S)
    qh = rope(rms_norm(split_heads(q, NSA_HEADS), qn), pos)
    qg = qh.reshape(b, NSA_KV_HEADS, NSA_GROUP, S, HEAD_DIM)

    nc = (S - CMP_BLOCK) // CMP_STRIDE + 1
    blk_idx = np.arange(nc)[:, None] * CMP_STRIDE + np.arange(CMP_BLOCK)[None, :]
    blk_end = blk_idx[:, -1]

    def compress(tk, pe, w1, w2):
        tb = split_heads(tk, NSA_KV_HEADS)[:, :, blk_idx] + pe
        flat = tb.reshape(b, NSA_KV_HEADS, nc, CMP_BLOCK * HEAD_DIM)
        return jax.nn.silu(flat @ w1) @ w2

    k_cmp = rope(rms_norm(compress(kc, ck_pos, ck_w1, ck_w2), kn), pos[:, blk_end])
    v_cmp = compress(vc, cv_pos, cv_w1, cv_w2)
    s_cmp = jnp.einsum("bhgqd,bhcd->bhgqc", qg, k_cmp, preferred_element_type=jnp.float32) * ATTN_SCALE
    cmask = blk_end[None, :] <= t[:, None]
    p_cmp, _ = masked_softmax(s_cmp, cmask)
    o_cmp = jnp.einsum("bhgqc,bhcd->bhgqd", p_cmp.astype(v_cmp.dtype), v_cmp)

    ns = S // SEL_BLOCK
    cs = np.arange(nc)[:, None] * CMP_STRIDE
    ss = np.arange(ns)[None, :] * SEL_BLOCK
    overlap = np.clip(np.minimum(cs + CMP_BLOCK, ss + SEL_BLOCK) - np.maximum(cs, ss), 0, None)
    m_map = jnp.asarray((overlap / CMP_BLOCK).astype(np.float32))
    imp = jnp.einsum("bhgqc,cn->bhqn", p_cmp, m_map)
    cur = (t // SEL_BLOCK)[:, None]
    j = jnp.arange(ns)[None, :]
    visible = j <= cur
    forced = (j == 0) | (j == cur) | (j == cur - 1)
    score = jnp.where(visible, jnp.where(forced, FORCE_SCORE, imp), -jnp.inf)
    n_top = min(SEL_TOP, ns)
    _, sel = lax.top_k(score, n_top)

    k_sel = rope(rms_norm(split_heads(ks, NSA_KV_HEADS), kn), pos).reshape(b, NSA_KV_HEADS, ns, SEL_BLOCK, HEAD_DIM)
    v_sel = split_heads(vs, NSA_KV_HEADS).reshape(b, NSA_KV_HEADS, ns, SEL_BLOCK, HEAD_DIM)
    nq = S // Q_BLOCK
    q_blocks = qg.reshape(b, NSA_KV_HEADS, NSA_GROUP, nq, Q_BLOCK, HEAD_DIM).transpose(3, 0, 1, 2, 4, 5)
    sel_blocks = sel.reshape(b, NSA_KV_HEADS, nq, Q_BLOCK, n_top).transpose(2, 0, 1, 3, 4)
    t_blocks = t.reshape(nq, Q_BLOCK)
    bi = jnp.arange(b)[:, None, None, None]
    hi = jnp.arange(NSA_KV_HEADS)[None, :, None, None]

    def sel_block(args):
        qc, ic, tc = args
        kg = k_sel[bi, hi, ic]
        vg = v_sel[bi, hi, ic]
        s = jnp.einsum("bhgqd,bhqnkd->bhgqnk", qc, kg, preferred_element_type=jnp.float32) * ATTN_SCALE
        kpos = ic[..., None] * SEL_BLOCK + jnp.arange(SEL_BLOCK)
        mask = (kpos <= tc[None, None, :, None, None]).reshape(b, NSA_KV_HEADS, 1, Q_BLOCK, n_top * SEL_BLOCK)
        p, _ = masked_softmax(s.reshape(b, NSA_KV_HEADS, NSA_GROUP, Q_BLOCK, n_top * SEL_BLOCK), mask)
        vflat = vg.reshape(b, NSA_KV_HEADS, Q_BLOCK, n_top * SEL_BLOCK, HEAD_DIM)
        return jnp.einsum("bhgqk,bhqkd->bhgqd", p.astype(vflat.dtype), vflat)

    o_sel = lax.map(sel_block, (q_blocks, sel_blocks, t_blocks))
    o_sel = o_sel.transpose(1, 2, 3, 0, 4, 5).reshape(b, NSA_KV_HEADS, NSA_GROUP, S, HEAD_DIM)

    k_win = rope(rms_norm(split_heads(kw, NSA_KV_HEADS), kn), pos)
    v_win = split_heads(vw, NSA_KV_HEADS)
    o_win, _ = banded_attention(qg, k_win, v_win, WIN - 1, Q_BLOCK)

    g = jax.nn.sigmoid(gates.astype(jnp.float32)).reshape(b, S, NSA_HEADS, 3).transpose(0, 2, 1, 3)
    g = g.reshape(b, NSA_KV_HEADS, NSA_GROUP, S, 3)
    o = g[..., 0:1] * o_cmp + g[..., 1:2] * o_sel + g[..., 2:3] * o_win
    o = o.reshape(b, NSA_HEADS, S, HEAD_DIM).transpose(0, 2, 1, 3).reshape(b, S, NSA_DIM)
    return o.astype(q.dtype)


def dilated_mixer(q, k, v, pos, qn, kn):
    b, S, _ = q.shape
    qs = jnp.split(q, 3, axis=-1)
    kss = jnp.split(k, 3, axis=-1)
    vss = jnp.split(v, 3, axis=-1)
    outs, lses = [], []
    for gi, (w, r) in enumerate(DIL_PAIRS):
        L = S // r
        qh = rope(rms_norm(split_heads(qs[gi], DIL_HEADS), qn), pos)
        kh = rope(rms_norm(split_heads(kss[gi], DIL_HEADS), kn), pos)
        vh = split_heads(vss[gi], DIL_HEADS)

        def phase(tt):
            return tt.reshape(b, DIL_HEADS, L, r, HEAD_DIM).transpose(0, 1, 3, 2, 4).reshape(b, DIL_HEADS * r, L, HEAD_DIM)

        o, lse = banded_attention(phase(qh)[:, :, None], phase(kh), phase(vh), w // r, Q_BLOCK)
        o = o[:, :, 0].reshape(b, DIL_HEADS, r, L, HEAD_DIM).transpose(0, 1, 3, 2, 4).reshape(b, DIL_HEADS, S, HEAD_DIM)
        lse = lse[:, :, 0].reshape(b, DIL_HEADS, r, L).transpose(0, 1, 3, 2).reshape(b, DIL_HEADS, S)
        outs.append(o)
        lses.append(lse)
    alpha = jax.nn.softmax(jnp.stack(lses), axis=0)
    o = jnp.sum(alpha[..., None] * jnp.stack(outs).astype(jnp.float32), axis=0)
    return o.transpose(0, 2, 1, 3).reshape(b, S, DIL_GROUP_DIM).astype(q.dtype)


def conv_ffn(h, w_up, dw, dw_b, w_down):
    u = causal_dwconv(h @ w_up, dw, dw_b)
    gt, val = jnp.split(u, 2, axis=-1)
    return (jax.nn.silu(gt) * val) @ w_down


def setup_inputs(seed: int = 0) -> dict:
    key = jax.random.key(seed)
    keys = iter(jax.random.split(key, 32))
    L = DEPTH

    def nrm(shape, scale):
        return jax.random.normal(next(keys), shape, jnp.float32) * scale

    def gain(shape):
        return 1.0 + nrm(shape, 0.02)

    inp = {}
    inp["x"] = nrm((BATCH, SEQ, D_MODEL), 1.0)
    inp["positions"] = jnp.broadcast_to(jnp.arange(SEQ, dtype=jnp.int32), (BATCH, SEQ))
    inp["attn_norm"] = gain((L, D_MODEL))
    inp["w_in"] = nrm((L, D_MODEL, IN_DIM), D_MODEL ** -0.5)
    inp["conv_dw"] = nrm((L, CONV_WIDTH, CONV_CH), CONV_WIDTH ** -0.5)
    inp["conv_dw_b"] = nrm((L, CONV_CH), 0.01)
    inp["conv_ln_g"] = gain((L, CONV_CH))
    inp["conv_ln_b"] = nrm((L, CONV_CH), 0.01)
    inp["conv_pw"] = nrm((L, CONV_CH, CONV_CH), CONV_CH ** -0.5)
    inp["conv_pw_b"] = nrm((L, CONV_CH), 0.01)
    inp["nsa_q_norm"] = gain((L, HEAD_DIM))
    inp["nsa_k_norm"] = gain((L, HEAD_DIM))
    inp["cmp_k_pos"] = nrm((L, CMP_BLOCK, HEAD_DIM), 0.1)
    inp["cmp_k_w1"] = nrm((L, CMP_BLOCK * HEAD_DIM, HEAD_DIM), (CMP_BLOCK * HEAD_DIM) ** -0.5)
    inp["cmp_k_w2"] = nrm((L, HEAD_DIM, HEAD_DIM), HEAD_DIM ** -0.5)
    inp["cmp_v_pos"] = nrm((L, CMP_BLOCK, HEAD_DIM), 0.1)
    inp["cmp_v_w1"] = nrm((L, CMP_BLOCK * HEAD_DIM, HEAD_DIM), (CMP_BLOCK * HEAD_DIM) ** -0.5)
    inp["cmp_v_w2"] = nrm((L, HEAD_DIM, HEAD_DIM), HEAD_DIM ** -0.5)
    inp["dil_q_norm"] = gain((L, HEAD_DIM))
    inp["dil_k_norm"] = gain((L, HEAD_DIM))
    inp["w_out"] = nrm((L, MIX_DIM, D_MODEL), MIX_DIM ** -0.5)
    inp["ffn_norm"] = gain((L, D_MODEL))
    inp["w_up"] = nrm((L, D_MODEL, 2 * D_FF), D_MODEL ** -0.5)
    inp["ffn_dw"] = nrm((L, FFN_CONV_WIDTH, 2 * D_FF), FFN_CONV_WIDTH ** -0.5)
    inp["ffn_dw_b"] = nrm((L, 2 * D_FF), 0.01)
    inp["w_down"] = nrm((L, D_FF, D_MODEL), D_FF ** -0.5)
    return inp


def reference(x, positions, attn_norm, w_in, conv_dw, conv_dw_b, conv_ln_g, conv_ln_b,
              conv_pw, conv_pw_b, nsa_q_norm, nsa_k_norm, cmp_k_pos, cmp_k_w1, cmp_k_w2,
              cmp_v_pos, cmp_v_w1, cmp_v_w2, dil_q_norm, dil_k_norm, w_out, ffn_norm,
              w_up, ffn_dw, ffn_dw_b, w_down):
    splits = [int(s) for s in np.cumsum(IN_WIDTHS)[:-1]]
    for l in range(DEPTH):
        h = rms_norm(x, attn_norm[l])
        (c_val, c_gate, n_q, n_kc, n_vc, n_ks, n_vs, n_kw, n_vw, n_gate,
         d_q, d_k, d_v) = jnp.split(h @ w_in[l], splits, axis=-1)
        y_a = conv_module(c_val, c_gate, conv_dw[l], conv_dw_b[l], conv_ln_g[l], conv_ln_b[l],
                          conv_pw[l], conv_pw_b[l])
        y_b = nsa_mixer(n_q, n_kc, n_vc, n_ks, n_vs, n_kw, n_vw, n_gate, positions,
                        nsa_q_norm[l], nsa_k_norm[l], cmp_k_pos[l], cmp_k_w1[l], cmp_k_w2[l],
                        cmp_v_pos[l], cmp_v_w1[l], cmp_v_w2[l])
        y_c = dilated_mixer(d_q, d_k, d_v, positions, dil_q_norm[l], dil_k_norm[l])
        x = x + jnp.concatenate([y_a, y_b, y_c], axis=-1) @ w_out[l]
        x = x + conv_ffn(rms_norm(x, ffn_norm[l]), w_up[l], ffn_dw[l], ffn_dw_b[l], w_down[l])
    return x
```

```python
import numpy as np
import concourse.bass as bass
import concourse.mybir as mybir
from concourse.bass_utils import run_bass_kernel_spmd

F32 = mybir.dt.float32
BF16 = mybir.dt.bfloat16
I32 = mybir.dt.int32
AF = mybir.ActivationFunctionType
ALU = mybir.AluOpType
AX = mybir.AxisListType

S = 4096
D = 1024
NT = S // 128
IN_DIM = 4120
D_FF = 2816
DEPTH = 4
NEG = -30000.0
EPS = 1e-6
SCALE = 0.125

C_VAL, C_GATE, N_Q, N_KC, N_VC, N_KS, N_VS, N_KW, N_VW, N_G, D_Q, D_K, D_V = (
    0, 256, 512, 1024, 1152, 1280, 1408, 1536, 1664, 1792, 1816, 2584, 3352)

ENGS = ("pe", "act", "dve", "pool", "sp")


class Op:
    __slots__ = ("eng", "fn", "deps", "kind", "sem", "count", "signal", "lane", "seq")

    def __init__(self, eng, fn, kind):
        self.eng = eng
        self.fn = fn
        self.kind = kind
        self.deps = []
        self.sem = None
        self.count = 0
        self.signal = False
        self.lane = None


class Sched:
    def __init__(self, nc, n_lanes=6, same_sync=True):
        self.nc = nc
        self.ops = {e: [] for e in ENGS}
        self.all = []
        self.lastw = {}
        self.readers = {}
        self.n_lanes = n_lanes
        self.same_sync = same_sync
        self.dma_n = {e: 0 for e in ENGS}
        self.pending_barrier = {}
        self.seq = 0

    def _track(self, op, reads, writes):
        deps = []
        for r in reads:
            w = self.lastw.get(r)
            if w is not None:
                deps.append(w)
        for w_ in writes:
            w = self.lastw.get(w_)
            if w is not None:
                deps.append(w)
            deps.extend(self.readers.get(w_, ()))
        for w_ in writes:
            self.lastw[w_] = op
            self.readers[w_] = []
        for r in reads:
            if r not in writes:
                self.readers.setdefault(r, []).append(op)
        if self.pending_barrier.get(op.eng):
            deps.extend(self.pending_barrier.pop(op.eng))
        best = {}
        for d in deps:
            if d is op:
                continue
            key = (d.eng, d.lane if d.kind == "dma" else -1)
            b = best.get(key)
            if b is None or d.seq > b.seq:
                best[key] = d
        op.deps = list(best.values())
        self.seq += 1
        op.seq = self.seq
        self.ops[op.eng].append(op)
        self.all.append(op)
        return op

    def op(self, eng, fn, reads=(), writes=()):
        import os
        lim = int(os.environ.get("MK_OPLIM", "-1"))
        self.nops = getattr(self, "nops", 0) + 1
        if lim >= 0 and self.nops > lim:
            return None
        return self._track(Op(eng, fn, "cmp"), tuple(reads), tuple(writes))

    def barrier(self):
        last = []
        for e in ENGS:
            cm = [o for o in self.ops[e] if o.kind == "cmp"]
            if cm:
                last.append(cm[-1])
            seen = set()
            for o in reversed(self.ops[e]):
                if o.kind == "dma" and o.lane not in seen:
                    seen.add(o.lane)
                    last.append(o)
                if len(seen) >= self.n_lanes:
                    break
        for e in ENGS:
            self.pending_barrier[e] = list(last)

    def mark(self, label):
        import os
        if os.environ.get("MK_MARKS"):
            print("MARK", label, getattr(self, "nops", 0), flush=True)

    def dma(self, out, in_, reads=(), writes=(), q="sp", **kw):
        o = Op(q, lambda e: e.dma_start(out=out, in_=in_, **kw), "dma")
        o.lane = self.dma_n[q] % self.n_lanes
        self.dma_n[q] += 1
        return self._track(o, tuple(reads), tuple(writes))

    def emit(self):
        nc = self.nc
        for op in self.all:
            for d in op.deps:
                if d.kind == "cmp" and (d.eng != op.eng or (self.same_sync and d.eng != "pe")):
                    d.signal = True
        esem = {e: nc.alloc_semaphore(name=f"s_{e}") for e in ENGS}
        lsem = {}
        lcnt = {}
        cnt = {e: 0 for e in ENGS}
        for op in self.all:
            if op.kind == "cmp":
                if op.signal:
                    cnt[op.eng] += 1
                    op.count = cnt[op.eng]
                    op.sem = ("e", op.eng)
            else:
                key = (op.eng, op.lane)
                if key not in lsem:
                    lsem[key] = nc.alloc_semaphore(name=f"l_{op.eng}{op.lane}")
                    lcnt[key] = 0
                lcnt[key] += 16
                op.count = lcnt[key]
                op.sem = ("l",) + key
        handles = {("e", e): esem[e] for e in ENGS}
        for key, h in lsem.items():
            handles[("l",) + key] = h
        self.max_counts = dict(cnt)

        def emit_engine(name, e):
            waited = {}
            for op in self.ops[name]:
                need = {}
                for d in op.deps:
                    if d.kind == "cmp" and d.eng == name and (name == "pe" or not self.same_sync):
                        continue
                    if need.get(d.sem, 0) < d.count:
                        need[d.sem] = d.count
                if op.kind == "dma" and op.count > 16:
                    if need.get(op.sem, 0) < op.count - 16:
                        need[op.sem] = op.count - 16
                for s, v in need.items():
                    if waited.get(s, 0) >= v:
                        continue
                    e.wait_ge(handles[s], v)
                    waited[s] = v
                ins = op.fn(e)
                if op.kind == "dma":
                    ins.then_inc(handles[op.sem], 16)
                elif op.signal:
                    ins.then_inc(handles[op.sem], 1)

        with nc.Block() as block:
            @block.tensor
            def _(e):
                emit_engine("pe", e)

            @block.scalar
            def _(e):
                emit_engine("act", e)

            @block.vector
            def _(e):
                emit_engine("dve", e)

            @block.gpsimd
            def _(e):
                emit_engine("pool", e)

            @block.sync
            def _(e):
                emit_engine("sp", e)


class SbAlloc:
    def __init__(self, nc, base=16384, limit=212000):
        self.nc = nc
        self.off = base
        self.limit = limit
        self.n = 0
        self.sched = None

    def mark(self):
        return self.off

    def reset(self, m):
        self.off = m
        if self.sched is not None:
            self.sched.barrier()

    def t(self, name, shape, dtype):
        esz = 2 if dtype == BF16 else 4
        nbytes = int(np.prod(shape[1:])) * esz
        off = (self.off + 63) // 64 * 64
        assert off + nbytes <= self.limit, f"SBUF overflow at {name}: {off}+{nbytes}"
        self.n += 1
        h = self.nc.alloc_sbuf_tensor_at(f"{name}_{self.n}", list(shape), dtype, offset=off)
        self.off = off + nbytes
        return h


class Ctx:
    pass


WSHAPES = {
    "attn_norm": (D,), "w_in": (D, IN_DIM), "conv_dw": (31, 256), "conv_dw_b": (256,),
    "conv_ln_g": (256,), "conv_ln_b": (256,), "conv_pw": (256, 256), "conv_pw_b": (256,),
    "nsa_q_norm": (64,), "nsa_k_norm": (64,), "cmp_k_pos": (32, 64), "cmp_k_w1": (2048, 64),
    "cmp_k_w2": (64, 64), "cmp_v_pos": (32, 64), "cmp_v_w1": (2048, 64), "cmp_v_w2": (64, 64),
    "dil_q_norm": (64,), "dil_k_norm": (64,), "w_out": (D, D), "ffn_norm": (D,),
    "w_up": (D, 2 * D_FF), "ffn_dw": (3, 2 * D_FF), "ffn_dw_b": (2 * D_FF,), "w_down": (D_FF, D),
}


def build_program(n_layers=DEPTH, stages=("s1", "s2", "s3", "s4", "s5a", "s5b"), ext=None):
    ext = ext or {}
    nc = bass.Bass("TRN2", target_bir_lowering=False)
    c = Ctx()
    c.nc = nc
    c.S = Sched(nc)
    c.sb = SbAlloc(nc)
    c.sb.sched = c.S
    c.in_names = []
    c.uid = 0

    def din(name, shape, dtype=F32):
        c.in_names.append(name)
        return nc.dram_tensor(name, list(shape), dtype, kind="ExternalInput").ap()

    c.w = {}

    def W(name):
        if name not in c.w:
            c.w[name] = din(name, [DEPTH] + list(WSHAPES[name]))
        return c.w[name]

    c.W = W

    def scratch(name, shape, dtype=F32):
        kind = {"in": "ExternalInput", "out": "ExternalOutput"}.get(ext.get(name), "Internal")
        if kind == "ExternalInput":
            c.in_names.append(name)
        return nc.dram_tensor(name, list(shape), dtype, kind=kind).ap()

    c.x = din("x", [S, D])
    c.ident_d = din("c_ident", [128, 128])
    c.out = nc.dram_tensor("out", [S, D], F32, kind="ExternalOutput").ap()
    c.proj = scratch("proj", [S, IN_DIM])
    c.mix = scratch("mix", [S, D])
    c.x1 = scratch("x1", [S, D])
    c.xa = scratch("xa", [S, D])
    c.xb = scratch("xb", [S, D])
    c.ps = [nc.alloc_psum_tensor(f"psb{i}", [128, 512], F32) for i in range(8)]

    setup_consts(c)
    if "s3" in stages or "s4" in stages or "rope" in stages:
        c.pos = din("positions", [S, 1], I32)
        c.invf_d = din("c_invf", [128, 32])
        c.masks_d = din("c_masks", [128, 4, 128])
        c.cs_tab = scratch("cs_tab", [S, 64])
        c.dilO = scratch("dilO", [3, S, 260])
        c.esel_d = din("c_esel", [S, 64])
        c.mmap_d = din("c_mmap", [256, 64])
        c.cm_d = din("c_cm", [128, 2, S])
        c.vis_d = din("c_vis", [S, 64])
        c.add_d = din("c_add", [S, 64])
        setup_rope(c)
    xin, xin_name = c.x, "x"
    for l in range(n_layers):
        last = l == n_layers - 1
        if "s1" in stages:
            linear_stage(c, f"L{l}s1", xin, xin_name, c.proj, "proj", W("w_in")[l], IN_DIM,
                         gain=W("attn_norm")[l])
        if "s2" in stages:
            stage2(c, l)
        if "s3" in stages:
            stage3(c, l)
        if "s4" in stages:
            stage4(c, l)
        if "s5a" in stages:
            linear_stage(c, f"L{l}s5a", c.mix, (lambda t: [f"mixA{t}", f"mixB{t}", f"mixC{t}"]), c.x1, "x1", W("w_out")[l], D,
                         resid=(xin, xin_name))
        if "s5b" in stages:
            if last:
                xout, xout_name = c.out, "out"
            else:
                xout, xout_name = (c.xa, "xa") if l % 2 == 0 else (c.xb, "xb")
            stage5b(c, l, c.x1, "x1", xout, xout_name)
            xin, xin_name = xout, xout_name
    finish(c)
    c.S.emit()
    c.nc_inputs = list(c.in_names)
    nc._mk_inputs = list(c.in_names)
    nc._mk_counts = (dict(c.S.max_counts), {e: len(v) for e, v in c.S.ops.items()})
    return nc


def setup_consts(c):
    sb, Sx = c.sb, c.S
    c.ident_f = sb.t("ident_f", [128, 128], F32)
    c.ident = sb.t("ident", [128, 128], BF16)
    Sx.dma(c.ident_f[:], c.ident_d, writes=["ident_f"])
    Sx.op("dve", lambda e: e.tensor_copy(out=c.ident[:], in_=c.ident_f[:]), reads=["ident_f"], writes=["ident"])


def dbg(c, name, ap, reads):
    import os
    if not os.environ.get("MK_DBG"):
        return
    shape = list(ap.shape)
    d = c.nc.dram_tensor("dbg_" + name, shape, ap.dtype, kind="ExternalOutput").ap()
    c.S.dma(d, ap, reads=reads, writes=["dbg_" + name])


def finish(c):
    Sx = c.S
    res = [r for r, w in Sx.lastw.items() if w.kind == "dma"]
    Sx.op("sp", lambda e: e.nop(), reads=res)


def load_weight_bf16(c, L, Wt, w_ap, nk, ncols, gT=None, seg=2060, tag="W"):
    sb, Sx = c.sb, c.S
    stg = [sb.t("wstg", [128, seg], F32) for _ in range(2)]
    wv = w_ap.rearrange("(k p) n -> p k n", p=128)
    i = 0
    for k in range(nk):
        for c0 in range(0, ncols, seg):
            cw = min(seg, ncols - c0)
            s_ = i % 2
            Sx.dma(stg[s_][:, 0:cw], wv[:, k, c0:c0 + cw], writes=[L + f"stg{s_}"])
            eng = "pool" if i % 2 == 0 else "dve"
            if gT is not None:
                Sx.op(eng, lambda e, s_=s_, k=k, c0=c0, cw=cw: e.tensor_scalar(
                    out=Wt[:, k, c0:c0 + cw], in0=stg[s_][:, 0:cw], scalar1=gT[:, k:k + 1], scalar2=None,
                    op0=ALU.mult), reads=[L + f"stg{s_}", L + "gT"], writes=[L + f"{tag}{k}"])
            else:
                Sx.op(eng, lambda e, s_=s_, k=k, c0=c0, cw=cw: e.tensor_copy(
                    out=Wt[:, k, c0:c0 + cw], in_=stg[s_][:, 0:cw]),
                    reads=[L + f"stg{s_}"], writes=[L + f"{tag}{k}"])
            i += 1
    return [L + f"{tag}{k}" for k in range(nk)]


def linear_stage(c, L, src, src_name, dst, dst_name, w_ap, ncols_total, gain=None, resid=None):
    sb, Sx = c.sb, c.S
    m = sb.mark()
    W = sb.t("lw", [128, 8, ncols_total], BF16)
    xt = [sb.t("xt", [128, D], F32) for _ in range(2)]
    xbf = [sb.t("xbf", [128, D], BF16) for _ in range(2)]
    junk = sb.t("junk", [128, D], BF16)
    hT = [sb.t("hT", [128, 8, 128], BF16) for _ in range(2)]
    pj = [sb.t("pj", [128, ncols_total], F32) for _ in range(2)]
    st = [sb.t("st", [128, 4], F32) for _ in range(2)]
    rt = [sb.t("rt", [128, D], F32) for _ in range(2)] if resid is not None else None
    gT = None
    if gain is not None:
        gT = sb.t("gT", [128, 8], F32)
        Sx.dma(gT[:], gain.rearrange("(k p) -> p k", p=128), writes=[L + "gT"], allow_slow_non_contiguous=True)
    Wres = load_weight_bf16(c, L, W, w_ap, 8, ncols_total, gT=gT, seg=min(2060, ncols_total))
    nch = (ncols_total + 511) // 512
    ncols = [(n * 512, min(512, ncols_total - n * 512)) for n in range(nch)]
    for t in range(NT):
        s_ = t % 2
        r0 = t * 128
        src_res = src_name(t) if callable(src_name) else [f"{src_name}{t}"]
        Sx.dma(xt[s_][:], src[r0:r0 + 128, :], reads=src_res, writes=[L + f"xt{s_}"])
        if resid is not None:
            Sx.dma(rt[s_][:], resid[0][r0:r0 + 128, :], reads=[f"{resid[1]}{t}"], writes=[L + f"rt{s_}"])
        if gain is not None:
            Sx.op("act", lambda e, s_=s_: e.activation(out=junk[:], in_=xt[s_][:], func=AF.Square,
                                                       accum_out=st[s_][:, 0:1]),
                  reads=[L + f"xt{s_}"], writes=[L + "junk", L + f"ss{s_}"])
            Sx.op("dve", lambda e, s_=s_: e.tensor_scalar(out=st[s_][:, 1:2], in0=st[s_][:, 0:1], scalar1=1.0 / D,
                                                          scalar2=EPS, op0=ALU.mult, op1=ALU.add),
                  reads=[L + f"ss{s_}"], writes=[L + f"ms{s_}"])
            Sx.op("act", lambda e, s_=s_: e.activation(out=st[s_][:, 3:4], in_=st[s_][:, 1:2], func=AF.Sqrt),
                  reads=[L + f"ms{s_}"], writes=[L + f"sd{s_}"])
            Sx.op("dve", lambda e, s_=s_: e.reciprocal(out=st[s_][:, 2:3], in_=st[s_][:, 3:4]),
                  reads=[L + f"sd{s_}"], writes=[L + f"rstd{s_}"])
        Sx.op("pool", lambda e, s_=s_: e.tensor_copy(out=xbf[s_][:], in_=xt[s_][:]),
              reads=[L + f"xt{s_}"], writes=[L + f"xbf{s_}"])
        for k in range(8):
            Sx.op("pe", lambda e, s_=s_, k=k: e.matmul(
                out=c.ps[k // 4][:, (k % 4) * 128:(k % 4 + 1) * 128], lhsT=xbf[s_][:, k * 128:(k + 1) * 128],
                rhs=c.ident[:], start=True, stop=True),
                reads=[L + f"xbf{s_}", "ident"], writes=[f"ps{k // 4}"])
        for hh in range(2):
            if hh == 0:
                Sx.op("act", lambda e, s_=s_, hh=hh: e.copy(
                    out=hT[s_][:, hh * 4:(hh + 1) * 4, :], in_=c.ps[hh][:].rearrange("p (k t) -> p k t", k=4)),
                    reads=[f"ps{hh}"], writes=[L + f"hT{s_}{hh}"])
            else:
                Sx.op("dve", lambda e, s_=s_, hh=hh: e.tensor_copy(
                    out=hT[s_][:, hh * 4:(hh + 1) * 4, :], in_=c.ps[hh][:].rearrange("p (k t) -> p k t", k=4)),
                    reads=[f"ps{hh}"], writes=[L + f"hT{s_}{hh}"])
        for n, (c0, cw) in enumerate(ncols):
            b = 2 + n % 4
            for k in range(8):
                Sx.op("pe", lambda e, s_=s_, k=k, b=b, c0=c0, cw=cw: e.matmul(
                    out=c.ps[b][:, 0:cw], lhsT=hT[s_][:, k, :], rhs=W[:, k, c0:c0 + cw],
                    start=(k == 0), stop=(k == 7)),
                    reads=[L + f"hT{s_}{k // 4}", Wres[k]], writes=[f"ps{b}"])
            if resid is not None:
                Sx.op("dve", lambda e, s_=s_, b=b, c0=c0, cw=cw: e.tensor_tensor(
                    out=pj[s_][:, c0:c0 + cw], in0=c.ps[b][:, 0:cw], in1=rt[s_][:, c0:c0 + cw], op=ALU.add),
                    reads=[f"ps{b}", L + f"rt{s_}"], writes=[L + f"pj{s_}_{n}"])
            elif n % 2 == 0:
                Sx.op("act", lambda e, s_=s_, b=b, c0=c0, cw=cw: e.activation(
                    out=pj[s_][:, c0:c0 + cw], in_=c.ps[b][:, 0:cw], func=AF.Copy, scale=st[s_][:, 2:3]),
                    reads=[f"ps{b}", L + f"rstd{s_}"], writes=[L + f"pj{s_}_{n}"])
            else:
                Sx.op("dve", lambda e, s_=s_, b=b, c0=c0, cw=cw: e.tensor_scalar(
                    out=pj[s_][:, c0:c0 + cw], in0=c.ps[b][:, 0:cw], scalar1=st[s_][:, 2:3], scalar2=None,
                    op0=ALU.mult),
                    reads=[f"ps{b}", L + f"rstd{s_}"], writes=[L + f"pj{s_}_{n}"])
        Sx.dma(dst[r0:r0 + 128, :], pj[s_][:], reads=[L + f"pj{s_}_{n}" for n in range(nch)],
               writes=[f"{dst_name}{t}"])
    sb.reset(m)


def stage5b(c, l, src, src_name, dst, dst_name):
    sb, Sx = c.sb, c.S
    L = f"L{l}s5b"
    m = sb.mark()
    NCH = 2 * D_FF // 128
    NP = NCH // 2
    WU = sb.t("wu", [128, 8, 2 * D_FF], BF16)
    WD = sb.t("wd", [128, NP, D], BF16)
    gT = sb.t("gT", [128, 8], F32)
    dwT = sb.t("dwT", [128, NCH, 4], F32)
    halo = sb.t("halo", [128, NCH, 2], F32)
    xt = sb.t("xt", [128, 2, D], F32)
    xbf = sb.t("xbf", [128, 2, D], BF16)
    junk = sb.t("junk", [128, D], BF16)
    st = sb.t("st", [128, 2, 4], F32)
    h2T = sb.t("h2T", [128, 8, 256], BF16)
    ub = [sb.t("ub", [128, 2, 258], F32) for _ in range(2)]
    acc = [sb.t("acc", [128, 2, 256], F32) for _ in range(2)]
    sg = [sb.t("sg", [128, 256], F32) for _ in range(2)]
    aT = [sb.t("aT", [128, 256], BF16) for _ in range(2)]
    ot = sb.t("ot", [128, 2, D], F32)
    Sx.dma(gT[:], c.W("ffn_norm")[l].rearrange("(k p) -> p k", p=128), writes=[L + "gT"],
           allow_slow_non_contiguous=True)
    dwv = c.W("ffn_dw")[l].rearrange("k (c p) -> p c k", p=128)
    for k in range(3):
        for c0 in range(0, NCH, 11):
            Sx.dma(dwT[:, c0:c0 + 11, k:k + 1], dwv[:, c0:c0 + 11, k:k + 1], writes=[L + "dwT"],
                   allow_slow_non_contiguous=True)
    bv = c.W("ffn_dw_b")[l].rearrange("(c p) -> p c", p=128)
    for c0 in range(0, NCH, 11):
        Sx.dma(dwT[:, c0:c0 + 11, 3:4], bv[:, c0:c0 + 11].unsqueeze(2), writes=[L + "dwT"],
               allow_slow_non_contiguous=True)
    WUres = load_weight_bf16(c, L, WU, c.W("w_up")[l], 8, 2 * D_FF, gT=gT, seg=1408, tag="WU")
    WDres = load_weight_bf16(c, L, WD, c.W("w_down")[l], NP, D, seg=1024, tag="WD")
    Sx.op("pool", lambda e: e.memset(halo[:], 0.0), writes=[L + f"halo{cc}" for cc in range(NCH)])
    for j in range(S // 256):
        r0 = j * 256
        Sx.dma(xt[:], src[r0:r0 + 256, :].rearrange("(a p) d -> p a d", p=128),
               reads=[f"{src_name}{2 * j}", f"{src_name}{2 * j + 1}"], writes=[L + "xt"])
        for a in range(2):
            Sx.op("act", lambda e, a=a: e.activation(out=junk[:], in_=xt[:, a, :], func=AF.Square,
                                                     accum_out=st[:, a, 0:1]),
                  reads=[L + "xt"], writes=[L + "junk", L + f"ss{a}"])
            Sx.op("dve", lambda e, a=a: e.tensor_scalar(out=st[:, a, 1:2], in0=st[:, a, 0:1], scalar1=1.0 / D,
                                                        scalar2=EPS, op0=ALU.mult, op1=ALU.add),
                  reads=[L + f"ss{a}"], writes=[L + f"ms{a}"])
            Sx.op("act", lambda e, a=a: e.activation(out=st[:, a, 3:4], in_=st[:, a, 1:2], func=AF.Sqrt),
                  reads=[L + f"ms{a}"], writes=[L + f"sd{a}"])
            Sx.op("dve", lambda e, a=a: e.reciprocal(out=st[:, a, 2:3], in_=st[:, a, 3:4]),
                  reads=[L + f"sd{a}"], writes=[L + f"rstd{a}"])
            Sx.op("pool", lambda e, a=a: e.tensor_scalar(out=xbf[:, a, :], in0=xt[:, a, :], scalar1=st[:, a, 2:3],
                                                         scalar2=None, op0=ALU.mult),
                  reads=[L + "xt", L + f"rstd{a}"], writes=[L + f"xbf{a}"])
            for k in range(8):
                Sx.op("pe", lambda e, a=a, k=k: e.matmul(
                    out=c.ps[k // 4][:, (k % 4) * 128:(k % 4 + 1) * 128], lhsT=xbf[:, a, k * 128:(k + 1) * 128],
                    rhs=c.ident[:], start=True, stop=True),
                    reads=[L + f"xbf{a}", "ident"], writes=[f"ps{k // 4}"])
            for hh in range(2):
                if hh == 0:
                    Sx.op("act", lambda e, a=a, hh=hh: e.copy(
                        out=h2T[:, hh * 4:(hh + 1) * 4, a * 128:(a + 1) * 128],
                        in_=c.ps[hh][:].rearrange("p (k t) -> p k t", k=4)),
                        reads=[f"ps{hh}"], writes=[L + "h2T"])
                else:
                    Sx.op("dve", lambda e, a=a, hh=hh: e.tensor_copy(
                        out=h2T[:, hh * 4:(hh + 1) * 4, a * 128:(a + 1) * 128],
                        in_=c.ps[hh][:].rearrange("p (k t) -> p k t", k=4)),
                        reads=[f"ps{hh}"], writes=[L + "h2T"])
        for i in range(NP):
            s_ = i % 2
            b = 2 + s_
            ve = "pool"
            for gv in range(2):
                cc = i + gv * NP
                for k in range(8):
                    Sx.op("pe", lambda e, b=b, gv=gv, cc=cc, k=k: e.matmul(
                        out=c.ps[b][:, gv * 256:(gv + 1) * 256], lhsT=WU[:, k, cc * 128:(cc + 1) * 128],
                        rhs=h2T[:, k, :], start=(k == 0), stop=(k == 7)),
                        reads=[L + "h2T", WUres[k]], writes=[f"ps{b}"])
            Sx.op("act", lambda e, s_=s_, b=b: e.copy(
                out=ub[s_][:, :, 2:258], in_=c.ps[b][:].rearrange("p (g t) -> p g t", g=2)),
                reads=[f"ps{b}"], writes=[L + f"ub{s_}"])
            for gv in range(2):
                cc = i + gv * NP
                Sx.op(ve, lambda e, s_=s_, gv=gv, cc=cc: e.tensor_copy(out=ub[s_][:, gv, 0:2], in_=halo[:, cc, :]),
                      reads=[L + f"halo{cc}"], writes=[L + f"ubh{s_}{gv}"])
                Sx.op(ve, lambda e, s_=s_, gv=gv, cc=cc: e.tensor_scalar(
                    out=acc[s_][:, gv, :], in0=ub[s_][:, gv, 2:258], scalar1=dwT[:, cc, 2:3], scalar2=dwT[:, cc, 3:4],
                    op0=ALU.mult, op1=ALU.add),
                    reads=[L + f"ub{s_}", L + "dwT"], writes=[L + f"acc{s_}{gv}"])
                Sx.op("dve", lambda e, s_=s_, gv=gv, cc=cc: e.scalar_tensor_tensor(
                    out=acc[s_][:, gv, :], in0=ub[s_][:, gv, 1:257], scalar=dwT[:, cc, 1:2], in1=acc[s_][:, gv, :],
                    op0=ALU.mult, op1=ALU.add),
                    reads=[L + f"ub{s_}", L + f"ubh{s_}{gv}", L + "dwT"], writes=[L + f"acc{s_}{gv}"])
                Sx.op("dve", lambda e, s_=s_, gv=gv, cc=cc: e.scalar_tensor_tensor(
                    out=acc[s_][:, gv, :], in0=ub[s_][:, gv, 0:256], scalar=dwT[:, cc, 0:1], in1=acc[s_][:, gv, :],
                    op0=ALU.mult, op1=ALU.add),
                    reads=[L + f"ub{s_}", L + f"ubh{s_}{gv}", L + "dwT"], writes=[L + f"acc{s_}{gv}"])
                Sx.op(ve, lambda e, s_=s_, gv=gv, cc=cc: e.tensor_copy(out=halo[:, cc, :], in_=ub[s_][:, gv, 256:258]),
                      reads=[L + f"ub{s_}"], writes=[L + f"halo{cc}"])
            Sx.op("act", lambda e, s_=s_: e.activation(out=sg[s_][:], in_=acc[s_][:, 0, :], func=AF.Silu),
                  reads=[L + f"acc{s_}0"], writes=[L + f"sg{s_}"])
            Sx.op(ve, lambda e, s_=s_: e.tensor_tensor(out=aT[s_][:], in0=sg[s_][:], in1=acc[s_][:, 1, :],
                                                       op=ALU.mult),
                  reads=[L + f"sg{s_}", L + f"acc{s_}1"], writes=[L + f"aT{s_}"])
            for a in range(2):
                for hf in range(2):
                    b2 = 4 + a * 2 + hf
                    Sx.op("pe", lambda e, s_=s_, a=a, hf=hf, b2=b2, i=i: e.matmul(
                        out=c.ps[b2][:], lhsT=aT[s_][:, a * 128:(a + 1) * 128], rhs=WD[:, i, hf * 512:(hf + 1) * 512],
                        start=(i == 0), stop=(i == NP - 1)),
                        reads=[L + f"aT{s_}", WDres[i]], writes=[f"ps{b2}"])
        for a in range(2):
            for hf in range(2):
                b2 = 4 + a * 2 + hf
                Sx.op("dve", lambda e, a=a, hf=hf, b2=b2: e.tensor_tensor(
                    out=ot[:, a, hf * 512:(hf + 1) * 512], in0=c.ps[b2][:], in1=xt[:, a, hf * 512:(hf + 1) * 512],
                    op=ALU.add),
                    reads=[f"ps{b2}", L + "xt"], writes=[L + "ot"])
        Sx.dma(dst[r0:r0 + 256, :].rearrange("(a p) d -> p a d", p=128), ot[:],
               reads=[L + "ot"], writes=[f"{dst_name}{2 * j}", f"{dst_name}{2 * j + 1}"])
    sb.reset(m)


def stage2(c, l):
    sb, Sx = c.sb, c.S
    L = f"L{l}s2"
    m = sb.mark()
    PAD = 30
    aT = sb.t("aT", [128, 2, PAD + S], F32)
    acc = sb.t("acc", [128, 2, S], F32)
    dwT = sb.t("dwT", [128, 2, 32], F32)
    lnp = sb.t("lnp", [128, 2, 2], F32)
    pwf = sb.t("pwf", [128, 2, 256], F32)
    pw = sb.t("pw", [128, 2, 256], BF16)
    pwb = sb.t("pwb", [128, 256], F32)
    ones = sb.t("ones", [128, 128], F32)
    vg = [sb.t("vg", [128, 512], F32) for _ in range(2)]
    sgm = [sb.t("sgm", [128, 256], F32) for _ in range(2)]
    atm = [sb.t("atm", [128, 256], F32) for _ in range(2)]
    sq = sb.t("sq", [128, 2, 512], F32)
    mean = sb.t("mean", [128, 512], F32)
    tmp = sb.t("tmp", [128, 512], F32)
    var = sb.t("var", [128, 512], F32)
    rstd = sb.t("rstd", [128, 512], F32)
    yn = sb.t("yn", [128, 2, 512], F32)
    zT = sb.t("zT", [128, 2, 512], BF16)
    yo = [sb.t("yo", [128, 256], F32) for _ in range(2)]
    dv = c.W("conv_dw")[l].rearrange("k (c p) -> p c k", p=128)
    for ch in range(2):
        Sx.dma(dwT[:, ch:ch + 1, 0:31], dv[:, ch:ch + 1, :], writes=[L + "dwT"], allow_slow_non_contiguous=True)
    Sx.dma(dwT[:, :, 31:32], c.W("conv_dw_b")[l].rearrange("(c p) -> p c", p=128).unsqueeze(2),
           writes=[L + "dwT"], allow_slow_non_contiguous=True)
    Sx.dma(lnp[:, :, 0:1], c.W("conv_ln_g")[l].rearrange("(c p) -> p c", p=128).unsqueeze(2),
           writes=[L + "lnp"], allow_slow_non_contiguous=True)
    Sx.dma(lnp[:, :, 1:2], c.W("conv_ln_b")[l].rearrange("(c p) -> p c", p=128).unsqueeze(2),
           writes=[L + "lnp"], allow_slow_non_contiguous=True)
    Sx.dma(pwf[:], c.W("conv_pw")[l].rearrange("(k p) n -> p k n", p=128), writes=[L + "pwf"])
    Sx.op("pool", lambda e: e.tensor_copy(out=pw[:], in_=pwf[:]), reads=[L + "pwf"], writes=[L + "pw"])
    Sx.dma(pwb[:], c.W("conv_pw_b")[l].partition_broadcast(128), writes=[L + "pwb"])
    Sx.op("pool", lambda e: e.memset(ones[:], 1.0), writes=[L + "ones"])
    Sx.op("pool", lambda e: e.memset(aT[:, :, 0:PAD], 0.0), writes=[L + "aTpad"])
    for t in range(NT):
        s_ = t % 2
        r0 = t * 128
        Sx.dma(vg[s_][:], c.proj[r0:r0 + 128, 0:512], reads=[f"proj{t}"], writes=[L + f"vg{s_}"])
        Sx.op("act", lambda e, s_=s_: e.activation(out=sgm[s_][:], in_=vg[s_][:, 256:512], func=AF.Sigmoid),
              reads=[L + f"vg{s_}"], writes=[L + f"sgm{s_}"])
        Sx.op("dve", lambda e, s_=s_: e.tensor_tensor(out=atm[s_][:], in0=vg[s_][:, 0:256], in1=sgm[s_][:],
                                                      op=ALU.mult),
              reads=[L + f"vg{s_}", L + f"sgm{s_}"], writes=[L + f"atm{s_}"])
        for ch in range(2):
            b = (2 * t + ch) % 2
            Sx.op("pe", lambda e, s_=s_, ch=ch, b=b: e.transpose(
                out=c.ps[b][:, 0:128], in_=atm[s_][:, ch * 128:(ch + 1) * 128], identity=c.ident_f[:]),
                reads=[L + f"atm{s_}", "ident_f"], writes=[f"ps{b}"])
            if ch == 0:
                Sx.op("act", lambda e, ch=ch, b=b, r0=r0: e.copy(
                    out=aT[:, ch, PAD + r0:PAD + r0 + 128], in_=c.ps[b][:, 0:128]),
                    reads=[f"ps{b}"], writes=[L + f"aT{t // 8}_{ch}"])
            else:
                Sx.op("pool" if False else "dve", lambda e, ch=ch, b=b, r0=r0: e.tensor_copy(
                    out=aT[:, ch, PAD + r0:PAD + r0 + 128], in_=c.ps[b][:, 0:128]),
                    reads=[f"ps{b}"], writes=[L + f"aT{t // 8}_{ch}"])
    for blk in range(4):
        t0 = blk * 1024
        for ch in range(2):
            rd = [L + f"aT{bb}_{ch}" for bb in range(max(0, blk - 1), blk + 1)] + [L + "aTpad", L + "dwT"]
            Sx.op("pool", lambda e, ch=ch, t0=t0: e.tensor_scalar(
                out=acc[:, ch, t0:t0 + 1024], in0=aT[:, ch, PAD + t0:PAD + t0 + 1024], scalar1=dwT[:, ch, 30:31],
                scalar2=dwT[:, ch, 31:32], op0=ALU.mult, op1=ALU.add),
                reads=rd, writes=[L + f"acc{blk}_{ch}"])
            for k in range(30):
                Sx.op("dve", lambda e, ch=ch, t0=t0, k=k: e.scalar_tensor_tensor(
                    out=acc[:, ch, t0:t0 + 1024], in0=aT[:, ch, k + t0:k + t0 + 1024], scalar=dwT[:, ch, k:k + 1],
                    in1=acc[:, ch, t0:t0 + 1024], op0=ALU.mult, op1=ALU.add),
                    reads=rd, writes=[L + f"acc{blk}_{ch}"])
    for q in range(8):
        t0 = q * 512
        blk = q // 2
        for ch in range(2):
            Sx.op("pool", lambda e, ch=ch, t0=t0: e.tensor_tensor(
                out=sq[:, ch, :], in0=acc[:, ch, t0:t0 + 512], in1=acc[:, ch, t0:t0 + 512], op=ALU.mult),
                reads=[L + f"acc{blk}_{ch}"], writes=[L + f"sq{ch}"])
        for ch in range(2):
            Sx.op("pe", lambda e, ch=ch, t0=t0: e.matmul(
                out=c.ps[2][:], lhsT=ones[:], rhs=acc[:, ch, t0:t0 + 512], start=(ch == 0), stop=(ch == 1)),
                reads=[L + "ones", L + f"acc{blk}_{ch}"], writes=["ps2"])
        for ch in range(2):
            Sx.op("pe", lambda e, ch=ch: e.matmul(
                out=c.ps[3][:], lhsT=ones[:], rhs=sq[:, ch, :], start=(ch == 0), stop=(ch == 1)),
                reads=[L + "ones", L + f"sq{ch}"], writes=["ps3"])
        Sx.op("act", lambda e: e.activation(out=mean[:], in_=c.ps[2][:], func=AF.Copy, scale=1.0 / 256),
              reads=["ps2"], writes=[L + "mean"])
        Sx.op("pool", lambda e: e.tensor_tensor(out=tmp[:], in0=mean[:], in1=mean[:], op=ALU.mult),
              reads=[L + "mean"], writes=[L + "tmp"])
        Sx.op("dve", lambda e: e.scalar_tensor_tensor(out=var[:], in0=c.ps[3][:], scalar=1.0 / 256, in1=tmp[:],
                                                      op0=ALU.mult, op1=ALU.subtract),
              reads=["ps3", L + "tmp"], writes=[L + "var"])
        Sx.op("dve", lambda e: e.tensor_scalar(out=var[:], in0=var[:], scalar1=EPS, scalar2=None, op0=ALU.add),
              reads=[L + "var"], writes=[L + "var"])
        Sx.op("act", lambda e: e.activation(out=tmp[:], in_=var[:], func=AF.Sqrt),
              reads=[L + "var"], writes=[L + "tmp"])
        Sx.op("dve", lambda e: e.reciprocal(out=rstd[:], in_=tmp[:]), reads=[L + "tmp"], writes=[L + "rstd"])
        for ch in range(2):
            Sx.op("pool", lambda e, ch=ch, t0=t0: e.tensor_tensor(
                out=yn[:, ch, :], in0=acc[:, ch, t0:t0 + 512], in1=mean[:], op=ALU.subtract),
                reads=[L + f"acc{blk}_{ch}", L + "mean"], writes=[L + f"yn{ch}"])
            Sx.op("dve", lambda e, ch=ch: e.tensor_tensor(
                out=yn[:, ch, :], in0=yn[:, ch, :], in1=rstd[:], op=ALU.mult),
                reads=[L + f"yn{ch}", L + "rstd"], writes=[L + f"yn{ch}"])
            Sx.op("act", lambda e, ch=ch: e.activation(
                out=zT[:, ch, :], in_=yn[:, ch, :], func=AF.Silu, scale=lnp[:, ch, 0:1], bias=lnp[:, ch, 1:2]),
                reads=[L + f"yn{ch}", L + "lnp"], writes=[L + f"zT{ch}"])
        for a in range(4):
            s_ = a % 2
            b = 4 + s_
            for ch in range(2):
                Sx.op("pe", lambda e, a=a, ch=ch, b=b: e.matmul(
                    out=c.ps[b][:, 0:256], lhsT=zT[:, ch, a * 128:(a + 1) * 128], rhs=pw[:, ch, :],
                    start=(ch == 0), stop=(ch == 1)),
                    reads=[L + f"zT{ch}", L + "pw"], writes=[f"ps{b}"])
            Sx.op("dve", lambda e, s_=s_, b=b: e.tensor_tensor(out=yo[s_][:], in0=c.ps[b][:, 0:256], in1=pwb[:],
                                                            op=ALU.add),
                  reads=[f"ps{b}", L + "pwb"], writes=[L + f"yo{s_}"])
            tt = q * 4 + a
            Sx.dma(c.mix[tt * 128:(tt + 1) * 128, 0:256], yo[s_][:], reads=[L + f"yo{s_}"], writes=[f"mixA{tt}"])
    sb.reset(m)


def bc(ap, shape, axis):
    return ap.unsqueeze(axis).to_broadcast(list(shape))


def setup_rope(c):
    sb, Sx = c.sb, c.S
    m = sb.mark()
    L = "rope"
    pos_i = sb.t("pos_i", [128, NT], I32)
    pos_f = sb.t("pos_f", [128, NT], F32)
    invf = sb.t("invf", [128, 32], F32)
    ang = sb.t("ang", [128, NT, 32], F32)
    red = sb.t("red", [128, NT, 32], F32)
    cs = sb.t("cs", [128, NT, 64], F32)
    negpi = sb.t("negpi", [128, 1], F32)
    pview = c.pos.rearrange("(t p) o -> p (t o)", p=128)
    for q4 in range(4):
        Sx.dma(pos_i[:, q4 * 8:(q4 + 1) * 8], pview[:, q4 * 8:(q4 + 1) * 8], writes=[L + "pos_i"],
               allow_slow_non_contiguous=True)
    Sx.dma(invf[:], c.invf_d, writes=[L + "invf"])
    Sx.op("pool", lambda e: e.memset(negpi[:], -float(np.pi)), writes=[L + "negpi"])
    Sx.op("dve", lambda e: e.tensor_copy(out=pos_f[:], in_=pos_i[:]), reads=[L + "pos_i"], writes=[L + "pos_f"])
    Sx.op("dve", lambda e: e.tensor_tensor(out=ang[:], in0=bc(pos_f[:], [128, NT, 32], 2),
                                           in1=bc(invf[:], [128, NT, 32], 1), op=ALU.mult),
          reads=[L + "pos_f", L + "invf"], writes=[L + "ang"])
    TWO_PI = float(2 * np.pi)
    C1, C2, C3 = 6.28125, 0.0019350051879882812, 3.019916050561733e-07
    kf = sb.t("kf", [128, NT, 32], F32)
    ki = sb.t("ki", [128, NT, 32], I32)
    msk = sb.t("msk", [128, NT, 32], F32)
    Sx.op("dve", lambda e: e.tensor_scalar(out=kf[:], in0=ang[:], scalar1=1.0 / TWO_PI, scalar2=None, op0=ALU.mult),
          reads=[L + "ang"], writes=[L + "kf"])
    Sx.op("dve", lambda e: e.tensor_copy(out=ki[:], in_=kf[:]), reads=[L + "kf"], writes=[L + "ki"])
    Sx.op("dve", lambda e: e.tensor_copy(out=kf[:], in_=ki[:]), reads=[L + "ki"], writes=[L + "kf"])
    for cst in (C1, C2, C3):
        Sx.op("dve", lambda e, cst=cst: e.scalar_tensor_tensor(out=ang[:], in0=kf[:], scalar=-cst, in1=ang[:],
                                                               op0=ALU.mult, op1=ALU.add),
              reads=[L + "ang", L + "kf"], writes=[L + "ang"])
    PI = float(np.pi)
    PI_LO = 3.1415925
    for which, shift in ((0, 0.5 * np.pi), (1, 0.0)):
        Sx.op("dve", lambda e, shift=shift: e.tensor_scalar(out=red[:], in0=ang[:], scalar1=float(shift), scalar2=None,
                                                            op0=ALU.add),
              reads=[L + "ang"], writes=[L + "red"])
        for cmp_op, thr, adj in ((ALU.is_gt, PI, -TWO_PI), (ALU.is_lt, -PI, TWO_PI)):
            Sx.op("dve", lambda e, cmp_op=cmp_op, thr=thr: e.tensor_scalar(out=msk[:], in0=red[:], scalar1=thr,
                                                                           scalar2=None, op0=cmp_op),
                  reads=[L + "red"], writes=[L + "msk"])
            Sx.op("dve", lambda e, adj=adj: e.scalar_tensor_tensor(out=red[:], in0=msk[:], scalar=adj, in1=red[:],
                                                                   op0=ALU.mult, op1=ALU.add),
                  reads=[L + "red", L + "msk"], writes=[L + "red"])
        Sx.op("dve", lambda e: e.tensor_scalar(out=red[:], in0=red[:], scalar1=-PI_LO, scalar2=PI_LO,
                                               op0=ALU.max, op1=ALU.min),
              reads=[L + "red"], writes=[L + "red"])
        Sx.op("act", lambda e, which=which: e.activation(out=cs[:, :, which * 32:(which + 1) * 32], in_=red[:],
                                                         func=AF.Sin),
              reads=[L + "red"], writes=[L + "cs"])
    cview = c.cs_tab.rearrange("(t p) n -> p t n", p=128)
    for q4 in range(4):
        Sx.dma(cview[:, q4 * 8:(q4 + 1) * 8, :], cs[:, q4 * 8:(q4 + 1) * 8, :], reads=[L + "cs"], writes=["cs_tab"])
    sb.reset(m)


def norm_rope(c, L, ve, x, gn, cs, out, nh, sq, tt, st):
    Sx = c.S
    R = L["scr"]
    sh = [128, nh, 64]
    hs = [128, nh, 32]
    Sx.op(ve, lambda e: e.tensor_tensor(out=sq, in0=x, in1=x, op=ALU.mult), reads=L["x"], writes=[R + "sq"])
    Sx.op("dve", lambda e: e.tensor_reduce(out=st[:, :, 0], in_=sq, axis=AX.X, op=ALU.add),
          reads=[R + "sq"], writes=[R + "ss"])
    Sx.op(ve, lambda e: e.tensor_scalar(out=st[:, :, 1], in0=st[:, :, 0], scalar1=1.0 / 64, scalar2=EPS,
                                        op0=ALU.mult, op1=ALU.add), reads=[R + "ss"], writes=[R + "ms"])
    Sx.op("act", lambda e: e.activation(out=st[:, :, 2], in_=st[:, :, 1], func=AF.Ln),
          reads=[R + "ms"], writes=[R + "ln"])
    Sx.op("act", lambda e: e.activation(out=st[:, :, 3], in_=st[:, :, 2], func=AF.Exp, scale=-0.5),
          reads=[R + "ln"], writes=[R + "rstd"])
    Sx.op(ve, lambda e: e.tensor_tensor(out=sq, in0=x, in1=bc(st[:, :, 3], sh, 2), op=ALU.mult),
          reads=L["x"] + [R + "rstd", R + "ss"], writes=[R + "sq"])
    Sx.op(ve, lambda e: e.tensor_tensor(out=sq, in0=sq, in1=bc(gn, sh, 1), op=ALU.mult),
          reads=[R + "sq"] + L["gn"], writes=[R + "sq"])
    cosb = bc(cs[:, 0:32], hs, 1)
    sinb = bc(cs[:, 32:64], hs, 1)
    x1 = sq[:, :, 0:32]
    x2 = sq[:, :, 32:64]
    Sx.op(ve, lambda e: e.tensor_tensor(out=tt[:, :, 0, :], in0=x1, in1=cosb, op=ALU.mult),
          reads=[R + "sq"] + L["cs"], writes=[R + "t0"])
    Sx.op(ve, lambda e: e.tensor_tensor(out=tt[:, :, 1, :], in0=x2, in1=sinb, op=ALU.mult),
          reads=[R + "sq"] + L["cs"], writes=[R + "t1"])
    Sx.op(ve, lambda e: e.tensor_tensor(out=out[:, :, 0:32], in0=tt[:, :, 0, :], in1=tt[:, :, 1, :], op=ALU.subtract),
          reads=[R + "t0", R + "t1"], writes=L["out"])
    Sx.op(ve, lambda e: e.tensor_tensor(out=tt[:, :, 0, :], in0=x2, in1=cosb, op=ALU.mult),
          reads=[R + "sq"] + L["cs"], writes=[R + "t0"])
    Sx.op(ve, lambda e: e.tensor_tensor(out=tt[:, :, 1, :], in0=x1, in1=sinb, op=ALU.mult),
          reads=[R + "sq"] + L["cs"], writes=[R + "t1"])
    Sx.op(ve, lambda e: e.tensor_tensor(out=out[:, :, 32:64], in0=tt[:, :, 0, :], in1=tt[:, :, 1, :], op=ALU.add),
          reads=[R + "t0", R + "t1"], writes=L["out"])


DIL_R = (1, 4, 16)


def stage4(c, l):
    sb, Sx = c.sb, c.S
    L = f"L{l}s4"
    m = sb.mark()
    NS = 3
    gq = sb.t("gq", [128, 64], F32)
    gk = sb.t("gk", [128, 64], F32)
    mk_f = sb.t("mk_f", [128, 4, 128], F32)
    mk = sb.t("mk", [128, 4, 128], BF16)
    X = [sb.t("X", [128, 3, 256], F32) for _ in range(2)]
    CS = [sb.t("CS", [128, 64], F32) for _ in range(2)]
    sq = [sb.t("sq", [128, 8, 64], F32) for _ in range(2)]
    tt = [sb.t("tt", [128, 8, 2, 32], F32) for _ in range(2)]
    stt = [sb.t("stt", [128, 8, 4], F32) for _ in range(2)]
    QK = [sb.t("QK", [128, 8, 64], BF16) for _ in range(2)]
    QZ = [sb.t("QZ", [128, 4, 128], BF16) for _ in range(2)]
    QKz = [sb.t("QKz", [128, 4, 128], BF16) for _ in range(2)]
    KT = [sb.t("KT", [128, 2, 128], BF16) for _ in range(NS)]
    V = [sb.t("V", [128, 4, 65], BF16) for _ in range(NS)]
    PT = [sb.t("PT", [128, 4, 2, 128], BF16) for _ in range(2)]
    O = [sb.t("O", [128, 4, 65], F32) for _ in range(2)]
    Sx.dma(gq[:], c.W("dil_q_norm")[l].partition_broadcast(128), writes=[L + "gq"])
    Sx.dma(gk[:], c.W("dil_k_norm")[l].partition_broadcast(128), writes=[L + "gk"])
    Sx.dma(mk_f[:], c.masks_d, writes=[L + "mk_f"])
    Sx.op("pool", lambda e: e.tensor_copy(out=mk[:], in_=mk_f[:]), reads=[L + "mk_f"], writes=[L + "mk"])
    for s_ in range(NS):
        Sx.op("pool", lambda e, s_=s_: e.memset(V[s_][:, :, 64:65], 1.0), writes=[L + f"Vone{s_}"])
    for s_ in range(2):
        Sx.op("pool", lambda e, s_=s_: e.memset(QKz[s_][:], 0.0), writes=[L + f"QKzz{s_}"])
    it = 0
    for g, r in enumerate(DIL_R):
        Lg = S // r
        tps = Lg // 128
        pv = c.proj.rearrange("(m r) n -> r m n", r=r)
        cv = c.cs_tab.rearrange("(m r) n -> r m n", r=r)
        dv = c.dilO[g].rearrange("(m r) n -> r m n", r=r)
        for T in range(NT):
            cc, mt = T // tps, T % tps
            s2 = it % 2
            s3 = it % NS
            sp = (it - 1) % NS
            ve = "dve" if it % 2 == 0 else "pool"
            lo = ((mt * 128) * r + cc) // 128
            hi = ((mt * 128 + 127) * r + cc) // 128
            nat = list(range(lo, hi + 1))
            Sx.dma(X[s2][:], pv[cc, mt * 128:(mt + 1) * 128, D_Q:D_Q + 2304].rearrange(
                "p (i gg j) -> p i gg j", i=3, gg=3)[:, :, g, :],
                reads=[f"proj{t}" for t in nat], writes=[L + f"X{s2}"])
            Sx.dma(CS[s2][:], cv[cc, mt * 128:(mt + 1) * 128, :], reads=["cs_tab"], writes=[L + f"CS{s2}"])
            for qk in range(2):
                norm_rope(c, {"x": [L + f"X{s2}"], "out": [L + f"QK{s2}_{qk}"], "gn": [L + ("gq", "gk")[qk]],
                              "cs": [L + f"CS{s2}"], "scr": L + f"nr{s2}{qk}"},
                          ve, X[s2][:, qk, :].rearrange("p (h d) -> p h d", h=4), (gq, gk)[qk][:], CS[s2][:],
                          QK[s2][:, qk * 4:(qk + 1) * 4, :], 4, sq[s2][:, qk * 4:(qk + 1) * 4, :],
                          tt[s2][:, qk * 4:(qk + 1) * 4, :, :], stt[s2][:, qk * 4:(qk + 1) * 4, :])
            Sx.op("pool" if ve == "dve" else "dve", lambda e, s2=s2, s3=s3: e.tensor_copy(
                out=V[s3][:, :, 0:64], in_=X[s2][:, 2, :].rearrange("p (h d) -> p h d", h=4)),
                reads=[L + f"X{s2}"], writes=[L + f"V{s3}"])
            qv = QK[s2][:, 0:4, :].rearrange("p (b two) d -> p b two d", two=2)
            zv = QKz[s2][:].rearrange("p (b two) n -> p b two n", two=2)
            for par in range(2):
                Sx.op("pool", lambda e, par=par, qv=qv, zv=zv: e.tensor_copy(
                    out=zv[:, :, par, par * 64:(par + 1) * 64], in_=qv[:, :, par, :]),
                    reads=[L + f"QK{s2}_0", L + f"QKzz{s2}"], writes=[L + f"QKz{s2}"])
            for h in range(4):
                Sx.op("pe", lambda e, s2=s2, h=h: e.matmul(
                    out=c.ps[0][:, h * 128:(h + 1) * 128], lhsT=QKz[s2][:, h, :], rhs=c.ident[:],
                    start=True, stop=True),
                    reads=[L + f"QKz{s2}", "ident"], writes=["ps0"])
            for blk in range(2):
                Sx.op("pe", lambda e, s2=s2, blk=blk: e.matmul(
                    out=c.ps[4][:, blk * 128:(blk + 1) * 128],
                    lhsT=QK[s2][:, 4 + 2 * blk:6 + 2 * blk, :].rearrange("p h d -> p (h d)"), rhs=c.ident[:],
                    start=True, stop=True),
                    reads=[L + f"QK{s2}_1", "ident"], writes=["ps4"])
            Sx.op("act", lambda e, s2=s2: e.copy(out=QZ[s2][:], in_=c.ps[0][:].rearrange("p (b t) -> p b t", b=4)),
                  reads=["ps0"], writes=[L + f"QZ{s2}"])
            Sx.op("dve", lambda e, s3=s3: e.tensor_copy(
                out=KT[s3][:], in_=c.ps[4][:, 0:256].rearrange("p (b t) -> p b t", b=2)),
                reads=["ps4"], writes=[L + f"KT{s3}"])
            first = mt == 0
            ks = [s3 if first else sp, s3]
            for h in range(4):
                pb = (h % 2) * 64
                for kk in range(2):
                    bank = 1 + h // 2
                    col = ((h % 2) * 2 + kk) * 128
                    Sx.op("pe", lambda e, h=h, kk=kk, bank=bank, col=col, s2=s2, ksl=ks[kk]: e.matmul(
                        out=c.ps[bank][:, col:col + 128], lhsT=KT[ksl][:, h // 2, :],
                        rhs=QZ[s2][:, h, :], start=True, stop=True),
                        reads=[L + f"KT{ks[kk]}", L + f"QZ{s2}"], writes=[f"ps{bank}"])
            for hp in range(2):
                Sx.op("act", lambda e, hp=hp, s2=s2: e.activation(
                    out=PT[s2][:, 2 * hp:2 * hp + 2, :, :].rearrange("p h k t -> p (h k t)"), in_=c.ps[1 + hp][:],
                    func=AF.Exp, scale=SCALE),
                    reads=[f"ps{1 + hp}"], writes=[L + f"PT{s2}_{hp}"])
            if first:
                for hp in range(2):
                    for kk, mi in ((0, 2), (1, 1)):
                        Sx.op("pool", lambda e, hp=hp, kk=kk, mi=mi, s2=s2: e.tensor_tensor(
                            out=PT[s2][:, 2 * hp:2 * hp + 2, kk, :], in0=PT[s2][:, 2 * hp:2 * hp + 2, kk, :],
                            in1=bc(mk[:, mi, :], [128, 2, 128], 1), op=ALU.mult),
                            reads=[L + f"PT{s2}_{hp}", L + "mk"], writes=[L + f"PT{s2}_{hp}"])
            else:
                for hp in range(2):
                    Sx.op("pool", lambda e, hp=hp, s2=s2: e.tensor_tensor(
                        out=PT[s2][:, 2 * hp:2 * hp + 2, :, :], in0=PT[s2][:, 2 * hp:2 * hp + 2, :, :],
                        in1=bc(mk[:, 0:2, :], [128, 2, 2, 128], 1), op=ALU.mult),
                        reads=[L + f"PT{s2}_{hp}", L + "mk"], writes=[L + f"PT{s2}_{hp}"])
            for h in range(4):
                for kk in range(2):
                    Sx.op("pe", lambda e, h=h, kk=kk, s2=s2, ksl=ks[kk]: e.matmul(
                        out=c.ps[3][:, h * 65:(h + 1) * 65], lhsT=PT[s2][:, h, kk, :], rhs=V[ksl][:, h, :],
                        start=(h == 0 and kk == 0), stop=(h == 3 and kk == 1), skip_group_check=True),
                        reads=[L + f"PT{s2}_{h // 2}", L + f"V{ks[kk]}", L + f"Vone{ks[kk]}"], writes=["ps3"])
            Sx.op("dve", lambda e, s2=s2: e.tensor_copy(out=O[s2][:].rearrange("p h d -> p (h d)"),
                                                        in_=c.ps[3][:, 0:260]),
                  reads=["ps3"], writes=[L + f"O{s2}"])
            Sx.dma(dv[cc, mt * 128:(mt + 1) * 128, :], O[s2][:].rearrange("p h d -> p (h d)"),
                   reads=[L + f"O{s2}"], writes=[f"dilO{g}_T{T}"])
            it += 1
    U = [sb.t("U", [128, 3, 260], F32) for _ in range(2)]
    rd = [sb.t("rd", [128, 4], F32) for _ in range(2)]
    yc = [sb.t("yc", [128, 4, 64], F32) for _ in range(2)]
    for t in range(NT):
        s_ = t % 2
        r0 = t * 128
        Sx.dma(U[s_][:], c.dilO[:, r0:r0 + 128, :].rearrange("g p n -> p g n"),
               reads=[f"dilO{g}_T{T}" for g in range(3) for T in range(NT)], writes=[L + f"U{s_}"])
        Sx.op("pool", lambda e, s_=s_: e.tensor_tensor(out=U[s_][:, 0, :], in0=U[s_][:, 0, :], in1=U[s_][:, 1, :],
                                                      op=ALU.add), reads=[L + f"U{s_}"], writes=[L + f"U{s_}"])
        Sx.op("pool", lambda e, s_=s_: e.tensor_tensor(out=U[s_][:, 0, :], in0=U[s_][:, 0, :], in1=U[s_][:, 2, :],
                                                      op=ALU.add), reads=[L + f"U{s_}"], writes=[L + f"U{s_}"])
        Uv = U[s_][:, 0, :].rearrange("p (h d) -> p h d", h=4)
        Sx.op("dve", lambda e, s_=s_, Uv=Uv: e.reciprocal(out=rd[s_][:], in_=Uv[:, :, 64]),
              reads=[L + f"U{s_}"], writes=[L + f"rd{s_}"])
        Sx.op("dve", lambda e, s_=s_, Uv=Uv: e.tensor_tensor(out=yc[s_][:], in0=Uv[:, :, 0:64],
                                                             in1=bc(rd[s_][:], [128, 4, 64], 2), op=ALU.mult),
              reads=[L + f"U{s_}", L + f"rd{s_}"], writes=[L + f"yc{s_}"])
        Sx.dma(c.mix[r0:r0 + 128, 768:1024], yc[s_][:].rearrange("p h d -> p (h d)"),
               reads=[L + f"yc{s_}"], writes=[f"mixC{t}"])
    sb.reset(m)


def stage3(c, l):
    sb, Sx = c.sb, c.S
    L = f"L{l}s3"
    m = sb.mark()
    KAs = sb.t("KAs", [128, 2, S], BF16)
    KWA = sb.t("KWA", [128, 2, S], BF16)
    KCr = sb.t("KCr", [128, 2, S], BF16)
    VS = sb.t("VS", [128, NT, 2, 65], BF16)
    VW = sb.t("VW", [128, NT, 2, 65], BF16)
    KCA = sb.t("KCA", [128, 2, 256], BF16)
    VCa = sb.t("VCa", [128, 2, 2, 129], BF16)
    gq = sb.t("gq", [128, 64], F32)
    gk = sb.t("gk", [128, 64], F32)
    Ebf = sb.t("Ebf", [128, NT, 64], BF16)
    madd = sb.t("madd", [128, 2, 128], BF16)
    Sx.dma(gq[:], c.W("nsa_q_norm")[l].partition_broadcast(128), writes=[L + "gq"])
    Sx.dma(gk[:], c.W("nsa_k_norm")[l].partition_broadcast(128), writes=[L + "gk"])
    m1 = sb.mark()
    Ef = sb.t("Ef", [128, NT, 64], F32)
    mk_f = sb.t("mk_f", [128, 4, 128], F32)
    Sx.dma(Ef[:], c.esel_d.rearrange("(t p) n -> p t n", p=128), writes=[L + "Ef"])
    Sx.op("pool", lambda e: e.tensor_copy(out=Ebf[:], in_=Ef[:]), reads=[L + "Ef"], writes=[L + "Ebf"])
    Sx.dma(mk_f[:], c.masks_d, writes=[L + "mk_f"])
    Sx.op("dve", lambda e: e.tensor_scalar(out=madd[:, 0, :], in0=mk_f[:, 1, :], scalar1=-NEG, scalar2=NEG,
                                           op0=ALU.mult, op1=ALU.add), reads=[L + "mk_f"], writes=[L + "madd"])
    Sx.op("dve", lambda e: e.tensor_scalar(out=madd[:, 1, :], in0=mk_f[:, 3, :], scalar1=-NEG, scalar2=NEG,
                                           op0=ALU.mult, op1=ALU.add), reads=[L + "mk_f"], writes=[L + "madd"])
    Sx.op("pool", lambda e: e.memset(VS[:, :, :, 64:65], 1.0), writes=[L + "VSone"])
    Sx.op("pool", lambda e: e.memset(VW[:, :, :, 64:65], 1.0), writes=[L + "VWone"])
    Sx.mark("s3 setup done")
    X = [sb.t("X", [128, 768], F32) for _ in range(2)]
    CS = [sb.t("CS", [128, 64], F32) for _ in range(2)]
    sq = [sb.t("sq", [128, 4, 64], F32) for _ in range(2)]
    tt = [sb.t("tt", [128, 4, 2, 32], F32) for _ in range(2)]
    stt = [sb.t("stt", [128, 4, 4], F32) for _ in range(2)]
    KMs = [sb.t("KMs", [128, 2, 128], BF16) for _ in range(2)]
    KMw = [sb.t("KMw", [128, 2, 128], BF16) for _ in range(2)]
    KMc = [sb.t("KMc", [128, 2, 128], BF16) for _ in range(2)]
    for s_ in range(2):
        Sx.op("pool", lambda e, s_=s_: e.memset(KMw[s_][:], 0.0), writes=[L + f"KMwz{s_}"])
    for t in range(NT):
        s_ = t % 2
        r0 = t * 128
        ve = "dve" if t % 2 == 0 else "pool"
        vo = "pool" if t % 2 == 0 else "dve"
        Sx.dma(X[s_][:], c.proj[r0:r0 + 128, N_KC:N_KC + 768], reads=[f"proj{t}"], writes=[L + f"X{s_}"])
        Sx.dma(CS[s_][:], c.cs_tab[r0:r0 + 128, :], reads=["cs_tab"], writes=[L + f"CS{s_}"])
        for br, (off, KM) in enumerate(((256, KMs), (512, KMw))):
            norm_rope(c, {"x": [L + f"X{s_}"], "out": [L + f"KM{br}{s_}"], "gn": [L + "gk"],
                          "cs": [L + f"CS{s_}"], "scr": L + f"nrA{s_}{br}"},
                      ve, X[s_][:, off:off + 128].rearrange("p (h d) -> p h d", h=2), gk[:], CS[s_][:],
                      KM[s_][:, :, 0:64], 2, sq[s_][:, 2 * br:2 * br + 2, :], tt[s_][:, 2 * br:2 * br + 2, :, :],
                      stt[s_][:, 2 * br:2 * br + 2, :])
        Sx.op(vo, lambda e, s_=s_, t=t: e.tensor_copy(out=KMs[s_][:, :, 64:128], in_=bc(Ebf[:, t, :], [128, 2, 64], 1)),
              reads=[L + "Ebf"], writes=[L + f"KM0{s_}"])
        Sx.op(vo, lambda e, s_=s_: e.tensor_copy(
            out=KMc[s_][:].rearrange("p h (kv d) -> p h kv d", kv=2),
            in_=X[s_][:, 0:256].rearrange("p (kv h d) -> p h kv d", kv=2, h=2)),
            reads=[L + f"X{s_}"], writes=[L + f"KMc{s_}"])
        Sx.op(vo, lambda e, s_=s_, t=t: e.tensor_copy(
            out=VS[:, t, :, 0:64], in_=X[s_][:, 384:512].rearrange("p (h d) -> p h d", h=2)),
            reads=[L + f"X{s_}"], writes=[L + f"VS{t}"])
        Sx.op(vo, lambda e, s_=s_, t=t: e.tensor_copy(
            out=VW[:, t, :, 0:64], in_=X[s_][:, 640:768].rearrange("p (h d) -> p h d", h=2)),
            reads=[L + f"X{s_}"], writes=[L + f"VW{t}"])
        for i, (KM, nm) in enumerate(((KMs, "KM0"), (KMw, "KM1"), (KMc, "KMc"))):
            for h in range(2):
                bank, col = i, h * 128
                rd = [L + f"{nm}{s_}", "ident"] + ([L + f"KMwz{s_}"] if i == 1 else [])
                Sx.op("pe", lambda e, KM=KM, s_=s_, h=h, bank=bank, col=col: e.matmul(
                    out=c.ps[bank][:, col:col + 128], lhsT=KM[s_][:, h, :], rhs=c.ident[:], start=True, stop=True),
                    reads=rd, writes=[f"ps{bank}"])
        Sx.op("act", lambda e, r0=r0: e.copy(out=KAs[:, :, r0:r0 + 128],
                                             in_=c.ps[0][:, 0:256].rearrange("p (h t) -> p h t", h=2)),
              reads=["ps0"], writes=[L + f"KAs{t}"])
        Sx.op("dve", lambda e, r0=r0: e.tensor_copy(out=KWA[:, :, r0:r0 + 128],
                                                    in_=c.ps[1][:, 0:256].rearrange("p (h t) -> p h t", h=2)),
              reads=["ps1"], writes=[L + f"KWA{t}"])
        Sx.op("act", lambda e, r0=r0: e.copy(out=KCr[:, :, r0:r0 + 128],
                                             in_=c.ps[2][:, 0:256].rearrange("p (h t) -> p h t", h=2)),
              reads=["ps2"], writes=[L + "KCr"])
        if t == 0:
            Sx.mark("phaseA tile0 done")
    sb.reset(m1)
    Sx.mark("phaseA done")
    W1f = sb.t("W1f", [128, 32, 128], F32)
    W1 = sb.t("W1", [128, 32, 128], BF16)
    W2f = sb.t("W2f", [128, 128], F32)
    W2 = sb.t("W2", [128, 128], BF16)
    PEf = sb.t("PEf", [128, 32], F32)
    PEb = sb.t("PEb", [128, 32], BF16)
    b1 = sb.t("b1", [128, 2], F32)
    HID = sb.t("HID", [128, 256], BF16)
    ex = sb.t("ex", [128, 256], F32)
    zz = sb.t("zz", [128, 256], F32)
    kraw = sb.t("kraw", [128, 2, 2, 64], F32)
    CSc = sb.t("CSc", [128, 2, 64], F32)
    mmf = sb.t("mmf", [128, 2, 64], F32)
    KCm = sb.t("KCm", [128, 2, 2, 128], BF16)
    sq2 = sb.t("sq2", [128, 2, 64], F32)
    tt2 = sb.t("tt2", [128, 2, 2, 32], F32)
    st2 = sb.t("st2", [128, 2, 4], F32)
    Sx.op("pool", lambda e: e.memset(W1f[:], 0.0), writes=[L + "W1f"])
    Sx.op("pool", lambda e: e.memset(W2f[:], 0.0), writes=[L + "W2f"])
    Sx.op("pool", lambda e: e.memset(KCm[:], 0.0), writes=[L + "KCmz"])
    Sx.op("pool", lambda e: e.memset(CSc[:], 0.0), writes=[L + "CSc"])
    for lh in range(2):
        Sx.dma(W1f[0:64, lh * 16:(lh + 1) * 16, 0:64],
               c.W("cmp_k_w1")[l].rearrange("(l d) j -> d l j", d=64)[:, lh * 16:(lh + 1) * 16, :], writes=[L + "W1f"])
        Sx.dma(W1f[64:128, lh * 16:(lh + 1) * 16, 64:128],
               c.W("cmp_v_w1")[l].rearrange("(l d) j -> d l j", d=64)[:, lh * 16:(lh + 1) * 16, :], writes=[L + "W1f"])
    Sx.dma(W2f[0:64, 0:64], c.W("cmp_k_w2")[l], writes=[L + "W2f"])
    Sx.dma(W2f[64:128, 64:128], c.W("cmp_v_w2")[l], writes=[L + "W2f"])
    for half in range(2):
        Sx.dma(PEf[0:64, half * 16:(half + 1) * 16],
               c.W("cmp_k_pos")[l].rearrange("l d -> d l")[:, half * 16:(half + 1) * 16],
               writes=[L + "PEf"], allow_slow_non_contiguous=True)
        Sx.dma(PEf[64:128, half * 16:(half + 1) * 16],
               c.W("cmp_v_pos")[l].rearrange("l d -> d l")[:, half * 16:(half + 1) * 16],
               writes=[L + "PEf"], allow_slow_non_contiguous=True)
    Sx.op("dve", lambda e: e.tensor_copy(out=W1[:], in_=W1f[:]), reads=[L + "W1f"], writes=[L + "W1"])
    Sx.op("pool", lambda e: e.tensor_copy(out=W2[:], in_=W2f[:]), reads=[L + "W2f"], writes=[L + "W2"])
    Sx.op("pool", lambda e: e.tensor_copy(out=PEb[:], in_=PEf[:]), reads=[L + "PEf"], writes=[L + "PEb"])
    csv = c.cs_tab.rearrange("(a b) n -> a b n", b=16)
    Sx.dma(CSc[:, 0, :], csv[1:129, 15, :], reads=["cs_tab"], writes=[L + "CSc"])
    Sx.dma(CSc[0:127, 1, :], csv[129:256, 15, :], reads=["cs_tab"], writes=[L + "CSc"])
    Sx.dma(mmf[:], c.mmap_d.rearrange("(ct p) n -> p ct n", p=128), writes=[L + "mmf"])
    for h in range(2):
        Sx.op("pool", lambda e, h=h: e.tensor_copy(out=VCa[:, :, h, 65:129], in_=mmf[:]),
              reads=[L + "mmf"], writes=[L + f"VCam{h}"])
    Sx.op("pool", lambda e: e.memset(VCa[:, :, :, 64:65], 1.0), writes=[L + "VCa1"])
    for li in range(32):
        Sx.op("pe", lambda e, li=li: e.matmul(out=c.ps[2][:, 0:1], lhsT=W1[:, li, :], rhs=PEb[:, li:li + 1],
                                              start=(li == 0), stop=(li == 31)),
              reads=[L + "W1", L + "PEb"], writes=["ps2"])
    Sx.op("dve", lambda e: e.tensor_copy(out=b1[:, 0:1], in_=c.ps[2][:, 0:1]), reads=["ps2"], writes=[L + "b1"])
    Sx.op("dve", lambda e: e.tensor_scalar(out=b1[:, 1:2], in0=b1[:, 0:1], scalar1=-1.0, scalar2=None, op0=ALU.mult),
          reads=[L + "b1"], writes=[L + "nb1"])
    Sx.mark("b1 done")
    for h in range(2):
        kv = KCr[:, h, :].rearrange("p (c l) -> p l c", l=16)
        for li in range(16):
            Sx.op("pe", lambda e, li=li, kv=kv: e.matmul(out=c.ps[3][:, 0:256], lhsT=W1[:, li, :], rhs=kv[:, li, :],
                                                        start=(li == 0), stop=False, skip_group_check=True),
                  reads=[L + "W1", L + "KCr"], writes=["ps3"])
        for li in range(16):
            Sx.op("pe", lambda e, li=li, kv=kv: e.matmul(out=c.ps[3][:, 0:255], lhsT=W1[:, 16 + li, :],
                                                        rhs=kv[:, li, 1:256], start=False, stop=(li == 15),
                                                        skip_group_check=True),
                  reads=[L + "W1", L + "KCr"], writes=["ps3"])
        Sx.op("act", lambda e: e.activation(out=ex[:], in_=c.ps[3][:, 0:256], func=AF.Exp, scale=-1.0, bias=b1[:, 1:2]),
              reads=["ps3", L + "nb1"], writes=[L + "ex"])
        Sx.op("dve", lambda e: e.tensor_scalar(out=ex[:], in0=ex[:], scalar1=1.0, scalar2=None, op0=ALU.add),
              reads=[L + "ex"], writes=[L + "ex"])
        Sx.op("dve", lambda e: e.reciprocal(out=ex[:], in_=ex[:]), reads=[L + "ex"], writes=[L + "ex"])
        Sx.op("dve", lambda e: e.tensor_scalar(out=zz[:], in0=c.ps[3][:, 0:256], scalar1=b1[:, 0:1], scalar2=None,
                                               op0=ALU.add), reads=["ps3", L + "b1"], writes=[L + "zz"])
        Sx.op("dve", lambda e: e.tensor_tensor(out=HID[:], in0=zz[:], in1=ex[:], op=ALU.mult),
              reads=[L + "zz", L + "ex"], writes=[L + "HID"])
        for ct in range(2):
            Sx.op("pe", lambda e, ct=ct: e.matmul(out=c.ps[2][:, ct * 128:(ct + 1) * 128],
                                                  lhsT=HID[:, ct * 128:(ct + 1) * 128], rhs=W2[:],
                                                  start=True, stop=True),
                  reads=[L + "HID", L + "W2"], writes=["ps2"])
        p2 = c.ps[2][:, 0:256].rearrange("p (ct n) -> p ct n", ct=2)
        Sx.op("dve", lambda e, h=h, p2=p2: e.tensor_copy(out=kraw[:, :, h, :], in_=p2[:, :, 0:64]),
              reads=["ps2"], writes=[L + f"kraw{h}"])
        Sx.op("dve", lambda e, h=h, p2=p2: e.tensor_copy(out=VCa[:, :, h, 0:64], in_=p2[:, :, 64:128]),
              reads=["ps2"], writes=[L + f"VCav{h}"])
    for ct in range(2):
        norm_rope(c, {"x": [L + "kraw0", L + "kraw1"], "out": [L + f"KCm{ct}"], "gn": [L + "gk"],
                      "cs": [L + "CSc"], "scr": L + f"nrC{ct}"},
                  "dve", kraw[:, ct, :, :], gk[:], CSc[:, ct, :], KCm[:, ct, :, 0:64], 2, sq2[:], tt2[:], st2[:])
        for h in range(2):
            Sx.op("pe", lambda e, ct=ct, h=h: e.matmul(out=c.ps[2][:, (ct * 2 + h) * 128:(ct * 2 + h + 1) * 128],
                                                       lhsT=KCm[:, ct, h, :], rhs=c.ident[:], start=True, stop=True),
                  reads=[L + f"KCm{ct}", L + "KCmz", "ident"], writes=["ps2"])
    Sx.op("dve", lambda e: e.tensor_copy(out=KCA[:].rearrange("p h (ct n) -> p ct h n", ct=2),
                                         in_=c.ps[2][:].rearrange("p (ct h n) -> p ct h n", ct=2, h=2)),
          reads=["ps2"], writes=[L + "KCA"])
    sb.reset(m1)
    Sx.mark("phaseA' done")
    NSL = 3
    QA0 = sb.t("QA0", [128, 8, 512], BF16)
    QA = sb.t("QA", [128, 8, 512], BF16)
    QM = sb.t("QM", [128, 4, 8, 128], BF16)
    Xq = [sb.t("Xq", [128, 512], F32) for _ in range(2)]
    Gt = sb.t("Gt", [128, 4, 24], F32)
    CSq = [sb.t("CSq", [128, 64], F32) for _ in range(2)]
    sq8 = [sb.t("sq8", [128, 8, 64], F32) for _ in range(2)]
    tt8 = [sb.t("tt8", [128, 8, 2, 32], F32) for _ in range(2)]
    st8 = [sb.t("st8", [128, 8, 4], F32) for _ in range(2)]
    PT = [sb.t("PT", [128, 512], BF16) for _ in range(NSL)]
    CMf = sb.t("CMf", [128, 2, 512], F32)
    CM = sb.t("CM", [128, 2, 512], BF16)
    VISt = sb.t("VISt", [128, 4, 64], F32)
    ADDt = sb.t("ADDt", [128, 4, 64], F32)
    IMP = sb.t("IMP", [128, 4, 2, 64], F32)
    Y = sb.t("Y", [128, 4, 8, 64], F32)
    tmpo = [sb.t("tmpo", [128, 4, 64], F32) for _ in range(2)]
    den = [sb.t("den", [128, 4, 3], F32) for _ in range(2)]
    sc = sb.t("sc", [128, 64], F32)
    sc2 = sb.t("sc2", [128, 64], F32)
    m8 = sb.t("m8", [128, 16], F32)
    nm = sb.t("nm", [128, 64], BF16)
    for j in range(S // 512):
        Lj = L + f"c{j}"
        Sx.dma(VISt[:], c.vis_d[j * 512:(j + 1) * 512, :].rearrange("(a p) n -> p a n", p=128), writes=[L + "VISt"])
        Sx.dma(ADDt[:], c.add_d[j * 512:(j + 1) * 512, :].rearrange("(a p) n -> p a n", p=128), writes=[L + "ADDt"])
        Sx.dma(CMf[:], c.cm_d[:, :, j * 512:(j + 1) * 512], writes=[L + "CMf"])
        Sx.op("pool", lambda e: e.tensor_copy(out=CM[:], in_=CMf[:]), reads=[L + "CMf"], writes=[L + "CM"])
        Sx.dma(Gt[:], c.proj[j * 512:(j + 1) * 512, N_G:N_G + 24].rearrange("(a p) n -> p a n", p=128),
               reads=[f"proj{4 * j + a}" for a in range(4)], writes=[L + "Gt"])
        Sx.op("act", lambda e: e.activation(out=Gt[:], in_=Gt[:], func=AF.Exp, scale=-1.0),
              reads=[L + "Gt"], writes=[L + "Gt"])
        Sx.op("dve", lambda e: e.tensor_scalar(out=Gt[:], in0=Gt[:], scalar1=1.0, scalar2=None, op0=ALU.add),
              reads=[L + "Gt"], writes=[L + "Gt"])
        Sx.op("dve", lambda e: e.reciprocal(out=Gt[:], in_=Gt[:]), reads=[L + "Gt"], writes=[L + "Gt"])
        Sx.op("pool", lambda e: e.memset(QM[:, :, :, 64:128], 0.0), writes=[L + f"QMm{a}" for a in range(4)])
        Sx.op("pool", lambda e: e.memset(IMP[:], 0.0), writes=[L + "IMP"])
        for a in range(4):
            t = 4 * j + a
            s_ = a % 2
            ve = "dve" if a % 2 == 0 else "pool"
            Sx.dma(Xq[s_][:], c.proj[t * 128:(t + 1) * 128, N_Q:N_Q + 512], reads=[f"proj{t}"], writes=[L + f"Xq{s_}"])
            Sx.dma(CSq[s_][:], c.cs_tab[t * 128:(t + 1) * 128, :], reads=["cs_tab"], writes=[L + f"CSq{s_}"])
            norm_rope(c, {"x": [L + f"Xq{s_}"], "out": [L + f"QMq{a}"], "gn": [L + "gq"],
                          "cs": [L + f"CSq{s_}"], "scr": L + f"nrQ{s_}"},
                      ve, Xq[s_][:].rearrange("p (h d) -> p h d", h=8), gq[:], CSq[s_][:],
                      QM[:, a, :, 0:64], 8, sq8[s_][:], tt8[s_][:], st8[s_][:])

        def transposes(dst, dst_name, with_mask):
            for h in range(8):
                for a in range(4):
                    Sx.op("pe", lambda e, h=h, a=a: e.matmul(
                        out=c.ps[3][:, a * 128:(a + 1) * 128], lhsT=QM[:, a, h, :], rhs=c.ident[:],
                        start=True, stop=True),
                        reads=[L + f"QMq{a}", L + f"QMm{a}", "ident"], writes=["ps3"])
                eng = "act" if h % 2 == 0 else "dve"
                if eng == "act":
                    Sx.op("act", lambda e, h=h: e.copy(out=dst[:, h, :], in_=c.ps[3][:]),
                          reads=["ps3"], writes=[L + f"{dst_name}{h}"])
                else:
                    Sx.op("dve", lambda e, h=h: e.tensor_copy(out=dst[:, h, :], in_=c.ps[3][:]),
                          reads=["ps3"], writes=[L + f"{dst_name}{h}"])

        Sx.mark(f"c{j} q prep done")
        transposes(QA0, "QA0_", False)
        Sx.mark(f"c{j} T0 done")

        units = []

        def add_unit(qsrc, qname, h, lhsT, lres, a_lo, a_hi, masks, vrhs, vres, acc, vw, first, last, fin):
            units.append(dict(qsrc=qsrc, qname=qname, h=h, lhsT=lhsT, lres=lres, a_lo=a_lo, a_hi=a_hi, masks=masks,
                              vrhs=vrhs, vres=vres, acc=acc, vw=vw, first=first, last=last, fin=fin))

        def run_units():
            n = len(units)
            LOOK = 2
            for i in range(n + LOOK):
                if i < n:
                    u = units[i]
                    sl = i % NSL
                    ncol = (u["a_hi"] - u["a_lo"] + 1) * 128
                    q0 = u["a_lo"] * 128
                    nmask = len(u["masks"])
                    Sx.op("pe", lambda e, u=u, sl=sl, ncol=ncol, q0=q0, nmask=nmask: e.matmul(
                        out=c.ps[sl][:, 0:ncol], lhsT=u["lhsT"], rhs=u["qsrc"][:, u["h"], q0:q0 + ncol],
                        start=True, stop=(nmask == 0), skip_group_check=True),
                        reads=u["lres"] + [L + f"{u['qname']}{u['h']}"], writes=[f"ps{sl}"])
                    for mi, (a, mrhs, mres) in enumerate(u["masks"]):
                        o0 = (a - u["a_lo"]) * 128 if a is not None else 0
                        wd = 128 if a is not None else ncol
                        Sx.op("pe", lambda e, sl=sl, o0=o0, wd=wd, mrhs=mrhs, mi=mi, nmask=nmask: e.matmul(
                            out=c.ps[sl][:, o0:o0 + wd], lhsT=c.ident[:], rhs=mrhs, start=False,
                            stop=(mi == nmask - 1), skip_group_check=True),
                            reads=["ident"] + mres, writes=[f"ps{sl}"])
                k = i - LOOK
                if k >= 0:
                    u = units[k]
                    sl = k % NSL
                    ncol = (u["a_hi"] - u["a_lo"] + 1) * 128
                    Sx.op("act", lambda e, sl=sl, ncol=ncol: e.activation(
                        out=PT[sl][:, 0:ncol], in_=c.ps[sl][:, 0:ncol], func=AF.Exp, scale=SCALE),
                        reads=[f"ps{sl}"], writes=[L + f"PT{sl}"])
                    vw = u["vw"]
                    started = set()
                    for a in range(u["a_lo"], u["a_hi"] + 1):
                        bank, col = u["acc"](a)
                        st_flag = u["first"] and (bank not in started) and a == u["first_a"].get(bank, -1)
                        started.add(bank)
                        Sx.op("pe", lambda e, u=u, sl=sl, a=a, bank=bank, col=col, vw=vw, st_flag=st_flag: e.matmul(
                            out=c.ps[bank][:, col:col + vw],
                            lhsT=PT[sl][:, (a - u["a_lo"]) * 128:(a - u["a_lo"] + 1) * 128], rhs=u["vrhs"],
                            start=st_flag, stop=False, skip_group_check=True),
                            reads=[L + f"PT{sl}"] + u["vres"], writes=[f"ps{bank}"])
                    if u["fin"] is not None:
                        u["fin"]()
            units.clear()

        def evac(h, br, banks_cols, vw, cmp_branch):
            g = h // 4
            ds = h % 2
            for a in range(4):
                bank, col = banks_cols(a)
                Sx.op("dve", lambda e, a=a, bank=bank, col=col, ds=ds: e.tensor_scalar(
                    out=den[ds][:, a, 0:1], in0=c.ps[bank][:, col + 64:col + 65], scalar1=1e-30, scalar2=None,
                    op0=ALU.max), reads=[f"ps{bank}"], writes=[L + f"den{ds}"])
            Sx.op("dve", lambda e, ds=ds: e.reciprocal(out=den[ds][:, :, 1], in_=den[ds][:, :, 0]),
                  reads=[L + f"den{ds}"], writes=[L + f"rden{ds}"])
            Sx.op("dve", lambda e, ds=ds, h=h, br=br: e.tensor_tensor(
                out=den[ds][:, :, 2], in0=den[ds][:, :, 1], in1=Gt[:, :, h * 3 + br], op=ALU.mult),
                reads=[L + f"rden{ds}", L + "Gt"], writes=[L + f"coef{ds}"])
            for a in range(4):
                bank, col = banks_cols(a)
                if cmp_branch:
                    Sx.op("dve", lambda e, a=a, bank=bank, col=col, ds=ds, h=h: e.tensor_scalar(
                        out=Y[:, a, h, :], in0=c.ps[bank][:, col:col + 64], scalar1=den[ds][:, a, 2:3], scalar2=None,
                        op0=ALU.mult), reads=[f"ps{bank}", L + f"coef{ds}"], writes=[L + f"Y{h}"])
                    Sx.op("dve", lambda e, a=a, bank=bank, col=col, ds=ds, g=g: e.scalar_tensor_tensor(
                        out=IMP[:, a, g, :], in0=c.ps[bank][:, col + 65:col + 129], scalar=den[ds][:, a, 1:2],
                        in1=IMP[:, a, g, :], op0=ALU.mult, op1=ALU.add),
                        reads=[f"ps{bank}", L + f"rden{ds}"], writes=[L + "IMP"])
                else:
                    Sx.op("dve", lambda e, a=a, bank=bank, col=col, ds=ds, h=h: e.scalar_tensor_tensor(
                        out=Y[:, a, h, :], in0=c.ps[bank][:, col:col + 64], scalar=den[ds][:, a, 2:3],
                        in1=Y[:, a, h, :], op0=ALU.mult, op1=ALU.add),
                        reads=[f"ps{bank}", L + f"coef{ds}"], writes=[L + f"Y{h}"])

        cts = [0] if j <= 3 else [0, 1]
        for h in range(8):
            g = h // 4
            bb = 4 + 2 * (h % 2)
            acc = (lambda a, bb=bb: (bb + a // 2, (a % 2) * 129))
            for ci, ct in enumerate(cts):
                add_unit(QA0, "QA0_", h, KCA[:, g, ct * 128:(ct + 1) * 128], [L + "KCA"], 0, 3,
                         [(None, CM[:, ct, :], [L + "CM"])], VCa[:, ct, g, :],
                         [L + f"VCav{g}", L + f"VCam{g}", L + "VCa1"], acc, 129, ci == 0, ci == len(cts) - 1,
                         (lambda h=h, acc=acc: evac(h, 0, acc, 129, True)) if ci == len(cts) - 1 else None)
                units[-1]["first_a"] = {bb: 0, bb + 1: 2}
        run_units()
        Sx.mark(f"c{j} cmp done")
        if j == 0:
            dbg(c, "Ycmp", Y[:].rearrange("p a h d -> p (a h d)"), [L + f"Y{h}" for h in range(8)])
            dbg(c, "IMP", IMP[:].rearrange("p a g n -> p (a g n)"), [L + "IMP"])
            dbg(c, "Gt", Gt[:].rearrange("p a n -> p (a n)"), [L + "Gt"])
            dbg(c, "QA0", QA0[:].rearrange("p h t -> p (h t)"), [L + f"QA0_{h}" for h in range(8)])
            dbg(c, "KCA", KCA[:].rearrange("p h t -> p (h t)"), [L + "KCA"])
            dbg(c, "VCa", VCa[:].rearrange("p a h t -> p (a h t)"), [L + "VCav0", L + "VCav1", L + "VCam0", L + "VCam1", L + "VCa1"])
            dbg(c, "KWA", KWA[:, 0, 0:512], [L + f"KWA{t}" for t in range(4)])
            dbg(c, "KAs", KAs[:, 0, 0:512], [L + f"KAs{t}" for t in range(4)])
        for h in range(8):
            g = h // 4
            bb = 4 + (h % 4)
            acc = (lambda a, bb=bb: (bb, a * 65))
            kts = list(range(max(0, 4 * j - 4), 4 * j + 4))
            seen = set()
            for ki, kt in enumerate(kts):
                a_lo = max(0, kt - 4 * j)
                a_hi = min(3, kt + 4 - 4 * j)
                masks = []
                if kt >= 4 * j:
                    masks.append((kt - 4 * j, madd[:, 0, :], [L + "madd"]))
                if kt + 4 - 4 * j <= 3:
                    masks.append((kt + 4 - 4 * j, madd[:, 1, :], [L + "madd"]))
                add_unit(QA0, "QA0_", h, KWA[:, g, kt * 128:(kt + 1) * 128], [L + f"KWA{kt}"], a_lo, a_hi, masks,
                         VW[:, kt, g, :], [L + f"VW{kt}", L + "VWone"], acc, 65, ki == 0, ki == len(kts) - 1,
                         (lambda h=h, acc=acc: evac(h, 2, acc, 65, False)) if ki == len(kts) - 1 else None)
                units[-1]["first_a"] = {bb: a_lo}
        run_units()
        Sx.mark(f"c{j} win done")
        if j == 0:
            dbg(c, "Ywin", Y[:].rearrange("p a h d -> p (a h d)"), [L + f"Y{h}" for h in range(8)])
        for a in range(4):
            for g in range(2):
                Sx.op("dve", lambda e, a=a, g=g: e.tensor_tensor(out=sc[:], in0=IMP[:, a, g, :], in1=VISt[:, a, :],
                                                                op=ALU.mult),
                      reads=[L + "IMP", L + "VISt"], writes=[L + "sc"])
                Sx.op("dve", lambda e, a=a: e.tensor_tensor(out=sc[:], in0=sc[:], in1=ADDt[:, a, :], op=ALU.add),
                      reads=[L + "sc", L + "ADDt"], writes=[L + "sc"])
                Sx.op("dve", lambda e: e.max(out=m8[:, 0:8], in_=sc[:]), reads=[L + "sc"], writes=[L + "m8a"])
                Sx.op("dve", lambda e: e.match_replace(out=sc2[:], in_to_replace=m8[:, 0:8], in_values=sc[:],
                                                       imm_value=-1e9),
                      reads=[L + "sc", L + "m8a"], writes=[L + "sc2"])
                Sx.op("dve", lambda e: e.max(out=m8[:, 8:16], in_=sc2[:]), reads=[L + "sc2"], writes=[L + "m8b"])
                Sx.op("dve", lambda e: e.tensor_reduce(out=m8[:, 0:1], in_=m8[:, 8:16], axis=AX.X, op=ALU.min),
                      reads=[L + "m8b"], writes=[L + "thr"])
                Sx.op("dve", lambda e: e.tensor_scalar(out=nm[:], in0=sc[:], scalar1=m8[:, 0:1], scalar2=NEG,
                                                       op0=ALU.is_lt, op1=ALU.mult),
                      reads=[L + "sc", L + "thr"], writes=[L + "nm"])
                Sx.op("pool", lambda e, a=a, g=g: e.tensor_copy(out=QM[:, a, 4 * g:4 * g + 4, 64:128],
                                                               in_=bc(nm[:], [128, 4, 64], 1)),
                      reads=[L + "nm"], writes=[L + f"QMm{a}"])
        Sx.mark(f"c{j} topk done")
        if j == 0:
            dbg(c, "QM", QM[:].rearrange("p a h d -> p (a h d)"), [L + f"QMm{a}" for a in range(4)] + [L + f"QMq{a}" for a in range(4)])
        transposes(QA, "QA_", True)
        for h in range(8):
            g = h // 4
            bb = 4 + (h % 4)
            acc = (lambda a, bb=bb: (bb, a * 65))
            kts = list(range(0, 4 * j + 4))
            for ki, kt in enumerate(kts):
                a_lo = max(0, kt - 4 * j)
                masks = []
                if kt >= 4 * j:
                    masks.append((kt - 4 * j, madd[:, 0, :], [L + "madd"]))
                add_unit(QA, "QA_", h, KAs[:, g, kt * 128:(kt + 1) * 128], [L + f"KAs{kt}"], a_lo, 3, masks,
                         VS[:, kt, g, :], [L + f"VS{kt}", L + "VSone"], acc, 65, ki == 0, ki == len(kts) - 1,
                         (lambda h=h, acc=acc: evac(h, 1, acc, 65, False)) if ki == len(kts) - 1 else None)
                units[-1]["first_a"] = {bb: a_lo}
        run_units()
        Sx.mark(f"c{j} sel done")
        for a in range(4):
            t = 4 * j + a
            Sx.dma(c.mix[t * 128:(t + 1) * 128, 256:768], Y[:, a, :, :].rearrange("p h d -> p (h d)"),
                   reads=[L + f"Y{h}" for h in range(8)], writes=[f"mixB{t}"])
    sb.reset(m)


def _prep_inputs(inputs, names, extra=None):
    ident = np.eye(128, dtype=np.float32)
    maps = []
    for b in range(8):
        m = {}
        for n in names:
            if n == "x":
                m[n] = np.ascontiguousarray(inputs["x"][b])
            elif n == "positions":
                m[n] = np.ascontiguousarray(inputs["positions"][b].reshape(S, 1))
            elif n == "c_ident":
                m[n] = ident
            elif n == "c_invf":
                invf = (10000.0 ** (-(np.arange(32, dtype=np.float32) / np.float32(32)))).astype(np.float32)
                m[n] = np.ascontiguousarray(np.broadcast_to(invf[None, :], (128, 32)))
            elif n in ("c_esel", "c_mmap", "c_cm", "c_vis", "c_add"):
                m[n] = _nsa_consts()[n]
            elif n == "c_masks":
                p = np.arange(128)[:, None]
                f = np.arange(128)[None, :]
                m[n] = np.ascontiguousarray(np.stack([(p >= f), (p <= f), np.zeros((128, 128), bool), (p > f)],
                                                     axis=1).astype(np.float32))
            elif n in WSHAPES:
                m[n] = np.ascontiguousarray(inputs[n])
            elif extra is not None and n in extra:
                m[n] = extra[n][b] if isinstance(extra[n], (list, tuple)) else extra[n]
        maps.append(m)
    return maps


_NC_CACHE = {}
_CONSTS = {}


def _nsa_consts():
    if _CONSTS:
        return _CONSTS
    t = np.arange(S)
    n = np.arange(64)
    cur = (t // 64)[:, None]
    nn = n[None, :]
    _CONSTS["c_esel"] = (cur == nn).astype(np.float32)
    forced = (nn == 0) | (nn == cur) | (nn == cur - 1)
    visible = nn <= cur
    _CONSTS["c_vis"] = (visible & ~forced).astype(np.float32)
    add = np.zeros((S, 64), np.float32)
    add = np.where(nn == cur - 1, 10000.0, add)
    add = np.where(nn == cur, 20000.0, add)
    add = np.where(nn == 0, 30000.0, add)
    add = np.where(~visible, -1.0 - nn, add)
    _CONSTS["c_add"] = add.astype(np.float32)
    cidx = np.arange(256)
    cs_ = cidx[:, None] * 16
    ss_ = n[None, :] * 64
    ov = np.clip(np.minimum(cs_ + 32, ss_ + 64) - np.maximum(cs_, ss_), 0, None) / 32.0
    ov[255] = 0.0
    _CONSTS["c_mmap"] = ov.astype(np.float32)
    cm = np.full((128, 2, S), NEG, np.float32)
    for ct in range(2):
        cc = ct * 128 + np.arange(128)
        vis = ((16 * cc + 31)[:, None] <= t[None, :]) & (cc < 255)[:, None]
        cm[:, ct, :] = np.where(vis, 0.0, NEG)
    _CONSTS["c_cm"] = cm
    return _CONSTS


def kernel(**inputs):
    inputs = {k: np.asarray(v) for k, v in inputs.items()}
    if "full" not in _NC_CACHE:
        _NC_CACHE["full"] = build_program()
    nc = _NC_CACHE["full"]
    res = run_bass_kernel_spmd(nc, _prep_inputs(inputs, nc._mk_inputs), core_ids=list(range(8)))
    return np.stack([r["out"] for r in res.results], axis=0).astype(np.float32)
```

```python
import numpy as np
import concourse.bass as bass
import concourse.mybir as mybir
from concourse.bass_utils import run_bass_kernel_spmd

F32 = mybir.dt.float32
BF16 = mybir.dt.bfloat16
I32 = mybir.dt.int32
AF = mybir.ActivationFunctionType
ALU = mybir.AluOpType
AX = mybir.AxisListType

S = 4096
D = 1024
NT = S // 128
IN_DIM = 4120
D_FF = 2816
DEPTH = 4
NEG = -30000.0
EPS = 1e-6
SCALE = 0.125

C_VAL, C_GATE, N_Q, N_KC, N_VC, N_KS, N_VS, N_KW, N_VW, N_G, D_Q, D_K, D_V = (
    0, 256, 512, 1024, 1152, 1280, 1408, 1536, 1664, 1792, 1816, 2584, 3352)

ENGS = ("pe", "act", "dve", "pool", "sp")


class Op:
    __slots__ = ("eng", "fn", "deps", "kind", "sem", "count", "signal", "lane", "seq")

    def __init__(self, eng, fn, kind):
        self.eng = eng
        self.fn = fn
        self.kind = kind
        self.deps = []
        self.sem = None
        self.count = 0
        self.signal = False
        self.lane = None


class Sched:
    def __init__(self, nc, n_lanes=6, same_sync=True):
        self.nc = nc
        self.ops = {e: [] for e in ENGS}
        self.all = []
        self.lastw = {}
        self.readers = {}
        self.n_lanes = n_lanes
        self.same_sync = same_sync
        import os
        self.sync_same = set(os.environ.get("MK_SYNC_SAME", "act,dve,pool").split(",")) - {""}
        self.dma_n = {e: 0 for e in ENGS}
        self.pending_barrier = {}
        self.seq = 0

    def _track(self, op, reads, writes):
        deps = []
        for r in reads:
            w = self.lastw.get(r)
            if w is not None:
                deps.append(w)
        for w_ in writes:
            w = self.lastw.get(w_)
            if w is not None:
                deps.append(w)
            deps.extend(self.readers.get(w_, ()))
        for w_ in writes:
            self.lastw[w_] = op
            self.readers[w_] = []
        for r in reads:
            if r not in writes:
                self.readers.setdefault(r, []).append(op)
        if self.pending_barrier.get(op.eng):
            deps.extend(self.pending_barrier.pop(op.eng))
        best = {}
        for d in deps:
            if d is op:
                continue
            key = (d.eng, d.lane if d.kind == "dma" else -1)
            b = best.get(key)
            if b is None or d.seq > b.seq:
                best[key] = d
        op.deps = list(best.values())
        self.seq += 1
        op.seq = self.seq
        self.ops[op.eng].append(op)
        self.all.append(op)
        return op

    def op(self, eng, fn, reads=(), writes=()):
        import os
        lim = int(os.environ.get("MK_OPLIM", "-1"))
        self.nops = getattr(self, "nops", 0) + 1
        if lim >= 0 and self.nops > lim:
            return None
        return self._track(Op(eng, fn, "cmp"), tuple(reads), tuple(writes))

    def barrier(self):
        last = []
        for e in ENGS:
            cm = [o for o in self.ops[e] if o.kind == "cmp"]
            if cm:
                last.append(cm[-1])
            seen = set()
            for o in reversed(self.ops[e]):
                if o.kind == "dma" and o.lane not in seen:
                    seen.add(o.lane)
                    last.append(o)
                if len(seen) >= self.n_lanes:
                    break
        for e in ENGS:
            self.pending_barrier[e] = list(last)

    def mark(self, label):
        import os
        if os.environ.get("MK_MARKS"):
            print("MARK", label, getattr(self, "nops", 0), flush=True)

    def dma(self, out, in_, reads=(), writes=(), q="sp", **kw):
        o = Op(q, lambda e: e.dma_start(out=out, in_=in_, **kw), "dma")
        o.lane = self.dma_n[q] % self.n_lanes
        self.dma_n[q] += 1
        return self._track(o, tuple(reads), tuple(writes))

    def emit(self):
        nc = self.nc
        for op in self.all:
            for d in op.deps:
                if d.kind == "cmp" and (d.eng != op.eng or d.eng in self.sync_same):
                    d.signal = True
        esem = {e: nc.alloc_semaphore(name=f"s_{e}") for e in ENGS}
        lsem = {}
        lcnt = {}
        cnt = {e: 0 for e in ENGS}
        for op in self.all:
            if op.kind == "cmp":
                if op.signal:
                    cnt[op.eng] += 1
                    op.count = cnt[op.eng]
                    op.sem = ("e", op.eng)
            else:
                key = (op.eng, op.lane)
                if key not in lsem:
                    lsem[key] = nc.alloc_semaphore(name=f"l_{op.eng}{op.lane}")
                    lcnt[key] = 0
                lcnt[key] += 16
                op.count = lcnt[key]
                op.sem = ("l",) + key
        handles = {("e", e): esem[e] for e in ENGS}
        for key, h in lsem.items():
            handles[("l",) + key] = h
        self.max_counts = dict(cnt)

        def emit_engine(name, e):
            waited = {}
            for op in self.ops[name]:
                need = {}
                for d in op.deps:
                    if d.kind == "cmp" and d.eng == name and name not in self.sync_same:
                        continue
                    if need.get(d.sem, 0) < d.count:
                        need[d.sem] = d.count
                if op.kind == "dma" and op.count > 16:
                    if need.get(op.sem, 0) < op.count - 16:
                        need[op.sem] = op.count - 16
                for s, v in need.items():
                    if waited.get(s, 0) >= v:
                        continue
                    e.wait_ge(handles[s], v)
                    waited[s] = v
                ins = op.fn(e)
                if op.kind == "dma":
                    ins.then_inc(handles[op.sem], 16)
                elif op.signal:
                    ins.then_inc(handles[op.sem], 1)

        with nc.Block() as block:
            @block.tensor
            def _(e):
                emit_engine("pe", e)

            @block.scalar
            def _(e):
                emit_engine("act", e)

            @block.vector
            def _(e):
                emit_engine("dve", e)

            @block.gpsimd
            def _(e):
                emit_engine("pool", e)

            @block.sync
            def _(e):
                emit_engine("sp", e)


class SbAlloc:
    def __init__(self, nc, base=16384, limit=212000):
        self.nc = nc
        self.off = base
        self.limit = limit
        self.n = 0
        self.sched = None

    def mark(self):
        return self.off

    def reset(self, m):
        self.off = m
        if self.sched is not None:
            self.sched.barrier()

    def t(self, name, shape, dtype):
        esz = 2 if dtype == BF16 else 4
        nbytes = int(np.prod(shape[1:])) * esz
        off = (self.off + 63) // 64 * 64
        assert off + nbytes <= self.limit, f"SBUF overflow at {name}: {off}+{nbytes}"
        self.n += 1
        h = self.nc.alloc_sbuf_tensor_at(f"{name}_{self.n}", list(shape), dtype, offset=off)
        self.off = off + nbytes
        return h


class Ctx:
    pass


WSHAPES = {
    "attn_norm": (D,), "w_in": (D, IN_DIM), "conv_dw": (31, 256), "conv_dw_b": (256,),
    "conv_ln_g": (256,), "conv_ln_b": (256,), "conv_pw": (256, 256), "conv_pw_b": (256,),
    "nsa_q_norm": (64,), "nsa_k_norm": (64,), "cmp_k_pos": (32, 64), "cmp_k_w1": (2048, 64),
    "cmp_k_w2": (64, 64), "cmp_v_pos": (32, 64), "cmp_v_w1": (2048, 64), "cmp_v_w2": (64, 64),
    "dil_q_norm": (64,), "dil_k_norm": (64,), "w_out": (D, D), "ffn_norm": (D,),
    "w_up": (D, 2 * D_FF), "ffn_dw": (3, 2 * D_FF), "ffn_dw_b": (2 * D_FF,), "w_down": (D_FF, D),
}


def build_program(n_layers=DEPTH, stages=("s1", "s2", "s3", "s4", "s5a", "s5b"), ext=None):
    ext = ext or {}
    nc = bass.Bass("TRN2", target_bir_lowering=False)
    c = Ctx()
    c.nc = nc
    c.S = Sched(nc)
    c.sb = SbAlloc(nc)
    c.sb.sched = c.S
    c.in_names = []
    c.uid = 0

    def din(name, shape, dtype=F32):
        c.in_names.append(name)
        return nc.dram_tensor(name, list(shape), dtype, kind="ExternalInput").ap()

    c.w = {}

    def W(name):
        if name not in c.w:
            c.w[name] = din(name, [DEPTH] + list(WSHAPES[name]))
        return c.w[name]

    c.W = W

    def scratch(name, shape, dtype=F32):
        kind = {"in": "ExternalInput", "out": "ExternalOutput"}.get(ext.get(name), "Internal")
        if kind == "ExternalInput":
            c.in_names.append(name)
        return nc.dram_tensor(name, list(shape), dtype, kind=kind).ap()

    c.x = din("x", [S, D])
    c.ident_d = din("c_ident", [128, 128])
    c.out = nc.dram_tensor("out", [S, D], F32, kind="ExternalOutput").ap()
    c.proj = scratch("proj", [S, IN_DIM])
    c.mix = scratch("mix", [S, D])
    c.x1 = scratch("x1", [S, D])
    c.xa = scratch("xa", [S, D])
    c.xb = scratch("xb", [S, D])
    c.ps = [nc.alloc_psum_tensor(f"psb{i}", [128, 512], F32) for i in range(8)]

    setup_consts(c)
    if "s3" in stages or "s4" in stages or "rope" in stages:
        c.pos = din("positions", [S, 1], I32)
        c.invf_d = din("c_invf", [128, 32])
        c.masks_d = din("c_masks", [128, 4, 128])
        c.cs_tab = scratch("cs_tab", [S, 64])
        c.dilO = scratch("dilO", [3, S, 260])
        c.esel_d = din("c_esel", [S, 64])
        c.mmap_d = din("c_mmap", [256, 64])
        c.cm_d = din("c_cm", [128, 2, S])
        c.vis_d = din("c_vis", [S, 64])
        c.add_d = din("c_add", [S, 64])
        setup_rope(c)
    xin, xin_name = c.x, "x"
    for l in range(n_layers):
        last = l == n_layers - 1
        if "s1" in stages:
            linear_stage(c, f"L{l}s1", xin, xin_name, c.proj, "proj", W("w_in")[l], IN_DIM,
                         gain=W("attn_norm")[l])
        if "s2" in stages:
            stage2(c, l)
        if "s3" in stages:
            stage3(c, l)
        if "s4" in stages:
            stage4(c, l)
        if "s5a" in stages:
            linear_stage(c, f"L{l}s5a", c.mix, (lambda t: [f"mixA{t}", f"mixB{t}", f"mixC{t}"]), c.x1, "x1", W("w_out")[l], D,
                         resid=(xin, xin_name))
        if "s5b" in stages:
            if last:
                xout, xout_name = c.out, "out"
            else:
                xout, xout_name = (c.xa, "xa") if l % 2 == 0 else (c.xb, "xb")
            stage5b(c, l, c.x1, "x1", xout, xout_name)
            xin, xin_name = xout, xout_name
    finish(c)
    c.S.emit()
    c.nc_inputs = list(c.in_names)
    nc._mk_inputs = list(c.in_names)
    nc._mk_counts = (dict(c.S.max_counts), {e: len(v) for e, v in c.S.ops.items()})
    return nc


def setup_consts(c):
    sb, Sx = c.sb, c.S
    c.ident_f = sb.t("ident_f", [128, 128], F32)
    c.ident = sb.t("ident", [128, 128], BF16)
    Sx.dma(c.ident_f[:], c.ident_d, writes=["ident_f"])
    Sx.op("dve", lambda e: e.tensor_copy(out=c.ident[:], in_=c.ident_f[:]), reads=["ident_f"], writes=["ident"])


def dbg(c, name, ap, reads):
    import os
    if not os.environ.get("MK_DBG"):
        return
    shape = list(ap.shape)
    d = c.nc.dram_tensor("dbg_" + name, shape, ap.dtype, kind="ExternalOutput").ap()
    c.S.dma(d, ap, reads=reads, writes=["dbg_" + name])


def finish(c):
    Sx = c.S
    res = [r for r, w in Sx.lastw.items() if w.kind == "dma"]
    Sx.op("sp", lambda e: e.nop(), reads=res)


def load_weight_bf16(c, L, Wt, w_ap, nk, ncols, gT=None, seg=2060, tag="W"):
    sb, Sx = c.sb, c.S
    stg = [sb.t("wstg", [128, seg], F32) for _ in range(2)]
    wv = w_ap.rearrange("(k p) n -> p k n", p=128)
    i = 0
    for k in range(nk):
        for c0 in range(0, ncols, seg):
            cw = min(seg, ncols - c0)
            s_ = i % 2
            Sx.dma(stg[s_][:, 0:cw], wv[:, k, c0:c0 + cw], writes=[L + f"stg{s_}"])
            eng = "pool" if i % 2 == 0 else "dve"
            if gT is not None:
                Sx.op(eng, lambda e, s_=s_, k=k, c0=c0, cw=cw: e.tensor_scalar(
                    out=Wt[:, k, c0:c0 + cw], in0=stg[s_][:, 0:cw], scalar1=gT[:, k:k + 1], scalar2=None,
                    op0=ALU.mult), reads=[L + f"stg{s_}", L + "gT"], writes=[L + f"{tag}{k}"])
            else:
                Sx.op(eng, lambda e, s_=s_, k=k, c0=c0, cw=cw: e.tensor_copy(
                    out=Wt[:, k, c0:c0 + cw], in_=stg[s_][:, 0:cw]),
                    reads=[L + f"stg{s_}"], writes=[L + f"{tag}{k}"])
            i += 1
    return [L + f"{tag}{k}" for k in range(nk)]


def linear_stage(c, L, src, src_name, dst, dst_name, w_ap, ncols_total, gain=None, resid=None):
    sb, Sx = c.sb, c.S
    m = sb.mark()
    W = sb.t("lw", [128, 8, ncols_total], BF16)
    xt = [sb.t("xt", [128, D], F32) for _ in range(2)]
    xbf = [sb.t("xbf", [128, D], BF16) for _ in range(2)]
    junk = sb.t("junk", [128, D], BF16)
    hT = [sb.t("hT", [128, 8, 128], BF16) for _ in range(2)]
    pj = [sb.t("pj", [128, ncols_total], F32) for _ in range(2)]
    st = [sb.t("st", [128, 4], F32) for _ in range(2)]
    rt = [sb.t("rt", [128, D], F32) for _ in range(2)] if resid is not None else None
    gT = None
    if gain is not None:
        gT = sb.t("gT", [128, 8], F32)
        Sx.dma(gT[:], gain.rearrange("(k p) -> p k", p=128), writes=[L + "gT"], allow_slow_non_contiguous=True)
    Wres = load_weight_bf16(c, L, W, w_ap, 8, ncols_total, gT=gT, seg=min(2060, ncols_total))
    nch = (ncols_total + 511) // 512
    ncols = [(n * 512, min(512, ncols_total - n * 512)) for n in range(nch)]
    def load(t):
        s_ = t % 2
        r0 = t * 128
        src_res = src_name(t) if callable(src_name) else [f"{src_name}{t}"]
        Sx.dma(xt[s_][:], src[r0:r0 + 128, :], reads=src_res, writes=[L + f"xt{s_}"])
        if resid is not None:
            Sx.dma(rt[s_][:], resid[0][r0:r0 + 128, :], reads=[f"{resid[1]}{t}"], writes=[L + f"rt{s_}"])

    load(0)
    for t in range(NT):
        s_ = t % 2
        r0 = t * 128
        if t + 1 < NT:
            load(t + 1)
        if gain is not None:
            Sx.op("act", lambda e, s_=s_: e.activation(out=junk[:], in_=xt[s_][:], func=AF.Square,
                                                       accum_out=st[s_][:, 0:1]),
                  reads=[L + f"xt{s_}"], writes=[L + "junk", L + f"ss{s_}"])
            Sx.op("dve", lambda e, s_=s_: e.tensor_scalar(out=st[s_][:, 1:2], in0=st[s_][:, 0:1], scalar1=1.0 / D,
                                                          scalar2=EPS, op0=ALU.mult, op1=ALU.add),
                  reads=[L + f"ss{s_}"], writes=[L + f"ms{s_}"])
            Sx.op("act", lambda e, s_=s_: e.activation(out=st[s_][:, 3:4], in_=st[s_][:, 1:2], func=AF.Sqrt),
                  reads=[L + f"ms{s_}"], writes=[L + f"sd{s_}"])
            Sx.op("dve", lambda e, s_=s_: e.reciprocal(out=st[s_][:, 2:3], in_=st[s_][:, 3:4]),
                  reads=[L + f"sd{s_}"], writes=[L + f"rstd{s_}"])
        Sx.op("pool", lambda e, s_=s_: e.tensor_copy(out=xbf[s_][:], in_=xt[s_][:]),
              reads=[L + f"xt{s_}"], writes=[L + f"xbf{s_}"])
        for k in range(8):
            Sx.op("pe", lambda e, s_=s_, k=k: e.matmul(
                out=c.ps[k // 4][:, (k % 4) * 128:(k % 4 + 1) * 128], lhsT=xbf[s_][:, k * 128:(k + 1) * 128],
                rhs=c.ident[:], start=True, stop=True),
                reads=[L + f"xbf{s_}", "ident"], writes=[f"ps{k // 4}"])
        for hh in range(2):
            if hh == 0:
                Sx.op("act", lambda e, s_=s_, hh=hh: e.copy(
                    out=hT[s_][:, hh * 4:(hh + 1) * 4, :], in_=c.ps[hh][:].rearrange("p (k t) -> p k t", k=4)),
                    reads=[f"ps{hh}"], writes=[L + f"hT{s_}{hh}"])
            else:
                Sx.op("dve", lambda e, s_=s_, hh=hh: e.tensor_copy(
                    out=hT[s_][:, hh * 4:(hh + 1) * 4, :], in_=c.ps[hh][:].rearrange("p (k t) -> p k t", k=4)),
                    reads=[f"ps{hh}"], writes=[L + f"hT{s_}{hh}"])
        for n, (c0, cw) in enumerate(ncols):
            b = 2 + n % 4
            for k in range(8):
                Sx.op("pe", lambda e, s_=s_, k=k, b=b, c0=c0, cw=cw: e.matmul(
                    out=c.ps[b][:, 0:cw], lhsT=hT[s_][:, k, :], rhs=W[:, k, c0:c0 + cw],
                    start=(k == 0), stop=(k == 7)),
                    reads=[L + f"hT{s_}{k // 4}", Wres[k]], writes=[f"ps{b}"])
            if resid is not None:
                Sx.op("dve", lambda e, s_=s_, b=b, c0=c0, cw=cw: e.tensor_tensor(
                    out=pj[s_][:, c0:c0 + cw], in0=c.ps[b][:, 0:cw], in1=rt[s_][:, c0:c0 + cw], op=ALU.add),
                    reads=[f"ps{b}", L + f"rt{s_}"], writes=[L + f"pj{s_}_{n}"])
            elif n % 2 == 0:
                Sx.op("act", lambda e, s_=s_, b=b, c0=c0, cw=cw: e.activation(
                    out=pj[s_][:, c0:c0 + cw], in_=c.ps[b][:, 0:cw], func=AF.Copy, scale=st[s_][:, 2:3]),
                    reads=[f"ps{b}", L + f"rstd{s_}"], writes=[L + f"pj{s_}_{n}"])
            else:
                Sx.op("dve", lambda e, s_=s_, b=b, c0=c0, cw=cw: e.tensor_scalar(
                    out=pj[s_][:, c0:c0 + cw], in0=c.ps[b][:, 0:cw], scalar1=st[s_][:, 2:3], scalar2=None,
                    op0=ALU.mult),
                    reads=[f"ps{b}", L + f"rstd{s_}"], writes=[L + f"pj{s_}_{n}"])
        Sx.dma(dst[r0:r0 + 128, :], pj[s_][:], reads=[L + f"pj{s_}_{n}" for n in range(nch)],
               writes=[f"{dst_name}{t}"])
    sb.reset(m)


def stage5b(c, l, src, src_name, dst, dst_name):
    sb, Sx = c.sb, c.S
    L = f"L{l}s5b"
    m = sb.mark()
    NCH = 2 * D_FF // 128
    NP = NCH // 2
    WU = sb.t("wu", [128, 8, 2 * D_FF], BF16)
    WD = sb.t("wd", [128, NP, D], BF16)
    gT = sb.t("gT", [128, 8], F32)
    dwT = sb.t("dwT", [128, NCH, 4], F32)
    xt = [sb.t("xt", [128, 2, D], F32) for _ in range(1)]
    xbf = sb.t("xbf", [128, 2, D], BF16)
    junk = sb.t("junk", [128, D], BF16)
    st = sb.t("st", [128, 2, 4], F32)
    h2T = [sb.t("h2T", [128, 8, 258], BF16) for _ in range(2)]
    acc = [sb.t("acc", [128, 2, 256], F32) for _ in range(3)]
    sg = [sb.t("sg", [128, 256], F32) for _ in range(3)]
    aT = [sb.t("aT", [128, 256], BF16) for _ in range(3)]
    ot = sb.t("ot", [128, 2, D], F32)
    Sx.dma(gT[:], c.W("ffn_norm")[l].rearrange("(k p) -> p k", p=128), writes=[L + "gT"],
           allow_slow_non_contiguous=True)
    dwv = c.W("ffn_dw")[l].rearrange("k (c p) -> p c k", p=128)
    for k in range(3):
        for c0 in range(0, NCH, 11):
            Sx.dma(dwT[:, c0:c0 + 11, k:k + 1], dwv[:, c0:c0 + 11, k:k + 1], writes=[L + "dwT"],
                   allow_slow_non_contiguous=True)
    bv = c.W("ffn_dw_b")[l].rearrange("(c p) -> p c", p=128)
    for c0 in range(0, NCH, 11):
        Sx.dma(dwT[:, c0:c0 + 11, 3:4], bv[:, c0:c0 + 11].unsqueeze(2), writes=[L + "dwT"],
               allow_slow_non_contiguous=True)
    WUres = load_weight_bf16(c, L, WU, c.W("w_up")[l], 8, 2 * D_FF, gT=gT, seg=704, tag="WU")
    WDres = load_weight_bf16(c, L, WD, c.W("w_down")[l], NP, D, seg=512, tag="WD")
    Sx.op("pool", lambda e: e.memset(h2T[0][:, :, 0:2], 0.0), writes=[L + "h2T0h"])
    NJ = S // 256
    for j in range(NJ):
        r0 = j * 256
        hs = j % 2
        hp = (j - 1) % 2
        xs = 0
        Sx.dma(xt[xs][:], src[r0:r0 + 256, :].rearrange("(a p) d -> p a d", p=128),
               reads=[f"{src_name}{2 * j}", f"{src_name}{2 * j + 1}"], writes=[L + f"xt{xs}"])
        if j > 0:
            Sx.op("pool", lambda e, hs=hs, hp=hp: e.tensor_copy(out=h2T[hs][:, :, 0:2], in_=h2T[hp][:, :, 256:258]),
                  reads=[L + f"h2T{hp}"], writes=[L + f"h2T{hs}h"])
        for a in range(2):
            Sx.op("act", lambda e, a=a, xs=xs: e.activation(out=junk[:], in_=xt[xs][:, a, :], func=AF.Square,
                                                            accum_out=st[:, a, 0:1]),
                  reads=[L + f"xt{xs}"], writes=[L + "junk", L + f"ss{a}"])
            Sx.op("dve", lambda e, a=a: e.tensor_scalar(out=st[:, a, 1:2], in0=st[:, a, 0:1], scalar1=1.0 / D,
                                                        scalar2=EPS, op0=ALU.mult, op1=ALU.add),
                  reads=[L + f"ss{a}"], writes=[L + f"ms{a}"])
            Sx.op("act", lambda e, a=a: e.activation(out=st[:, a, 3:4], in_=st[:, a, 1:2], func=AF.Sqrt),
                  reads=[L + f"ms{a}"], writes=[L + f"sd{a}"])
            Sx.op("dve", lambda e, a=a: e.reciprocal(out=st[:, a, 2:3], in_=st[:, a, 3:4]),
                  reads=[L + f"sd{a}"], writes=[L + f"rstd{a}"])
            Sx.op("pool", lambda e, a=a, xs=xs: e.tensor_scalar(out=xbf[:, a, :], in0=xt[xs][:, a, :],
                                                                scalar1=st[:, a, 2:3], scalar2=None, op0=ALU.mult),
                  reads=[L + f"xt{xs}", L + f"rstd{a}"], writes=[L + f"xbf{a}"])
            for k in range(8):
                Sx.op("pe", lambda e, a=a, k=k: e.matmul(
                    out=c.ps[k // 4][:, (k % 4) * 128:(k % 4 + 1) * 128], lhsT=xbf[:, a, k * 128:(k + 1) * 128],
                    rhs=c.ident[:], start=True, stop=True),
                    reads=[L + f"xbf{a}", "ident"], writes=[f"ps{k // 4}"])
            for hh in range(2):
                if hh == 0:
                    Sx.op("act", lambda e, a=a, hh=hh, hs=hs: e.copy(
                        out=h2T[hs][:, hh * 4:(hh + 1) * 4, 2 + a * 128:2 + (a + 1) * 128],
                        in_=c.ps[hh][:].rearrange("p (k t) -> p k t", k=4)),
                        reads=[f"ps{hh}"], writes=[L + f"h2T{hs}"])
                else:
                    Sx.op("dve", lambda e, a=a, hh=hh, hs=hs: e.tensor_copy(
                        out=h2T[hs][:, hh * 4:(hh + 1) * 4, 2 + a * 128:2 + (a + 1) * 128],
                        in_=c.ps[hh][:].rearrange("p (k t) -> p k t", k=4)),
                        reads=[f"ps{hh}"], writes=[L + f"h2T{hs}"])

        def up(i, hs=hs):
            s2 = i % 2
            s3 = i % 3
            for gv in range(2):
                cc = i + gv * NP
                b = 2 * s2 + gv
                for k in range(8):
                    Sx.op("pe", lambda e, b=b, cc=cc, k=k: e.matmul(
                        out=c.ps[b][:, 0:258], lhsT=WU[:, k, cc * 128:(cc + 1) * 128],
                        rhs=h2T[hs][:, k, :], start=(k == 0), stop=(k == 7)),
                        reads=[L + f"h2T{hs}", L + f"h2T{hs}h", WUres[k]], writes=[f"ps{b}"])
                Sx.op("act", lambda e, s3=s3, gv=gv, cc=cc, b=b: e.activation(
                    out=acc[s3][:, gv, :], in_=c.ps[b][:, 2:258], func=AF.Identity, scale=dwT[:, cc, 2:3],
                    bias=dwT[:, cc, 3:4]),
                    reads=[f"ps{b}", L + "dwT"], writes=[L + f"acc{s3}{gv}"])
                Sx.op("dve", lambda e, s3=s3, gv=gv, cc=cc, b=b: e.scalar_tensor_tensor(
                    out=acc[s3][:, gv, :], in0=c.ps[b][:, 1:257], scalar=dwT[:, cc, 1:2], in1=acc[s3][:, gv, :],
                    op0=ALU.mult, op1=ALU.add),
                    reads=[f"ps{b}", L + "dwT"], writes=[L + f"acc{s3}{gv}"])
                Sx.op("dve", lambda e, s3=s3, gv=gv, cc=cc, b=b: e.scalar_tensor_tensor(
                    out=acc[s3][:, gv, :], in0=c.ps[b][:, 0:256], scalar=dwT[:, cc, 0:1], in1=acc[s3][:, gv, :],
                    op0=ALU.mult, op1=ALU.add),
                    reads=[f"ps{b}", L + "dwT"], writes=[L + f"acc{s3}{gv}"])
            Sx.op("act", lambda e, s3=s3: e.activation(out=sg[s3][:], in_=acc[s3][:, 0, :], func=AF.Silu),
                  reads=[L + f"acc{s3}0"], writes=[L + f"sg{s3}"])
            Sx.op("pool", lambda e, s3=s3: e.tensor_tensor(out=aT[s3][:], in0=sg[s3][:], in1=acc[s3][:, 1, :],
                                                          op=ALU.mult),
                  reads=[L + f"sg{s3}", L + f"acc{s3}1"], writes=[L + f"aT{s3}"])

        def down(i, hs=hs):
            s3 = i % 3
            for a in range(2):
                for hf in range(2):
                    b2 = 4 + a * 2 + hf
                    Sx.op("pe", lambda e, s3=s3, a=a, hf=hf, b2=b2, i=i: e.matmul(
                        out=c.ps[b2][:], lhsT=aT[s3][:, a * 128:(a + 1) * 128], rhs=WD[:, i, hf * 512:(hf + 1) * 512],
                        start=(i == 0), stop=(i == NP - 1)),
                        reads=[L + f"aT{s3}", WDres[i]], writes=[f"ps{b2}"])

        for i in range(NP + 2):
            if i < NP:
                up(i)
            if i >= 2:
                down(i - 2)
        for a in range(2):
            for hf in range(2):
                b2 = 4 + a * 2 + hf
                Sx.op("dve", lambda e, a=a, hf=hf, b2=b2, xs=xs: e.tensor_tensor(
                    out=ot[:, a, hf * 512:(hf + 1) * 512], in0=c.ps[b2][:], in1=xt[xs][:, a, hf * 512:(hf + 1) * 512],
                    op=ALU.add),
                    reads=[f"ps{b2}", L + f"xt{xs}"], writes=[L + "ot"])
        Sx.dma(dst[r0:r0 + 256, :].rearrange("(a p) d -> p a d", p=128), ot[:],
               reads=[L + "ot"], writes=[f"{dst_name}{2 * j}", f"{dst_name}{2 * j + 1}"])
    sb.reset(m)


def stage2(c, l):
    sb, Sx = c.sb, c.S
    L = f"L{l}s2"
    m = sb.mark()
    PAD = 30
    aT = sb.t("aT", [128, 2, PAD + S], F32)
    acc = sb.t("acc", [128, 2, S], F32)
    dwT = sb.t("dwT", [128, 2, 32], F32)
    lnp = sb.t("lnp", [128, 2, 2], F32)
    pwf = sb.t("pwf", [128, 2, 256], F32)
    pw = sb.t("pw", [128, 2, 256], BF16)
    pwb = sb.t("pwb", [128, 256], F32)
    ones = sb.t("ones", [128, 128], F32)
    vg = [sb.t("vg", [128, 512], F32) for _ in range(2)]
    sgm = [sb.t("sgm", [128, 256], F32) for _ in range(2)]
    atm = [sb.t("atm", [128, 256], F32) for _ in range(2)]
    sq = sb.t("sq", [128, 2, 512], F32)
    mean = sb.t("mean", [128, 512], F32)
    tmp = sb.t("tmp", [128, 512], F32)
    var = sb.t("var", [128, 512], F32)
    rstd = sb.t("rstd", [128, 512], F32)
    yn = sb.t("yn", [128, 2, 512], F32)
    zT = sb.t("zT", [128, 2, 512], BF16)
    yo = [sb.t("yo", [128, 256], F32) for _ in range(2)]
    dv = c.W("conv_dw")[l].rearrange("k (c p) -> p c k", p=128)
    for ch in range(2):
        Sx.dma(dwT[:, ch:ch + 1, 0:31], dv[:, ch:ch + 1, :], writes=[L + "dwT"], allow_slow_non_contiguous=True)
    Sx.dma(dwT[:, :, 31:32], c.W("conv_dw_b")[l].rearrange("(c p) -> p c", p=128).unsqueeze(2),
           writes=[L + "dwT"], allow_slow_non_contiguous=True)
    Sx.dma(lnp[:, :, 0:1], c.W("conv_ln_g")[l].rearrange("(c p) -> p c", p=128).unsqueeze(2),
           writes=[L + "lnp"], allow_slow_non_contiguous=True)
    Sx.dma(lnp[:, :, 1:2], c.W("conv_ln_b")[l].rearrange("(c p) -> p c", p=128).unsqueeze(2),
           writes=[L + "lnp"], allow_slow_non_contiguous=True)
    Sx.dma(pwf[:], c.W("conv_pw")[l].rearrange("(k p) n -> p k n", p=128), writes=[L + "pwf"])
    Sx.op("pool", lambda e: e.tensor_copy(out=pw[:], in_=pwf[:]), reads=[L + "pwf"], writes=[L + "pw"])
    Sx.dma(pwb[:], c.W("conv_pw_b")[l].partition_broadcast(128), writes=[L + "pwb"])
    Sx.op("pool", lambda e: e.memset(ones[:], 1.0), writes=[L + "ones"])
    Sx.op("pool", lambda e: e.memset(aT[:, :, 0:PAD], 0.0), writes=[L + "aTpad"])
    for t in range(NT):
        s_ = t % 2
        r0 = t * 128
        Sx.dma(vg[s_][:], c.proj[r0:r0 + 128, 0:512], reads=[f"proj{t}"], writes=[L + f"vg{s_}"])
        Sx.op("act", lambda e, s_=s_: e.activation(out=sgm[s_][:], in_=vg[s_][:, 256:512], func=AF.Sigmoid),
              reads=[L + f"vg{s_}"], writes=[L + f"sgm{s_}"])
        Sx.op("dve", lambda e, s_=s_: e.tensor_tensor(out=atm[s_][:], in0=vg[s_][:, 0:256], in1=sgm[s_][:],
                                                      op=ALU.mult),
              reads=[L + f"vg{s_}", L + f"sgm{s_}"], writes=[L + f"atm{s_}"])
        for ch in range(2):
            b = (2 * t + ch) % 2
            Sx.op("pe", lambda e, s_=s_, ch=ch, b=b: e.transpose(
                out=c.ps[b][:, 0:128], in_=atm[s_][:, ch * 128:(ch + 1) * 128], identity=c.ident_f[:]),
                reads=[L + f"atm{s_}", "ident_f"], writes=[f"ps{b}"])
            if ch == 0:
                Sx.op("act", lambda e, ch=ch, b=b, r0=r0: e.copy(
                    out=aT[:, ch, PAD + r0:PAD + r0 + 128], in_=c.ps[b][:, 0:128]),
                    reads=[f"ps{b}"], writes=[L + f"aT{t // 8}_{ch}"])
            else:
                Sx.op("pool" if False else "dve", lambda e, ch=ch, b=b, r0=r0: e.tensor_copy(
                    out=aT[:, ch, PAD + r0:PAD + r0 + 128], in_=c.ps[b][:, 0:128]),
                    reads=[f"ps{b}"], writes=[L + f"aT{t // 8}_{ch}"])
    for blk in range(4):
        t0 = blk * 1024
        for ch in range(2):
            rd = [L + f"aT{bb}_{ch}" for bb in range(max(0, blk - 1), blk + 1)] + [L + "aTpad", L + "dwT"]
            Sx.op("pool", lambda e, ch=ch, t0=t0: e.tensor_scalar(
                out=acc[:, ch, t0:t0 + 1024], in0=aT[:, ch, PAD + t0:PAD + t0 + 1024], scalar1=dwT[:, ch, 30:31],
                scalar2=dwT[:, ch, 31:32], op0=ALU.mult, op1=ALU.add),
                reads=rd, writes=[L + f"acc{blk}_{ch}"])
            for k in range(30):
                Sx.op("dve", lambda e, ch=ch, t0=t0, k=k: e.scalar_tensor_tensor(
                    out=acc[:, ch, t0:t0 + 1024], in0=aT[:, ch, k + t0:k + t0 + 1024], scalar=dwT[:, ch, k:k + 1],
                    in1=acc[:, ch, t0:t0 + 1024], op0=ALU.mult, op1=ALU.add),
                    reads=rd, writes=[L + f"acc{blk}_{ch}"])
    for q in range(8):
        t0 = q * 512
        blk = q // 2
        for ch in range(2):
            Sx.op("pool", lambda e, ch=ch, t0=t0: e.tensor_tensor(
                out=sq[:, ch, :], in0=acc[:, ch, t0:t0 + 512], in1=acc[:, ch, t0:t0 + 512], op=ALU.mult),
                reads=[L + f"acc{blk}_{ch}"], writes=[L + f"sq{ch}"])
        for ch in range(2):
            Sx.op("pe", lambda e, ch=ch, t0=t0: e.matmul(
                out=c.ps[2][:], lhsT=ones[:], rhs=acc[:, ch, t0:t0 + 512], start=(ch == 0), stop=(ch == 1)),
                reads=[L + "ones", L + f"acc{blk}_{ch}"], writes=["ps2"])
        for ch in range(2):
            Sx.op("pe", lambda e, ch=ch: e.matmul(
                out=c.ps[3][:], lhsT=ones[:], rhs=sq[:, ch, :], start=(ch == 0), stop=(ch == 1)),
                reads=[L + "ones", L + f"sq{ch}"], writes=["ps3"])
        Sx.op("act", lambda e: e.activation(out=mean[:], in_=c.ps[2][:], func=AF.Copy, scale=1.0 / 256),
              reads=["ps2"], writes=[L + "mean"])
        Sx.op("pool", lambda e: e.tensor_tensor(out=tmp[:], in0=mean[:], in1=mean[:], op=ALU.mult),
              reads=[L + "mean"], writes=[L + "tmp"])
        Sx.op("dve", lambda e: e.scalar_tensor_tensor(out=var[:], in0=c.ps[3][:], scalar=1.0 / 256, in1=tmp[:],
                                                      op0=ALU.mult, op1=ALU.subtract),
              reads=["ps3", L + "tmp"], writes=[L + "var"])
        Sx.op("dve", lambda e: e.tensor_scalar(out=var[:], in0=var[:], scalar1=EPS, scalar2=None, op0=ALU.add),
              reads=[L + "var"], writes=[L + "var"])
        Sx.op("act", lambda e: e.activation(out=tmp[:], in_=var[:], func=AF.Sqrt),
              reads=[L + "var"], writes=[L + "tmp"])
        Sx.op("dve", lambda e: e.reciprocal(out=rstd[:], in_=tmp[:]), reads=[L + "tmp"], writes=[L + "rstd"])
        for ch in range(2):
            Sx.op("pool", lambda e, ch=ch, t0=t0: e.tensor_tensor(
                out=yn[:, ch, :], in0=acc[:, ch, t0:t0 + 512], in1=mean[:], op=ALU.subtract),
                reads=[L + f"acc{blk}_{ch}", L + "mean"], writes=[L + f"yn{ch}"])
            Sx.op("dve", lambda e, ch=ch: e.tensor_tensor(
                out=yn[:, ch, :], in0=yn[:, ch, :], in1=rstd[:], op=ALU.mult),
                reads=[L + f"yn{ch}", L + "rstd"], writes=[L + f"yn{ch}"])
            Sx.op("act", lambda e, ch=ch: e.activation(
                out=zT[:, ch, :], in_=yn[:, ch, :], func=AF.Silu, scale=lnp[:, ch, 0:1], bias=lnp[:, ch, 1:2]),
                reads=[L + f"yn{ch}", L + "lnp"], writes=[L + f"zT{ch}"])
        for a in range(4):
            s_ = a % 2
            b = 4 + s_
            for ch in range(2):
                Sx.op("pe", lambda e, a=a, ch=ch, b=b: e.matmul(
                    out=c.ps[b][:, 0:256], lhsT=zT[:, ch, a * 128:(a + 1) * 128], rhs=pw[:, ch, :],
                    start=(ch == 0), stop=(ch == 1)),
                    reads=[L + f"zT{ch}", L + "pw"], writes=[f"ps{b}"])
            Sx.op("dve", lambda e, s_=s_, b=b: e.tensor_tensor(out=yo[s_][:], in0=c.ps[b][:, 0:256], in1=pwb[:],
                                                            op=ALU.add),
                  reads=[f"ps{b}", L + "pwb"], writes=[L + f"yo{s_}"])
            tt = q * 4 + a
            Sx.dma(c.mix[tt * 128:(tt + 1) * 128, 0:256], yo[s_][:], reads=[L + f"yo{s_}"], writes=[f"mixA{tt}"])
    sb.reset(m)


def bc(ap, shape, axis):
    return ap.unsqueeze(axis).to_broadcast(list(shape))


def setup_rope(c):
    sb, Sx = c.sb, c.S
    m = sb.mark()
    L = "rope"
    pos_i = sb.t("pos_i", [128, NT], I32)
    pos_f = sb.t("pos_f", [128, NT], F32)
    invf = sb.t("invf", [128, 32], F32)
    ang = sb.t("ang", [128, NT, 32], F32)
    red = sb.t("red", [128, NT, 32], F32)
    cs = sb.t("cs", [128, NT, 64], F32)
    negpi = sb.t("negpi", [128, 1], F32)
    pview = c.pos.rearrange("(t p) o -> p (t o)", p=128)
    for q4 in range(4):
        Sx.dma(pos_i[:, q4 * 8:(q4 + 1) * 8], pview[:, q4 * 8:(q4 + 1) * 8], writes=[L + "pos_i"],
               allow_slow_non_contiguous=True)
    Sx.dma(invf[:], c.invf_d, writes=[L + "invf"])
    Sx.op("pool", lambda e: e.memset(negpi[:], -float(np.pi)), writes=[L + "negpi"])
    Sx.op("dve", lambda e: e.tensor_copy(out=pos_f[:], in_=pos_i[:]), reads=[L + "pos_i"], writes=[L + "pos_f"])
    Sx.op("dve", lambda e: e.tensor_tensor(out=ang[:], in0=bc(pos_f[:], [128, NT, 32], 2),
                                           in1=bc(invf[:], [128, NT, 32], 1), op=ALU.mult),
          reads=[L + "pos_f", L + "invf"], writes=[L + "ang"])
    TWO_PI = float(2 * np.pi)
    C1, C2, C3 = 6.28125, 0.0019350051879882812, 3.019916050561733e-07
    kf = sb.t("kf", [128, NT, 32], F32)
    ki = sb.t("ki", [128, NT, 32], I32)
    msk = sb.t("msk", [128, NT, 32], F32)
    Sx.op("dve", lambda e: e.tensor_scalar(out=kf[:], in0=ang[:], scalar1=1.0 / TWO_PI, scalar2=None, op0=ALU.mult),
          reads=[L + "ang"], writes=[L + "kf"])
    Sx.op("dve", lambda e: e.tensor_copy(out=ki[:], in_=kf[:]), reads=[L + "kf"], writes=[L + "ki"])
    Sx.op("dve", lambda e: e.tensor_copy(out=kf[:], in_=ki[:]), reads=[L + "ki"], writes=[L + "kf"])
    for cst in (C1, C2, C3):
        Sx.op("dve", lambda e, cst=cst: e.scalar_tensor_tensor(out=ang[:], in0=kf[:], scalar=-cst, in1=ang[:],
                                                               op0=ALU.mult, op1=ALU.add),
              reads=[L + "ang", L + "kf"], writes=[L + "ang"])
    PI = float(np.pi)
    PI_LO = 3.1415925
    for which, shift in ((0, 0.5 * np.pi), (1, 0.0)):
        Sx.op("dve", lambda e, shift=shift: e.tensor_scalar(out=red[:], in0=ang[:], scalar1=float(shift), scalar2=None,
                                                            op0=ALU.add),
              reads=[L + "ang"], writes=[L + "red"])
        for cmp_op, thr, adj in ((ALU.is_gt, PI, -TWO_PI), (ALU.is_lt, -PI, TWO_PI)):
            Sx.op("dve", lambda e, cmp_op=cmp_op, thr=thr: e.tensor_scalar(out=msk[:], in0=red[:], scalar1=thr,
                                                                           scalar2=None, op0=cmp_op),
                  reads=[L + "red"], writes=[L + "msk"])
            Sx.op("dve", lambda e, adj=adj: e.scalar_tensor_tensor(out=red[:], in0=msk[:], scalar=adj, in1=red[:],
                                                                   op0=ALU.mult, op1=ALU.add),
                  reads=[L + "red", L + "msk"], writes=[L + "red"])
        Sx.op("dve", lambda e: e.tensor_scalar(out=red[:], in0=red[:], scalar1=-PI_LO, scalar2=PI_LO,
                                               op0=ALU.max, op1=ALU.min),
              reads=[L + "red"], writes=[L + "red"])
        Sx.op("act", lambda e, which=which: e.activation(out=cs[:, :, which * 32:(which + 1) * 32], in_=red[:],
                                                         func=AF.Sin),
              reads=[L + "red"], writes=[L + "cs"])
    cview = c.cs_tab.rearrange("(t p) n -> p t n", p=128)
    for q4 in range(4):
        Sx.dma(cview[:, q4 * 8:(q4 + 1) * 8, :], cs[:, q4 * 8:(q4 + 1) * 8, :], reads=[L + "cs"], writes=["cs_tab"])
    sb.reset(m)


def norm_rope(c, L, ve, x, gn, cs, out, nh, sq, tt, st, gfull=None):
    Sx = c.S
    R = L["scr"]
    sh = [128, nh, 64]
    hs = [128, nh, 32]
    Sx.op(ve, lambda e: e.tensor_tensor(out=sq, in0=x, in1=x, op=ALU.mult), reads=L["x"], writes=[R + "sq"])
    Sx.op("dve", lambda e: e.tensor_reduce(out=st[:, :, 0], in_=sq, axis=AX.X, op=ALU.add),
          reads=[R + "sq"], writes=[R + "ss"])
    Sx.op(ve, lambda e: e.tensor_scalar(out=st[:, :, 1], in0=st[:, :, 0], scalar1=1.0 / 64, scalar2=EPS,
                                        op0=ALU.mult, op1=ALU.add), reads=[R + "ss"], writes=[R + "ms"])
    Sx.op("act", lambda e: e.activation(out=st[:, :, 2], in_=st[:, :, 1], func=AF.Ln),
          reads=[R + "ms"], writes=[R + "ln"])
    Sx.op("act", lambda e: e.activation(out=st[:, :, 3], in_=st[:, :, 2], func=AF.Exp, scale=-0.5),
          reads=[R + "ln"], writes=[R + "rstd"])
    Sx.op(ve, lambda e: e.tensor_tensor(out=sq, in0=x, in1=bc(st[:, :, 3], sh, 2), op=ALU.mult),
          reads=L["x"] + [R + "rstd", R + "ss"], writes=[R + "sq"])
    gin = gfull if gfull is not None else bc(gn, sh, 1)
    Sx.op(ve, lambda e: e.tensor_tensor(out=sq, in0=sq, in1=gin, op=ALU.mult),
          reads=[R + "sq"] + L["gn"], writes=[R + "sq"])
    cosb = bc(cs[:, 0:32], hs, 1)
    sinb = bc(cs[:, 32:64], hs, 1)
    x1 = sq[:, :, 0:32]
    x2 = sq[:, :, 32:64]
    Sx.op(ve, lambda e: e.tensor_tensor(out=tt[:, :, 0, :], in0=x1, in1=cosb, op=ALU.mult),
          reads=[R + "sq"] + L["cs"], writes=[R + "t0"])
    Sx.op(ve, lambda e: e.tensor_tensor(out=tt[:, :, 1, :], in0=x2, in1=sinb, op=ALU.mult),
          reads=[R + "sq"] + L["cs"], writes=[R + "t1"])
    Sx.op(ve, lambda e: e.tensor_tensor(out=out[:, :, 0:32], in0=tt[:, :, 0, :], in1=tt[:, :, 1, :], op=ALU.subtract),
          reads=[R + "t0", R + "t1"], writes=L["out"])
    Sx.op(ve, lambda e: e.tensor_tensor(out=tt[:, :, 0, :], in0=x2, in1=cosb, op=ALU.mult),
          reads=[R + "sq"] + L["cs"], writes=[R + "t0"])
    Sx.op(ve, lambda e: e.tensor_tensor(out=tt[:, :, 1, :], in0=x1, in1=sinb, op=ALU.mult),
          reads=[R + "sq"] + L["cs"], writes=[R + "t1"])
    Sx.op(ve, lambda e: e.tensor_tensor(out=out[:, :, 32:64], in0=tt[:, :, 0, :], in1=tt[:, :, 1, :], op=ALU.add),
          reads=[R + "t0", R + "t1"], writes=L["out"])


DIL_R = (1, 4, 16)


def stage4(c, l):
    sb, Sx = c.sb, c.S
    L = f"L{l}s4"
    m = sb.mark()
    NS = 3
    gq = sb.t("gq", [128, 64], F32)
    gk = sb.t("gk", [128, 64], F32)
    mk_f = sb.t("mk_f", [128, 4, 128], F32)
    mk = sb.t("mk", [128, 4, 128], BF16)
    X = [sb.t("X", [128, 3, 256], F32) for _ in range(2)]
    CS = [sb.t("CS", [128, 64], F32) for _ in range(2)]
    sq = [sb.t("sq", [128, 8, 64], F32) for _ in range(2)]
    tt = [sb.t("tt", [128, 8, 2, 32], F32) for _ in range(2)]
    stt = [sb.t("stt", [128, 8, 4], F32) for _ in range(2)]
    QK = [sb.t("QK", [128, 8, 64], BF16) for _ in range(2)]
    QZ = [sb.t("QZ", [128, 4, 128], BF16) for _ in range(2)]
    QKz = [sb.t("QKz", [128, 4, 128], BF16) for _ in range(2)]
    KT = [sb.t("KT", [128, 2, 128], BF16) for _ in range(NS)]
    V = [sb.t("V", [128, 4, 65], BF16) for _ in range(NS)]
    PT = [sb.t("PT", [128, 4, 2, 128], BF16) for _ in range(2)]
    O = [sb.t("O", [128, 4, 65], F32) for _ in range(4)]
    Sx.dma(gq[:], c.W("dil_q_norm")[l].partition_broadcast(128), writes=[L + "gq"])
    Sx.dma(gk[:], c.W("dil_k_norm")[l].partition_broadcast(128), writes=[L + "gk"])
    gqk = sb.t("gqk", [128, 8, 64], F32)
    Sx.op("pool", lambda e: e.tensor_copy(out=gqk[:, 0:4, :], in_=bc(gq[:], [128, 4, 64], 1)),
          reads=[L + "gq"], writes=[L + "gqk"])
    Sx.op("pool", lambda e: e.tensor_copy(out=gqk[:, 4:8, :], in_=bc(gk[:], [128, 4, 64], 1)),
          reads=[L + "gk"], writes=[L + "gqk"])
    Sx.dma(mk_f[:], c.masks_d, writes=[L + "mk_f"])
    Sx.op("pool", lambda e: e.tensor_copy(out=mk[:], in_=mk_f[:]), reads=[L + "mk_f"], writes=[L + "mk"])
    for s_ in range(NS):
        Sx.op("pool", lambda e, s_=s_: e.memset(V[s_][:, :, 64:65], 1.0), writes=[L + f"Vone{s_}"])
    for s_ in range(2):
        Sx.op("pool", lambda e, s_=s_: e.memset(QKz[s_][:], 0.0), writes=[L + f"QKzz{s_}"])
    its = []
    for g, r in enumerate(DIL_R):
        tps = (S // r) // 128
        for T in range(NT):
            its.append((g, r, T, T // tps, T % tps))

    def stage_a(it):
        g, r, T, cc, mt = its[it]
        pv = c.proj.rearrange("(m r) n -> r m n", r=r)
        cv = c.cs_tab.rearrange("(m r) n -> r m n", r=r)
        s2 = it % 2
        s3 = it % NS
        ve = "dve" if it % 2 == 0 else "pool"
        if True:
            lo = ((mt * 128) * r + cc) // 128
            hi = ((mt * 128 + 127) * r + cc) // 128
            nat = list(range(lo, hi + 1))
            Sx.dma(X[s2][:], pv[cc, mt * 128:(mt + 1) * 128, D_Q:D_Q + 2304].rearrange(
                "p (i gg j) -> p i gg j", i=3, gg=3)[:, :, g, :],
                reads=[f"proj{t}" for t in nat], writes=[L + f"X{s2}"])
            Sx.dma(CS[s2][:], cv[cc, mt * 128:(mt + 1) * 128, :], reads=["cs_tab"], writes=[L + f"CS{s2}"])
            norm_rope(c, {"x": [L + f"X{s2}"], "out": [L + f"QK{s2}_0", L + f"QK{s2}_1"], "gn": [L + "gqk"],
                          "cs": [L + f"CS{s2}"], "scr": L + f"nr{s2}"},
                      ve, X[s2][:, 0:2, :].rearrange("p i (h d) -> p (i h) d", h=4), None, CS[s2][:],
                      QK[s2][:], 8, sq[s2][:], tt[s2][:], stt[s2][:], gfull=gqk[:])
            Sx.op("pool" if ve == "dve" else "dve", lambda e, s2=s2, s3=s3: e.tensor_copy(
                out=V[s3][:, :, 0:64], in_=X[s2][:, 2, :].rearrange("p (h d) -> p h d", h=4)),
                reads=[L + f"X{s2}"], writes=[L + f"V{s3}"])

    def stage_b(it):
        g, r, T, cc, mt = its[it]
        dv = c.dilO[g].rearrange("(m r) n -> r m n", r=r)
        s2 = it % 2
        s3 = it % NS
        sp = (it - 1) % NS
        if True:
            qv = QK[s2][:, 0:4, :].rearrange("p (b two) d -> p b two d", two=2)
            zv = QKz[s2][:].rearrange("p (b two) n -> p b two n", two=2)
            for par in range(2):
                Sx.op("pool", lambda e, par=par, qv=qv, zv=zv: e.tensor_copy(
                    out=zv[:, :, par, par * 64:(par + 1) * 64], in_=qv[:, :, par, :]),
                    reads=[L + f"QK{s2}_0", L + f"QKzz{s2}"], writes=[L + f"QKz{s2}"])
            for h in range(4):
                Sx.op("pe", lambda e, s2=s2, h=h: e.matmul(
                    out=c.ps[0][:, h * 128:(h + 1) * 128], lhsT=QKz[s2][:, h, :], rhs=c.ident[:],
                    start=True, stop=True),
                    reads=[L + f"QKz{s2}", "ident"], writes=["ps0"])
            for blk in range(2):
                Sx.op("pe", lambda e, s2=s2, blk=blk: e.matmul(
                    out=c.ps[4][:, blk * 128:(blk + 1) * 128],
                    lhsT=QK[s2][:, 4 + 2 * blk:6 + 2 * blk, :].rearrange("p h d -> p (h d)"), rhs=c.ident[:],
                    start=True, stop=True),
                    reads=[L + f"QK{s2}_1", "ident"], writes=["ps4"])
            Sx.op("act", lambda e, s2=s2: e.copy(out=QZ[s2][:], in_=c.ps[0][:].rearrange("p (b t) -> p b t", b=4)),
                  reads=["ps0"], writes=[L + f"QZ{s2}"])
            Sx.op("dve", lambda e, s3=s3: e.tensor_copy(
                out=KT[s3][:], in_=c.ps[4][:, 0:256].rearrange("p (b t) -> p b t", b=2)),
                reads=["ps4"], writes=[L + f"KT{s3}"])
            first = mt == 0
            ks = [s3 if first else sp, s3]
            for h in range(4):
                pb = (h % 2) * 64
                for kk in range(2):
                    bank = 1 + h // 2
                    col = ((h % 2) * 2 + kk) * 128
                    Sx.op("pe", lambda e, h=h, kk=kk, bank=bank, col=col, s2=s2, ksl=ks[kk]: e.matmul(
                        out=c.ps[bank][:, col:col + 128], lhsT=KT[ksl][:, h // 2, :],
                        rhs=QZ[s2][:, h, :], start=True, stop=True),
                        reads=[L + f"KT{ks[kk]}", L + f"QZ{s2}"], writes=[f"ps{bank}"])
            for hp in range(2):
                Sx.op("act", lambda e, hp=hp, s2=s2: e.activation(
                    out=PT[s2][:, 2 * hp:2 * hp + 2, :, :].rearrange("p h k t -> p (h k t)"), in_=c.ps[1 + hp][:],
                    func=AF.Exp, scale=SCALE),
                    reads=[f"ps{1 + hp}"], writes=[L + f"PT{s2}_{hp}"])
            if first:
                for hp in range(2):
                    for kk, mi in ((0, 2), (1, 1)):
                        Sx.op("pool", lambda e, hp=hp, kk=kk, mi=mi, s2=s2: e.tensor_tensor(
                            out=PT[s2][:, 2 * hp:2 * hp + 2, kk, :], in0=PT[s2][:, 2 * hp:2 * hp + 2, kk, :],
                            in1=bc(mk[:, mi, :], [128, 2, 128], 1), op=ALU.mult),
                            reads=[L + f"PT{s2}_{hp}", L + "mk"], writes=[L + f"PT{s2}_{hp}"])
            else:
                for hp in range(2):
                    Sx.op("pool", lambda e, hp=hp, s2=s2: e.tensor_tensor(
                        out=PT[s2][:, 2 * hp:2 * hp + 2, :, :], in0=PT[s2][:, 2 * hp:2 * hp + 2, :, :],
                        in1=bc(mk[:, 0:2, :], [128, 2, 2, 128], 1), op=ALU.mult),
                        reads=[L + f"PT{s2}_{hp}", L + "mk"], writes=[L + f"PT{s2}_{hp}"])
            for h in range(4):
                for kk in range(2):
                    Sx.op("pe", lambda e, h=h, kk=kk, s2=s2, ksl=ks[kk]: e.matmul(
                        out=c.ps[3][:, h * 65:(h + 1) * 65], lhsT=PT[s2][:, h, kk, :], rhs=V[ksl][:, h, :],
                        start=(h == 0 and kk == 0), stop=(h == 3 and kk == 1), skip_group_check=True),
                        reads=[L + f"PT{s2}_{h // 2}", L + f"V{ks[kk]}", L + f"Vone{ks[kk]}"], writes=["ps3"])
            s4 = it % 4
            Sx.op("dve", lambda e, s4=s4: e.tensor_copy(out=O[s4][:].rearrange("p h d -> p (h d)"),
                                                        in_=c.ps[3][:, 0:260]),
                  reads=["ps3"], writes=[L + f"O{s4}"])

    def store(it):
        g, r, T, cc, mt = its[it]
        dv = c.dilO[g].rearrange("(m r) n -> r m n", r=r)
        s4 = it % 4
        Sx.dma(dv[cc, mt * 128:(mt + 1) * 128, :], O[s4][:].rearrange("p h d -> p (h d)"),
               reads=[L + f"O{s4}"], writes=[f"dilO{g}_T{T}"])

    NIT = len(its)
    for it in range(NIT + 3):
        if it < NIT:
            stage_a(it)
        if 1 <= it <= NIT:
            stage_b(it - 1)
        if it >= 3:
            store(it - 3)
    U = [sb.t("U", [128, 3, 260], F32) for _ in range(2)]
    rd = [sb.t("rd", [128, 4], F32) for _ in range(2)]
    yc = [sb.t("yc", [128, 4, 64], F32) for _ in range(2)]
    for t in range(NT):
        s_ = t % 2
        r0 = t * 128
        Sx.dma(U[s_][:], c.dilO[:, r0:r0 + 128, :].rearrange("g p n -> p g n"),
               reads=[f"dilO{g}_T{T}" for g in range(3) for T in range(NT)], writes=[L + f"U{s_}"])
        Sx.op("pool", lambda e, s_=s_: e.tensor_tensor(out=U[s_][:, 0, :], in0=U[s_][:, 0, :], in1=U[s_][:, 1, :],
                                                      op=ALU.add), reads=[L + f"U{s_}"], writes=[L + f"U{s_}"])
        Sx.op("pool", lambda e, s_=s_: e.tensor_tensor(out=U[s_][:, 0, :], in0=U[s_][:, 0, :], in1=U[s_][:, 2, :],
                                                      op=ALU.add), reads=[L + f"U{s_}"], writes=[L + f"U{s_}"])
        Uv = U[s_][:, 0, :].rearrange("p (h d) -> p h d", h=4)
        Sx.op("dve", lambda e, s_=s_, Uv=Uv: e.reciprocal(out=rd[s_][:], in_=Uv[:, :, 64]),
              reads=[L + f"U{s_}"], writes=[L + f"rd{s_}"])
        Sx.op("dve", lambda e, s_=s_, Uv=Uv: e.tensor_tensor(out=yc[s_][:], in0=Uv[:, :, 0:64],
                                                             in1=bc(rd[s_][:], [128, 4, 64], 2), op=ALU.mult),
              reads=[L + f"U{s_}", L + f"rd{s_}"], writes=[L + f"yc{s_}"])
        Sx.dma(c.mix[r0:r0 + 128, 768:1024], yc[s_][:].rearrange("p h d -> p (h d)"),
               reads=[L + f"yc{s_}"], writes=[f"mixC{t}"])
    sb.reset(m)


def stage3(c, l):
    sb, Sx = c.sb, c.S
    L = f"L{l}s3"
    m = sb.mark()
    KAs = sb.t("KAs", [128, 2, S], BF16)
    KWA = sb.t("KWA", [128, 2, S], BF16)
    KCr = sb.t("KCr", [128, 2, S], BF16)
    VS = sb.t("VS", [128, NT, 2, 65], BF16)
    VW = sb.t("VW", [128, NT, 2, 65], BF16)
    KCA = sb.t("KCA", [128, 2, 256], BF16)
    VCa = sb.t("VCa", [128, 2, 2, 129], BF16)
    gq = sb.t("gq", [128, 64], F32)
    gk = sb.t("gk", [128, 64], F32)
    Ebf = sb.t("Ebf", [128, NT, 64], BF16)
    madd = sb.t("madd", [128, 2, 128], BF16)
    Sx.dma(gq[:], c.W("nsa_q_norm")[l].partition_broadcast(128), writes=[L + "gq"])
    Sx.dma(gk[:], c.W("nsa_k_norm")[l].partition_broadcast(128), writes=[L + "gk"])
    m1 = sb.mark()
    Ef = sb.t("Ef", [128, NT, 64], F32)
    mk_f = sb.t("mk_f", [128, 4, 128], F32)
    Sx.dma(Ef[:], c.esel_d.rearrange("(t p) n -> p t n", p=128), writes=[L + "Ef"])
    Sx.op("pool", lambda e: e.tensor_copy(out=Ebf[:], in_=Ef[:]), reads=[L + "Ef"], writes=[L + "Ebf"])
    Sx.dma(mk_f[:], c.masks_d, writes=[L + "mk_f"])
    Sx.op("dve", lambda e: e.tensor_scalar(out=madd[:, 0, :], in0=mk_f[:, 1, :], scalar1=-NEG, scalar2=NEG,
                                           op0=ALU.mult, op1=ALU.add), reads=[L + "mk_f"], writes=[L + "madd"])
    Sx.op("dve", lambda e: e.tensor_scalar(out=madd[:, 1, :], in0=mk_f[:, 3, :], scalar1=-NEG, scalar2=NEG,
                                           op0=ALU.mult, op1=ALU.add), reads=[L + "mk_f"], writes=[L + "madd"])
    Sx.op("pool", lambda e: e.memset(VS[:, :, :, 64:65], 1.0), writes=[L + "VSone"])
    Sx.op("pool", lambda e: e.memset(VW[:, :, :, 64:65], 1.0), writes=[L + "VWone"])
    Sx.mark("s3 setup done")
    X = [sb.t("X", [128, 768], F32) for _ in range(2)]
    CS = [sb.t("CS", [128, 64], F32) for _ in range(2)]
    sq = [sb.t("sq", [128, 4, 64], F32) for _ in range(2)]
    tt = [sb.t("tt", [128, 4, 2, 32], F32) for _ in range(2)]
    stt = [sb.t("stt", [128, 4, 4], F32) for _ in range(2)]
    KMs = [sb.t("KMs", [128, 2, 128], BF16) for _ in range(2)]
    KMw = [sb.t("KMw", [128, 2, 128], BF16) for _ in range(2)]
    KMc = [sb.t("KMc", [128, 2, 128], BF16) for _ in range(2)]
    for s_ in range(2):
        Sx.op("pool", lambda e, s_=s_: e.memset(KMw[s_][:], 0.0), writes=[L + f"KMwz{s_}"])
    for t in range(NT):
        s_ = t % 2
        r0 = t * 128
        ve = "dve" if t % 2 == 0 else "pool"
        vo = "pool" if t % 2 == 0 else "dve"
        Sx.dma(X[s_][:], c.proj[r0:r0 + 128, N_KC:N_KC + 768], reads=[f"proj{t}"], writes=[L + f"X{s_}"])
        Sx.dma(CS[s_][:], c.cs_tab[r0:r0 + 128, :], reads=["cs_tab"], writes=[L + f"CS{s_}"])
        for br, (off, KM) in enumerate(((256, KMs), (512, KMw))):
            norm_rope(c, {"x": [L + f"X{s_}"], "out": [L + f"KM{br}{s_}"], "gn": [L + "gk"],
                          "cs": [L + f"CS{s_}"], "scr": L + f"nrA{s_}{br}"},
                      ve, X[s_][:, off:off + 128].rearrange("p (h d) -> p h d", h=2), gk[:], CS[s_][:],
                      KM[s_][:, :, 0:64], 2, sq[s_][:, 2 * br:2 * br + 2, :], tt[s_][:, 2 * br:2 * br + 2, :, :],
                      stt[s_][:, 2 * br:2 * br + 2, :])
        Sx.op(vo, lambda e, s_=s_, t=t: e.tensor_copy(out=KMs[s_][:, :, 64:128], in_=bc(Ebf[:, t, :], [128, 2, 64], 1)),
              reads=[L + "Ebf"], writes=[L + f"KM0{s_}"])
        Sx.op(vo, lambda e, s_=s_: e.tensor_copy(
            out=KMc[s_][:].rearrange("p h (kv d) -> p h kv d", kv=2),
            in_=X[s_][:, 0:256].rearrange("p (kv h d) -> p h kv d", kv=2, h=2)),
            reads=[L + f"X{s_}"], writes=[L + f"KMc{s_}"])
        Sx.op(vo, lambda e, s_=s_, t=t: e.tensor_copy(
            out=VS[:, t, :, 0:64], in_=X[s_][:, 384:512].rearrange("p (h d) -> p h d", h=2)),
            reads=[L + f"X{s_}"], writes=[L + f"VS{t}"])
        Sx.op(vo, lambda e, s_=s_, t=t: e.tensor_copy(
            out=VW[:, t, :, 0:64], in_=X[s_][:, 640:768].rearrange("p (h d) -> p h d", h=2)),
            reads=[L + f"X{s_}"], writes=[L + f"VW{t}"])
        for i, (KM, nm) in enumerate(((KMs, "KM0"), (KMw, "KM1"), (KMc, "KMc"))):
            for h in range(2):
                bank, col = i, h * 128
                rd = [L + f"{nm}{s_}", "ident"] + ([L + f"KMwz{s_}"] if i == 1 else [])
                Sx.op("pe", lambda e, KM=KM, s_=s_, h=h, bank=bank, col=col: e.matmul(
                    out=c.ps[bank][:, col:col + 128], lhsT=KM[s_][:, h, :], rhs=c.ident[:], start=True, stop=True),
                    reads=rd, writes=[f"ps{bank}"])
        Sx.op("act", lambda e, r0=r0: e.copy(out=KAs[:, :, r0:r0 + 128],
                                             in_=c.ps[0][:, 0:256].rearrange("p (h t) -> p h t", h=2)),
              reads=["ps0"], writes=[L + f"KAs{t}"])
        Sx.op("dve", lambda e, r0=r0: e.tensor_copy(out=KWA[:, :, r0:r0 + 128],
                                                    in_=c.ps[1][:, 0:256].rearrange("p (h t) -> p h t", h=2)),
              reads=["ps1"], writes=[L + f"KWA{t}"])
        Sx.op("act", lambda e, r0=r0: e.copy(out=KCr[:, :, r0:r0 + 128],
                                             in_=c.ps[2][:, 0:256].rearrange("p (h t) -> p h t", h=2)),
              reads=["ps2"], writes=[L + "KCr"])
        if t == 0:
            Sx.mark("phaseA tile0 done")
    sb.reset(m1)
    Sx.mark("phaseA done")
    W1f = sb.t("W1f", [128, 32, 128], F32)
    W1 = sb.t("W1", [128, 32, 128], BF16)
    W2f = sb.t("W2f", [128, 128], F32)
    W2 = sb.t("W2", [128, 128], BF16)
    PEf = sb.t("PEf", [128, 32], F32)
    PEb = sb.t("PEb", [128, 32], BF16)
    b1 = sb.t("b1", [128, 2], F32)
    HID = sb.t("HID", [128, 256], BF16)
    ex = sb.t("ex", [128, 256], F32)
    zz = sb.t("zz", [128, 256], F32)
    kraw = sb.t("kraw", [128, 2, 2, 64], F32)
    CSc = sb.t("CSc", [128, 2, 64], F32)
    mmf = sb.t("mmf", [128, 2, 64], F32)
    KCm = sb.t("KCm", [128, 2, 2, 128], BF16)
    sq2 = sb.t("sq2", [128, 2, 64], F32)
    tt2 = sb.t("tt2", [128, 2, 2, 32], F32)
    st2 = sb.t("st2", [128, 2, 4], F32)
    Sx.op("pool", lambda e: e.memset(W1f[:], 0.0), writes=[L + "W1f"])
    Sx.op("pool", lambda e: e.memset(W2f[:], 0.0), writes=[L + "W2f"])
    Sx.op("pool", lambda e: e.memset(KCm[:], 0.0), writes=[L + "KCmz"])
    Sx.op("pool", lambda e: e.memset(CSc[:], 0.0), writes=[L + "CSc"])
    for lh in range(2):
        Sx.dma(W1f[0:64, lh * 16:(lh + 1) * 16, 0:64],
               c.W("cmp_k_w1")[l].rearrange("(l d) j -> d l j", d=64)[:, lh * 16:(lh + 1) * 16, :], writes=[L + "W1f"])
        Sx.dma(W1f[64:128, lh * 16:(lh + 1) * 16, 64:128],
               c.W("cmp_v_w1")[l].rearrange("(l d) j -> d l j", d=64)[:, lh * 16:(lh + 1) * 16, :], writes=[L + "W1f"])
    Sx.dma(W2f[0:64, 0:64], c.W("cmp_k_w2")[l], writes=[L + "W2f"])
    Sx.dma(W2f[64:128, 64:128], c.W("cmp_v_w2")[l], writes=[L + "W2f"])
    for half in range(2):
        Sx.dma(PEf[0:64, half * 16:(half + 1) * 16],
               c.W("cmp_k_pos")[l].rearrange("l d -> d l")[:, half * 16:(half + 1) * 16],
               writes=[L + "PEf"], allow_slow_non_contiguous=True)
        Sx.dma(PEf[64:128, half * 16:(half + 1) * 16],
               c.W("cmp_v_pos")[l].rearrange("l d -> d l")[:, half * 16:(half + 1) * 16],
               writes=[L + "PEf"], allow_slow_non_contiguous=True)
    Sx.op("dve", lambda e: e.tensor_copy(out=W1[:], in_=W1f[:]), reads=[L + "W1f"], writes=[L + "W1"])
    Sx.op("pool", lambda e: e.tensor_copy(out=W2[:], in_=W2f[:]), reads=[L + "W2f"], writes=[L + "W2"])
    Sx.op("pool", lambda e: e.tensor_copy(out=PEb[:], in_=PEf[:]), reads=[L + "PEf"], writes=[L + "PEb"])
    csv = c.cs_tab.rearrange("(a b) n -> a b n", b=16)
    Sx.dma(CSc[:, 0, :], csv[1:129, 15, :], reads=["cs_tab"], writes=[L + "CSc"])
    Sx.dma(CSc[0:127, 1, :], csv[129:256, 15, :], reads=["cs_tab"], writes=[L + "CSc"])
    Sx.dma(mmf[:], c.mmap_d.rearrange("(ct p) n -> p ct n", p=128), writes=[L + "mmf"])
    for h in range(2):
        Sx.op("pool", lambda e, h=h: e.tensor_copy(out=VCa[:, :, h, 65:129], in_=mmf[:]),
              reads=[L + "mmf"], writes=[L + f"VCam{h}"])
    Sx.op("pool", lambda e: e.memset(VCa[:, :, :, 64:65], 1.0), writes=[L + "VCa1"])
    for li in range(32):
        Sx.op("pe", lambda e, li=li: e.matmul(out=c.ps[2][:, 0:1], lhsT=W1[:, li, :], rhs=PEb[:, li:li + 1],
                                              start=(li == 0), stop=(li == 31)),
              reads=[L + "W1", L + "PEb"], writes=["ps2"])
    Sx.op("dve", lambda e: e.tensor_copy(out=b1[:, 0:1], in_=c.ps[2][:, 0:1]), reads=["ps2"], writes=[L + "b1"])
    Sx.op("dve", lambda e: e.tensor_scalar(out=b1[:, 1:2], in0=b1[:, 0:1], scalar1=-1.0, scalar2=None, op0=ALU.mult),
          reads=[L + "b1"], writes=[L + "nb1"])
    Sx.mark("b1 done")
    for h in range(2):
        kv = KCr[:, h, :].rearrange("p (c l) -> p l c", l=16)
        for li in range(16):
            Sx.op("pe", lambda e, li=li, kv=kv: e.matmul(out=c.ps[3][:, 0:256], lhsT=W1[:, li, :], rhs=kv[:, li, :],
                                                        start=(li == 0), stop=False, skip_group_check=True),
                  reads=[L + "W1", L + "KCr"], writes=["ps3"])
        for li in range(16):
            Sx.op("pe", lambda e, li=li, kv=kv: e.matmul(out=c.ps[3][:, 0:255], lhsT=W1[:, 16 + li, :],
                                                        rhs=kv[:, li, 1:256], start=False, stop=(li == 15),
                                                        skip_group_check=True),
                  reads=[L + "W1", L + "KCr"], writes=["ps3"])
        Sx.op("act", lambda e: e.activation(out=ex[:], in_=c.ps[3][:, 0:256], func=AF.Exp, scale=-1.0, bias=b1[:, 1:2]),
              reads=["ps3", L + "nb1"], writes=[L + "ex"])
        Sx.op("dve", lambda e: e.tensor_scalar(out=ex[:], in0=ex[:], scalar1=1.0, scalar2=None, op0=ALU.add),
              reads=[L + "ex"], writes=[L + "ex"])
        Sx.op("dve", lambda e: e.reciprocal(out=ex[:], in_=ex[:]), reads=[L + "ex"], writes=[L + "ex"])
        Sx.op("dve", lambda e: e.tensor_scalar(out=zz[:], in0=c.ps[3][:, 0:256], scalar1=b1[:, 0:1], scalar2=None,
                                               op0=ALU.add), reads=["ps3", L + "b1"], writes=[L + "zz"])
        Sx.op("dve", lambda e: e.tensor_tensor(out=HID[:], in0=zz[:], in1=ex[:], op=ALU.mult),
              reads=[L + "zz", L + "ex"], writes=[L + "HID"])
        for ct in range(2):
            Sx.op("pe", lambda e, ct=ct: e.matmul(out=c.ps[2][:, ct * 128:(ct + 1) * 128],
                                                  lhsT=HID[:, ct * 128:(ct + 1) * 128], rhs=W2[:],
                                                  start=True, stop=True),
                  reads=[L + "HID", L + "W2"], writes=["ps2"])
        p2 = c.ps[2][:, 0:256].rearrange("p (ct n) -> p ct n", ct=2)
        Sx.op("dve", lambda e, h=h, p2=p2: e.tensor_copy(out=kraw[:, :, h, :], in_=p2[:, :, 0:64]),
              reads=["ps2"], writes=[L + f"kraw{h}"])
        Sx.op("dve", lambda e, h=h, p2=p2: e.tensor_copy(out=VCa[:, :, h, 0:64], in_=p2[:, :, 64:128]),
              reads=["ps2"], writes=[L + f"VCav{h}"])
    for ct in range(2):
        norm_rope(c, {"x": [L + "kraw0", L + "kraw1"], "out": [L + f"KCm{ct}"], "gn": [L + "gk"],
                      "cs": [L + "CSc"], "scr": L + f"nrC{ct}"},
                  "dve", kraw[:, ct, :, :], gk[:], CSc[:, ct, :], KCm[:, ct, :, 0:64], 2, sq2[:], tt2[:], st2[:])
        for h in range(2):
            Sx.op("pe", lambda e, ct=ct, h=h: e.matmul(out=c.ps[2][:, (ct * 2 + h) * 128:(ct * 2 + h + 1) * 128],
                                                       lhsT=KCm[:, ct, h, :], rhs=c.ident[:], start=True, stop=True),
                  reads=[L + f"KCm{ct}", L + "KCmz", "ident"], writes=["ps2"])
    Sx.op("dve", lambda e: e.tensor_copy(out=KCA[:].rearrange("p h (ct n) -> p ct h n", ct=2),
                                         in_=c.ps[2][:].rearrange("p (ct h n) -> p ct h n", ct=2, h=2)),
          reads=["ps2"], writes=[L + "KCA"])
    sb.reset(m1)
    Sx.mark("phaseA' done")
    NSL = 3
    QA0 = sb.t("QA0", [128, 8, 512], BF16)
    QA = sb.t("QA", [128, 8, 512], BF16)
    QM = sb.t("QM", [128, 4, 8, 128], BF16)
    Xq = [sb.t("Xq", [128, 512], F32) for _ in range(2)]
    Gt = sb.t("Gt", [128, 4, 24], F32)
    CSq = [sb.t("CSq", [128, 64], F32) for _ in range(2)]
    sq8 = [sb.t("sq8", [128, 8, 64], F32) for _ in range(2)]
    tt8 = [sb.t("tt8", [128, 8, 2, 32], F32) for _ in range(2)]
    st8 = [sb.t("st8", [128, 8, 4], F32) for _ in range(2)]
    PT = [sb.t("PT", [128, 512], BF16) for _ in range(NSL)]
    CMf = sb.t("CMf", [128, 2, 512], F32)
    CM = sb.t("CM", [128, 2, 512], BF16)
    VISt = sb.t("VISt", [128, 4, 64], F32)
    ADDt = sb.t("ADDt", [128, 4, 64], F32)
    IMP = sb.t("IMP", [128, 4, 2, 64], F32)
    Y = sb.t("Y", [128, 4, 8, 64], F32)
    tmpo = [sb.t("tmpo", [128, 4, 64], F32) for _ in range(2)]
    den = [sb.t("den", [128, 4, 3], F32) for _ in range(2)]
    sc = sb.t("sc", [128, 64], F32)
    sc2 = sb.t("sc2", [128, 64], F32)
    m8 = sb.t("m8", [128, 16], F32)
    nm = sb.t("nm", [128, 64], BF16)
    for j in range(S // 512):
        Lj = L + f"c{j}"
        Sx.dma(VISt[:], c.vis_d[j * 512:(j + 1) * 512, :].rearrange("(a p) n -> p a n", p=128), writes=[L + "VISt"])
        Sx.dma(ADDt[:], c.add_d[j * 512:(j + 1) * 512, :].rearrange("(a p) n -> p a n", p=128), writes=[L + "ADDt"])
        Sx.dma(CMf[:], c.cm_d[:, :, j * 512:(j + 1) * 512], writes=[L + "CMf"])
        Sx.op("pool", lambda e: e.tensor_copy(out=CM[:], in_=CMf[:]), reads=[L + "CMf"], writes=[L + "CM"])
        Sx.dma(Gt[:], c.proj[j * 512:(j + 1) * 512, N_G:N_G + 24].rearrange("(a p) n -> p a n", p=128),
               reads=[f"proj{4 * j + a}" for a in range(4)], writes=[L + "Gt"])
        Sx.op("act", lambda e: e.activation(out=Gt[:], in_=Gt[:], func=AF.Exp, scale=-1.0),
              reads=[L + "Gt"], writes=[L + "Gt"])
        Sx.op("dve", lambda e: e.tensor_scalar(out=Gt[:], in0=Gt[:], scalar1=1.0, scalar2=None, op0=ALU.add),
              reads=[L + "Gt"], writes=[L + "Gt"])
        Sx.op("dve", lambda e: e.reciprocal(out=Gt[:], in_=Gt[:]), reads=[L + "Gt"], writes=[L + "Gt"])
        Sx.op("pool", lambda e: e.memset(QM[:, :, :, 64:128], 0.0), writes=[L + f"QMm{a}" for a in range(4)])
        Sx.op("pool", lambda e: e.memset(IMP[:], 0.0), writes=[L + "IMP"])
        for a in range(4):
            t = 4 * j + a
            s_ = a % 2
            ve = "dve" if a % 2 == 0 else "pool"
            Sx.dma(Xq[s_][:], c.proj[t * 128:(t + 1) * 128, N_Q:N_Q + 512], reads=[f"proj{t}"], writes=[L + f"Xq{s_}"])
            Sx.dma(CSq[s_][:], c.cs_tab[t * 128:(t + 1) * 128, :], reads=["cs_tab"], writes=[L + f"CSq{s_}"])
            norm_rope(c, {"x": [L + f"Xq{s_}"], "out": [L + f"QMq{a}"], "gn": [L + "gq"],
                          "cs": [L + f"CSq{s_}"], "scr": L + f"nrQ{s_}"},
                      ve, Xq[s_][:].rearrange("p (h d) -> p h d", h=8), gq[:], CSq[s_][:],
                      QM[:, a, :, 0:64], 8, sq8[s_][:], tt8[s_][:], st8[s_][:])

        def transposes(dst, dst_name, with_mask):
            for h in range(8):
                for a in range(4):
                    Sx.op("pe", lambda e, h=h, a=a: e.matmul(
                        out=c.ps[3][:, a * 128:(a + 1) * 128], lhsT=QM[:, a, h, :], rhs=c.ident[:],
                        start=True, stop=True),
                        reads=[L + f"QMq{a}", L + f"QMm{a}", "ident"], writes=["ps3"])
                eng = "act" if h % 2 == 0 else "dve"
                if eng == "act":
                    Sx.op("act", lambda e, h=h: e.copy(out=dst[:, h, :], in_=c.ps[3][:]),
                          reads=["ps3"], writes=[L + f"{dst_name}{h}"])
                else:
                    Sx.op("dve", lambda e, h=h: e.tensor_copy(out=dst[:, h, :], in_=c.ps[3][:]),
                          reads=["ps3"], writes=[L + f"{dst_name}{h}"])

        Sx.mark(f"c{j} q prep done")
        transposes(QA0, "QA0_", False)
        Sx.mark(f"c{j} T0 done")

        units = []

        def add_unit(qsrc, qname, h, lhsT, lres, a_lo, a_hi, masks, vrhs, vres, acc, vw, first, last, fin):
            units.append(dict(qsrc=qsrc, qname=qname, h=h, lhsT=lhsT, lres=lres, a_lo=a_lo, a_hi=a_hi, masks=masks,
                              vrhs=vrhs, vres=vres, acc=acc, vw=vw, first=first, last=last, fin=fin))

        def run_units():
            n = len(units)
            LOOK = 2
            for i in range(n + LOOK):
                if i < n:
                    u = units[i]
                    sl = i % NSL
                    ncol = (u["a_hi"] - u["a_lo"] + 1) * 128
                    q0 = u["a_lo"] * 128
                    nmask = len(u["masks"])
                    Sx.op("pe", lambda e, u=u, sl=sl, ncol=ncol, q0=q0, nmask=nmask: e.matmul(
                        out=c.ps[sl][:, 0:ncol], lhsT=u["lhsT"], rhs=u["qsrc"][:, u["h"], q0:q0 + ncol],
                        start=True, stop=(nmask == 0), skip_group_check=True),
                        reads=u["lres"] + [L + f"{u['qname']}{u['h']}"], writes=[f"ps{sl}"])
                    for mi, (a, mrhs, mres) in enumerate(u["masks"]):
                        o0 = (a - u["a_lo"]) * 128 if a is not None else 0
                        wd = 128 if a is not None else ncol
                        Sx.op("pe", lambda e, sl=sl, o0=o0, wd=wd, mrhs=mrhs, mi=mi, nmask=nmask: e.matmul(
                            out=c.ps[sl][:, o0:o0 + wd], lhsT=c.ident[:], rhs=mrhs, start=False,
                            stop=(mi == nmask - 1), skip_group_check=True),
                            reads=["ident"] + mres, writes=[f"ps{sl}"])
                k = i - LOOK
                if k >= 0:
                    u = units[k]
                    sl = k % NSL
                    ncol = (u["a_hi"] - u["a_lo"] + 1) * 128
                    Sx.op("act", lambda e, sl=sl, ncol=ncol: e.activation(
                        out=PT[sl][:, 0:ncol], in_=c.ps[sl][:, 0:ncol], func=AF.Exp, scale=SCALE),
                        reads=[f"ps{sl}"], writes=[L + f"PT{sl}"])
                    vw = u["vw"]
                    started = set()
                    for a in range(u["a_lo"], u["a_hi"] + 1):
                        bank, col = u["acc"](a)
                        st_flag = u["first"] and (bank not in started) and a == u["first_a"].get(bank, -1)
                        started.add(bank)
                        Sx.op("pe", lambda e, u=u, sl=sl, a=a, bank=bank, col=col, vw=vw, st_flag=st_flag: e.matmul(
                            out=c.ps[bank][:, col:col + vw],
                            lhsT=PT[sl][:, (a - u["a_lo"]) * 128:(a - u["a_lo"] + 1) * 128], rhs=u["vrhs"],
                            start=st_flag, stop=False, skip_group_check=True),
                            reads=[L + f"PT{sl}"] + u["vres"], writes=[f"ps{bank}"])
                    if u["fin"] is not None:
                        u["fin"]()
            units.clear()

        def evac(h, br, banks_cols, vw, cmp_branch):
            g = h // 4
            ds = h % 2
            for a in range(4):
                bank, col = banks_cols(a)
                Sx.op("dve", lambda e, a=a, bank=bank, col=col, ds=ds: e.tensor_scalar(
                    out=den[ds][:, a, 0:1], in0=c.ps[bank][:, col + 64:col + 65], scalar1=1e-30, scalar2=None,
                    op0=ALU.max), reads=[f"ps{bank}"], writes=[L + f"den{ds}"])
            Sx.op("dve", lambda e, ds=ds: e.reciprocal(out=den[ds][:, :, 1], in_=den[ds][:, :, 0]),
                  reads=[L + f"den{ds}"], writes=[L + f"rden{ds}"])
            Sx.op("dve", lambda e, ds=ds, h=h, br=br: e.tensor_tensor(
                out=den[ds][:, :, 2], in0=den[ds][:, :, 1], in1=Gt[:, :, h * 3 + br], op=ALU.mult),
                reads=[L + f"rden{ds}", L + "Gt"], writes=[L + f"coef{ds}"])
            for a in range(4):
                bank, col = banks_cols(a)
                if cmp_branch:
                    Sx.op("dve", lambda e, a=a, bank=bank, col=col, ds=ds, h=h: e.tensor_scalar(
                        out=Y[:, a, h, :], in0=c.ps[bank][:, col:col + 64], scalar1=den[ds][:, a, 2:3], scalar2=None,
                        op0=ALU.mult), reads=[f"ps{bank}", L + f"coef{ds}"], writes=[L + f"Y{h}"])
                    Sx.op("dve", lambda e, a=a, bank=bank, col=col, ds=ds, g=g: e.scalar_tensor_tensor(
                        out=IMP[:, a, g, :], in0=c.ps[bank][:, col + 65:col + 129], scalar=den[ds][:, a, 1:2],
                        in1=IMP[:, a, g, :], op0=ALU.mult, op1=ALU.add),
                        reads=[f"ps{bank}", L + f"rden{ds}"], writes=[L + "IMP"])
                else:
                    Sx.op("dve", lambda e, a=a, bank=bank, col=col, ds=ds, h=h: e.scalar_tensor_tensor(
                        out=Y[:, a, h, :], in0=c.ps[bank][:, col:col + 64], scalar=den[ds][:, a, 2:3],
                        in1=Y[:, a, h, :], op0=ALU.mult, op1=ALU.add),
                        reads=[f"ps{bank}", L + f"coef{ds}"], writes=[L + f"Y{h}"])

        cts = [0] if j <= 3 else [0, 1]
        for h in range(8):
            g = h // 4
            bb = 4 + 2 * (h % 2)
            acc = (lambda a, bb=bb: (bb + a // 2, (a % 2) * 129))
            for ci, ct in enumerate(cts):
                add_unit(QA0, "QA0_", h, KCA[:, g, ct * 128:(ct + 1) * 128], [L + "KCA"], 0, 3,
                         [(None, CM[:, ct, :], [L + "CM"])], VCa[:, ct, g, :],
                         [L + f"VCav{g}", L + f"VCam{g}", L + "VCa1"], acc, 129, ci == 0, ci == len(cts) - 1,
                         (lambda h=h, acc=acc: evac(h, 0, acc, 129, True)) if ci == len(cts) - 1 else None)
                units[-1]["first_a"] = {bb: 0, bb + 1: 2}
        run_units()
        Sx.mark(f"c{j} cmp done")
        if j == 0:
            dbg(c, "Ycmp", Y[:].rearrange("p a h d -> p (a h d)"), [L + f"Y{h}" for h in range(8)])
            dbg(c, "IMP", IMP[:].rearrange("p a g n -> p (a g n)"), [L + "IMP"])
            dbg(c, "Gt", Gt[:].rearrange("p a n -> p (a n)"), [L + "Gt"])
            dbg(c, "QA0", QA0[:].rearrange("p h t -> p (h t)"), [L + f"QA0_{h}" for h in range(8)])
            dbg(c, "KCA", KCA[:].rearrange("p h t -> p (h t)"), [L + "KCA"])
            dbg(c, "VCa", VCa[:].rearrange("p a h t -> p (a h t)"), [L + "VCav0", L + "VCav1", L + "VCam0", L + "VCam1", L + "VCa1"])
            dbg(c, "KWA", KWA[:, 0, 0:512], [L + f"KWA{t}" for t in range(4)])
            dbg(c, "KAs", KAs[:, 0, 0:512], [L + f"KAs{t}" for t in range(4)])
        for h in range(8):
            g = h // 4
            bb = 4 + (h % 4)
            acc = (lambda a, bb=bb: (bb, a * 65))
            kts = list(range(max(0, 4 * j - 4), 4 * j + 4))
            seen = set()
            for ki, kt in enumerate(kts):
                a_lo = max(0, kt - 4 * j)
                a_hi = min(3, kt + 4 - 4 * j)
                masks = []
                if kt >= 4 * j:
                    masks.append((kt - 4 * j, madd[:, 0, :], [L + "madd"]))
                if kt + 4 - 4 * j <= 3:
                    masks.append((kt + 4 - 4 * j, madd[:, 1, :], [L + "madd"]))
                add_unit(QA0, "QA0_", h, KWA[:, g, kt * 128:(kt + 1) * 128], [L + f"KWA{kt}"], a_lo, a_hi, masks,
                         VW[:, kt, g, :], [L + f"VW{kt}", L + "VWone"], acc, 65, ki == 0, ki == len(kts) - 1,
                         (lambda h=h, acc=acc: evac(h, 2, acc, 65, False)) if ki == len(kts) - 1 else None)
                units[-1]["first_a"] = {bb: a_lo}
        run_units()
        Sx.mark(f"c{j} win done")
        if j == 0:
            dbg(c, "Ywin", Y[:].rearrange("p a h d -> p (a h d)"), [L + f"Y{h}" for h in range(8)])
        for a in range(4):
            for g in range(2):
                Sx.op("dve", lambda e, a=a, g=g: e.tensor_tensor(out=sc[:], in0=IMP[:, a, g, :], in1=VISt[:, a, :],
                                                                op=ALU.mult),
                      reads=[L + "IMP", L + "VISt"], writes=[L + "sc"])
                Sx.op("dve", lambda e, a=a: e.tensor_tensor(out=sc[:], in0=sc[:], in1=ADDt[:, a, :], op=ALU.add),
                      reads=[L + "sc", L + "ADDt"], writes=[L + "sc"])
                Sx.op("dve", lambda e: e.max(out=m8[:, 0:8], in_=sc[:]), reads=[L + "sc"], writes=[L + "m8a"])
                Sx.op("dve", lambda e: e.match_replace(out=sc2[:], in_to_replace=m8[:, 0:8], in_values=sc[:],
                                                       imm_value=-1e9),
                      reads=[L + "sc", L + "m8a"], writes=[L + "sc2"])
                Sx.op("dve", lambda e: e.max(out=m8[:, 8:16], in_=sc2[:]), reads=[L + "sc2"], writes=[L + "m8b"])
                Sx.op("dve", lambda e: e.tensor_reduce(out=m8[:, 0:1], in_=m8[:, 8:16], axis=AX.X, op=ALU.min),
                      reads=[L + "m8b"], writes=[L + "thr"])
                Sx.op("dve", lambda e: e.tensor_scalar(out=nm[:], in0=sc[:], scalar1=m8[:, 0:1], scalar2=NEG,
                                                       op0=ALU.is_lt, op1=ALU.mult),
                      reads=[L + "sc", L + "thr"], writes=[L + "nm"])
                Sx.op("pool", lambda e, a=a, g=g: e.tensor_copy(out=QM[:, a, 4 * g:4 * g + 4, 64:128],
                                                               in_=bc(nm[:], [128, 4, 64], 1)),
                      reads=[L + "nm"], writes=[L + f"QMm{a}"])
        Sx.mark(f"c{j} topk done")
        if j == 0:
            dbg(c, "QM", QM[:].rearrange("p a h d -> p (a h d)"), [L + f"QMm{a}" for a in range(4)] + [L + f"QMq{a}" for a in range(4)])
        transposes(QA, "QA_", True)
        for h in range(8):
            g = h // 4
            bb = 4 + (h % 4)
            acc = (lambda a, bb=bb: (bb, a * 65))
            kts = list(range(0, 4 * j + 4))
            for ki, kt in enumerate(kts):
                a_lo = max(0, kt - 4 * j)
                masks = []
                if kt >= 4 * j:
                    masks.append((kt - 4 * j, madd[:, 0, :], [L + "madd"]))
                add_unit(QA, "QA_", h, KAs[:, g, kt * 128:(kt + 1) * 128], [L + f"KAs{kt}"], a_lo, 3, masks,
                         VS[:, kt, g, :], [L + f"VS{kt}", L + "VSone"], acc, 65, ki == 0, ki == len(kts) - 1,
                         (lambda h=h, acc=acc: evac(h, 1, acc, 65, False)) if ki == len(kts) - 1 else None)
                units[-1]["first_a"] = {bb: a_lo}
        run_units()
        Sx.mark(f"c{j} sel done")
        for a in range(4):
            t = 4 * j + a
            Sx.dma(c.mix[t * 128:(t + 1) * 128, 256:768], Y[:, a, :, :].rearrange("p h d -> p (h d)"),
                   reads=[L + f"Y{h}" for h in range(8)], writes=[f"mixB{t}"])
    sb.reset(m)


def _prep_inputs(inputs, names, extra=None):
    ident = np.eye(128, dtype=np.float32)
    maps = []
    for b in range(8):
        m = {}
        for n in names:
            if n == "x":
                m[n] = np.ascontiguousarray(inputs["x"][b])
            elif n == "positions":
                m[n] = np.ascontiguousarray(inputs["positions"][b].reshape(S, 1))
            elif n == "c_ident":
                m[n] = ident
            elif n == "c_invf":
                invf = (10000.0 ** (-(np.arange(32, dtype=np.float32) / np.float32(32)))).astype(np.float32)
                m[n] = np.ascontiguousarray(np.broadcast_to(invf[None, :], (128, 32)))
            elif n in ("c_esel", "c_mmap", "c_cm", "c_vis", "c_add"):
                m[n] = _nsa_consts()[n]
            elif n == "c_masks":
                p = np.arange(128)[:, None]
                f = np.arange(128)[None, :]
                m[n] = np.ascontiguousarray(np.stack([(p >= f), (p <= f), np.zeros((128, 128), bool), (p > f)],
                                                     axis=1).astype(np.float32))
            elif n in WSHAPES:
                m[n] = np.ascontiguousarray(inputs[n])
            elif extra is not None and n in extra:
                m[n] = extra[n][b] if isinstance(extra[n], (list, tuple)) else extra[n]
        maps.append(m)
    return maps


_NC_CACHE = {}
_CONSTS = {}


def _nsa_consts():
    if _CONSTS:
        return _CONSTS
    t = np.arange(S)
    n = np.arange(64)
    cur = (t // 64)[:, None]
    nn = n[None, :]
    _CONSTS["c_esel"] = (cur == nn).astype(np.float32)
    forced = (nn == 0) | (nn == cur) | (nn == cur - 1)
    visible = nn <= cur
    _CONSTS["c_vis"] = (visible & ~forced).astype(np.float32)
    add = np.zeros((S, 64), np.float32)
    add = np.where(nn == cur - 1, 10000.0, add)
    add = np.where(nn == cur, 20000.0, add)
    add = np.where(nn == 0, 30000.0, add)
    add = np.where(~visible, -1.0 - nn, add)
    _CONSTS["c_add"] = add.astype(np.float32)
    cidx = np.arange(256)
    cs_ = cidx[:, None] * 16
    ss_ = n[None, :] * 64
    ov = np.clip(np.minimum(cs_ + 32, ss_ + 64) - np.maximum(cs_, ss_), 0, None) / 32.0
    ov[255] = 0.0
    _CONSTS["c_mmap"] = ov.astype(np.float32)
    cm = np.full((128, 2, S), NEG, np.float32)
    for ct in range(2):
        cc = ct * 128 + np.arange(128)
        vis = ((16 * cc + 31)[:, None] <= t[None, :]) & (cc < 255)[:, None]
        cm[:, ct, :] = np.where(vis, 0.0, NEG)
    _CONSTS["c_cm"] = cm
    return _CONSTS


def kernel(**inputs):
    inputs = {k: np.asarray(v) for k, v in inputs.items()}
    if "full" not in _NC_CACHE:
        _NC_CACHE["full"] = build_program()
    nc = _NC_CACHE["full"]
    res = run_bass_kernel_spmd(nc, _prep_inputs(inputs, nc._mk_inputs), core_ids=list(range(8)))
    return np.stack([r["out"] for r in res.results], axis=0).astype(np.float32)
```

```python
import numpy as np
import concourse.bass as bass
import concourse.mybir as mybir
from concourse.bass_utils import run_bass_kernel_spmd

F32 = mybir.dt.float32
BF16 = mybir.dt.bfloat16
I32 = mybir.dt.int32
AF = mybir.ActivationFunctionType
ALU = mybir.AluOpType
AX = mybir.AxisListType

S = 4096
D = 1024
NT = S // 128
IN_DIM = 4120
D_FF = 2816
DEPTH = 4
NEG = -30000.0
EPS = 1e-6
SCALE = 0.125

C_VAL, C_GATE, N_Q, N_KC, N_VC, N_KS, N_VS, N_KW, N_VW, N_G, D_Q, D_K, D_V = (
    0, 256, 512, 1024, 1152, 1280, 1408, 1536, 1664, 1792, 1816, 2584, 3352)

ENGS = ("pe", "act", "dve", "pool", "sp")


class Op:
    __slots__ = ("eng", "fn", "deps", "kind", "sem", "count", "signal", "lane", "seq")

    def __init__(self, eng, fn, kind):
        self.eng = eng
        self.fn = fn
        self.kind = kind
        self.deps = []
        self.sem = None
        self.count = 0
        self.signal = False
        self.lane = None


class Sched:
    def __init__(self, nc, n_lanes=6, same_sync=True):
        self.nc = nc
        self.ops = {e: [] for e in ENGS}
        self.all = []
        self.lastw = {}
        self.readers = {}
        self.n_lanes = n_lanes
        self.same_sync = same_sync
        import os
        self.sync_same = set(os.environ.get("MK_SYNC_SAME", "act,dve,pool").split(",")) - {""}
        self.dma_n = {e: 0 for e in ENGS}
        self.pending_barrier = {}
        self.seq = 0

    def _track(self, op, reads, writes):
        deps = []
        for r in reads:
            w = self.lastw.get(r)
            if w is not None:
                deps.append(w)
        for w_ in writes:
            w = self.lastw.get(w_)
            if w is not None:
                deps.append(w)
            deps.extend(self.readers.get(w_, ()))
        for w_ in writes:
            self.lastw[w_] = op
            self.readers[w_] = []
        for r in reads:
            if r not in writes:
                self.readers.setdefault(r, []).append(op)
        if self.pending_barrier.get(op.eng):
            deps.extend(self.pending_barrier.pop(op.eng))
        best = {}
        for d in deps:
            if d is op:
                continue
            key = (d.eng, d.lane if d.kind == "dma" else -1)
            b = best.get(key)
            if b is None or d.seq > b.seq:
                best[key] = d
        op.deps = list(best.values())
        self.seq += 1
        op.seq = self.seq
        self.ops[op.eng].append(op)
        self.all.append(op)
        return op

    def op(self, eng, fn, reads=(), writes=()):
        import os
        lim = int(os.environ.get("MK_OPLIM", "-1"))
        self.nops = getattr(self, "nops", 0) + 1
        if lim >= 0 and self.nops > lim:
            return None
        return self._track(Op(eng, fn, "cmp"), tuple(reads), tuple(writes))

    def barrier(self):
        last = []
        for e in ENGS:
            cm = [o for o in self.ops[e] if o.kind == "cmp"]
            if cm:
                last.append(cm[-1])
            seen = set()
            for o in reversed(self.ops[e]):
                if o.kind == "dma" and o.lane not in seen:
                    seen.add(o.lane)
                    last.append(o)
                if len(seen) >= self.n_lanes:
                    break
        for e in ENGS:
            self.pending_barrier[e] = list(last)

    def mark(self, label):
        import os
        if os.environ.get("MK_MARKS"):
            print("MARK", label, getattr(self, "nops", 0), flush=True)

    def dma(self, out, in_, reads=(), writes=(), q="sp", **kw):
        o = Op(q, lambda e: e.dma_start(out=out, in_=in_, **kw), "dma")
        o.lane = self.dma_n[q] % self.n_lanes
        self.dma_n[q] += 1
        return self._track(o, tuple(reads), tuple(writes))

    def emit(self):
        nc = self.nc
        for op in self.all:
            for d in op.deps:
                if d.kind == "cmp" and (d.eng != op.eng or d.eng in self.sync_same):
                    d.signal = True
        esem = {e: nc.alloc_semaphore(name=f"s_{e}") for e in ENGS}
        lsem = {}
        lcnt = {}
        cnt = {e: 0 for e in ENGS}
        for op in self.all:
            if op.kind == "cmp":
                if op.signal:
                    cnt[op.eng] += 1
                    op.count = cnt[op.eng]
                    op.sem = ("e", op.eng)
            else:
                key = (op.eng, op.lane)
                if key not in lsem:
                    lsem[key] = nc.alloc_semaphore(name=f"l_{op.eng}{op.lane}")
                    lcnt[key] = 0
                lcnt[key] += 16
                op.count = lcnt[key]
                op.sem = ("l",) + key
        handles = {("e", e): esem[e] for e in ENGS}
        for key, h in lsem.items():
            handles[("l",) + key] = h
        self.max_counts = dict(cnt)

        def emit_engine(name, e):
            waited = {}
            for op in self.ops[name]:
                need = {}
                for d in op.deps:
                    if d.kind == "cmp" and d.eng == name and name not in self.sync_same:
                        continue
                    if need.get(d.sem, 0) < d.count:
                        need[d.sem] = d.count
                if op.kind == "dma" and op.count > 16:
                    if need.get(op.sem, 0) < op.count - 16:
                        need[op.sem] = op.count - 16
                for s, v in need.items():
                    if waited.get(s, 0) >= v:
                        continue
                    e.wait_ge(handles[s], v)
                    waited[s] = v
                ins = op.fn(e)
                if op.kind == "dma":
                    ins.then_inc(handles[op.sem], 16)
                elif op.signal:
                    ins.then_inc(handles[op.sem], 1)

        with nc.Block() as block:
            @block.tensor
            def _(e):
                emit_engine("pe", e)

            @block.scalar
            def _(e):
                emit_engine("act", e)

            @block.vector
            def _(e):
                emit_engine("dve", e)

            @block.gpsimd
            def _(e):
                emit_engine("pool", e)

            @block.sync
            def _(e):
                emit_engine("sp", e)


class SbAlloc:
    def __init__(self, nc, base=16384, limit=212000):
        self.nc = nc
        self.off = base
        self.limit = limit
        self.n = 0
        self.sched = None

    def mark(self):
        return self.off

    def reset(self, m):
        self.off = m
        if self.sched is not None:
            self.sched.barrier()

    def t(self, name, shape, dtype):
        esz = 2 if dtype == BF16 else 4
        nbytes = int(np.prod(shape[1:])) * esz
        off = (self.off + 63) // 64 * 64
        assert off + nbytes <= self.limit, f"SBUF overflow at {name}: {off}+{nbytes}"
        self.n += 1
        h = self.nc.alloc_sbuf_tensor_at(f"{name}_{self.n}", list(shape), dtype, offset=off)
        self.off = off + nbytes
        return h


class Ctx:
    pass


WSHAPES = {
    "attn_norm": (D,), "w_in": (D, IN_DIM), "conv_dw": (31, 256), "conv_dw_b": (256,),
    "conv_ln_g": (256,), "conv_ln_b": (256,), "conv_pw": (256, 256), "conv_pw_b": (256,),
    "nsa_q_norm": (64,), "nsa_k_norm": (64,), "cmp_k_pos": (32, 64), "cmp_k_w1": (2048, 64),
    "cmp_k_w2": (64, 64), "cmp_v_pos": (32, 64), "cmp_v_w1": (2048, 64), "cmp_v_w2": (64, 64),
    "dil_q_norm": (64,), "dil_k_norm": (64,), "w_out": (D, D), "ffn_norm": (D,),
    "w_up": (D, 2 * D_FF), "ffn_dw": (3, 2 * D_FF), "ffn_dw_b": (2 * D_FF,), "w_down": (D_FF, D),
}


def build_program(n_layers=DEPTH, stages=("s1", "s2", "s3", "s4", "s5a", "s5b"), ext=None):
    ext = ext or {}
    nc = bass.Bass("TRN2", target_bir_lowering=False)
    c = Ctx()
    c.nc = nc
    c.S = Sched(nc)
    c.sb = SbAlloc(nc)
    c.sb.sched = c.S
    c.in_names = []
    c.uid = 0

    def din(name, shape, dtype=F32):
        c.in_names.append(name)
        return nc.dram_tensor(name, list(shape), dtype, kind="ExternalInput").ap()

    c.w = {}

    def W(name):
        if name not in c.w:
            c.w[name] = din(name, [DEPTH] + list(WSHAPES[name]))
        return c.w[name]

    c.W = W

    def scratch(name, shape, dtype=F32):
        kind = {"in": "ExternalInput", "out": "ExternalOutput"}.get(ext.get(name), "Internal")
        if kind == "ExternalInput":
            c.in_names.append(name)
        return nc.dram_tensor(name, list(shape), dtype, kind=kind).ap()

    c.x = din("x", [S, D])
    c.ident_d = din("c_ident", [128, 128])
    c.out = nc.dram_tensor("out", [S, D], F32, kind="ExternalOutput").ap()
    c.proj = scratch("proj", [S, IN_DIM])
    c.mix = scratch("mix", [S, D])
    c.x1 = scratch("x1", [S, D])
    c.xa = scratch("xa", [S, D])
    c.xb = scratch("xb", [S, D])
    c.ps = [nc.alloc_psum_tensor(f"psb{i}", [128, 512], F32) for i in range(8)]

    setup_consts(c)
    if "s3" in stages or "s4" in stages or "rope" in stages:
        c.pos = din("positions", [S, 1], I32)
        c.invf_d = din("c_invf", [128, 32])
        c.masks_d = din("c_masks", [128, 4, 128])
        c.cs_tab = scratch("cs_tab", [S, 64])
        c.dilO = scratch("dilO", [3, S, 260])
        c.esel_d = din("c_esel", [S, 64])
        c.mmap_d = din("c_mmap", [256, 64])
        c.cm_d = din("c_cm", [128, 2, S])
        c.vis_d = din("c_vis", [S, 64])
        c.add_d = din("c_add", [S, 64])
        setup_rope(c)
    xin, xin_name = c.x, "x"
    for l in range(n_layers):
        last = l == n_layers - 1
        if "s1" in stages:
            linear_stage(c, f"L{l}s1", xin, xin_name, c.proj, "proj", W("w_in")[l], IN_DIM,
                         gain=W("attn_norm")[l])
        if "s2" in stages:
            stage2(c, l)
        if "s3" in stages:
            stage3(c, l)
        if "s4" in stages:
            stage4(c, l)
        if "s5a" in stages:
            linear_stage(c, f"L{l}s5a", c.mix, (lambda t: [f"mixA{t}", f"mixB{t}", f"mixC{t}"]), c.x1, "x1", W("w_out")[l], D,
                         resid=(xin, xin_name))
        if "s5b" in stages:
            if last:
                xout, xout_name = c.out, "out"
            else:
                xout, xout_name = (c.xa, "xa") if l % 2 == 0 else (c.xb, "xb")
            stage5b(c, l, c.x1, "x1", xout, xout_name)
            xin, xin_name = xout, xout_name
    finish(c)
    c.S.emit()
    c.nc_inputs = list(c.in_names)
    nc._mk_inputs = list(c.in_names)
    nc._mk_counts = (dict(c.S.max_counts), {e: len(v) for e, v in c.S.ops.items()})
    return nc


def setup_consts(c):
    sb, Sx = c.sb, c.S
    c.ident_f = sb.t("ident_f", [128, 128], F32)
    c.ident = sb.t("ident", [128, 128], BF16)
    Sx.dma(c.ident_f[:], c.ident_d, writes=["ident_f"])
    Sx.op("dve", lambda e: e.tensor_copy(out=c.ident[:], in_=c.ident_f[:]), reads=["ident_f"], writes=["ident"])


def dbg(c, name, ap, reads):
    import os
    if not os.environ.get("MK_DBG"):
        return
    shape = list(ap.shape)
    d = c.nc.dram_tensor("dbg_" + name, shape, ap.dtype, kind="ExternalOutput").ap()
    c.S.dma(d, ap, reads=reads, writes=["dbg_" + name])


def finish(c):
    Sx = c.S
    res = [r for r, w in Sx.lastw.items() if w.kind == "dma"]
    Sx.op("sp", lambda e: e.nop(), reads=res)


def load_weight_bf16(c, L, Wt, w_ap, nk, ncols, gT=None, seg=2060, tag="W", nslots=4, stg=None):
    sb, Sx = c.sb, c.S
    if stg is None:
        stg = [sb.t("wstg", [128, seg], F32) for _ in range(nslots)]
    nslots = len(stg)
    wv = w_ap.rearrange("(k p) n -> p k n", p=128)
    i = 0
    for k in range(nk):
        for c0 in range(0, ncols, seg):
            cw = min(seg, ncols - c0)
            s_ = i % nslots
            Sx.dma(stg[s_][:, 0:cw], wv[:, k, c0:c0 + cw], writes=[L + f"stg{s_}"])
            eng = "pool" if i % 2 == 0 else "dve"
            if gT is not None:
                Sx.op(eng, lambda e, s_=s_, k=k, c0=c0, cw=cw: e.tensor_scalar(
                    out=Wt[:, k, c0:c0 + cw], in0=stg[s_][:, 0:cw], scalar1=gT[:, k:k + 1], scalar2=None,
                    op0=ALU.mult), reads=[L + f"stg{s_}", L + "gT"], writes=[L + f"{tag}{k}"])
            else:
                Sx.op(eng, lambda e, s_=s_, k=k, c0=c0, cw=cw: e.tensor_copy(
                    out=Wt[:, k, c0:c0 + cw], in_=stg[s_][:, 0:cw]),
                    reads=[L + f"stg{s_}"], writes=[L + f"{tag}{k}"])
            i += 1
    return [L + f"{tag}{k}" for k in range(nk)]


def linear_stage(c, L, src, src_name, dst, dst_name, w_ap, ncols_total, gain=None, resid=None):
    sb, Sx = c.sb, c.S
    m = sb.mark()
    W = sb.t("lw", [128, 8, ncols_total], BF16)
    xt = [sb.t("xt", [128, D], F32) for _ in range(2)]
    xbf = [sb.t("xbf", [128, D], BF16) for _ in range(2)]
    junk = sb.t("junk", [128, D], BF16)
    hT = [sb.t("hT", [128, 8, 128], BF16) for _ in range(2)]
    pj = [sb.t("pj", [128, ncols_total], F32) for _ in range(2)]
    st = [sb.t("st", [128, 4], F32) for _ in range(2)]
    rt = [sb.t("rt", [128, D], F32) for _ in range(2)] if resid is not None else None
    gT = None
    if gain is not None:
        gT = sb.t("gT", [128, 8], F32)
        Sx.dma(gT[:], gain.rearrange("(k p) -> p k", p=128), writes=[L + "gT"], allow_slow_non_contiguous=True)
    Wres = load_weight_bf16(c, L, W, w_ap, 8, ncols_total, gT=gT, seg=min(2060, ncols_total))
    nch = (ncols_total + 511) // 512
    ncols = [(n * 512, min(512, ncols_total - n * 512)) for n in range(nch)]
    def load(t):
        s_ = t % 2
        r0 = t * 128
        src_res = src_name(t) if callable(src_name) else [f"{src_name}{t}"]
        Sx.dma(xt[s_][:], src[r0:r0 + 128, :], reads=src_res, writes=[L + f"xt{s_}"])
        if resid is not None:
            Sx.dma(rt[s_][:], resid[0][r0:r0 + 128, :], reads=[f"{resid[1]}{t}"], writes=[L + f"rt{s_}"])

    load(0)
    for t in range(NT):
        s_ = t % 2
        r0 = t * 128
        if t + 1 < NT:
            load(t + 1)
        if gain is not None:
            Sx.op("act", lambda e, s_=s_: e.activation(out=junk[:], in_=xt[s_][:], func=AF.Square,
                                                       accum_out=st[s_][:, 0:1]),
                  reads=[L + f"xt{s_}"], writes=[L + "junk", L + f"ss{s_}"])
            Sx.op("dve", lambda e, s_=s_: e.tensor_scalar(out=st[s_][:, 1:2], in0=st[s_][:, 0:1], scalar1=1.0 / D,
                                                          scalar2=EPS, op0=ALU.mult, op1=ALU.add),
                  reads=[L + f"ss{s_}"], writes=[L + f"ms{s_}"])
            Sx.op("act", lambda e, s_=s_: e.activation(out=st[s_][:, 3:4], in_=st[s_][:, 1:2], func=AF.Sqrt),
                  reads=[L + f"ms{s_}"], writes=[L + f"sd{s_}"])
            Sx.op("dve", lambda e, s_=s_: e.reciprocal(out=st[s_][:, 2:3], in_=st[s_][:, 3:4]),
                  reads=[L + f"sd{s_}"], writes=[L + f"rstd{s_}"])
        Sx.op("pool", lambda e, s_=s_: e.tensor_copy(out=xbf[s_][:], in_=xt[s_][:]),
              reads=[L + f"xt{s_}"], writes=[L + f"xbf{s_}"])
        for k in range(8):
            Sx.op("pe", lambda e, s_=s_, k=k: e.matmul(
                out=c.ps[k // 4][:, (k % 4) * 128:(k % 4 + 1) * 128], lhsT=xbf[s_][:, k * 128:(k + 1) * 128],
                rhs=c.ident[:], start=True, stop=True),
                reads=[L + f"xbf{s_}", "ident"], writes=[f"ps{k // 4}"])
        for hh in range(2):
            if hh == 0:
                Sx.op("act", lambda e, s_=s_, hh=hh: e.copy(
                    out=hT[s_][:, hh * 4:(hh + 1) * 4, :], in_=c.ps[hh][:].rearrange("p (k t) -> p k t", k=4)),
                    reads=[f"ps{hh}"], writes=[L + f"hT{s_}{hh}"])
            else:
                Sx.op("dve", lambda e, s_=s_, hh=hh: e.tensor_copy(
                    out=hT[s_][:, hh * 4:(hh + 1) * 4, :], in_=c.ps[hh][:].rearrange("p (k t) -> p k t", k=4)),
                    reads=[f"ps{hh}"], writes=[L + f"hT{s_}{hh}"])
        for n, (c0, cw) in enumerate(ncols):
            b = 2 + n % 4
            for k in range(8):
                Sx.op("pe", lambda e, s_=s_, k=k, b=b, c0=c0, cw=cw: e.matmul(
                    out=c.ps[b][:, 0:cw], lhsT=hT[s_][:, k, :], rhs=W[:, k, c0:c0 + cw],
                    start=(k == 0), stop=(k == 7)),
                    reads=[L + f"hT{s_}{k // 4}", Wres[k]], writes=[f"ps{b}"])
            if resid is not None:
                Sx.op("dve", lambda e, s_=s_, b=b, c0=c0, cw=cw: e.tensor_tensor(
                    out=pj[s_][:, c0:c0 + cw], in0=c.ps[b][:, 0:cw], in1=rt[s_][:, c0:c0 + cw], op=ALU.add),
                    reads=[f"ps{b}", L + f"rt{s_}"], writes=[L + f"pj{s_}_{n}"])
            elif n % 2 == 0:
                Sx.op("act", lambda e, s_=s_, b=b, c0=c0, cw=cw: e.activation(
                    out=pj[s_][:, c0:c0 + cw], in_=c.ps[b][:, 0:cw], func=AF.Copy, scale=st[s_][:, 2:3]),
                    reads=[f"ps{b}", L + f"rstd{s_}"], writes=[L + f"pj{s_}_{n}"])
            else:
                Sx.op("dve", lambda e, s_=s_, b=b, c0=c0, cw=cw: e.tensor_scalar(
                    out=pj[s_][:, c0:c0 + cw], in0=c.ps[b][:, 0:cw], scalar1=st[s_][:, 2:3], scalar2=None,
                    op0=ALU.mult),
                    reads=[f"ps{b}", L + f"rstd{s_}"], writes=[L + f"pj{s_}_{n}"])
        Sx.dma(dst[r0:r0 + 128, :], pj[s_][:], reads=[L + f"pj{s_}_{n}" for n in range(nch)],
               writes=[f"{dst_name}{t}"])
    sb.reset(m)


def stage5b(c, l, src, src_name, dst, dst_name):
    sb, Sx = c.sb, c.S
    L = f"L{l}s5b"
    m = sb.mark()
    NCH = 2 * D_FF // 128
    NP = NCH // 2
    WU = sb.t("wu", [128, 8, 2 * D_FF], BF16)
    WD = sb.t("wd", [128, NP, D], BF16)
    gT = sb.t("gT", [128, 8], F32)
    dwT = sb.t("dwT", [128, NCH, 4], F32)
    xt = [sb.t("xt", [128, 2, D], F32) for _ in range(2)]
    xbf = sb.t("xbf", [128, 2, D], BF16)
    junk = sb.t("junk", [128, D], BF16)
    st = sb.t("st", [128, 2, 4], F32)
    h2T = [sb.t("h2T", [128, 8, 258], BF16) for _ in range(2)]
    acc = [sb.t("acc", [128, 2, 256], F32) for _ in range(3)]
    sg = [sb.t("sg", [128, 256], F32) for _ in range(3)]
    aT = [sb.t("aT", [128, 256], BF16) for _ in range(3)]
    Sx.dma(gT[:], c.W("ffn_norm")[l].rearrange("(k p) -> p k", p=128), writes=[L + "gT"],
           allow_slow_non_contiguous=True)
    dwv = c.W("ffn_dw")[l].rearrange("k (c p) -> p c k", p=128)
    for k in range(3):
        for c0 in range(0, NCH, 11):
            Sx.dma(dwT[:, c0:c0 + 11, k:k + 1], dwv[:, c0:c0 + 11, k:k + 1], writes=[L + "dwT"],
                   allow_slow_non_contiguous=True)
    bv = c.W("ffn_dw_b")[l].rearrange("(c p) -> p c", p=128)
    for c0 in range(0, NCH, 11):
        Sx.dma(dwT[:, c0:c0 + 11, 3:4], bv[:, c0:c0 + 11].unsqueeze(2), writes=[L + "dwT"],
               allow_slow_non_contiguous=True)
    wstg = [sb.t("wstg", [128, 704], F32) for _ in range(4)]
    WUres = load_weight_bf16(c, L, WU, c.W("w_up")[l], 8, 2 * D_FF, gT=gT, seg=704, tag="WU", stg=wstg)
    WDres = load_weight_bf16(c, L, WD, c.W("w_down")[l], NP, D, seg=512, tag="WD", stg=wstg)
    Sx.op("pool", lambda e: e.memset(h2T[0][:, :, 0:2], 0.0), writes=[L + "h2T0h"])
    NJ = S // 256
    for j in range(NJ):
        r0 = j * 256
        hs = j % 2
        hp = (j - 1) % 2
        xs = j % 2
        if j == 0:
            Sx.dma(xt[0][:], src[0:256, :].rearrange("(a p) d -> p a d", p=128),
                   reads=[f"{src_name}0", f"{src_name}1"], writes=[L + "xt0"])
        if j + 1 < NJ:
            Sx.dma(xt[1 - xs][:], src[r0 + 256:r0 + 512, :].rearrange("(a p) d -> p a d", p=128),
                   reads=[f"{src_name}{2 * j + 2}", f"{src_name}{2 * j + 3}"], writes=[L + f"xt{1 - xs}"])
        if j > 0:
            Sx.op("pool", lambda e, hs=hs, hp=hp: e.tensor_copy(out=h2T[hs][:, :, 0:2], in_=h2T[hp][:, :, 256:258]),
                  reads=[L + f"h2T{hp}"], writes=[L + f"h2T{hs}h"])
        for a in range(2):
            Sx.op("act", lambda e, a=a, xs=xs: e.activation(out=junk[:], in_=xt[xs][:, a, :], func=AF.Square,
                                                            accum_out=st[:, a, 0:1]),
                  reads=[L + f"xt{xs}"], writes=[L + "junk", L + f"ss{a}"])
            Sx.op("dve", lambda e, a=a: e.tensor_scalar(out=st[:, a, 1:2], in0=st[:, a, 0:1], scalar1=1.0 / D,
                                                        scalar2=EPS, op0=ALU.mult, op1=ALU.add),
                  reads=[L + f"ss{a}"], writes=[L + f"ms{a}"])
            Sx.op("act", lambda e, a=a: e.activation(out=st[:, a, 3:4], in_=st[:, a, 1:2], func=AF.Sqrt),
                  reads=[L + f"ms{a}"], writes=[L + f"sd{a}"])
            Sx.op("dve", lambda e, a=a: e.reciprocal(out=st[:, a, 2:3], in_=st[:, a, 3:4]),
                  reads=[L + f"sd{a}"], writes=[L + f"rstd{a}"])
            Sx.op("pool", lambda e, a=a, xs=xs: e.tensor_scalar(out=xbf[:, a, :], in0=xt[xs][:, a, :],
                                                                scalar1=st[:, a, 2:3], scalar2=None, op0=ALU.mult),
                  reads=[L + f"xt{xs}", L + f"rstd{a}"], writes=[L + f"xbf{a}"])
            for k in range(8):
                Sx.op("pe", lambda e, a=a, k=k: e.matmul(
                    out=c.ps[k // 4][:, (k % 4) * 128:(k % 4 + 1) * 128], lhsT=xbf[:, a, k * 128:(k + 1) * 128],
                    rhs=c.ident[:], start=True, stop=True),
                    reads=[L + f"xbf{a}", "ident"], writes=[f"ps{k // 4}"])
            for hh in range(2):
                if hh == 0:
                    Sx.op("act", lambda e, a=a, hh=hh, hs=hs: e.copy(
                        out=h2T[hs][:, hh * 4:(hh + 1) * 4, 2 + a * 128:2 + (a + 1) * 128],
                        in_=c.ps[hh][:].rearrange("p (k t) -> p k t", k=4)),
                        reads=[f"ps{hh}"], writes=[L + f"h2T{hs}"])
                else:
                    Sx.op("dve", lambda e, a=a, hh=hh, hs=hs: e.tensor_copy(
                        out=h2T[hs][:, hh * 4:(hh + 1) * 4, 2 + a * 128:2 + (a + 1) * 128],
                        in_=c.ps[hh][:].rearrange("p (k t) -> p k t", k=4)),
                        reads=[f"ps{hh}"], writes=[L + f"h2T{hs}"])

        def up(i, hs=hs):
            s2 = i % 2
            s3 = i % 3
            for gv in range(2):
                cc = i + gv * NP
                b = 2 * s2 + gv
                for k in range(8):
                    Sx.op("pe", lambda e, b=b, cc=cc, k=k: e.matmul(
                        out=c.ps[b][:, 0:258], lhsT=WU[:, k, cc * 128:(cc + 1) * 128],
                        rhs=h2T[hs][:, k, :], start=(k == 0), stop=(k == 7)),
                        reads=[L + f"h2T{hs}", L + f"h2T{hs}h", WUres[k]], writes=[f"ps{b}"])
                Sx.op("act", lambda e, s3=s3, gv=gv, cc=cc, b=b: e.activation(
                    out=acc[s3][:, gv, :], in_=c.ps[b][:, 2:258], func=AF.Identity, scale=dwT[:, cc, 2:3],
                    bias=dwT[:, cc, 3:4]),
                    reads=[f"ps{b}", L + "dwT"], writes=[L + f"acc{s3}{gv}"])
                Sx.op("dve", lambda e, s3=s3, gv=gv, cc=cc, b=b: e.scalar_tensor_tensor(
                    out=acc[s3][:, gv, :], in0=c.ps[b][:, 1:257], scalar=dwT[:, cc, 1:2], in1=acc[s3][:, gv, :],
                    op0=ALU.mult, op1=ALU.add),
                    reads=[f"ps{b}", L + "dwT"], writes=[L + f"acc{s3}{gv}"])
                Sx.op("dve", lambda e, s3=s3, gv=gv, cc=cc, b=b: e.scalar_tensor_tensor(
                    out=acc[s3][:, gv, :], in0=c.ps[b][:, 0:256], scalar=dwT[:, cc, 0:1], in1=acc[s3][:, gv, :],
                    op0=ALU.mult, op1=ALU.add),
                    reads=[f"ps{b}", L + "dwT"], writes=[L + f"acc{s3}{gv}"])
            Sx.op("act", lambda e, s3=s3: e.activation(out=sg[s3][:], in_=acc[s3][:, 0, :], func=AF.Silu),
                  reads=[L + f"acc{s3}0"], writes=[L + f"sg{s3}"])
            Sx.op("pool", lambda e, s3=s3: e.tensor_tensor(out=aT[s3][:], in0=sg[s3][:], in1=acc[s3][:, 1, :],
                                                          op=ALU.mult),
                  reads=[L + f"sg{s3}", L + f"acc{s3}1"], writes=[L + f"aT{s3}"])

        def down(i, hs=hs):
            s3 = i % 3
            for a in range(2):
                for hf in range(2):
                    b2 = 4 + a * 2 + hf
                    Sx.op("pe", lambda e, s3=s3, a=a, hf=hf, b2=b2, i=i: e.matmul(
                        out=c.ps[b2][:], lhsT=aT[s3][:, a * 128:(a + 1) * 128], rhs=WD[:, i, hf * 512:(hf + 1) * 512],
                        start=(i == 0), stop=(i == NP - 1)),
                        reads=[L + f"aT{s3}", WDres[i]], writes=[f"ps{b2}"])

        for i in range(NP + 2):
            if i < NP:
                up(i)
            if i >= 2:
                down(i - 2)
        for a in range(2):
            for hf in range(2):
                b2 = 4 + a * 2 + hf
                Sx.op("dve", lambda e, a=a, hf=hf, b2=b2, xs=xs: e.tensor_tensor(
                    out=xt[xs][:, a, hf * 512:(hf + 1) * 512], in0=c.ps[b2][:],
                    in1=xt[xs][:, a, hf * 512:(hf + 1) * 512], op=ALU.add),
                    reads=[f"ps{b2}"], writes=[L + f"xt{xs}"])
        Sx.dma(dst[r0:r0 + 256, :].rearrange("(a p) d -> p a d", p=128), xt[xs][:],
               reads=[L + f"xt{xs}"], writes=[f"{dst_name}{2 * j}", f"{dst_name}{2 * j + 1}"])
    sb.reset(m)


def stage2(c, l):
    sb, Sx = c.sb, c.S
    L = f"L{l}s2"
    m = sb.mark()
    PAD = 30
    aT = sb.t("aT", [128, 2, PAD + S], F32)
    acc = sb.t("acc", [128, 2, S], F32)
    dwT = sb.t("dwT", [128, 2, 32], F32)
    lnp = sb.t("lnp", [128, 2, 2], F32)
    pwf = sb.t("pwf", [128, 2, 256], F32)
    pw = sb.t("pw", [128, 2, 256], BF16)
    pwb = sb.t("pwb", [128, 256], F32)
    ones = sb.t("ones", [128, 128], F32)
    vg = [sb.t("vg", [128, 512], F32) for _ in range(2)]
    sgm = [sb.t("sgm", [128, 256], F32) for _ in range(2)]
    atm = [sb.t("atm", [128, 256], F32) for _ in range(2)]
    sq = sb.t("sq", [128, 2, 512], F32)
    mean = sb.t("mean", [128, 512], F32)
    tmp = sb.t("tmp", [128, 512], F32)
    var = sb.t("var", [128, 512], F32)
    rstd = sb.t("rstd", [128, 512], F32)
    yn = sb.t("yn", [128, 2, 512], F32)
    zT = sb.t("zT", [128, 2, 512], BF16)
    yo = [sb.t("yo", [128, 256], F32) for _ in range(2)]
    dv = c.W("conv_dw")[l].rearrange("k (c p) -> p c k", p=128)
    for ch in range(2):
        Sx.dma(dwT[:, ch:ch + 1, 0:31], dv[:, ch:ch + 1, :], writes=[L + "dwT"], allow_slow_non_contiguous=True)
    Sx.dma(dwT[:, :, 31:32], c.W("conv_dw_b")[l].rearrange("(c p) -> p c", p=128).unsqueeze(2),
           writes=[L + "dwT"], allow_slow_non_contiguous=True)
    Sx.dma(lnp[:, :, 0:1], c.W("conv_ln_g")[l].rearrange("(c p) -> p c", p=128).unsqueeze(2),
           writes=[L + "lnp"], allow_slow_non_contiguous=True)
    Sx.dma(lnp[:, :, 1:2], c.W("conv_ln_b")[l].rearrange("(c p) -> p c", p=128).unsqueeze(2),
           writes=[L + "lnp"], allow_slow_non_contiguous=True)
    Sx.dma(pwf[:], c.W("conv_pw")[l].rearrange("(k p) n -> p k n", p=128), writes=[L + "pwf"])
    Sx.op("pool", lambda e: e.tensor_copy(out=pw[:], in_=pwf[:]), reads=[L + "pwf"], writes=[L + "pw"])
    Sx.dma(pwb[:], c.W("conv_pw_b")[l].partition_broadcast(128), writes=[L + "pwb"])
    Sx.op("pool", lambda e: e.memset(ones[:], 1.0), writes=[L + "ones"])
    Sx.op("pool", lambda e: e.memset(aT[:, :, 0:PAD], 0.0), writes=[L + "aTpad"])
    for t in range(NT):
        s_ = t % 2
        r0 = t * 128
        Sx.dma(vg[s_][:], c.proj[r0:r0 + 128, 0:512], reads=[f"proj{t}"], writes=[L + f"vg{s_}"])
        Sx.op("act", lambda e, s_=s_: e.activation(out=sgm[s_][:], in_=vg[s_][:, 256:512], func=AF.Sigmoid),
              reads=[L + f"vg{s_}"], writes=[L + f"sgm{s_}"])
        Sx.op("dve", lambda e, s_=s_: e.tensor_tensor(out=atm[s_][:], in0=vg[s_][:, 0:256], in1=sgm[s_][:],
                                                      op=ALU.mult),
              reads=[L + f"vg{s_}", L + f"sgm{s_}"], writes=[L + f"atm{s_}"])
        for ch in range(2):
            b = (2 * t + ch) % 2
            Sx.op("pe", lambda e, s_=s_, ch=ch, b=b: e.transpose(
                out=c.ps[b][:, 0:128], in_=atm[s_][:, ch * 128:(ch + 1) * 128], identity=c.ident_f[:]),
                reads=[L + f"atm{s_}", "ident_f"], writes=[f"ps{b}"])
            if ch == 0:
                Sx.op("act", lambda e, ch=ch, b=b, r0=r0: e.copy(
                    out=aT[:, ch, PAD + r0:PAD + r0 + 128], in_=c.ps[b][:, 0:128]),
                    reads=[f"ps{b}"], writes=[L + f"aT{t // 8}_{ch}"])
            else:
                Sx.op("pool" if False else "dve", lambda e, ch=ch, b=b, r0=r0: e.tensor_copy(
                    out=aT[:, ch, PAD + r0:PAD + r0 + 128], in_=c.ps[b][:, 0:128]),
                    reads=[f"ps{b}"], writes=[L + f"aT{t // 8}_{ch}"])
    for blk in range(4):
        t0 = blk * 1024
        for ch in range(2):
            rd = [L + f"aT{bb}_{ch}" for bb in range(max(0, blk - 1), blk + 1)] + [L + "aTpad", L + "dwT"]
            Sx.op("pool", lambda e, ch=ch, t0=t0: e.tensor_scalar(
                out=acc[:, ch, t0:t0 + 1024], in0=aT[:, ch, PAD + t0:PAD + t0 + 1024], scalar1=dwT[:, ch, 30:31],
                scalar2=dwT[:, ch, 31:32], op0=ALU.mult, op1=ALU.add),
                reads=rd, writes=[L + f"acc{blk}_{ch}"])
            for k in range(30):
                Sx.op("dve", lambda e, ch=ch, t0=t0, k=k: e.scalar_tensor_tensor(
                    out=acc[:, ch, t0:t0 + 1024], in0=aT[:, ch, k + t0:k + t0 + 1024], scalar=dwT[:, ch, k:k + 1],
                    in1=acc[:, ch, t0:t0 + 1024], op0=ALU.mult, op1=ALU.add),
                    reads=rd, writes=[L + f"acc{blk}_{ch}"])
    for q in range(8):
        t0 = q * 512
        blk = q // 2
        for ch in range(2):
            Sx.op("pool", lambda e, ch=ch, t0=t0: e.tensor_tensor(
                out=sq[:, ch, :], in0=acc[:, ch, t0:t0 + 512], in1=acc[:, ch, t0:t0 + 512], op=ALU.mult),
                reads=[L + f"acc{blk}_{ch}"], writes=[L + f"sq{ch}"])
        for ch in range(2):
            Sx.op("pe", lambda e, ch=ch, t0=t0: e.matmul(
                out=c.ps[2][:], lhsT=ones[:], rhs=acc[:, ch, t0:t0 + 512], start=(ch == 0), stop=(ch == 1)),
                reads=[L + "ones", L + f"acc{blk}_{ch}"], writes=["ps2"])
        for ch in range(2):
            Sx.op("pe", lambda e, ch=ch: e.matmul(
                out=c.ps[3][:], lhsT=ones[:], rhs=sq[:, ch, :], start=(ch == 0), stop=(ch == 1)),
                reads=[L + "ones", L + f"sq{ch}"], writes=["ps3"])
        Sx.op("act", lambda e: e.activation(out=mean[:], in_=c.ps[2][:], func=AF.Copy, scale=1.0 / 256),
              reads=["ps2"], writes=[L + "mean"])
        Sx.op("pool", lambda e: e.tensor_tensor(out=tmp[:], in0=mean[:], in1=mean[:], op=ALU.mult),
              reads=[L + "mean"], writes=[L + "tmp"])
        Sx.op("dve", lambda e: e.scalar_tensor_tensor(out=var[:], in0=c.ps[3][:], scalar=1.0 / 256, in1=tmp[:],
                                                      op0=ALU.mult, op1=ALU.subtract),
              reads=["ps3", L + "tmp"], writes=[L + "var"])
        Sx.op("dve", lambda e: e.tensor_scalar(out=var[:], in0=var[:], scalar1=EPS, scalar2=None, op0=ALU.add),
              reads=[L + "var"], writes=[L + "var"])
        Sx.op("act", lambda e: e.activation(out=tmp[:], in_=var[:], func=AF.Sqrt),
              reads=[L + "var"], writes=[L + "tmp"])
        Sx.op("dve", lambda e: e.reciprocal(out=rstd[:], in_=tmp[:]), reads=[L + "tmp"], writes=[L + "rstd"])
        for ch in range(2):
            Sx.op("pool", lambda e, ch=ch, t0=t0: e.tensor_tensor(
                out=yn[:, ch, :], in0=acc[:, ch, t0:t0 + 512], in1=mean[:], op=ALU.subtract),
                reads=[L + f"acc{blk}_{ch}", L + "mean"], writes=[L + f"yn{ch}"])
            Sx.op("dve", lambda e, ch=ch: e.tensor_tensor(
                out=yn[:, ch, :], in0=yn[:, ch, :], in1=rstd[:], op=ALU.mult),
                reads=[L + f"yn{ch}", L + "rstd"], writes=[L + f"yn{ch}"])
            Sx.op("act", lambda e, ch=ch: e.activation(
                out=zT[:, ch, :], in_=yn[:, ch, :], func=AF.Silu, scale=lnp[:, ch, 0:1], bias=lnp[:, ch, 1:2]),
                reads=[L + f"yn{ch}", L + "lnp"], writes=[L + f"zT{ch}"])
        for a in range(4):
            s_ = a % 2
            b = 4 + s_
            for ch in range(2):
                Sx.op("pe", lambda e, a=a, ch=ch, b=b: e.matmul(
                    out=c.ps[b][:, 0:256], lhsT=zT[:, ch, a * 128:(a + 1) * 128], rhs=pw[:, ch, :],
                    start=(ch == 0), stop=(ch == 1)),
                    reads=[L + f"zT{ch}", L + "pw"], writes=[f"ps{b}"])
            Sx.op("dve", lambda e, s_=s_, b=b: e.tensor_tensor(out=yo[s_][:], in0=c.ps[b][:, 0:256], in1=pwb[:],
                                                            op=ALU.add),
                  reads=[f"ps{b}", L + "pwb"], writes=[L + f"yo{s_}"])
            tt = q * 4 + a
            Sx.dma(c.mix[tt * 128:(tt + 1) * 128, 0:256], yo[s_][:], reads=[L + f"yo{s_}"], writes=[f"mixA{tt}"])
    sb.reset(m)


def bc(ap, shape, axis):
    return ap.unsqueeze(axis).to_broadcast(list(shape))


def setup_rope(c):
    sb, Sx = c.sb, c.S
    m = sb.mark()
    L = "rope"
    pos_i = sb.t("pos_i", [128, NT], I32)
    pos_f = sb.t("pos_f", [128, NT], F32)
    invf = sb.t("invf", [128, 32], F32)
    ang = sb.t("ang", [128, NT, 32], F32)
    red = sb.t("red", [128, NT, 32], F32)
    cs = sb.t("cs", [128, NT, 64], F32)
    negpi = sb.t("negpi", [128, 1], F32)
    pview = c.pos.rearrange("(t p) o -> p (t o)", p=128)
    for q4 in range(4):
        Sx.dma(pos_i[:, q4 * 8:(q4 + 1) * 8], pview[:, q4 * 8:(q4 + 1) * 8], writes=[L + "pos_i"],
               allow_slow_non_contiguous=True)
    Sx.dma(invf[:], c.invf_d, writes=[L + "invf"])
    Sx.op("pool", lambda e: e.memset(negpi[:], -float(np.pi)), writes=[L + "negpi"])
    Sx.op("dve", lambda e: e.tensor_copy(out=pos_f[:], in_=pos_i[:]), reads=[L + "pos_i"], writes=[L + "pos_f"])
    Sx.op("dve", lambda e: e.tensor_tensor(out=ang[:], in0=bc(pos_f[:], [128, NT, 32], 2),
                                           in1=bc(invf[:], [128, NT, 32], 1), op=ALU.mult),
          reads=[L + "pos_f", L + "invf"], writes=[L + "ang"])
    TWO_PI = float(2 * np.pi)
    C1, C2, C3 = 6.28125, 0.0019350051879882812, 3.019916050561733e-07
    kf = sb.t("kf", [128, NT, 32], F32)
    ki = sb.t("ki", [128, NT, 32], I32)
    msk = sb.t("msk", [128, NT, 32], F32)
    Sx.op("dve", lambda e: e.tensor_scalar(out=kf[:], in0=ang[:], scalar1=1.0 / TWO_PI, scalar2=None, op0=ALU.mult),
          reads=[L + "ang"], writes=[L + "kf"])
    Sx.op("dve", lambda e: e.tensor_copy(out=ki[:], in_=kf[:]), reads=[L + "kf"], writes=[L + "ki"])
    Sx.op("dve", lambda e: e.tensor_copy(out=kf[:], in_=ki[:]), reads=[L + "ki"], writes=[L + "kf"])
    for cst in (C1, C2, C3):
        Sx.op("dve", lambda e, cst=cst: e.scalar_tensor_tensor(out=ang[:], in0=kf[:], scalar=-cst, in1=ang[:],
                                                               op0=ALU.mult, op1=ALU.add),
              reads=[L + "ang", L + "kf"], writes=[L + "ang"])
    PI = float(np.pi)
    PI_LO = 3.1415925
    for which, shift in ((0, 0.5 * np.pi), (1, 0.0)):
        Sx.op("dve", lambda e, shift=shift: e.tensor_scalar(out=red[:], in0=ang[:], scalar1=float(shift), scalar2=None,
                                                            op0=ALU.add),
              reads=[L + "ang"], writes=[L + "red"])
        for cmp_op, thr, adj in ((ALU.is_gt, PI, -TWO_PI), (ALU.is_lt, -PI, TWO_PI)):
            Sx.op("dve", lambda e, cmp_op=cmp_op, thr=thr: e.tensor_scalar(out=msk[:], in0=red[:], scalar1=thr,
                                                                           scalar2=None, op0=cmp_op),
                  reads=[L + "red"], writes=[L + "msk"])
            Sx.op("dve", lambda e, adj=adj: e.scalar_tensor_tensor(out=red[:], in0=msk[:], scalar=adj, in1=red[:],
                                                                   op0=ALU.mult, op1=ALU.add),
                  reads=[L + "red", L + "msk"], writes=[L + "red"])
        Sx.op("dve", lambda e: e.tensor_scalar(out=red[:], in0=red[:], scalar1=-PI_LO, scalar2=PI_LO,
                                               op0=ALU.max, op1=ALU.min),
              reads=[L + "red"], writes=[L + "red"])
        Sx.op("act", lambda e, which=which: e.activation(out=cs[:, :, which * 32:(which + 1) * 32], in_=red[:],
                                                         func=AF.Sin),
              reads=[L + "red"], writes=[L + "cs"])
    cview = c.cs_tab.rearrange("(t p) n -> p t n", p=128)
    for q4 in range(4):
        Sx.dma(cview[:, q4 * 8:(q4 + 1) * 8, :], cs[:, q4 * 8:(q4 + 1) * 8, :], reads=[L + "cs"], writes=["cs_tab"])
    sb.reset(m)


def norm_rope(c, L, ve, x, gn, cs, out, nh, sq, tt, st, gfull=None, ve2=None):
    Sx = c.S
    R = L["scr"]
    if ve2 is None:
        ve2 = ve
    sh = [128, nh, 64]
    hs = [128, nh, 32]
    Sx.op(ve, lambda e: e.tensor_tensor(out=sq, in0=x, in1=x, op=ALU.mult), reads=L["x"], writes=[R + "sq"])
    Sx.op("dve", lambda e: e.tensor_reduce(out=st[:, :, 0], in_=sq, axis=AX.X, op=ALU.add),
          reads=[R + "sq"], writes=[R + "ss"])
    Sx.op(ve, lambda e: e.tensor_scalar(out=st[:, :, 1], in0=st[:, :, 0], scalar1=1.0 / 64, scalar2=EPS,
                                        op0=ALU.mult, op1=ALU.add), reads=[R + "ss"], writes=[R + "ms"])
    Sx.op("act", lambda e: e.activation(out=st[:, :, 2], in_=st[:, :, 1], func=AF.Ln),
          reads=[R + "ms"], writes=[R + "ln"])
    Sx.op("act", lambda e: e.activation(out=st[:, :, 3], in_=st[:, :, 2], func=AF.Exp, scale=-0.5),
          reads=[R + "ln"], writes=[R + "rstd"])
    Sx.op(ve, lambda e: e.tensor_tensor(out=sq, in0=x, in1=bc(st[:, :, 3], sh, 2), op=ALU.mult),
          reads=L["x"] + [R + "rstd", R + "ss"], writes=[R + "sq"])
    gin = gfull if gfull is not None else bc(gn, sh, 1)
    Sx.op(ve, lambda e: e.tensor_tensor(out=sq, in0=sq, in1=gin, op=ALU.mult),
          reads=[R + "sq"] + L["gn"], writes=[R + "sq"])
    cosb = bc(cs[:, 0:32], hs, 1)
    sinb = bc(cs[:, 32:64], hs, 1)
    x1 = sq[:, :, 0:32]
    x2 = sq[:, :, 32:64]
    Sx.op(ve, lambda e: e.tensor_tensor(out=tt[:, :, 0, :], in0=x1, in1=cosb, op=ALU.mult),
          reads=[R + "sq"] + L["cs"], writes=[R + "t0"])
    Sx.op(ve, lambda e: e.tensor_tensor(out=tt[:, :, 1, :], in0=x2, in1=sinb, op=ALU.mult),
          reads=[R + "sq"] + L["cs"], writes=[R + "t1"])
    Sx.op(ve, lambda e: e.tensor_tensor(out=out[:, :, 0:32], in0=tt[:, :, 0, :], in1=tt[:, :, 1, :], op=ALU.subtract),
          reads=[R + "t0", R + "t1"], writes=L["out"])
    Sx.op(ve2, lambda e: e.tensor_tensor(out=tt[:, :, 2, :], in0=x2, in1=cosb, op=ALU.mult),
          reads=[R + "sq"] + L["cs"], writes=[R + "t2"])
    Sx.op(ve2, lambda e: e.tensor_tensor(out=tt[:, :, 3, :], in0=x1, in1=sinb, op=ALU.mult),
          reads=[R + "sq"] + L["cs"], writes=[R + "t3"])
    Sx.op(ve2, lambda e: e.tensor_tensor(out=out[:, :, 32:64], in0=tt[:, :, 2, :], in1=tt[:, :, 3, :], op=ALU.add),
          reads=[R + "t2", R + "t3"], writes=L["out"])


DIL_R = (1, 4, 16)


def stage4(c, l):
    sb, Sx = c.sb, c.S
    L = f"L{l}s4"
    m = sb.mark()
    NS = 3
    gq = sb.t("gq", [128, 64], F32)
    gk = sb.t("gk", [128, 64], F32)
    mk_f = sb.t("mk_f", [128, 4, 128], F32)
    mk = sb.t("mk", [128, 4, 128], BF16)
    X = [sb.t("X", [128, 3, 256], F32) for _ in range(2)]
    CS = [sb.t("CS", [128, 64], F32) for _ in range(2)]
    sq = [sb.t("sq", [128, 8, 64], F32) for _ in range(2)]
    tt = [sb.t("tt", [128, 8, 4, 32], F32) for _ in range(2)]
    stt = [sb.t("stt", [128, 8, 4], F32) for _ in range(2)]
    QK = [sb.t("QK", [128, 8, 64], BF16) for _ in range(2)]
    QZ = [sb.t("QZ", [128, 4, 128], BF16) for _ in range(2)]
    QKz = [sb.t("QKz", [128, 4, 128], BF16) for _ in range(2)]
    KT = [sb.t("KT", [128, 2, 128], BF16) for _ in range(NS)]
    V = [sb.t("V", [128, 4, 65], BF16) for _ in range(NS)]
    PT = [sb.t("PT", [128, 4, 2, 128], BF16) for _ in range(2)]
    O = [sb.t("O", [128, 4, 65], F32) for _ in range(4)]
    Sx.dma(gq[:], c.W("dil_q_norm")[l].partition_broadcast(128), writes=[L + "gq"])
    Sx.dma(gk[:], c.W("dil_k_norm")[l].partition_broadcast(128), writes=[L + "gk"])
    gqk = sb.t("gqk", [128, 8, 64], F32)
    Sx.op("pool", lambda e: e.tensor_copy(out=gqk[:, 0:4, :], in_=bc(gq[:], [128, 4, 64], 1)),
          reads=[L + "gq"], writes=[L + "gqk"])
    Sx.op("pool", lambda e: e.tensor_copy(out=gqk[:, 4:8, :], in_=bc(gk[:], [128, 4, 64], 1)),
          reads=[L + "gk"], writes=[L + "gqk"])
    Sx.dma(mk_f[:], c.masks_d, writes=[L + "mk_f"])
    Sx.op("pool", lambda e: e.tensor_copy(out=mk[:], in_=mk_f[:]), reads=[L + "mk_f"], writes=[L + "mk"])
    for s_ in range(NS):
        Sx.op("pool", lambda e, s_=s_: e.memset(V[s_][:, :, 64:65], 1.0), writes=[L + f"Vone{s_}"])
    for s_ in range(2):
        Sx.op("pool", lambda e, s_=s_: e.memset(QKz[s_][:], 0.0), writes=[L + f"QKzz{s_}"])
    its = []
    for g, r in enumerate(DIL_R):
        tps = (S // r) // 128
        for T in range(NT):
            its.append((g, r, T, T // tps, T % tps))

    def stage_a(it):
        g, r, T, cc, mt = its[it]
        pv = c.proj.rearrange("(m r) n -> r m n", r=r)
        cv = c.cs_tab.rearrange("(m r) n -> r m n", r=r)
        s2 = it % 2
        s3 = it % NS
        ve = "dve" if it % 2 == 0 else "pool"
        if True:
            lo = ((mt * 128) * r + cc) // 128
            hi = ((mt * 128 + 127) * r + cc) // 128
            nat = list(range(lo, hi + 1))
            Sx.dma(X[s2][:], pv[cc, mt * 128:(mt + 1) * 128, D_Q:D_Q + 2304].rearrange(
                "p (i gg j) -> p i gg j", i=3, gg=3)[:, :, g, :],
                reads=[f"proj{t}" for t in nat], writes=[L + f"X{s2}"])
            Sx.dma(CS[s2][:], cv[cc, mt * 128:(mt + 1) * 128, :], reads=["cs_tab"], writes=[L + f"CS{s2}"])
            norm_rope(c, {"x": [L + f"X{s2}"], "out": [L + f"QK{s2}_0", L + f"QK{s2}_1"], "gn": [L + "gqk"],
                          "cs": [L + f"CS{s2}"], "scr": L + f"nr{s2}"},
                      ve, X[s2][:, 0:2, :].rearrange("p i (h d) -> p (i h) d", h=4), None, CS[s2][:],
                      QK[s2][:], 8, sq[s2][:], tt[s2][:], stt[s2][:], gfull=gqk[:])
            Sx.op("pool" if ve == "dve" else "dve", lambda e, s2=s2, s3=s3: e.tensor_copy(
                out=V[s3][:, :, 0:64], in_=X[s2][:, 2, :].rearrange("p (h d) -> p h d", h=4)),
                reads=[L + f"X{s2}"], writes=[L + f"V{s3}"])

    def stage_b(it):
        g, r, T, cc, mt = its[it]
        dv = c.dilO[g].rearrange("(m r) n -> r m n", r=r)
        s2 = it % 2
        s3 = it % NS
        sp = (it - 1) % NS
        if True:
            qv = QK[s2][:, 0:4, :].rearrange("p (b two) d -> p b two d", two=2)
            zv = QKz[s2][:].rearrange("p (b two) n -> p b two n", two=2)
            for par in range(2):
                Sx.op("pool", lambda e, par=par, qv=qv, zv=zv: e.tensor_copy(
                    out=zv[:, :, par, par * 64:(par + 1) * 64], in_=qv[:, :, par, :]),
                    reads=[L + f"QK{s2}_0", L + f"QKzz{s2}"], writes=[L + f"QKz{s2}"])
            for h in range(4):
                Sx.op("pe", lambda e, s2=s2, h=h: e.matmul(
                    out=c.ps[0][:, h * 128:(h + 1) * 128], lhsT=QKz[s2][:, h, :], rhs=c.ident[:],
                    start=True, stop=True),
                    reads=[L + f"QKz{s2}", "ident"], writes=["ps0"])
            for blk in range(2):
                Sx.op("pe", lambda e, s2=s2, blk=blk: e.matmul(
                    out=c.ps[4][:, blk * 128:(blk + 1) * 128],
                    lhsT=QK[s2][:, 4 + 2 * blk:6 + 2 * blk, :].rearrange("p h d -> p (h d)"), rhs=c.ident[:],
                    start=True, stop=True),
                    reads=[L + f"QK{s2}_1", "ident"], writes=["ps4"])
            Sx.op("act", lambda e, s2=s2: e.copy(out=QZ[s2][:], in_=c.ps[0][:].rearrange("p (b t) -> p b t", b=4)),
                  reads=["ps0"], writes=[L + f"QZ{s2}"])
            Sx.op("dve", lambda e, s3=s3: e.tensor_copy(
                out=KT[s3][:], in_=c.ps[4][:, 0:256].rearrange("p (b t) -> p b t", b=2)),
                reads=["ps4"], writes=[L + f"KT{s3}"])
            first = mt == 0
            ks = [s3 if first else sp, s3]
            for h in range(4):
                pb = (h % 2) * 64
                for kk in range(2):
                    bank = 1 + h // 2
                    col = ((h % 2) * 2 + kk) * 128
                    Sx.op("pe", lambda e, h=h, kk=kk, bank=bank, col=col, s2=s2, ksl=ks[kk]: e.matmul(
                        out=c.ps[bank][:, col:col + 128], lhsT=KT[ksl][:, h // 2, :],
                        rhs=QZ[s2][:, h, :], start=True, stop=True),
                        reads=[L + f"KT{ks[kk]}", L + f"QZ{s2}"], writes=[f"ps{bank}"])
            for hp in range(2):
                Sx.op("act", lambda e, hp=hp, s2=s2: e.activation(
                    out=PT[s2][:, 2 * hp:2 * hp + 2, :, :].rearrange("p h k t -> p (h k t)"), in_=c.ps[1 + hp][:],
                    func=AF.Exp, scale=SCALE),
                    reads=[f"ps{1 + hp}"], writes=[L + f"PT{s2}_{hp}"])
            if first:
                for hp in range(2):
                    for kk, mi in ((0, 2), (1, 1)):
                        Sx.op("pool", lambda e, hp=hp, kk=kk, mi=mi, s2=s2: e.tensor_tensor(
                            out=PT[s2][:, 2 * hp:2 * hp + 2, kk, :], in0=PT[s2][:, 2 * hp:2 * hp + 2, kk, :],
                            in1=bc(mk[:, mi, :], [128, 2, 128], 1), op=ALU.mult),
                            reads=[L + f"PT{s2}_{hp}", L + "mk"], writes=[L + f"PT{s2}_{hp}"])
            else:
                for hp in range(2):
                    Sx.op("pool", lambda e, hp=hp, s2=s2: e.tensor_tensor(
                        out=PT[s2][:, 2 * hp:2 * hp + 2, :, :], in0=PT[s2][:, 2 * hp:2 * hp + 2, :, :],
                        in1=bc(mk[:, 0:2, :], [128, 2, 2, 128], 1), op=ALU.mult),
                        reads=[L + f"PT{s2}_{hp}", L + "mk"], writes=[L + f"PT{s2}_{hp}"])
            for h in range(4):
                for kk in range(2):
                    Sx.op("pe", lambda e, h=h, kk=kk, s2=s2, ksl=ks[kk]: e.matmul(
                        out=c.ps[3][:, h * 65:(h + 1) * 65], lhsT=PT[s2][:, h, kk, :], rhs=V[ksl][:, h, :],
                        start=(h == 0 and kk == 0), stop=(h == 3 and kk == 1), skip_group_check=True),
                        reads=[L + f"PT{s2}_{h // 2}", L + f"V{ks[kk]}", L + f"Vone{ks[kk]}"], writes=["ps3"])
            s4 = it % 4
            Sx.op("dve", lambda e, s4=s4: e.tensor_copy(out=O[s4][:].rearrange("p h d -> p (h d)"),
                                                        in_=c.ps[3][:, 0:260]),
                  reads=["ps3"], writes=[L + f"O{s4}"])

    def store(it):
        g, r, T, cc, mt = its[it]
        dv = c.dilO[g].rearrange("(m r) n -> r m n", r=r)
        s4 = it % 4
        Sx.dma(dv[cc, mt * 128:(mt + 1) * 128, :], O[s4][:].rearrange("p h d -> p (h d)"),
               reads=[L + f"O{s4}"], writes=[f"dilO{g}_T{T}"])

    NIT = len(its)
    for it in range(NIT + 3):
        if it < NIT:
            stage_a(it)
        if 1 <= it <= NIT:
            stage_b(it - 1)
        if it >= 3:
            store(it - 3)
    U = [sb.t("U", [128, 3, 260], F32) for _ in range(2)]
    rd = [sb.t("rd", [128, 4], F32) for _ in range(2)]
    yc = [sb.t("yc", [128, 4, 64], F32) for _ in range(2)]
    for t in range(NT):
        s_ = t % 2
        r0 = t * 128
        Sx.dma(U[s_][:], c.dilO[:, r0:r0 + 128, :].rearrange("g p n -> p g n"),
               reads=[f"dilO{g}_T{T}" for g in range(3) for T in range(NT)], writes=[L + f"U{s_}"])
        Sx.op("pool", lambda e, s_=s_: e.tensor_tensor(out=U[s_][:, 0, :], in0=U[s_][:, 0, :], in1=U[s_][:, 1, :],
                                                      op=ALU.add), reads=[L + f"U{s_}"], writes=[L + f"U{s_}"])
        Sx.op("pool", lambda e, s_=s_: e.tensor_tensor(out=U[s_][:, 0, :], in0=U[s_][:, 0, :], in1=U[s_][:, 2, :],
                                                      op=ALU.add), reads=[L + f"U{s_}"], writes=[L + f"U{s_}"])
        Uv = U[s_][:, 0, :].rearrange("p (h d) -> p h d", h=4)
        Sx.op("dve", lambda e, s_=s_, Uv=Uv: e.reciprocal(out=rd[s_][:], in_=Uv[:, :, 64]),
              reads=[L + f"U{s_}"], writes=[L + f"rd{s_}"])
        Sx.op("dve", lambda e, s_=s_, Uv=Uv: e.tensor_tensor(out=yc[s_][:], in0=Uv[:, :, 0:64],
                                                             in1=bc(rd[s_][:], [128, 4, 64], 2), op=ALU.mult),
              reads=[L + f"U{s_}", L + f"rd{s_}"], writes=[L + f"yc{s_}"])
        Sx.dma(c.mix[r0:r0 + 128, 768:1024], yc[s_][:].rearrange("p h d -> p (h d)"),
               reads=[L + f"yc{s_}"], writes=[f"mixC{t}"])
    sb.reset(m)


def stage3(c, l):
    sb, Sx = c.sb, c.S
    L = f"L{l}s3"
    m = sb.mark()
    KAs = sb.t("KAs", [128, 2, S], BF16)
    KWA = sb.t("KWA", [128, 2, S], BF16)
    KCr = sb.t("KCr", [128, 2, S], BF16)
    VS = sb.t("VS", [128, NT, 2, 65], BF16)
    VW = sb.t("VW", [128, NT, 2, 65], BF16)
    KCA = sb.t("KCA", [128, 2, 256], BF16)
    VCa = sb.t("VCa", [128, 2, 2, 129], BF16)
    gq = sb.t("gq", [128, 64], F32)
    gk = sb.t("gk", [128, 64], F32)
    Ebf = sb.t("Ebf", [128, NT, 64], BF16)
    madd = sb.t("madd", [128, 2, 128], BF16)
    Sx.dma(gq[:], c.W("nsa_q_norm")[l].partition_broadcast(128), writes=[L + "gq"])
    Sx.dma(gk[:], c.W("nsa_k_norm")[l].partition_broadcast(128), writes=[L + "gk"])
    m1 = sb.mark()
    Ef = sb.t("Ef", [128, NT, 64], F32)
    mk_f = sb.t("mk_f", [128, 4, 128], F32)
    Sx.dma(Ef[:], c.esel_d.rearrange("(t p) n -> p t n", p=128), writes=[L + "Ef"])
    Sx.op("pool", lambda e: e.tensor_copy(out=Ebf[:], in_=Ef[:]), reads=[L + "Ef"], writes=[L + "Ebf"])
    Sx.dma(mk_f[:], c.masks_d, writes=[L + "mk_f"])
    Sx.op("dve", lambda e: e.tensor_scalar(out=madd[:, 0, :], in0=mk_f[:, 1, :], scalar1=-NEG, scalar2=NEG,
                                           op0=ALU.mult, op1=ALU.add), reads=[L + "mk_f"], writes=[L + "madd"])
    Sx.op("dve", lambda e: e.tensor_scalar(out=madd[:, 1, :], in0=mk_f[:, 3, :], scalar1=-NEG, scalar2=NEG,
                                           op0=ALU.mult, op1=ALU.add), reads=[L + "mk_f"], writes=[L + "madd"])
    Sx.op("pool", lambda e: e.memset(VS[:, :, :, 64:65], 1.0), writes=[L + "VSone"])
    Sx.op("pool", lambda e: e.memset(VW[:, :, :, 64:65], 1.0), writes=[L + "VWone"])
    Sx.mark("s3 setup done")
    X = [sb.t("X", [128, 768], F32) for _ in range(2)]
    CS = [sb.t("CS", [128, 64], F32) for _ in range(2)]
    sq = [sb.t("sq", [128, 4, 64], F32) for _ in range(2)]
    tt = [sb.t("tt", [128, 4, 4, 32], F32) for _ in range(2)]
    stt = [sb.t("stt", [128, 4, 4], F32) for _ in range(2)]
    KMs = [sb.t("KMs", [128, 2, 128], BF16) for _ in range(2)]
    KMw = [sb.t("KMw", [128, 2, 128], BF16) for _ in range(2)]
    KMc = [sb.t("KMc", [128, 2, 128], BF16) for _ in range(2)]
    for s_ in range(2):
        Sx.op("pool", lambda e, s_=s_: e.memset(KMw[s_][:], 0.0), writes=[L + f"KMwz{s_}"])
    for t in range(NT):
        s_ = t % 2
        r0 = t * 128
        ve = "dve" if t % 2 == 0 else "pool"
        vo = "pool" if t % 2 == 0 else "dve"
        Sx.dma(X[s_][:], c.proj[r0:r0 + 128, N_KC:N_KC + 768], reads=[f"proj{t}"], writes=[L + f"X{s_}"])
        Sx.dma(CS[s_][:], c.cs_tab[r0:r0 + 128, :], reads=["cs_tab"], writes=[L + f"CS{s_}"])
        for br, (off, KM) in enumerate(((256, KMs), (512, KMw))):
            norm_rope(c, {"x": [L + f"X{s_}"], "out": [L + f"KM{br}{s_}"], "gn": [L + "gk"],
                          "cs": [L + f"CS{s_}"], "scr": L + f"nrA{s_}{br}"},
                      ve, X[s_][:, off:off + 128].rearrange("p (h d) -> p h d", h=2), gk[:], CS[s_][:],
                      KM[s_][:, :, 0:64], 2, sq[s_][:, 2 * br:2 * br + 2, :], tt[s_][:, 2 * br:2 * br + 2, :, :],
                      stt[s_][:, 2 * br:2 * br + 2, :])
        Sx.op(vo, lambda e, s_=s_, t=t: e.tensor_copy(out=KMs[s_][:, :, 64:128], in_=bc(Ebf[:, t, :], [128, 2, 64], 1)),
              reads=[L + "Ebf"], writes=[L + f"KM0{s_}"])
        Sx.op(vo, lambda e, s_=s_: e.tensor_copy(
            out=KMc[s_][:].rearrange("p h (kv d) -> p h kv d", kv=2),
            in_=X[s_][:, 0:256].rearrange("p (kv h d) -> p h kv d", kv=2, h=2)),
            reads=[L + f"X{s_}"], writes=[L + f"KMc{s_}"])
        Sx.op(vo, lambda e, s_=s_, t=t: e.tensor_copy(
            out=VS[:, t, :, 0:64], in_=X[s_][:, 384:512].rearrange("p (h d) -> p h d", h=2)),
            reads=[L + f"X{s_}"], writes=[L + f"VS{t}"])
        Sx.op(vo, lambda e, s_=s_, t=t: e.tensor_copy(
            out=VW[:, t, :, 0:64], in_=X[s_][:, 640:768].rearrange("p (h d) -> p h d", h=2)),
            reads=[L + f"X{s_}"], writes=[L + f"VW{t}"])
        for i, (KM, nm) in enumerate(((KMs, "KM0"), (KMw, "KM1"), (KMc, "KMc"))):
            for h in range(2):
                bank, col = i, h * 128
                rd = [L + f"{nm}{s_}", "ident"] + ([L + f"KMwz{s_}"] if i == 1 else [])
                Sx.op("pe", lambda e, KM=KM, s_=s_, h=h, bank=bank, col=col: e.matmul(
                    out=c.ps[bank][:, col:col + 128], lhsT=KM[s_][:, h, :], rhs=c.ident[:], start=True, stop=True),
                    reads=rd, writes=[f"ps{bank}"])
        Sx.op("act", lambda e, r0=r0: e.copy(out=KAs[:, :, r0:r0 + 128],
                                             in_=c.ps[0][:, 0:256].rearrange("p (h t) -> p h t", h=2)),
              reads=["ps0"], writes=[L + f"KAs{t}"])
        Sx.op("dve", lambda e, r0=r0: e.tensor_copy(out=KWA[:, :, r0:r0 + 128],
                                                    in_=c.ps[1][:, 0:256].rearrange("p (h t) -> p h t", h=2)),
              reads=["ps1"], writes=[L + f"KWA{t}"])
        Sx.op("act", lambda e, r0=r0: e.copy(out=KCr[:, :, r0:r0 + 128],
                                             in_=c.ps[2][:, 0:256].rearrange("p (h t) -> p h t", h=2)),
              reads=["ps2"], writes=[L + "KCr"])
        if t == 0:
            Sx.mark("phaseA tile0 done")
    sb.reset(m1)
    Sx.mark("phaseA done")
    W1f = sb.t("W1f", [128, 32, 128], F32)
    W1 = sb.t("W1", [128, 32, 128], BF16)
    W2f = sb.t("W2f", [128, 128], F32)
    W2 = sb.t("W2", [128, 128], BF16)
    PEf = sb.t("PEf", [128, 32], F32)
    PEb = sb.t("PEb", [128, 32], BF16)
    b1 = sb.t("b1", [128, 2], F32)
    HID = sb.t("HID", [128, 256], BF16)
    ex = sb.t("ex", [128, 256], F32)
    zz = sb.t("zz", [128, 256], F32)
    kraw = sb.t("kraw", [128, 2, 2, 64], F32)
    CSc = sb.t("CSc", [128, 2, 64], F32)
    mmf = sb.t("mmf", [128, 2, 64], F32)
    KCm = sb.t("KCm", [128, 2, 2, 128], BF16)
    sq2 = sb.t("sq2", [128, 2, 64], F32)
    tt2 = sb.t("tt2", [128, 2, 4, 32], F32)
    st2 = sb.t("st2", [128, 2, 4], F32)
    Sx.op("pool", lambda e: e.memset(W1f[:], 0.0), writes=[L + "W1f"])
    Sx.op("pool", lambda e: e.memset(W2f[:], 0.0), writes=[L + "W2f"])
    Sx.op("pool", lambda e: e.memset(KCm[:], 0.0), writes=[L + "KCmz"])
    Sx.op("pool", lambda e: e.memset(CSc[:], 0.0), writes=[L + "CSc"])
    for lh in range(2):
        Sx.dma(W1f[0:64, lh * 16:(lh + 1) * 16, 0:64],
               c.W("cmp_k_w1")[l].rearrange("(l d) j -> d l j", d=64)[:, lh * 16:(lh + 1) * 16, :], writes=[L + "W1f"])
        Sx.dma(W1f[64:128, lh * 16:(lh + 1) * 16, 64:128],
               c.W("cmp_v_w1")[l].rearrange("(l d) j -> d l j", d=64)[:, lh * 16:(lh + 1) * 16, :], writes=[L + "W1f"])
    Sx.dma(W2f[0:64, 0:64], c.W("cmp_k_w2")[l], writes=[L + "W2f"])
    Sx.dma(W2f[64:128, 64:128], c.W("cmp_v_w2")[l], writes=[L + "W2f"])
    for half in range(2):
        Sx.dma(PEf[0:64, half * 16:(half + 1) * 16],
               c.W("cmp_k_pos")[l].rearrange("l d -> d l")[:, half * 16:(half + 1) * 16],
               writes=[L + "PEf"], allow_slow_non_contiguous=True)
        Sx.dma(PEf[64:128, half * 16:(half + 1) * 16],
               c.W("cmp_v_pos")[l].rearrange("l d -> d l")[:, half * 16:(half + 1) * 16],
               writes=[L + "PEf"], allow_slow_non_contiguous=True)
    Sx.op("dve", lambda e: e.tensor_copy(out=W1[:], in_=W1f[:]), reads=[L + "W1f"], writes=[L + "W1"])
    Sx.op("pool", lambda e: e.tensor_copy(out=W2[:], in_=W2f[:]), reads=[L + "W2f"], writes=[L + "W2"])
    Sx.op("pool", lambda e: e.tensor_copy(out=PEb[:], in_=PEf[:]), reads=[L + "PEf"], writes=[L + "PEb"])
    csv = c.cs_tab.rearrange("(a b) n -> a b n", b=16)
    Sx.dma(CSc[:, 0, :], csv[1:129, 15, :], reads=["cs_tab"], writes=[L + "CSc"])
    Sx.dma(CSc[0:127, 1, :], csv[129:256, 15, :], reads=["cs_tab"], writes=[L + "CSc"])
    Sx.dma(mmf[:], c.mmap_d.rearrange("(ct p) n -> p ct n", p=128), writes=[L + "mmf"])
    for h in range(2):
        Sx.op("pool", lambda e, h=h: e.tensor_copy(out=VCa[:, :, h, 65:129], in_=mmf[:]),
              reads=[L + "mmf"], writes=[L + f"VCam{h}"])
    Sx.op("pool", lambda e: e.memset(VCa[:, :, :, 64:65], 1.0), writes=[L + "VCa1"])
    for li in range(32):
        Sx.op("pe", lambda e, li=li: e.matmul(out=c.ps[2][:, 0:1], lhsT=W1[:, li, :], rhs=PEb[:, li:li + 1],
                                              start=(li == 0), stop=(li == 31)),
              reads=[L + "W1", L + "PEb"], writes=["ps2"])
    Sx.op("dve", lambda e: e.tensor_copy(out=b1[:, 0:1], in_=c.ps[2][:, 0:1]), reads=["ps2"], writes=[L + "b1"])
    Sx.op("dve", lambda e: e.tensor_scalar(out=b1[:, 1:2], in0=b1[:, 0:1], scalar1=-1.0, scalar2=None, op0=ALU.mult),
          reads=[L + "b1"], writes=[L + "nb1"])
    Sx.mark("b1 done")
    for h in range(2):
        kv = KCr[:, h, :].rearrange("p (c l) -> p l c", l=16)
        for li in range(16):
            Sx.op("pe", lambda e, li=li, kv=kv: e.matmul(out=c.ps[3][:, 0:256], lhsT=W1[:, li, :], rhs=kv[:, li, :],
                                                        start=(li == 0), stop=False, skip_group_check=True),
                  reads=[L + "W1", L + "KCr"], writes=["ps3"])
        for li in range(16):
            Sx.op("pe", lambda e, li=li, kv=kv: e.matmul(out=c.ps[3][:, 0:255], lhsT=W1[:, 16 + li, :],
                                                        rhs=kv[:, li, 1:256], start=False, stop=(li == 15),
                                                        skip_group_check=True),
                  reads=[L + "W1", L + "KCr"], writes=["ps3"])
        Sx.op("act", lambda e: e.activation(out=ex[:], in_=c.ps[3][:, 0:256], func=AF.Exp, scale=-1.0, bias=b1[:, 1:2]),
              reads=["ps3", L + "nb1"], writes=[L + "ex"])
        Sx.op("dve", lambda e: e.tensor_scalar(out=ex[:], in0=ex[:], scalar1=1.0, scalar2=None, op0=ALU.add),
              reads=[L + "ex"], writes=[L + "ex"])
        Sx.op("dve", lambda e: e.reciprocal(out=ex[:], in_=ex[:]), reads=[L + "ex"], writes=[L + "ex"])
        Sx.op("dve", lambda e: e.tensor_scalar(out=zz[:], in0=c.ps[3][:, 0:256], scalar1=b1[:, 0:1], scalar2=None,
                                               op0=ALU.add), reads=["ps3", L + "b1"], writes=[L + "zz"])
        Sx.op("dve", lambda e: e.tensor_tensor(out=HID[:], in0=zz[:], in1=ex[:], op=ALU.mult),
              reads=[L + "zz", L + "ex"], writes=[L + "HID"])
        for ct in range(2):
            Sx.op("pe", lambda e, ct=ct: e.matmul(out=c.ps[2][:, ct * 128:(ct + 1) * 128],
                                                  lhsT=HID[:, ct * 128:(ct + 1) * 128], rhs=W2[:],
                                                  start=True, stop=True),
                  reads=[L + "HID", L + "W2"], writes=["ps2"])
        p2 = c.ps[2][:, 0:256].rearrange("p (ct n) -> p ct n", ct=2)
        Sx.op("dve", lambda e, h=h, p2=p2: e.tensor_copy(out=kraw[:, :, h, :], in_=p2[:, :, 0:64]),
              reads=["ps2"], writes=[L + f"kraw{h}"])
        Sx.op("dve", lambda e, h=h, p2=p2: e.tensor_copy(out=VCa[:, :, h, 0:64], in_=p2[:, :, 64:128]),
              reads=["ps2"], writes=[L + f"VCav{h}"])
    for ct in range(2):
        norm_rope(c, {"x": [L + "kraw0", L + "kraw1"], "out": [L + f"KCm{ct}"], "gn": [L + "gk"],
                      "cs": [L + "CSc"], "scr": L + f"nrC{ct}"},
                  "dve", kraw[:, ct, :, :], gk[:], CSc[:, ct, :], KCm[:, ct, :, 0:64], 2, sq2[:], tt2[:], st2[:])
        for h in range(2):
            Sx.op("pe", lambda e, ct=ct, h=h: e.matmul(out=c.ps[2][:, (ct * 2 + h) * 128:(ct * 2 + h + 1) * 128],
                                                       lhsT=KCm[:, ct, h, :], rhs=c.ident[:], start=True, stop=True),
                  reads=[L + f"KCm{ct}", L + "KCmz", "ident"], writes=["ps2"])
    Sx.op("dve", lambda e: e.tensor_copy(out=KCA[:].rearrange("p h (ct n) -> p ct h n", ct=2),
                                         in_=c.ps[2][:].rearrange("p (ct h n) -> p ct h n", ct=2, h=2)),
          reads=["ps2"], writes=[L + "KCA"])
    sb.reset(m1)
    Sx.mark("phaseA' done")
    NSL = 3
    QA0 = sb.t("QA0", [128, 8, 512], BF16)
    QA = sb.t("QA", [128, 8, 512], BF16)
    QM = sb.t("QM", [128, 4, 8, 128], BF16)
    Xq = [sb.t("Xq", [128, 512], F32) for _ in range(2)]
    Gt = sb.t("Gt", [128, 4, 24], F32)
    CSq = [sb.t("CSq", [128, 64], F32) for _ in range(2)]
    sq8 = [sb.t("sq8", [128, 8, 64], F32) for _ in range(2)]
    tt8 = [sb.t("tt8", [128, 8, 4, 32], F32) for _ in range(2)]
    st8 = [sb.t("st8", [128, 8, 4], F32) for _ in range(2)]
    PT = [sb.t("PT", [128, 512], BF16) for _ in range(NSL)]
    CMf = sb.t("CMf", [128, 2, 512], F32)
    CM = sb.t("CM", [128, 2, 512], BF16)
    VISt = sb.t("VISt", [128, 4, 64], F32)
    ADDt = sb.t("ADDt", [128, 4, 64], F32)
    IMP = sb.t("IMP", [128, 4, 2, 64], F32)
    Y = sb.t("Y", [128, 4, 8, 64], F32)
    tmpo = [sb.t("tmpo", [128, 4, 64], F32) for _ in range(2)]
    den = [sb.t("den", [128, 4, 3], F32) for _ in range(2)]
    sc = sb.t("sc", [128, 64], F32)
    sc2 = sb.t("sc2", [128, 64], F32)
    m8 = sb.t("m8", [128, 16], F32)
    nm = sb.t("nm", [128, 64], BF16)
    for j in range(S // 512):
        Lj = L + f"c{j}"
        Sx.dma(VISt[:], c.vis_d[j * 512:(j + 1) * 512, :].rearrange("(a p) n -> p a n", p=128), writes=[L + "VISt"])
        Sx.dma(ADDt[:], c.add_d[j * 512:(j + 1) * 512, :].rearrange("(a p) n -> p a n", p=128), writes=[L + "ADDt"])
        Sx.dma(CMf[:], c.cm_d[:, :, j * 512:(j + 1) * 512], writes=[L + "CMf"])
        Sx.op("pool", lambda e: e.tensor_copy(out=CM[:], in_=CMf[:]), reads=[L + "CMf"], writes=[L + "CM"])
        Sx.dma(Gt[:], c.proj[j * 512:(j + 1) * 512, N_G:N_G + 24].rearrange("(a p) n -> p a n", p=128),
               reads=[f"proj{4 * j + a}" for a in range(4)], writes=[L + "Gt"])
        Sx.op("act", lambda e: e.activation(out=Gt[:], in_=Gt[:], func=AF.Exp, scale=-1.0),
              reads=[L + "Gt"], writes=[L + "Gt"])
        Sx.op("dve", lambda e: e.tensor_scalar(out=Gt[:], in0=Gt[:], scalar1=1.0, scalar2=None, op0=ALU.add),
              reads=[L + "Gt"], writes=[L + "Gt"])
        Sx.op("dve", lambda e: e.reciprocal(out=Gt[:], in_=Gt[:]), reads=[L + "Gt"], writes=[L + "Gt"])
        Sx.op("pool", lambda e: e.memset(QM[:, :, :, 64:128], 0.0), writes=[L + f"QMm{a}" for a in range(4)])
        Sx.op("pool", lambda e: e.memset(IMP[:], 0.0), writes=[L + "IMP"])
        for a in range(4):
            t = 4 * j + a
            s_ = a % 2
            ve = "dve" if a % 2 == 0 else "pool"
            Sx.dma(Xq[s_][:], c.proj[t * 128:(t + 1) * 128, N_Q:N_Q + 512], reads=[f"proj{t}"], writes=[L + f"Xq{s_}"])
            Sx.dma(CSq[s_][:], c.cs_tab[t * 128:(t + 1) * 128, :], reads=["cs_tab"], writes=[L + f"CSq{s_}"])
            norm_rope(c, {"x": [L + f"Xq{s_}"], "out": [L + f"QMq{a}"], "gn": [L + "gq"],
                          "cs": [L + f"CSq{s_}"], "scr": L + f"nrQ{s_}"},
                      ve, Xq[s_][:].rearrange("p (h d) -> p h d", h=8), gq[:], CSq[s_][:],
                      QM[:, a, :, 0:64], 8, sq8[s_][:], tt8[s_][:], st8[s_][:])

        def transposes(dst, dst_name, with_mask):
            for h in range(8):
                for a in range(4):
                    Sx.op("pe", lambda e, h=h, a=a: e.matmul(
                        out=c.ps[3][:, a * 128:(a + 1) * 128], lhsT=QM[:, a, h, :], rhs=c.ident[:],
                        start=True, stop=True),
                        reads=[L + f"QMq{a}", L + f"QMm{a}", "ident"], writes=["ps3"])
                eng = "act" if h % 2 == 0 else "dve"
                if eng == "act":
                    Sx.op("act", lambda e, h=h: e.copy(out=dst[:, h, :], in_=c.ps[3][:]),
                          reads=["ps3"], writes=[L + f"{dst_name}{h}"])
                else:
                    Sx.op("dve", lambda e, h=h: e.tensor_copy(out=dst[:, h, :], in_=c.ps[3][:]),
                          reads=["ps3"], writes=[L + f"{dst_name}{h}"])

        Sx.mark(f"c{j} q prep done")
        transposes(QA0, "QA0_", False)
        Sx.mark(f"c{j} T0 done")

        units = []

        def add_unit(qsrc, qname, h, lhsT, lres, a_lo, a_hi, masks, vrhs, vres, acc, vw, first, last, fin):
            units.append(dict(qsrc=qsrc, qname=qname, h=h, lhsT=lhsT, lres=lres, a_lo=a_lo, a_hi=a_hi, masks=masks,
                              vrhs=vrhs, vres=vres, acc=acc, vw=vw, first=first, last=last, fin=fin))

        def run_units():
            n = len(units)
            LOOK = 2
            for i in range(n + LOOK):
                if i < n:
                    u = units[i]
                    sl = i % NSL
                    ncol = (u["a_hi"] - u["a_lo"] + 1) * 128
                    q0 = u["a_lo"] * 128
                    nmask = len(u["masks"])
                    Sx.op("pe", lambda e, u=u, sl=sl, ncol=ncol, q0=q0, nmask=nmask: e.matmul(
                        out=c.ps[sl][:, 0:ncol], lhsT=u["lhsT"], rhs=u["qsrc"][:, u["h"], q0:q0 + ncol],
                        start=True, stop=(nmask == 0), skip_group_check=True),
                        reads=u["lres"] + [L + f"{u['qname']}{u['h']}"], writes=[f"ps{sl}"])
                    for mi, (a, mrhs, mres) in enumerate(u["masks"]):
                        o0 = (a - u["a_lo"]) * 128 if a is not None else 0
                        wd = 128 if a is not None else ncol
                        Sx.op("pe", lambda e, sl=sl, o0=o0, wd=wd, mrhs=mrhs, mi=mi, nmask=nmask: e.matmul(
                            out=c.ps[sl][:, o0:o0 + wd], lhsT=c.ident[:], rhs=mrhs, start=False,
                            stop=(mi == nmask - 1), skip_group_check=True),
                            reads=["ident"] + mres, writes=[f"ps{sl}"])
                k = i - LOOK
                if k >= 0:
                    u = units[k]
                    sl = k % NSL
                    ncol = (u["a_hi"] - u["a_lo"] + 1) * 128
                    Sx.op("act", lambda e, sl=sl, ncol=ncol: e.activation(
                        out=PT[sl][:, 0:ncol], in_=c.ps[sl][:, 0:ncol], func=AF.Exp, scale=SCALE),
                        reads=[f"ps{sl}"], writes=[L + f"PT{sl}"])
                    vw = u["vw"]
                    started = set()
                    for a in range(u["a_lo"], u["a_hi"] + 1):
                        bank, col = u["acc"](a)
                        st_flag = u["first"] and (bank not in started) and a == u["first_a"].get(bank, -1)
                        started.add(bank)
                        Sx.op("pe", lambda e, u=u, sl=sl, a=a, bank=bank, col=col, vw=vw, st_flag=st_flag: e.matmul(
                            out=c.ps[bank][:, col:col + vw],
                            lhsT=PT[sl][:, (a - u["a_lo"]) * 128:(a - u["a_lo"] + 1) * 128], rhs=u["vrhs"],
                            start=st_flag, stop=False, skip_group_check=True),
                            reads=[L + f"PT{sl}"] + u["vres"], writes=[f"ps{bank}"])
                    if u["fin"] is not None:
                        u["fin"]()
            units.clear()

        def evac(h, br, banks_cols, vw, cmp_branch):
            g = h // 4
            ds = h % 2
            for a in range(4):
                bank, col = banks_cols(a)
                Sx.op("dve", lambda e, a=a, bank=bank, col=col, ds=ds: e.tensor_scalar(
                    out=den[ds][:, a, 0:1], in0=c.ps[bank][:, col + 64:col + 65], scalar1=1e-30, scalar2=None,
                    op0=ALU.max), reads=[f"ps{bank}"], writes=[L + f"den{ds}"])
            Sx.op("dve", lambda e, ds=ds: e.reciprocal(out=den[ds][:, :, 1], in_=den[ds][:, :, 0]),
                  reads=[L + f"den{ds}"], writes=[L + f"rden{ds}"])
            Sx.op("dve", lambda e, ds=ds, h=h, br=br: e.tensor_tensor(
                out=den[ds][:, :, 2], in0=den[ds][:, :, 1], in1=Gt[:, :, h * 3 + br], op=ALU.mult),
                reads=[L + f"rden{ds}", L + "Gt"], writes=[L + f"coef{ds}"])
            for a in range(4):
                bank, col = banks_cols(a)
                if cmp_branch:
                    Sx.op("dve", lambda e, a=a, bank=bank, col=col, ds=ds, h=h: e.tensor_scalar(
                        out=Y[:, a, h, :], in0=c.ps[bank][:, col:col + 64], scalar1=den[ds][:, a, 2:3], scalar2=None,
                        op0=ALU.mult), reads=[f"ps{bank}", L + f"coef{ds}"], writes=[L + f"Y{h}"])
                    Sx.op("dve", lambda e, a=a, bank=bank, col=col, ds=ds, g=g: e.scalar_tensor_tensor(
                        out=IMP[:, a, g, :], in0=c.ps[bank][:, col + 65:col + 129], scalar=den[ds][:, a, 1:2],
                        in1=IMP[:, a, g, :], op0=ALU.mult, op1=ALU.add),
                        reads=[f"ps{bank}", L + f"rden{ds}"], writes=[L + "IMP"])
                else:
                    Sx.op("dve", lambda e, a=a, bank=bank, col=col, ds=ds, h=h: e.scalar_tensor_tensor(
                        out=Y[:, a, h, :], in0=c.ps[bank][:, col:col + 64], scalar=den[ds][:, a, 2:3],
                        in1=Y[:, a, h, :], op0=ALU.mult, op1=ALU.add),
                        reads=[f"ps{bank}", L + f"coef{ds}"], writes=[L + f"Y{h}"])

        cts = [0] if j <= 3 else [0, 1]
        for h in range(8):
            g = h // 4
            bb = 4 + 2 * (h % 2)
            acc = (lambda a, bb=bb: (bb + a // 2, (a % 2) * 129))
            for ci, ct in enumerate(cts):
                add_unit(QA0, "QA0_", h, KCA[:, g, ct * 128:(ct + 1) * 128], [L + "KCA"], 0, 3,
                         [(None, CM[:, ct, :], [L + "CM"])], VCa[:, ct, g, :],
                         [L + f"VCav{g}", L + f"VCam{g}", L + "VCa1"], acc, 129, ci == 0, ci == len(cts) - 1,
                         (lambda h=h, acc=acc: evac(h, 0, acc, 129, True)) if ci == len(cts) - 1 else None)
                units[-1]["first_a"] = {bb: 0, bb + 1: 2}
        run_units()
        Sx.mark(f"c{j} cmp done")
        if j == 0:
            dbg(c, "Ycmp", Y[:].rearrange("p a h d -> p (a h d)"), [L + f"Y{h}" for h in range(8)])
            dbg(c, "IMP", IMP[:].rearrange("p a g n -> p (a g n)"), [L + "IMP"])
            dbg(c, "Gt", Gt[:].rearrange("p a n -> p (a n)"), [L + "Gt"])
            dbg(c, "QA0", QA0[:].rearrange("p h t -> p (h t)"), [L + f"QA0_{h}" for h in range(8)])
            dbg(c, "KCA", KCA[:].rearrange("p h t -> p (h t)"), [L + "KCA"])
            dbg(c, "VCa", VCa[:].rearrange("p a h t -> p (a h t)"), [L + "VCav0", L + "VCav1", L + "VCam0", L + "VCam1", L + "VCa1"])
            dbg(c, "KWA", KWA[:, 0, 0:512], [L + f"KWA{t}" for t in range(4)])
            dbg(c, "KAs", KAs[:, 0, 0:512], [L + f"KAs{t}" for t in range(4)])
        for h in range(8):
            g = h // 4
            bb = 4 + (h % 4)
            acc = (lambda a, bb=bb: (bb, a * 65))
            kts = list(range(max(0, 4 * j - 4), 4 * j + 4))
            seen = set()
            for ki, kt in enumerate(kts):
                a_lo = max(0, kt - 4 * j)
                a_hi = min(3, kt + 4 - 4 * j)
                masks = []
                if kt >= 4 * j:
                    masks.append((kt - 4 * j, madd[:, 0, :], [L + "madd"]))
                if kt + 4 - 4 * j <= 3:
                    masks.append((kt + 4 - 4 * j, madd[:, 1, :], [L + "madd"]))
                add_unit(QA0, "QA0_", h, KWA[:, g, kt * 128:(kt + 1) * 128], [L + f"KWA{kt}"], a_lo, a_hi, masks,
                         VW[:, kt, g, :], [L + f"VW{kt}", L + "VWone"], acc, 65, ki == 0, ki == len(kts) - 1,
                         (lambda h=h, acc=acc: evac(h, 2, acc, 65, False)) if ki == len(kts) - 1 else None)
                units[-1]["first_a"] = {bb: a_lo}
        run_units()
        Sx.mark(f"c{j} win done")
        if j == 0:
            dbg(c, "Ywin", Y[:].rearrange("p a h d -> p (a h d)"), [L + f"Y{h}" for h in range(8)])
        for a in range(4):
            for g in range(2):
                Sx.op("dve", lambda e, a=a, g=g: e.tensor_tensor(out=sc[:], in0=IMP[:, a, g, :], in1=VISt[:, a, :],
                                                                op=ALU.mult),
                      reads=[L + "IMP", L + "VISt"], writes=[L + "sc"])
                Sx.op("dve", lambda e, a=a: e.tensor_tensor(out=sc[:], in0=sc[:], in1=ADDt[:, a, :], op=ALU.add),
                      reads=[L + "sc", L + "ADDt"], writes=[L + "sc"])
                Sx.op("dve", lambda e: e.max(out=m8[:, 0:8], in_=sc[:]), reads=[L + "sc"], writes=[L + "m8a"])
                Sx.op("dve", lambda e: e.match_replace(out=sc2[:], in_to_replace=m8[:, 0:8], in_values=sc[:],
                                                       imm_value=-1e9),
                      reads=[L + "sc", L + "m8a"], writes=[L + "sc2"])
                Sx.op("dve", lambda e: e.max(out=m8[:, 8:16], in_=sc2[:]), reads=[L + "sc2"], writes=[L + "m8b"])
                Sx.op("dve", lambda e: e.tensor_reduce(out=m8[:, 0:1], in_=m8[:, 8:16], axis=AX.X, op=ALU.min),
                      reads=[L + "m8b"], writes=[L + "thr"])
                Sx.op("dve", lambda e: e.tensor_scalar(out=nm[:], in0=sc[:], scalar1=m8[:, 0:1], scalar2=NEG,
                                                       op0=ALU.is_lt, op1=ALU.mult),
                      reads=[L + "sc", L + "thr"], writes=[L + "nm"])
                Sx.op("pool", lambda e, a=a, g=g: e.tensor_copy(out=QM[:, a, 4 * g:4 * g + 4, 64:128],
                                                               in_=bc(nm[:], [128, 4, 64], 1)),
                      reads=[L + "nm"], writes=[L + f"QMm{a}"])
        Sx.mark(f"c{j} topk done")
        if j == 0:
            dbg(c, "QM", QM[:].rearrange("p a h d -> p (a h d)"), [L + f"QMm{a}" for a in range(4)] + [L + f"QMq{a}" for a in range(4)])
        transposes(QA, "QA_", True)
        for h in range(8):
            g = h // 4
            bb = 4 + (h % 4)
            acc = (lambda a, bb=bb: (bb, a * 65))
            kts = list(range(0, 4 * j + 4))
            for ki, kt in enumerate(kts):
                a_lo = max(0, kt - 4 * j)
                masks = []
                if kt >= 4 * j:
                    masks.append((kt - 4 * j, madd[:, 0, :], [L + "madd"]))
                add_unit(QA, "QA_", h, KAs[:, g, kt * 128:(kt + 1) * 128], [L + f"KAs{kt}"], a_lo, 3, masks,
                         VS[:, kt, g, :], [L + f"VS{kt}", L + "VSone"], acc, 65, ki == 0, ki == len(kts) - 1,
                         (lambda h=h, acc=acc: evac(h, 1, acc, 65, False)) if ki == len(kts) - 1 else None)
                units[-1]["first_a"] = {bb: a_lo}
        run_units()
        Sx.mark(f"c{j} sel done")
        for a in range(4):
            t = 4 * j + a
            Sx.dma(c.mix[t * 128:(t + 1) * 128, 256:768], Y[:, a, :, :].rearrange("p h d -> p (h d)"),
                   reads=[L + f"Y{h}" for h in range(8)], writes=[f"mixB{t}"])
    sb.reset(m)


def _prep_inputs(inputs, names, extra=None):
    ident = np.eye(128, dtype=np.float32)
    maps = []
    for b in range(8):
        m = {}
        for n in names:
            if n == "x":
                m[n] = np.ascontiguousarray(inputs["x"][b])
            elif n == "positions":
                m[n] = np.ascontiguousarray(inputs["positions"][b].reshape(S, 1))
            elif n == "c_ident":
                m[n] = ident
            elif n == "c_invf":
                invf = (10000.0 ** (-(np.arange(32, dtype=np.float32) / np.float32(32)))).astype(np.float32)
                m[n] = np.ascontiguousarray(np.broadcast_to(invf[None, :], (128, 32)))
            elif n in ("c_esel", "c_mmap", "c_cm", "c_vis", "c_add"):
                m[n] = _nsa_consts()[n]
            elif n == "c_masks":
                p = np.arange(128)[:, None]
                f = np.arange(128)[None, :]
                m[n] = np.ascontiguousarray(np.stack([(p >= f), (p <= f), np.zeros((128, 128), bool), (p > f)],
                                                     axis=1).astype(np.float32))
            elif n in WSHAPES:
                m[n] = np.ascontiguousarray(inputs[n])
            elif extra is not None and n in extra:
                m[n] = extra[n][b] if isinstance(extra[n], (list, tuple)) else extra[n]
        maps.append(m)
    return maps


_NC_CACHE = {}
_CONSTS = {}


def _nsa_consts():
    if _CONSTS:
        return _CONSTS
    t = np.arange(S)
    n = np.arange(64)
    cur = (t // 64)[:, None]
    nn = n[None, :]
    _CONSTS["c_esel"] = (cur == nn).astype(np.float32)
    forced = (nn == 0) | (nn == cur) | (nn == cur - 1)
    visible = nn <= cur
    _CONSTS["c_vis"] = (visible & ~forced).astype(np.float32)
    add = np.zeros((S, 64), np.float32)
    add = np.where(nn == cur - 1, 10000.0, add)
    add = np.where(nn == cur, 20000.0, add)
    add = np.where(nn == 0, 30000.0, add)
    add = np.where(~visible, -1.0 - nn, add)
    _CONSTS["c_add"] = add.astype(np.float32)
    cidx = np.arange(256)
    cs_ = cidx[:, None] * 16
    ss_ = n[None, :] * 64
    ov = np.clip(np.minimum(cs_ + 32, ss_ + 64) - np.maximum(cs_, ss_), 0, None) / 32.0
    ov[255] = 0.0
    _CONSTS["c_mmap"] = ov.astype(np.float32)
    cm = np.full((128, 2, S), NEG, np.float32)
    for ct in range(2):
        cc = ct * 128 + np.arange(128)
        vis = ((16 * cc + 31)[:, None] <= t[None, :]) & (cc < 255)[:, None]
        cm[:, ct, :] = np.where(vis, 0.0, NEG)
    _CONSTS["c_cm"] = cm
    return _CONSTS


def kernel(**inputs):
    inputs = {k: np.asarray(v) for k, v in inputs.items()}
    if "full" not in _NC_CACHE:
        _NC_CACHE["full"] = build_program()
    nc = _NC_CACHE["full"]
    res = run_bass_kernel_spmd(nc, _prep_inputs(inputs, nc._mk_inputs), core_ids=list(range(8)))
    return np.stack([r["out"] for r in res.results], axis=0).astype(np.float32)
```

```python
import numpy as np
import concourse.bass as bass
import concourse.mybir as mybir
from concourse.bass_utils import run_bass_kernel_spmd

F32 = mybir.dt.float32
BF16 = mybir.dt.bfloat16
I32 = mybir.dt.int32
AF = mybir.ActivationFunctionType
ALU = mybir.AluOpType
AX = mybir.AxisListType

S = 4096
D = 1024
NT = S // 128
IN_DIM = 4120
D_FF = 2816
DEPTH = 4
NEG = -30000.0
EPS = 1e-6
SCALE = 0.125

C_VAL, C_GATE, N_Q, N_KC, N_VC, N_KS, N_VS, N_KW, N_VW, N_G, D_Q, D_K, D_V = (
    0, 256, 512, 1024, 1152, 1280, 1408, 1536, 1664, 1792, 1816, 2584, 3352)

ENGS = ("pe", "act", "dve", "pool", "sp")


class Op:
    __slots__ = ("eng", "fn", "deps", "kind", "sem", "count", "signal", "lane", "seq")

    def __init__(self, eng, fn, kind):
        self.eng = eng
        self.fn = fn
        self.kind = kind
        self.deps = []
        self.sem = None
        self.count = 0
        self.signal = False
        self.lane = None


class Sched:
    def __init__(self, nc, n_lanes=6, same_sync=True):
        self.nc = nc
        self.ops = {e: [] for e in ENGS}
        self.all = []
        self.lastw = {}
        self.readers = {}
        self.n_lanes = n_lanes
        self.same_sync = same_sync
        import os
        self.sync_same = set(os.environ.get("MK_SYNC_SAME", "act,dve").split(",")) - {""}
        self.dma_n = {e: 0 for e in ENGS}
        self.pending_barrier = {}
        self.seq = 0

    def _track(self, op, reads, writes):
        deps = []
        for r in reads:
            w = self.lastw.get(r)
            if w is not None:
                deps.append(w)
        for w_ in writes:
            w = self.lastw.get(w_)
            if w is not None:
                deps.append(w)
            deps.extend(self.readers.get(w_, ()))
        for w_ in writes:
            self.lastw[w_] = op
            self.readers[w_] = []
        for r in reads:
            if r not in writes:
                self.readers.setdefault(r, []).append(op)
        if self.pending_barrier.get(op.eng):
            deps.extend(self.pending_barrier.pop(op.eng))
        best = {}
        for d in deps:
            if d is op:
                continue
            key = (d.eng, d.lane if d.kind == "dma" else -1)
            b = best.get(key)
            if b is None or d.seq > b.seq:
                best[key] = d
        op.deps = list(best.values())
        self.seq += 1
        op.seq = self.seq
        self.ops[op.eng].append(op)
        self.all.append(op)
        return op

    def op(self, eng, fn, reads=(), writes=()):
        import os
        lim = int(os.environ.get("MK_OPLIM", "-1"))
        self.nops = getattr(self, "nops", 0) + 1
        if lim >= 0 and self.nops > lim:
            return None
        return self._track(Op(eng, fn, "cmp"), tuple(reads), tuple(writes))

    def barrier(self):
        last = []
        for e in ENGS:
            cm = [o for o in self.ops[e] if o.kind == "cmp"]
            if cm:
                last.append(cm[-1])
            seen = set()
            for o in reversed(self.ops[e]):
                if o.kind == "dma" and o.lane not in seen:
                    seen.add(o.lane)
                    last.append(o)
                if len(seen) >= self.n_lanes:
                    break
        for e in ENGS:
            self.pending_barrier[e] = list(last)

    def mark(self, label):
        import os
        if os.environ.get("MK_MARKS"):
            print("MARK", label, getattr(self, "nops", 0), flush=True)

    def dma(self, out, in_, reads=(), writes=(), q="sp", **kw):
        o = Op(q, lambda e: e.dma_start(out=out, in_=in_, **kw), "dma")
        o.lane = self.dma_n[q] % self.n_lanes
        self.dma_n[q] += 1
        return self._track(o, tuple(reads), tuple(writes))

    def emit(self):
        nc = self.nc
        for op in self.all:
            for d in op.deps:
                if d.kind == "cmp" and (d.eng != op.eng or d.eng in self.sync_same):
                    d.signal = True
        esem = {e: nc.alloc_semaphore(name=f"s_{e}") for e in ENGS}
        lsem = {}
        lcnt = {}
        cnt = {e: 0 for e in ENGS}
        for op in self.all:
            if op.kind == "cmp":
                if op.signal:
                    cnt[op.eng] += 1
                    op.count = cnt[op.eng]
                    op.sem = ("e", op.eng)
            else:
                key = (op.eng, op.lane)
                if key not in lsem:
                    lsem[key] = nc.alloc_semaphore(name=f"l_{op.eng}{op.lane}")
                    lcnt[key] = 0
                lcnt[key] += 16
                op.count = lcnt[key]
                op.sem = ("l",) + key
        handles = {("e", e): esem[e] for e in ENGS}
        for key, h in lsem.items():
            handles[("l",) + key] = h
        self.max_counts = dict(cnt)

        def emit_engine(name, e):
            waited = {}
            for op in self.ops[name]:
                need = {}
                for d in op.deps:
                    if d.kind == "cmp" and d.eng == name and name not in self.sync_same:
                        continue
                    if need.get(d.sem, 0) < d.count:
                        need[d.sem] = d.count
                if op.kind == "dma" and op.count > 16:
                    if need.get(op.sem, 0) < op.count - 16:
                        need[op.sem] = op.count - 16
                for s, v in need.items():
                    if waited.get(s, 0) >= v:
                        continue
                    e.wait_ge(handles[s], v)
                    waited[s] = v
                ins = op.fn(e)
                if op.kind == "dma":
                    ins.then_inc(handles[op.sem], 16)
                elif op.signal:
                    ins.then_inc(handles[op.sem], 1)

        with nc.Block() as block:
            @block.tensor
            def _(e):
                emit_engine("pe", e)

            @block.scalar
            def _(e):
                emit_engine("act", e)

            @block.vector
            def _(e):
                emit_engine("dve", e)

            @block.gpsimd
            def _(e):
                emit_engine("pool", e)

            @block.sync
            def _(e):
                emit_engine("sp", e)


class SbAlloc:
    def __init__(self, nc, base=16384, limit=212000):
        self.nc = nc
        self.off = base
        self.limit = limit
        self.n = 0
        self.sched = None

    def mark(self):
        return self.off

    def reset(self, m):
        self.off = m
        if self.sched is not None:
            self.sched.barrier()

    def t(self, name, shape, dtype):
        esz = 2 if dtype == BF16 else 4
        nbytes = int(np.prod(shape[1:])) * esz
        off = (self.off + 63) // 64 * 64
        assert off + nbytes <= self.limit, f"SBUF overflow at {name}: {off}+{nbytes}"
        self.n += 1
        h = self.nc.alloc_sbuf_tensor_at(f"{name}_{self.n}", list(shape), dtype, offset=off)
        self.off = off + nbytes
        return h


class Ctx:
    pass


WSHAPES = {
    "attn_norm": (D,), "w_in": (D, IN_DIM), "conv_dw": (31, 256), "conv_dw_b": (256,),
    "conv_ln_g": (256,), "conv_ln_b": (256,), "conv_pw": (256, 256), "conv_pw_b": (256,),
    "nsa_q_norm": (64,), "nsa_k_norm": (64,), "cmp_k_pos": (32, 64), "cmp_k_w1": (2048, 64),
    "cmp_k_w2": (64, 64), "cmp_v_pos": (32, 64), "cmp_v_w1": (2048, 64), "cmp_v_w2": (64, 64),
    "dil_q_norm": (64,), "dil_k_norm": (64,), "w_out": (D, D), "ffn_norm": (D,),
    "w_up": (D, 2 * D_FF), "ffn_dw": (3, 2 * D_FF), "ffn_dw_b": (2 * D_FF,), "w_down": (D_FF, D),
}


def build_program(n_layers=DEPTH, stages=("s1", "s2", "s3", "s4", "s5a", "s5b"), ext=None):
    ext = ext or {}
    nc = bass.Bass("TRN2", target_bir_lowering=False)
    c = Ctx()
    c.nc = nc
    c.S = Sched(nc)
    c.sb = SbAlloc(nc)
    c.sb.sched = c.S
    c.in_names = []
    c.uid = 0

    def din(name, shape, dtype=F32):
        c.in_names.append(name)
        return nc.dram_tensor(name, list(shape), dtype, kind="ExternalInput").ap()

    c.w = {}

    def W(name):
        if name not in c.w:
            c.w[name] = din(name, [DEPTH] + list(WSHAPES[name]))
        return c.w[name]

    c.W = W

    def scratch(name, shape, dtype=F32):
        kind = {"in": "ExternalInput", "out": "ExternalOutput"}.get(ext.get(name), "Internal")
        if kind == "ExternalInput":
            c.in_names.append(name)
        return nc.dram_tensor(name, list(shape), dtype, kind=kind).ap()

    c.x = din("x", [S, D])
    c.ident_d = din("c_ident", [128, 128])
    c.out = nc.dram_tensor("out", [S, D], F32, kind="ExternalOutput").ap()
    c.proj = scratch("proj", [S, IN_DIM])
    c.mix = scratch("mix", [S, D])
    c.x1 = scratch("x1", [S, D])
    c.xa = scratch("xa", [S, D])
    c.xb = scratch("xb", [S, D])
    c.ps = [nc.alloc_psum_tensor(f"psb{i}", [128, 512], F32) for i in range(8)]

    setup_consts(c)
    if "s3" in stages or "s4" in stages or "rope" in stages:
        c.pos = din("positions", [S, 1], I32)
        c.invf_d = din("c_invf", [128, 32])
        c.masks_d = din("c_masks", [128, 4, 128])
        c.cs_tab = scratch("cs_tab", [S, 64])
        c.dilO = scratch("dilO", [3, S, 260])
        c.esel_d = din("c_esel", [S, 64])
        c.mmap_d = din("c_mmap", [256, 64])
        c.cm_d = din("c_cm", [128, 2, S])
        c.vis_d = din("c_vis", [S, 64])
        c.add_d = din("c_add", [S, 64])
        setup_rope(c)
    xin, xin_name = c.x, "x"
    for l in range(n_layers):
        last = l == n_layers - 1
        if "s1" in stages:
            linear_stage(c, f"L{l}s1", xin, xin_name, c.proj, "proj", W("w_in")[l], IN_DIM,
                         gain=W("attn_norm")[l])
        if "s2" in stages:
            stage2(c, l)
        if "s3" in stages:
            stage3(c, l)
        if "s4" in stages:
            stage4(c, l)
        if "s5a" in stages:
            linear_stage(c, f"L{l}s5a", c.mix, (lambda t: [f"mixA{t}", f"mixB{t}", f"mixC{t}"]), c.x1, "x1", W("w_out")[l], D,
                         resid=(xin, xin_name))
        if "s5b" in stages:
            if last:
                xout, xout_name = c.out, "out"
            else:
                xout, xout_name = (c.xa, "xa") if l % 2 == 0 else (c.xb, "xb")
            stage5b(c, l, c.x1, "x1", xout, xout_name)
            xin, xin_name = xout, xout_name
    finish(c)
    c.S.emit()
    c.nc_inputs = list(c.in_names)
    nc._mk_inputs = list(c.in_names)
    nc._mk_counts = (dict(c.S.max_counts), {e: len(v) for e, v in c.S.ops.items()})
    return nc


def setup_consts(c):
    sb, Sx = c.sb, c.S
    c.ident_f = sb.t("ident_f", [128, 128], F32)
    c.ident = sb.t("ident", [128, 128], BF16)
    Sx.dma(c.ident_f[:], c.ident_d, writes=["ident_f"])
    Sx.op("dve", lambda e: e.tensor_copy(out=c.ident[:], in_=c.ident_f[:]), reads=["ident_f"], writes=["ident"])


def dbg(c, name, ap, reads):
    import os
    if not os.environ.get("MK_DBG"):
        return
    shape = list(ap.shape)
    d = c.nc.dram_tensor("dbg_" + name, shape, ap.dtype, kind="ExternalOutput").ap()
    c.S.dma(d, ap, reads=reads, writes=["dbg_" + name])


def finish(c):
    Sx = c.S
    res = [r for r, w in Sx.lastw.items() if w.kind == "dma"]
    Sx.op("sp", lambda e: e.nop(), reads=res)


def load_weight_bf16(c, L, Wt, w_ap, nk, ncols, gT=None, seg=2060, tag="W", nslots=4, stg=None):
    sb, Sx = c.sb, c.S
    if stg is None:
        stg = [sb.t("wstg", [128, seg], F32) for _ in range(nslots)]
    nslots = len(stg)
    wv = w_ap.rearrange("(k p) n -> p k n", p=128)
    i = 0
    for k in range(nk):
        for c0 in range(0, ncols, seg):
            cw = min(seg, ncols - c0)
            s_ = i % nslots
            Sx.dma(stg[s_][:, 0:cw], wv[:, k, c0:c0 + cw], writes=[L + f"stg{s_}"])
            eng = "pool" if i % 2 == 0 else "dve"
            if gT is not None:
                Sx.op(eng, lambda e, s_=s_, k=k, c0=c0, cw=cw: e.tensor_scalar(
                    out=Wt[:, k, c0:c0 + cw], in0=stg[s_][:, 0:cw], scalar1=gT[:, k:k + 1], scalar2=None,
                    op0=ALU.mult), reads=[L + f"stg{s_}", L + "gT"], writes=[L + f"{tag}{k}"])
            else:
                Sx.op(eng, lambda e, s_=s_, k=k, c0=c0, cw=cw: e.tensor_copy(
                    out=Wt[:, k, c0:c0 + cw], in_=stg[s_][:, 0:cw]),
                    reads=[L + f"stg{s_}"], writes=[L + f"{tag}{k}"])
            i += 1
    return [L + f"{tag}{k}" for k in range(nk)]


def linear_stage(c, L, src, src_name, dst, dst_name, w_ap, ncols_total, gain=None, resid=None):
    sb, Sx = c.sb, c.S
    m = sb.mark()
    W = sb.t("lw", [128, 8, ncols_total], BF16)
    xt = [sb.t("xt", [128, D], F32) for _ in range(2)]
    xbf = [sb.t("xbf", [128, D], BF16) for _ in range(2)]
    junk = sb.t("junk", [128, D], BF16)
    hT = [sb.t("hT", [128, 8, 128], BF16) for _ in range(2)]
    pj = [sb.t("pj", [128, ncols_total], F32) for _ in range(2)]
    st = [sb.t("st", [128, 4], F32) for _ in range(2)]
    rt = [sb.t("rt", [128, D], F32) for _ in range(2)] if resid is not None else None
    gT = None
    if gain is not None:
        gT = sb.t("gT", [128, 8], F32)
        Sx.dma(gT[:], gain.rearrange("(k p) -> p k", p=128), writes=[L + "gT"], allow_slow_non_contiguous=True)
    Wres = load_weight_bf16(c, L, W, w_ap, 8, ncols_total, gT=gT, seg=min(2060, ncols_total))
    nch = (ncols_total + 511) // 512
    ncols = [(n * 512, min(512, ncols_total - n * 512)) for n in range(nch)]
    def load(t):
        s_ = t % 2
        r0 = t * 128
        src_res = src_name(t) if callable(src_name) else [f"{src_name}{t}"]
        Sx.dma(xt[s_][:], src[r0:r0 + 128, :], reads=src_res, writes=[L + f"xt{s_}"])
        if resid is not None:
            Sx.dma(rt[s_][:], resid[0][r0:r0 + 128, :], reads=[f"{resid[1]}{t}"], writes=[L + f"rt{s_}"])

    load(0)
    for t in range(NT):
        s_ = t % 2
        r0 = t * 128
        if t + 1 < NT:
            load(t + 1)
        if gain is not None:
            Sx.op("act", lambda e, s_=s_: e.activation(out=junk[:], in_=xt[s_][:], func=AF.Square,
                                                       accum_out=st[s_][:, 0:1]),
                  reads=[L + f"xt{s_}"], writes=[L + "junk", L + f"ss{s_}"])
            Sx.op("dve", lambda e, s_=s_: e.tensor_scalar(out=st[s_][:, 1:2], in0=st[s_][:, 0:1], scalar1=1.0 / D,
                                                          scalar2=EPS, op0=ALU.mult, op1=ALU.add),
                  reads=[L + f"ss{s_}"], writes=[L + f"ms{s_}"])
            Sx.op("act", lambda e, s_=s_: e.activation(out=st[s_][:, 3:4], in_=st[s_][:, 1:2], func=AF.Sqrt),
                  reads=[L + f"ms{s_}"], writes=[L + f"sd{s_}"])
            Sx.op("dve", lambda e, s_=s_: e.reciprocal(out=st[s_][:, 2:3], in_=st[s_][:, 3:4]),
                  reads=[L + f"sd{s_}"], writes=[L + f"rstd{s_}"])
        Sx.op("pool", lambda e, s_=s_: e.tensor_copy(out=xbf[s_][:], in_=xt[s_][:]),
              reads=[L + f"xt{s_}"], writes=[L + f"xbf{s_}"])
        for k in range(8):
            Sx.op("pe", lambda e, s_=s_, k=k: e.matmul(
                out=c.ps[k // 4][:, (k % 4) * 128:(k % 4 + 1) * 128], lhsT=xbf[s_][:, k * 128:(k + 1) * 128],
                rhs=c.ident[:], start=True, stop=True),
                reads=[L + f"xbf{s_}", "ident"], writes=[f"ps{k // 4}"])
        for hh in range(2):
            if hh == 0:
                Sx.op("act", lambda e, s_=s_, hh=hh: e.copy(
                    out=hT[s_][:, hh * 4:(hh + 1) * 4, :], in_=c.ps[hh][:].rearrange("p (k t) -> p k t", k=4)),
                    reads=[f"ps{hh}"], writes=[L + f"hT{s_}{hh}"])
            else:
                Sx.op("dve", lambda e, s_=s_, hh=hh: e.tensor_copy(
                    out=hT[s_][:, hh * 4:(hh + 1) * 4, :], in_=c.ps[hh][:].rearrange("p (k t) -> p k t", k=4)),
                    reads=[f"ps{hh}"], writes=[L + f"hT{s_}{hh}"])
        for n, (c0, cw) in enumerate(ncols):
            b = 2 + n % 4
            for k in range(8):
                Sx.op("pe", lambda e, s_=s_, k=k, b=b, c0=c0, cw=cw: e.matmul(
                    out=c.ps[b][:, 0:cw], lhsT=hT[s_][:, k, :], rhs=W[:, k, c0:c0 + cw],
                    start=(k == 0), stop=(k == 7)),
                    reads=[L + f"hT{s_}{k // 4}", Wres[k]], writes=[f"ps{b}"])
            if resid is not None:
                Sx.op("dve", lambda e, s_=s_, b=b, c0=c0, cw=cw: e.tensor_tensor(
                    out=pj[s_][:, c0:c0 + cw], in0=c.ps[b][:, 0:cw], in1=rt[s_][:, c0:c0 + cw], op=ALU.add),
                    reads=[f"ps{b}", L + f"rt{s_}"], writes=[L + f"pj{s_}_{n}"])
            elif n % 2 == 0:
                Sx.op("act", lambda e, s_=s_, b=b, c0=c0, cw=cw: e.activation(
                    out=pj[s_][:, c0:c0 + cw], in_=c.ps[b][:, 0:cw], func=AF.Copy, scale=st[s_][:, 2:3]),
                    reads=[f"ps{b}", L + f"rstd{s_}"], writes=[L + f"pj{s_}_{n}"])
            else:
                Sx.op("dve", lambda e, s_=s_, b=b, c0=c0, cw=cw: e.tensor_scalar(
                    out=pj[s_][:, c0:c0 + cw], in0=c.ps[b][:, 0:cw], scalar1=st[s_][:, 2:3], scalar2=None,
                    op0=ALU.mult),
                    reads=[f"ps{b}", L + f"rstd{s_}"], writes=[L + f"pj{s_}_{n}"])
        Sx.dma(dst[r0:r0 + 128, :], pj[s_][:], reads=[L + f"pj{s_}_{n}" for n in range(nch)],
               writes=[f"{dst_name}{t}"])
    sb.reset(m)


def stage5b(c, l, src, src_name, dst, dst_name):
    sb, Sx = c.sb, c.S
    L = f"L{l}s5b"
    m = sb.mark()
    NCH = 2 * D_FF // 128
    NP = NCH // 2
    WU = sb.t("wu", [128, 8, 2 * D_FF], BF16)
    WD = sb.t("wd", [128, NP, D], BF16)
    gT = sb.t("gT", [128, 8], F32)
    dwT = sb.t("dwT", [128, NCH, 4], F32)
    xt = [sb.t("xt", [128, 2, D], F32) for _ in range(2)]
    xbf = sb.t("xbf", [128, 2, D], BF16)
    junk = sb.t("junk", [128, D], BF16)
    st = sb.t("st", [128, 2, 4], F32)
    h2T = [sb.t("h2T", [128, 8, 258], BF16) for _ in range(2)]
    acc = [sb.t("acc", [128, 2, 256], F32) for _ in range(3)]
    sg = [sb.t("sg", [128, 256], F32) for _ in range(3)]
    aT = [sb.t("aT", [128, 256], BF16) for _ in range(3)]
    Sx.dma(gT[:], c.W("ffn_norm")[l].rearrange("(k p) -> p k", p=128), writes=[L + "gT"],
           allow_slow_non_contiguous=True)
    dwv = c.W("ffn_dw")[l].rearrange("k (c p) -> p c k", p=128)
    for k in range(3):
        for c0 in range(0, NCH, 11):
            Sx.dma(dwT[:, c0:c0 + 11, k:k + 1], dwv[:, c0:c0 + 11, k:k + 1], writes=[L + "dwT"],
                   allow_slow_non_contiguous=True)
    bv = c.W("ffn_dw_b")[l].rearrange("(c p) -> p c", p=128)
    for c0 in range(0, NCH, 11):
        Sx.dma(dwT[:, c0:c0 + 11, 3:4], bv[:, c0:c0 + 11].unsqueeze(2), writes=[L + "dwT"],
               allow_slow_non_contiguous=True)
    wstg = [sb.t("wstg", [128, 704], F32) for _ in range(4)]
    WUres = load_weight_bf16(c, L, WU, c.W("w_up")[l], 8, 2 * D_FF, gT=gT, seg=704, tag="WU", stg=wstg)
    WDres = load_weight_bf16(c, L, WD, c.W("w_down")[l], NP, D, seg=512, tag="WD", stg=wstg)
    Sx.op("pool", lambda e: e.memset(h2T[0][:, :, 0:2], 0.0), writes=[L + "h2T0h"])
    NJ = S // 256
    for j in range(NJ):
        r0 = j * 256
        hs = j % 2
        hp = (j - 1) % 2
        xs = j % 2
        if j == 0:
            Sx.dma(xt[0][:], src[0:256, :].rearrange("(a p) d -> p a d", p=128),
                   reads=[f"{src_name}0", f"{src_name}1"], writes=[L + "xt0"])
        if j + 1 < NJ:
            Sx.dma(xt[1 - xs][:], src[r0 + 256:r0 + 512, :].rearrange("(a p) d -> p a d", p=128),
                   reads=[f"{src_name}{2 * j + 2}", f"{src_name}{2 * j + 3}"], writes=[L + f"xt{1 - xs}"])
        if j > 0:
            Sx.op("pool", lambda e, hs=hs, hp=hp: e.tensor_copy(out=h2T[hs][:, :, 0:2], in_=h2T[hp][:, :, 256:258]),
                  reads=[L + f"h2T{hp}"], writes=[L + f"h2T{hs}h"])
        for a in range(2):
            Sx.op("act", lambda e, a=a, xs=xs: e.activation(out=junk[:], in_=xt[xs][:, a, :], func=AF.Square,
                                                            accum_out=st[:, a, 0:1]),
                  reads=[L + f"xt{xs}"], writes=[L + "junk", L + f"ss{a}"])
            Sx.op("dve", lambda e, a=a: e.tensor_scalar(out=st[:, a, 1:2], in0=st[:, a, 0:1], scalar1=1.0 / D,
                                                        scalar2=EPS, op0=ALU.mult, op1=ALU.add),
                  reads=[L + f"ss{a}"], writes=[L + f"ms{a}"])
            Sx.op("act", lambda e, a=a: e.activation(out=st[:, a, 3:4], in_=st[:, a, 1:2], func=AF.Sqrt),
                  reads=[L + f"ms{a}"], writes=[L + f"sd{a}"])
            Sx.op("dve", lambda e, a=a: e.reciprocal(out=st[:, a, 2:3], in_=st[:, a, 3:4]),
                  reads=[L + f"sd{a}"], writes=[L + f"rstd{a}"])
            Sx.op("pool", lambda e, a=a, xs=xs: e.tensor_scalar(out=xbf[:, a, :], in0=xt[xs][:, a, :],
                                                                scalar1=st[:, a, 2:3], scalar2=None, op0=ALU.mult),
                  reads=[L + f"xt{xs}", L + f"rstd{a}"], writes=[L + f"xbf{a}"])
            for k in range(8):
                Sx.op("pe", lambda e, a=a, k=k: e.matmul(
                    out=c.ps[k // 4][:, (k % 4) * 128:(k % 4 + 1) * 128], lhsT=xbf[:, a, k * 128:(k + 1) * 128],
                    rhs=c.ident[:], start=True, stop=True),
                    reads=[L + f"xbf{a}", "ident"], writes=[f"ps{k // 4}"])
            for hh in range(2):
                if hh == 0:
                    Sx.op("act", lambda e, a=a, hh=hh, hs=hs: e.copy(
                        out=h2T[hs][:, hh * 4:(hh + 1) * 4, 2 + a * 128:2 + (a + 1) * 128],
                        in_=c.ps[hh][:].rearrange("p (k t) -> p k t", k=4)),
                        reads=[f"ps{hh}"], writes=[L + f"h2T{hs}"])
                else:
                    Sx.op("dve", lambda e, a=a, hh=hh, hs=hs: e.tensor_copy(
                        out=h2T[hs][:, hh * 4:(hh + 1) * 4, 2 + a * 128:2 + (a + 1) * 128],
                        in_=c.ps[hh][:].rearrange("p (k t) -> p k t", k=4)),
                        reads=[f"ps{hh}"], writes=[L + f"h2T{hs}"])

        def up(i, hs=hs):
            s2 = i % 2
            s3 = i % 3
            for gv in range(2):
                cc = i + gv * NP
                b = 2 * s2 + gv
                for k in range(8):
                    Sx.op("pe", lambda e, b=b, cc=cc, k=k: e.matmul(
                        out=c.ps[b][:, 0:258], lhsT=WU[:, k, cc * 128:(cc + 1) * 128],
                        rhs=h2T[hs][:, k, :], start=(k == 0), stop=(k == 7)),
                        reads=[L + f"h2T{hs}", L + f"h2T{hs}h", WUres[k]], writes=[f"ps{b}"])
                Sx.op("act", lambda e, s3=s3, gv=gv, cc=cc, b=b: e.activation(
                    out=acc[s3][:, gv, :], in_=c.ps[b][:, 2:258], func=AF.Identity, scale=dwT[:, cc, 2:3],
                    bias=dwT[:, cc, 3:4]),
                    reads=[f"ps{b}", L + "dwT"], writes=[L + f"acc{s3}{gv}"])
                Sx.op("dve", lambda e, s3=s3, gv=gv, cc=cc, b=b: e.scalar_tensor_tensor(
                    out=acc[s3][:, gv, :], in0=c.ps[b][:, 1:257], scalar=dwT[:, cc, 1:2], in1=acc[s3][:, gv, :],
                    op0=ALU.mult, op1=ALU.add),
                    reads=[f"ps{b}", L + "dwT"], writes=[L + f"acc{s3}{gv}"])
                Sx.op("dve", lambda e, s3=s3, gv=gv, cc=cc, b=b: e.scalar_tensor_tensor(
                    out=acc[s3][:, gv, :], in0=c.ps[b][:, 0:256], scalar=dwT[:, cc, 0:1], in1=acc[s3][:, gv, :],
                    op0=ALU.mult, op1=ALU.add),
                    reads=[f"ps{b}", L + "dwT"], writes=[L + f"acc{s3}{gv}"])
            Sx.op("act", lambda e, s3=s3: e.activation(out=sg[s3][:], in_=acc[s3][:, 0, :], func=AF.Silu),
                  reads=[L + f"acc{s3}0"], writes=[L + f"sg{s3}"])
            Sx.op("pool", lambda e, s3=s3: e.tensor_tensor(out=aT[s3][:], in0=sg[s3][:], in1=acc[s3][:, 1, :],
                                                          op=ALU.mult),
                  reads=[L + f"sg{s3}", L + f"acc{s3}1"], writes=[L + f"aT{s3}"])

        def down(i, hs=hs):
            s3 = i % 3
            for a in range(2):
                for hf in range(2):
                    b2 = 4 + a * 2 + hf
                    Sx.op("pe", lambda e, s3=s3, a=a, hf=hf, b2=b2, i=i: e.matmul(
                        out=c.ps[b2][:], lhsT=aT[s3][:, a * 128:(a + 1) * 128], rhs=WD[:, i, hf * 512:(hf + 1) * 512],
                        start=(i == 0), stop=(i == NP - 1)),
                        reads=[L + f"aT{s3}", WDres[i]], writes=[f"ps{b2}"])

        for i in range(NP + 2):
            if i < NP:
                up(i)
            if i >= 2:
                down(i - 2)
        for a in range(2):
            for hf in range(2):
                b2 = 4 + a * 2 + hf
                Sx.op("dve", lambda e, a=a, hf=hf, b2=b2, xs=xs: e.tensor_tensor(
                    out=xt[xs][:, a, hf * 512:(hf + 1) * 512], in0=c.ps[b2][:],
                    in1=xt[xs][:, a, hf * 512:(hf + 1) * 512], op=ALU.add),
                    reads=[f"ps{b2}"], writes=[L + f"xt{xs}"])
        Sx.dma(dst[r0:r0 + 256, :].rearrange("(a p) d -> p a d", p=128), xt[xs][:],
               reads=[L + f"xt{xs}"], writes=[f"{dst_name}{2 * j}", f"{dst_name}{2 * j + 1}"])
    sb.reset(m)


def stage2(c, l):
    sb, Sx = c.sb, c.S
    L = f"L{l}s2"
    m = sb.mark()
    PAD = 30
    aT = sb.t("aT", [128, 2, PAD + S], F32)
    acc = sb.t("acc", [128, 2, S], F32)
    dwT = sb.t("dwT", [128, 2, 32], F32)
    lnp = sb.t("lnp", [128, 2, 2], F32)
    pwf = sb.t("pwf", [128, 2, 256], F32)
    pw = sb.t("pw", [128, 2, 256], BF16)
    pwb = sb.t("pwb", [128, 256], F32)
    ones = sb.t("ones", [128, 128], F32)
    vg = [sb.t("vg", [128, 512], F32) for _ in range(2)]
    sgm = [sb.t("sgm", [128, 256], F32) for _ in range(2)]
    atm = [sb.t("atm", [128, 256], F32) for _ in range(2)]
    sq = sb.t("sq", [128, 2, 512], F32)
    mean = sb.t("mean", [128, 512], F32)
    tmp = sb.t("tmp", [128, 512], F32)
    var = sb.t("var", [128, 512], F32)
    rstd = sb.t("rstd", [128, 512], F32)
    yn = sb.t("yn", [128, 2, 512], F32)
    zT = sb.t("zT", [128, 2, 512], BF16)
    yo = [sb.t("yo", [128, 256], F32) for _ in range(2)]
    dv = c.W("conv_dw")[l].rearrange("k (c p) -> p c k", p=128)
    for ch in range(2):
        Sx.dma(dwT[:, ch:ch + 1, 0:31], dv[:, ch:ch + 1, :], writes=[L + "dwT"], allow_slow_non_contiguous=True)
    Sx.dma(dwT[:, :, 31:32], c.W("conv_dw_b")[l].rearrange("(c p) -> p c", p=128).unsqueeze(2),
           writes=[L + "dwT"], allow_slow_non_contiguous=True)
    Sx.dma(lnp[:, :, 0:1], c.W("conv_ln_g")[l].rearrange("(c p) -> p c", p=128).unsqueeze(2),
           writes=[L + "lnp"], allow_slow_non_contiguous=True)
    Sx.dma(lnp[:, :, 1:2], c.W("conv_ln_b")[l].rearrange("(c p) -> p c", p=128).unsqueeze(2),
           writes=[L + "lnp"], allow_slow_non_contiguous=True)
    Sx.dma(pwf[:], c.W("conv_pw")[l].rearrange("(k p) n -> p k n", p=128), writes=[L + "pwf"])
    Sx.op("pool", lambda e: e.tensor_copy(out=pw[:], in_=pwf[:]), reads=[L + "pwf"], writes=[L + "pw"])
    Sx.dma(pwb[:], c.W("conv_pw_b")[l].partition_broadcast(128), writes=[L + "pwb"])
    Sx.op("pool", lambda e: e.memset(ones[:], 1.0), writes=[L + "ones"])
    Sx.op("pool", lambda e: e.memset(aT[:, :, 0:PAD], 0.0), writes=[L + "aTpad"])
    for t in range(NT):
        s_ = t % 2
        r0 = t * 128
        Sx.dma(vg[s_][:], c.proj[r0:r0 + 128, 0:512], reads=[f"proj{t}"], writes=[L + f"vg{s_}"])
        Sx.op("act", lambda e, s_=s_: e.activation(out=sgm[s_][:], in_=vg[s_][:, 256:512], func=AF.Sigmoid),
              reads=[L + f"vg{s_}"], writes=[L + f"sgm{s_}"])
        Sx.op("dve", lambda e, s_=s_: e.tensor_tensor(out=atm[s_][:], in0=vg[s_][:, 0:256], in1=sgm[s_][:],
                                                      op=ALU.mult),
              reads=[L + f"vg{s_}", L + f"sgm{s_}"], writes=[L + f"atm{s_}"])
        for ch in range(2):
            b = (2 * t + ch) % 2
            Sx.op("pe", lambda e, s_=s_, ch=ch, b=b: e.transpose(
                out=c.ps[b][:, 0:128], in_=atm[s_][:, ch * 128:(ch + 1) * 128], identity=c.ident_f[:]),
                reads=[L + f"atm{s_}", "ident_f"], writes=[f"ps{b}"])
            if ch == 0:
                Sx.op("act", lambda e, ch=ch, b=b, r0=r0: e.copy(
                    out=aT[:, ch, PAD + r0:PAD + r0 + 128], in_=c.ps[b][:, 0:128]),
                    reads=[f"ps{b}"], writes=[L + f"aT{t // 8}_{ch}"])
            else:
                Sx.op("pool" if False else "dve", lambda e, ch=ch, b=b, r0=r0: e.tensor_copy(
                    out=aT[:, ch, PAD + r0:PAD + r0 + 128], in_=c.ps[b][:, 0:128]),
                    reads=[f"ps{b}"], writes=[L + f"aT{t // 8}_{ch}"])
    for blk in range(4):
        t0 = blk * 1024
        for ch in range(2):
            rd = [L + f"aT{bb}_{ch}" for bb in range(max(0, blk - 1), blk + 1)] + [L + "aTpad", L + "dwT"]
            Sx.op("pool", lambda e, ch=ch, t0=t0: e.tensor_scalar(
                out=acc[:, ch, t0:t0 + 1024], in0=aT[:, ch, PAD + t0:PAD + t0 + 1024], scalar1=dwT[:, ch, 30:31],
                scalar2=dwT[:, ch, 31:32], op0=ALU.mult, op1=ALU.add),
                reads=rd, writes=[L + f"acc{blk}_{ch}"])
            for k in range(30):
                Sx.op("dve", lambda e, ch=ch, t0=t0, k=k: e.scalar_tensor_tensor(
                    out=acc[:, ch, t0:t0 + 1024], in0=aT[:, ch, k + t0:k + t0 + 1024], scalar=dwT[:, ch, k:k + 1],
                    in1=acc[:, ch, t0:t0 + 1024], op0=ALU.mult, op1=ALU.add),
                    reads=rd, writes=[L + f"acc{blk}_{ch}"])
    for q in range(8):
        t0 = q * 512
        blk = q // 2
        for ch in range(2):
            Sx.op("pool", lambda e, ch=ch, t0=t0: e.tensor_tensor(
                out=sq[:, ch, :], in0=acc[:, ch, t0:t0 + 512], in1=acc[:, ch, t0:t0 + 512], op=ALU.mult),
                reads=[L + f"acc{blk}_{ch}"], writes=[L + f"sq{ch}"])
        for ch in range(2):
            Sx.op("pe", lambda e, ch=ch, t0=t0: e.matmul(
                out=c.ps[2][:], lhsT=ones[:], rhs=acc[:, ch, t0:t0 + 512], start=(ch == 0), stop=(ch == 1)),
                reads=[L + "ones", L + f"acc{blk}_{ch}"], writes=["ps2"])
        for ch in range(2):
            Sx.op("pe", lambda e, ch=ch: e.matmul(
                out=c.ps[3][:], lhsT=ones[:], rhs=sq[:, ch, :], start=(ch == 0), stop=(ch == 1)),
                reads=[L + "ones", L + f"sq{ch}"], writes=["ps3"])
        Sx.op("act", lambda e: e.activation(out=mean[:], in_=c.ps[2][:], func=AF.Copy, scale=1.0 / 256),
              reads=["ps2"], writes=[L + "mean"])
        Sx.op("pool", lambda e: e.tensor_tensor(out=tmp[:], in0=mean[:], in1=mean[:], op=ALU.mult),
              reads=[L + "mean"], writes=[L + "tmp"])
        Sx.op("dve", lambda e: e.scalar_tensor_tensor(out=var[:], in0=c.ps[3][:], scalar=1.0 / 256, in1=tmp[:],
                                                      op0=ALU.mult, op1=ALU.subtract),
              reads=["ps3", L + "tmp"], writes=[L + "var"])
        Sx.op("dve", lambda e: e.tensor_scalar(out=var[:], in0=var[:], scalar1=EPS, scalar2=None, op0=ALU.add),
              reads=[L + "var"], writes=[L + "var"])
        Sx.op("act", lambda e: e.activation(out=tmp[:], in_=var[:], func=AF.Sqrt),
              reads=[L + "var"], writes=[L + "tmp"])
        Sx.op("dve", lambda e: e.reciprocal(out=rstd[:], in_=tmp[:]), reads=[L + "tmp"], writes=[L + "rstd"])
        for ch in range(2):
            Sx.op("pool", lambda e, ch=ch, t0=t0: e.tensor_tensor(
                out=yn[:, ch, :], in0=acc[:, ch, t0:t0 + 512], in1=mean[:], op=ALU.subtract),
                reads=[L + f"acc{blk}_{ch}", L + "mean"], writes=[L + f"yn{ch}"])
            Sx.op("dve", lambda e, ch=ch: e.tensor_tensor(
                out=yn[:, ch, :], in0=yn[:, ch, :], in1=rstd[:], op=ALU.mult),
                reads=[L + f"yn{ch}", L + "rstd"], writes=[L + f"yn{ch}"])
            Sx.op("act", lambda e, ch=ch: e.activation(
                out=zT[:, ch, :], in_=yn[:, ch, :], func=AF.Silu, scale=lnp[:, ch, 0:1], bias=lnp[:, ch, 1:2]),
                reads=[L + f"yn{ch}", L + "lnp"], writes=[L + f"zT{ch}"])
        for a in range(4):
            s_ = a % 2
            b = 4 + s_
            for ch in range(2):
                Sx.op("pe", lambda e, a=a, ch=ch, b=b: e.matmul(
                    out=c.ps[b][:, 0:256], lhsT=zT[:, ch, a * 128:(a + 1) * 128], rhs=pw[:, ch, :],
                    start=(ch == 0), stop=(ch == 1)),
                    reads=[L + f"zT{ch}", L + "pw"], writes=[f"ps{b}"])
            Sx.op("dve", lambda e, s_=s_, b=b: e.tensor_tensor(out=yo[s_][:], in0=c.ps[b][:, 0:256], in1=pwb[:],
                                                            op=ALU.add),
                  reads=[f"ps{b}", L + "pwb"], writes=[L + f"yo{s_}"])
            tt = q * 4 + a
            Sx.dma(c.mix[tt * 128:(tt + 1) * 128, 0:256], yo[s_][:], reads=[L + f"yo{s_}"], writes=[f"mixA{tt}"])
    sb.reset(m)


def bc(ap, shape, axis):
    return ap.unsqueeze(axis).to_broadcast(list(shape))


def setup_rope(c):
    sb, Sx = c.sb, c.S
    m = sb.mark()
    L = "rope"
    pos_i = sb.t("pos_i", [128, NT], I32)
    pos_f = sb.t("pos_f", [128, NT], F32)
    invf = sb.t("invf", [128, 32], F32)
    ang = sb.t("ang", [128, NT, 32], F32)
    red = sb.t("red", [128, NT, 32], F32)
    cs = sb.t("cs", [128, NT, 64], F32)
    negpi = sb.t("negpi", [128, 1], F32)
    pview = c.pos.rearrange("(t p) o -> p (t o)", p=128)
    for q4 in range(4):
        Sx.dma(pos_i[:, q4 * 8:(q4 + 1) * 8], pview[:, q4 * 8:(q4 + 1) * 8], writes=[L + "pos_i"],
               allow_slow_non_contiguous=True)
    Sx.dma(invf[:], c.invf_d, writes=[L + "invf"])
    Sx.op("pool", lambda e: e.memset(negpi[:], -float(np.pi)), writes=[L + "negpi"])
    Sx.op("dve", lambda e: e.tensor_copy(out=pos_f[:], in_=pos_i[:]), reads=[L + "pos_i"], writes=[L + "pos_f"])
    Sx.op("dve", lambda e: e.tensor_tensor(out=ang[:], in0=bc(pos_f[:], [128, NT, 32], 2),
                                           in1=bc(invf[:], [128, NT, 32], 1), op=ALU.mult),
          reads=[L + "pos_f", L + "invf"], writes=[L + "ang"])
    TWO_PI = float(2 * np.pi)
    C1, C2, C3 = 6.28125, 0.0019350051879882812, 3.019916050561733e-07
    kf = sb.t("kf", [128, NT, 32], F32)
    ki = sb.t("ki", [128, NT, 32], I32)
    msk = sb.t("msk", [128, NT, 32], F32)
    Sx.op("dve", lambda e: e.tensor_scalar(out=kf[:], in0=ang[:], scalar1=1.0 / TWO_PI, scalar2=None, op0=ALU.mult),
          reads=[L + "ang"], writes=[L + "kf"])
    Sx.op("dve", lambda e: e.tensor_copy(out=ki[:], in_=kf[:]), reads=[L + "kf"], writes=[L + "ki"])
    Sx.op("dve", lambda e: e.tensor_copy(out=kf[:], in_=ki[:]), reads=[L + "ki"], writes=[L + "kf"])
    for cst in (C1, C2, C3):
        Sx.op("dve", lambda e, cst=cst: e.scalar_tensor_tensor(out=ang[:], in0=kf[:], scalar=-cst, in1=ang[:],
                                                               op0=ALU.mult, op1=ALU.add),
              reads=[L + "ang", L + "kf"], writes=[L + "ang"])
    PI = float(np.pi)
    PI_LO = 3.1415925
    for which, shift in ((0, 0.5 * np.pi), (1, 0.0)):
        Sx.op("dve", lambda e, shift=shift: e.tensor_scalar(out=red[:], in0=ang[:], scalar1=float(shift), scalar2=None,
                                                            op0=ALU.add),
              reads=[L + "ang"], writes=[L + "red"])
        for cmp_op, thr, adj in ((ALU.is_gt, PI, -TWO_PI), (ALU.is_lt, -PI, TWO_PI)):
            Sx.op("dve", lambda e, cmp_op=cmp_op, thr=thr: e.tensor_scalar(out=msk[:], in0=red[:], scalar1=thr,
                                                                           scalar2=None, op0=cmp_op),
                  reads=[L + "red"], writes=[L + "msk"])
            Sx.op("dve", lambda e, adj=adj: e.scalar_tensor_tensor(out=red[:], in0=msk[:], scalar=adj, in1=red[:],
                                                                   op0=ALU.mult, op1=ALU.add),
                  reads=[L + "red", L + "msk"], writes=[L + "red"])
        Sx.op("dve", lambda e: e.tensor_scalar(out=red[:], in0=red[:], scalar1=-PI_LO, scalar2=PI_LO,
                                               op0=ALU.max, op1=ALU.min),
              reads=[L + "red"], writes=[L + "red"])
        Sx.op("act", lambda e, which=which: e.activation(out=cs[:, :, which * 32:(which + 1) * 32], in_=red[:],
                                                         func=AF.Sin),
              reads=[L + "red"], writes=[L + "cs"])
    cview = c.cs_tab.rearrange("(t p) n -> p t n", p=128)
    for q4 in range(4):
        Sx.dma(cview[:, q4 * 8:(q4 + 1) * 8, :], cs[:, q4 * 8:(q4 + 1) * 8, :], reads=[L + "cs"], writes=["cs_tab"])
    sb.reset(m)


def norm_rope(c, L, ve, x, gn, cs, out, nh, sq, tt, st, gfull=None, ve2=None):
    Sx = c.S
    R = L["scr"]
    if ve2 is None:
        ve2 = ve
    sh = [128, nh, 64]
    hs = [128, nh, 32]
    Sx.op(ve, lambda e: e.tensor_tensor(out=sq, in0=x, in1=x, op=ALU.mult), reads=L["x"], writes=[R + "sq"])
    Sx.op("dve", lambda e: e.tensor_reduce(out=st[:, :, 0], in_=sq, axis=AX.X, op=ALU.add),
          reads=[R + "sq"], writes=[R + "ss"])
    Sx.op(ve, lambda e: e.tensor_scalar(out=st[:, :, 1], in0=st[:, :, 0], scalar1=1.0 / 64, scalar2=EPS,
                                        op0=ALU.mult, op1=ALU.add), reads=[R + "ss"], writes=[R + "ms"])
    Sx.op("act", lambda e: e.activation(out=st[:, :, 2], in_=st[:, :, 1], func=AF.Ln),
          reads=[R + "ms"], writes=[R + "ln"])
    Sx.op("act", lambda e: e.activation(out=st[:, :, 3], in_=st[:, :, 2], func=AF.Exp, scale=-0.5),
          reads=[R + "ln"], writes=[R + "rstd"])
    Sx.op(ve, lambda e: e.tensor_tensor(out=sq, in0=x, in1=bc(st[:, :, 3], sh, 2), op=ALU.mult),
          reads=L["x"] + [R + "rstd", R + "ss"], writes=[R + "sq"])
    gin = gfull if gfull is not None else bc(gn, sh, 1)
    Sx.op(ve, lambda e: e.tensor_tensor(out=sq, in0=sq, in1=gin, op=ALU.mult),
          reads=[R + "sq"] + L["gn"], writes=[R + "sq"])
    cosb = bc(cs[:, 0:32], hs, 1)
    sinb = bc(cs[:, 32:64], hs, 1)
    x1 = sq[:, :, 0:32]
    x2 = sq[:, :, 32:64]
    Sx.op(ve, lambda e: e.tensor_tensor(out=tt[:, :, 0, :], in0=x1, in1=cosb, op=ALU.mult),
          reads=[R + "sq"] + L["cs"], writes=[R + "t0"])
    Sx.op(ve, lambda e: e.tensor_tensor(out=tt[:, :, 1, :], in0=x2, in1=sinb, op=ALU.mult),
          reads=[R + "sq"] + L["cs"], writes=[R + "t1"])
    Sx.op(ve, lambda e: e.tensor_tensor(out=out[:, :, 0:32], in0=tt[:, :, 0, :], in1=tt[:, :, 1, :], op=ALU.subtract),
          reads=[R + "t0", R + "t1"], writes=L["out"])
    Sx.op(ve2, lambda e: e.tensor_tensor(out=tt[:, :, 2, :], in0=x2, in1=cosb, op=ALU.mult),
          reads=[R + "sq"] + L["cs"], writes=[R + "t2"])
    Sx.op(ve2, lambda e: e.tensor_tensor(out=tt[:, :, 3, :], in0=x1, in1=sinb, op=ALU.mult),
          reads=[R + "sq"] + L["cs"], writes=[R + "t3"])
    Sx.op(ve2, lambda e: e.tensor_tensor(out=out[:, :, 32:64], in0=tt[:, :, 2, :], in1=tt[:, :, 3, :], op=ALU.add),
          reads=[R + "t2", R + "t3"], writes=L["out"])


DIL_R = (1, 4, 16)


def stage4(c, l):
    sb, Sx = c.sb, c.S
    L = f"L{l}s4"
    m = sb.mark()
    NS = 3
    gq = sb.t("gq", [128, 64], F32)
    gk = sb.t("gk", [128, 64], F32)
    mk_f = sb.t("mk_f", [128, 4, 128], F32)
    mk = sb.t("mk", [128, 4, 128], BF16)
    X = [sb.t("X", [128, 3, 256], F32) for _ in range(2)]
    CS = [sb.t("CS", [128, 64], F32) for _ in range(2)]
    sq = [sb.t("sq", [128, 8, 64], F32) for _ in range(2)]
    tt = [sb.t("tt", [128, 8, 4, 32], F32) for _ in range(2)]
    stt = [sb.t("stt", [128, 8, 4], F32) for _ in range(2)]
    QK = [sb.t("QK", [128, 8, 64], BF16) for _ in range(2)]
    QZ = [sb.t("QZ", [128, 4, 128], BF16) for _ in range(2)]
    QKz = [sb.t("QKz", [128, 4, 128], BF16) for _ in range(2)]
    KT = [sb.t("KT", [128, 2, 128], BF16) for _ in range(NS)]
    V = [sb.t("V", [128, 4, 65], BF16) for _ in range(NS)]
    PT = [sb.t("PT", [128, 4, 2, 128], BF16) for _ in range(2)]
    O = [sb.t("O", [128, 4, 65], F32) for _ in range(4)]
    Sx.dma(gq[:], c.W("dil_q_norm")[l].partition_broadcast(128), writes=[L + "gq"])
    Sx.dma(gk[:], c.W("dil_k_norm")[l].partition_broadcast(128), writes=[L + "gk"])
    gqk = sb.t("gqk", [128, 8, 64], F32)
    Sx.op("pool", lambda e: e.tensor_copy(out=gqk[:, 0:4, :], in_=bc(gq[:], [128, 4, 64], 1)),
          reads=[L + "gq"], writes=[L + "gqk"])
    Sx.op("pool", lambda e: e.tensor_copy(out=gqk[:, 4:8, :], in_=bc(gk[:], [128, 4, 64], 1)),
          reads=[L + "gk"], writes=[L + "gqk"])
    Sx.dma(mk_f[:], c.masks_d, writes=[L + "mk_f"])
    Sx.op("pool", lambda e: e.tensor_copy(out=mk[:], in_=mk_f[:]), reads=[L + "mk_f"], writes=[L + "mk"])
    for s_ in range(NS):
        Sx.op("pool", lambda e, s_=s_: e.memset(V[s_][:, :, 64:65], 1.0), writes=[L + f"Vone{s_}"])
    for s_ in range(2):
        Sx.op("pool", lambda e, s_=s_: e.memset(QKz[s_][:], 0.0), writes=[L + f"QKzz{s_}"])
    its = []
    for g, r in enumerate(DIL_R):
        tps = (S // r) // 128
        for T in range(NT):
            its.append((g, r, T, T // tps, T % tps))

    def stage_a(it):
        g, r, T, cc, mt = its[it]
        pv = c.proj.rearrange("(m r) n -> r m n", r=r)
        cv = c.cs_tab.rearrange("(m r) n -> r m n", r=r)
        s2 = it % 2
        s3 = it % NS
        ve = "dve" if it % 2 == 0 else "pool"
        if True:
            lo = ((mt * 128) * r + cc) // 128
            hi = ((mt * 128 + 127) * r + cc) // 128
            nat = list(range(lo, hi + 1))
            Sx.dma(X[s2][:], pv[cc, mt * 128:(mt + 1) * 128, D_Q:D_Q + 2304].rearrange(
                "p (i gg j) -> p i gg j", i=3, gg=3)[:, :, g, :],
                reads=[f"proj{t}" for t in nat], writes=[L + f"X{s2}"])
            Sx.dma(CS[s2][:], cv[cc, mt * 128:(mt + 1) * 128, :], reads=["cs_tab"], writes=[L + f"CS{s2}"])
            norm_rope(c, {"x": [L + f"X{s2}"], "out": [L + f"QK{s2}_0", L + f"QK{s2}_1"], "gn": [L + "gqk"],
                          "cs": [L + f"CS{s2}"], "scr": L + f"nr{s2}"},
                      ve, X[s2][:, 0:2, :].rearrange("p i (h d) -> p (i h) d", h=4), None, CS[s2][:],
                      QK[s2][:], 8, sq[s2][:], tt[s2][:], stt[s2][:], gfull=gqk[:])
            Sx.op("pool" if ve == "dve" else "dve", lambda e, s2=s2, s3=s3: e.tensor_copy(
                out=V[s3][:, :, 0:64], in_=X[s2][:, 2, :].rearrange("p (h d) -> p h d", h=4)),
                reads=[L + f"X{s2}"], writes=[L + f"V{s3}"])

    def stage_b(it):
        g, r, T, cc, mt = its[it]
        dv = c.dilO[g].rearrange("(m r) n -> r m n", r=r)
        s2 = it % 2
        s3 = it % NS
        sp = (it - 1) % NS
        if True:
            qv = QK[s2][:, 0:4, :].rearrange("p (b two) d -> p b two d", two=2)
            zv = QKz[s2][:].rearrange("p (b two) n -> p b two n", two=2)
            for par in range(2):
                Sx.op("pool", lambda e, par=par, qv=qv, zv=zv: e.tensor_copy(
                    out=zv[:, :, par, par * 64:(par + 1) * 64], in_=qv[:, :, par, :]),
                    reads=[L + f"QK{s2}_0", L + f"QKzz{s2}"], writes=[L + f"QKz{s2}"])
            for h in range(4):
                Sx.op("pe", lambda e, s2=s2, h=h: e.matmul(
                    out=c.ps[0][:, h * 128:(h + 1) * 128], lhsT=QKz[s2][:, h, :], rhs=c.ident[:],
                    start=True, stop=True),
                    reads=[L + f"QKz{s2}", "ident"], writes=["ps0"])
            for blk in range(2):
                Sx.op("pe", lambda e, s2=s2, blk=blk: e.matmul(
                    out=c.ps[4][:, blk * 128:(blk + 1) * 128],
                    lhsT=QK[s2][:, 4 + 2 * blk:6 + 2 * blk, :].rearrange("p h d -> p (h d)"), rhs=c.ident[:],
                    start=True, stop=True),
                    reads=[L + f"QK{s2}_1", "ident"], writes=["ps4"])
            Sx.op("act", lambda e, s2=s2: e.copy(out=QZ[s2][:], in_=c.ps[0][:].rearrange("p (b t) -> p b t", b=4)),
                  reads=["ps0"], writes=[L + f"QZ{s2}"])
            Sx.op("dve", lambda e, s3=s3: e.tensor_copy(
                out=KT[s3][:], in_=c.ps[4][:, 0:256].rearrange("p (b t) -> p b t", b=2)),
                reads=["ps4"], writes=[L + f"KT{s3}"])
            first = mt == 0
            ks = [s3 if first else sp, s3]
            for h in range(4):
                pb = (h % 2) * 64
                for kk in range(2):
                    bank = 1 + h // 2
                    col = ((h % 2) * 2 + kk) * 128
                    Sx.op("pe", lambda e, h=h, kk=kk, bank=bank, col=col, s2=s2, ksl=ks[kk]: e.matmul(
                        out=c.ps[bank][:, col:col + 128], lhsT=KT[ksl][:, h // 2, :],
                        rhs=QZ[s2][:, h, :], start=True, stop=True),
                        reads=[L + f"KT{ks[kk]}", L + f"QZ{s2}"], writes=[f"ps{bank}"])
            for hp in range(2):
                Sx.op("act", lambda e, hp=hp, s2=s2: e.activation(
                    out=PT[s2][:, 2 * hp:2 * hp + 2, :, :].rearrange("p h k t -> p (h k t)"), in_=c.ps[1 + hp][:],
                    func=AF.Exp, scale=SCALE),
                    reads=[f"ps{1 + hp}"], writes=[L + f"PT{s2}_{hp}"])
            if first:
                for hp in range(2):
                    for kk, mi in ((0, 2), (1, 1)):
                        Sx.op("pool", lambda e, hp=hp, kk=kk, mi=mi, s2=s2: e.tensor_tensor(
                            out=PT[s2][:, 2 * hp:2 * hp + 2, kk, :], in0=PT[s2][:, 2 * hp:2 * hp + 2, kk, :],
                            in1=bc(mk[:, mi, :], [128, 2, 128], 1), op=ALU.mult),
                            reads=[L + f"PT{s2}_{hp}", L + "mk"], writes=[L + f"PT{s2}_{hp}"])
            else:
                for hp in range(2):
                    Sx.op("pool", lambda e, hp=hp, s2=s2: e.tensor_tensor(
                        out=PT[s2][:, 2 * hp:2 * hp + 2, :, :], in0=PT[s2][:, 2 * hp:2 * hp + 2, :, :],
                        in1=bc(mk[:, 0:2, :], [128, 2, 2, 128], 1), op=ALU.mult),
                        reads=[L + f"PT{s2}_{hp}", L + "mk"], writes=[L + f"PT{s2}_{hp}"])
            for h in range(4):
                for kk in range(2):
                    Sx.op("pe", lambda e, h=h, kk=kk, s2=s2, ksl=ks[kk]: e.matmul(
                        out=c.ps[3][:, h * 65:(h + 1) * 65], lhsT=PT[s2][:, h, kk, :], rhs=V[ksl][:, h, :],
                        start=(h == 0 and kk == 0), stop=(h == 3 and kk == 1), skip_group_check=True),
                        reads=[L + f"PT{s2}_{h // 2}", L + f"V{ks[kk]}", L + f"Vone{ks[kk]}"], writes=["ps3"])
            s4 = it % 4
            Sx.op("dve", lambda e, s4=s4: e.tensor_copy(out=O[s4][:].rearrange("p h d -> p (h d)"),
                                                        in_=c.ps[3][:, 0:260]),
                  reads=["ps3"], writes=[L + f"O{s4}"])

    def store(it):
        g, r, T, cc, mt = its[it]
        dv = c.dilO[g].rearrange("(m r) n -> r m n", r=r)
        s4 = it % 4
        Sx.dma(dv[cc, mt * 128:(mt + 1) * 128, :], O[s4][:].rearrange("p h d -> p (h d)"),
               reads=[L + f"O{s4}"], writes=[f"dilO{g}_T{T}"])

    NIT = len(its)
    for it in range(NIT + 3):
        if it < NIT:
            stage_a(it)
        if 1 <= it <= NIT:
            stage_b(it - 1)
        if it >= 3:
            store(it - 3)
    U = [sb.t("U", [128, 3, 260], F32) for _ in range(2)]
    rd = [sb.t("rd", [128, 4], F32) for _ in range(2)]
    yc = [sb.t("yc", [128, 4, 64], F32) for _ in range(2)]
    for t in range(NT):
        s_ = t % 2
        r0 = t * 128
        Sx.dma(U[s_][:], c.dilO[:, r0:r0 + 128, :].rearrange("g p n -> p g n"),
               reads=[f"dilO{g}_T{T}" for g in range(3) for T in range(NT)], writes=[L + f"U{s_}"])
        Sx.op("pool", lambda e, s_=s_: e.tensor_tensor(out=U[s_][:, 0, :], in0=U[s_][:, 0, :], in1=U[s_][:, 1, :],
                                                      op=ALU.add), reads=[L + f"U{s_}"], writes=[L + f"U{s_}"])
        Sx.op("pool", lambda e, s_=s_: e.tensor_tensor(out=U[s_][:, 0, :], in0=U[s_][:, 0, :], in1=U[s_][:, 2, :],
                                                      op=ALU.add), reads=[L + f"U{s_}"], writes=[L + f"U{s_}"])
        Uv = U[s_][:, 0, :].rearrange("p (h d) -> p h d", h=4)
        Sx.op("dve", lambda e, s_=s_, Uv=Uv: e.reciprocal(out=rd[s_][:], in_=Uv[:, :, 64]),
              reads=[L + f"U{s_}"], writes=[L + f"rd{s_}"])
        Sx.op("dve", lambda e, s_=s_, Uv=Uv: e.tensor_tensor(out=yc[s_][:], in0=Uv[:, :, 0:64],
                                                             in1=bc(rd[s_][:], [128, 4, 64], 2), op=ALU.mult),
              reads=[L + f"U{s_}", L + f"rd{s_}"], writes=[L + f"yc{s_}"])
        Sx.dma(c.mix[r0:r0 + 128, 768:1024], yc[s_][:].rearrange("p h d -> p (h d)"),
               reads=[L + f"yc{s_}"], writes=[f"mixC{t}"])
    sb.reset(m)


def stage3(c, l):
    sb, Sx = c.sb, c.S
    L = f"L{l}s3"
    m = sb.mark()
    KAs = sb.t("KAs", [128, 2, S], BF16)
    KWA = sb.t("KWA", [128, 2, S], BF16)
    KCr = sb.t("KCr", [128, 2, S], BF16)
    VS = sb.t("VS", [128, NT, 2, 65], BF16)
    VW = sb.t("VW", [128, NT, 2, 65], BF16)
    KCA = sb.t("KCA", [128, 2, 256], BF16)
    VCa = sb.t("VCa", [128, 2, 2, 129], BF16)
    gq = sb.t("gq", [128, 64], F32)
    gk = sb.t("gk", [128, 64], F32)
    Ebf = sb.t("Ebf", [128, NT, 64], BF16)
    madd = sb.t("madd", [128, 2, 128], BF16)
    Sx.dma(gq[:], c.W("nsa_q_norm")[l].partition_broadcast(128), writes=[L + "gq"])
    Sx.dma(gk[:], c.W("nsa_k_norm")[l].partition_broadcast(128), writes=[L + "gk"])
    m1 = sb.mark()
    Ef = sb.t("Ef", [128, NT, 64], F32)
    mk_f = sb.t("mk_f", [128, 4, 128], F32)
    Sx.dma(Ef[:], c.esel_d.rearrange("(t p) n -> p t n", p=128), writes=[L + "Ef"])
    Sx.op("pool", lambda e: e.tensor_copy(out=Ebf[:], in_=Ef[:]), reads=[L + "Ef"], writes=[L + "Ebf"])
    Sx.dma(mk_f[:], c.masks_d, writes=[L + "mk_f"])
    Sx.op("dve", lambda e: e.tensor_scalar(out=madd[:, 0, :], in0=mk_f[:, 1, :], scalar1=-NEG, scalar2=NEG,
                                           op0=ALU.mult, op1=ALU.add), reads=[L + "mk_f"], writes=[L + "madd"])
    Sx.op("dve", lambda e: e.tensor_scalar(out=madd[:, 1, :], in0=mk_f[:, 3, :], scalar1=-NEG, scalar2=NEG,
                                           op0=ALU.mult, op1=ALU.add), reads=[L + "mk_f"], writes=[L + "madd"])
    Sx.op("pool", lambda e: e.memset(VS[:, :, :, 64:65], 1.0), writes=[L + "VSone"])
    Sx.op("pool", lambda e: e.memset(VW[:, :, :, 64:65], 1.0), writes=[L + "VWone"])
    Sx.mark("s3 setup done")
    X = [sb.t("X", [128, 768], F32) for _ in range(2)]
    CS = [sb.t("CS", [128, 64], F32) for _ in range(2)]
    sq = [sb.t("sq", [128, 4, 64], F32) for _ in range(2)]
    tt = [sb.t("tt", [128, 4, 4, 32], F32) for _ in range(2)]
    stt = [sb.t("stt", [128, 4, 4], F32) for _ in range(2)]
    KMs = [sb.t("KMs", [128, 2, 128], BF16) for _ in range(2)]
    KMw = [sb.t("KMw", [128, 2, 128], BF16) for _ in range(2)]
    KMc = [sb.t("KMc", [128, 2, 128], BF16) for _ in range(2)]
    for s_ in range(2):
        Sx.op("pool", lambda e, s_=s_: e.memset(KMw[s_][:], 0.0), writes=[L + f"KMwz{s_}"])
    for t in range(NT):
        s_ = t % 2
        r0 = t * 128
        ve = "dve" if t % 2 == 0 else "pool"
        vo = "pool" if t % 2 == 0 else "dve"
        Sx.dma(X[s_][:], c.proj[r0:r0 + 128, N_KC:N_KC + 768], reads=[f"proj{t}"], writes=[L + f"X{s_}"])
        Sx.dma(CS[s_][:], c.cs_tab[r0:r0 + 128, :], reads=["cs_tab"], writes=[L + f"CS{s_}"])
        for br, (off, KM) in enumerate(((256, KMs), (512, KMw))):
            norm_rope(c, {"x": [L + f"X{s_}"], "out": [L + f"KM{br}{s_}"], "gn": [L + "gk"],
                          "cs": [L + f"CS{s_}"], "scr": L + f"nrA{s_}{br}"},
                      ve, X[s_][:, off:off + 128].rearrange("p (h d) -> p h d", h=2), gk[:], CS[s_][:],
                      KM[s_][:, :, 0:64], 2, sq[s_][:, 2 * br:2 * br + 2, :], tt[s_][:, 2 * br:2 * br + 2, :, :],
                      stt[s_][:, 2 * br:2 * br + 2, :])
        Sx.op(vo, lambda e, s_=s_, t=t: e.tensor_copy(out=KMs[s_][:, :, 64:128], in_=bc(Ebf[:, t, :], [128, 2, 64], 1)),
              reads=[L + "Ebf"], writes=[L + f"KM0{s_}"])
        Sx.op(vo, lambda e, s_=s_: e.tensor_copy(
            out=KMc[s_][:].rearrange("p h (kv d) -> p h kv d", kv=2),
            in_=X[s_][:, 0:256].rearrange("p (kv h d) -> p h kv d", kv=2, h=2)),
            reads=[L + f"X{s_}"], writes=[L + f"KMc{s_}"])
        Sx.op(vo, lambda e, s_=s_, t=t: e.tensor_copy(
            out=VS[:, t, :, 0:64], in_=X[s_][:, 384:512].rearrange("p (h d) -> p h d", h=2)),
            reads=[L + f"X{s_}"], writes=[L + f"VS{t}"])
        Sx.op(vo, lambda e, s_=s_, t=t: e.tensor_copy(
            out=VW[:, t, :, 0:64], in_=X[s_][:, 640:768].rearrange("p (h d) -> p h d", h=2)),
            reads=[L + f"X{s_}"], writes=[L + f"VW{t}"])
        for i, (KM, nm) in enumerate(((KMs, "KM0"), (KMw, "KM1"), (KMc, "KMc"))):
            for h in range(2):
                bank, col = i, h * 128
                rd = [L + f"{nm}{s_}", "ident"] + ([L + f"KMwz{s_}"] if i == 1 else [])
                Sx.op("pe", lambda e, KM=KM, s_=s_, h=h, bank=bank, col=col: e.matmul(
                    out=c.ps[bank][:, col:col + 128], lhsT=KM[s_][:, h, :], rhs=c.ident[:], start=True, stop=True),
                    reads=rd, writes=[f"ps{bank}"])
        Sx.op("act", lambda e, r0=r0: e.copy(out=KAs[:, :, r0:r0 + 128],
                                             in_=c.ps[0][:, 0:256].rearrange("p (h t) -> p h t", h=2)),
              reads=["ps0"], writes=[L + f"KAs{t}"])
        Sx.op("dve", lambda e, r0=r0: e.tensor_copy(out=KWA[:, :, r0:r0 + 128],
                                                    in_=c.ps[1][:, 0:256].rearrange("p (h t) -> p h t", h=2)),
              reads=["ps1"], writes=[L + f"KWA{t}"])
        Sx.op("act", lambda e, r0=r0: e.copy(out=KCr[:, :, r0:r0 + 128],
                                             in_=c.ps[2][:, 0:256].rearrange("p (h t) -> p h t", h=2)),
              reads=["ps2"], writes=[L + "KCr"])
        if t == 0:
            Sx.mark("phaseA tile0 done")
    sb.reset(m1)
    Sx.mark("phaseA done")
    W1f = sb.t("W1f", [128, 32, 128], F32)
    W1 = sb.t("W1", [128, 32, 128], BF16)
    W2f = sb.t("W2f", [128, 128], F32)
    W2 = sb.t("W2", [128, 128], BF16)
    PEf = sb.t("PEf", [128, 32], F32)
    PEb = sb.t("PEb", [128, 32], BF16)
    b1 = sb.t("b1", [128, 2], F32)
    HID = sb.t("HID", [128, 256], BF16)
    ex = sb.t("ex", [128, 256], F32)
    zz = sb.t("zz", [128, 256], F32)
    kraw = sb.t("kraw", [128, 2, 2, 64], F32)
    CSc = sb.t("CSc", [128, 2, 64], F32)
    mmf = sb.t("mmf", [128, 2, 64], F32)
    KCm = sb.t("KCm", [128, 2, 2, 128], BF16)
    sq2 = sb.t("sq2", [128, 2, 64], F32)
    tt2 = sb.t("tt2", [128, 2, 4, 32], F32)
    st2 = sb.t("st2", [128, 2, 4], F32)
    Sx.op("pool", lambda e: e.memset(W1f[:], 0.0), writes=[L + "W1f"])
    Sx.op("pool", lambda e: e.memset(W2f[:], 0.0), writes=[L + "W2f"])
    Sx.op("pool", lambda e: e.memset(KCm[:], 0.0), writes=[L + "KCmz"])
    Sx.op("pool", lambda e: e.memset(CSc[:], 0.0), writes=[L + "CSc"])
    for lh in range(2):
        Sx.dma(W1f[0:64, lh * 16:(lh + 1) * 16, 0:64],
               c.W("cmp_k_w1")[l].rearrange("(l d) j -> d l j", d=64)[:, lh * 16:(lh + 1) * 16, :], writes=[L + "W1f"])
        Sx.dma(W1f[64:128, lh * 16:(lh + 1) * 16, 64:128],
               c.W("cmp_v_w1")[l].rearrange("(l d) j -> d l j", d=64)[:, lh * 16:(lh + 1) * 16, :], writes=[L + "W1f"])
    Sx.dma(W2f[0:64, 0:64], c.W("cmp_k_w2")[l], writes=[L + "W2f"])
    Sx.dma(W2f[64:128, 64:128], c.W("cmp_v_w2")[l], writes=[L + "W2f"])
    for half in range(2):
        Sx.dma(PEf[0:64, half * 16:(half + 1) * 16],
               c.W("cmp_k_pos")[l].rearrange("l d -> d l")[:, half * 16:(half + 1) * 16],
               writes=[L + "PEf"], allow_slow_non_contiguous=True)
        Sx.dma(PEf[64:128, half * 16:(half + 1) * 16],
               c.W("cmp_v_pos")[l].rearrange("l d -> d l")[:, half * 16:(half + 1) * 16],
               writes=[L + "PEf"], allow_slow_non_contiguous=True)
    Sx.op("dve", lambda e: e.tensor_copy(out=W1[:], in_=W1f[:]), reads=[L + "W1f"], writes=[L + "W1"])
    Sx.op("pool", lambda e: e.tensor_copy(out=W2[:], in_=W2f[:]), reads=[L + "W2f"], writes=[L + "W2"])
    Sx.op("pool", lambda e: e.tensor_copy(out=PEb[:], in_=PEf[:]), reads=[L + "PEf"], writes=[L + "PEb"])
    csv = c.cs_tab.rearrange("(a b) n -> a b n", b=16)
    Sx.dma(CSc[:, 0, :], csv[1:129, 15, :], reads=["cs_tab"], writes=[L + "CSc"])
    Sx.dma(CSc[0:127, 1, :], csv[129:256, 15, :], reads=["cs_tab"], writes=[L + "CSc"])
    Sx.dma(mmf[:], c.mmap_d.rearrange("(ct p) n -> p ct n", p=128), writes=[L + "mmf"])
    for h in range(2):
        Sx.op("pool", lambda e, h=h: e.tensor_copy(out=VCa[:, :, h, 65:129], in_=mmf[:]),
              reads=[L + "mmf"], writes=[L + f"VCam{h}"])
    Sx.op("pool", lambda e: e.memset(VCa[:, :, :, 64:65], 1.0), writes=[L + "VCa1"])
    for li in range(32):
        Sx.op("pe", lambda e, li=li: e.matmul(out=c.ps[2][:, 0:1], lhsT=W1[:, li, :], rhs=PEb[:, li:li + 1],
                                              start=(li == 0), stop=(li == 31)),
              reads=[L + "W1", L + "PEb"], writes=["ps2"])
    Sx.op("dve", lambda e: e.tensor_copy(out=b1[:, 0:1], in_=c.ps[2][:, 0:1]), reads=["ps2"], writes=[L + "b1"])
    Sx.op("dve", lambda e: e.tensor_scalar(out=b1[:, 1:2], in0=b1[:, 0:1], scalar1=-1.0, scalar2=None, op0=ALU.mult),
          reads=[L + "b1"], writes=[L + "nb1"])
    Sx.mark("b1 done")
    for h in range(2):
        kv = KCr[:, h, :].rearrange("p (c l) -> p l c", l=16)
        for li in range(16):
            Sx.op("pe", lambda e, li=li, kv=kv: e.matmul(out=c.ps[3][:, 0:256], lhsT=W1[:, li, :], rhs=kv[:, li, :],
                                                        start=(li == 0), stop=False, skip_group_check=True),
                  reads=[L + "W1", L + "KCr"], writes=["ps3"])
        for li in range(16):
            Sx.op("pe", lambda e, li=li, kv=kv: e.matmul(out=c.ps[3][:, 0:255], lhsT=W1[:, 16 + li, :],
                                                        rhs=kv[:, li, 1:256], start=False, stop=(li == 15),
                                                        skip_group_check=True),
                  reads=[L + "W1", L + "KCr"], writes=["ps3"])
        Sx.op("act", lambda e: e.activation(out=ex[:], in_=c.ps[3][:, 0:256], func=AF.Exp, scale=-1.0, bias=b1[:, 1:2]),
              reads=["ps3", L + "nb1"], writes=[L + "ex"])
        Sx.op("dve", lambda e: e.tensor_scalar(out=ex[:], in0=ex[:], scalar1=1.0, scalar2=None, op0=ALU.add),
              reads=[L + "ex"], writes=[L + "ex"])
        Sx.op("dve", lambda e: e.reciprocal(out=ex[:], in_=ex[:]), reads=[L + "ex"], writes=[L + "ex"])
        Sx.op("dve", lambda e: e.tensor_scalar(out=zz[:], in0=c.ps[3][:, 0:256], scalar1=b1[:, 0:1], scalar2=None,
                                               op0=ALU.add), reads=["ps3", L + "b1"], writes=[L + "zz"])
        Sx.op("dve", lambda e: e.tensor_tensor(out=HID[:], in0=zz[:], in1=ex[:], op=ALU.mult),
              reads=[L + "zz", L + "ex"], writes=[L + "HID"])
        for ct in range(2):
            Sx.op("pe", lambda e, ct=ct: e.matmul(out=c.ps[2][:, ct * 128:(ct + 1) * 128],
                                                  lhsT=HID[:, ct * 128:(ct + 1) * 128], rhs=W2[:],
                                                  start=True, stop=True),
                  reads=[L + "HID", L + "W2"], writes=["ps2"])
        p2 = c.ps[2][:, 0:256].rearrange("p (ct n) -> p ct n", ct=2)
        Sx.op("dve", lambda e, h=h, p2=p2: e.tensor_copy(out=kraw[:, :, h, :], in_=p2[:, :, 0:64]),
              reads=["ps2"], writes=[L + f"kraw{h}"])
        Sx.op("dve", lambda e, h=h, p2=p2: e.tensor_copy(out=VCa[:, :, h, 0:64], in_=p2[:, :, 64:128]),
              reads=["ps2"], writes=[L + f"VCav{h}"])
    for ct in range(2):
        norm_rope(c, {"x": [L + "kraw0", L + "kraw1"], "out": [L + f"KCm{ct}"], "gn": [L + "gk"],
                      "cs": [L + "CSc"], "scr": L + f"nrC{ct}"},
                  "dve", kraw[:, ct, :, :], gk[:], CSc[:, ct, :], KCm[:, ct, :, 0:64], 2, sq2[:], tt2[:], st2[:])
        for h in range(2):
            Sx.op("pe", lambda e, ct=ct, h=h: e.matmul(out=c.ps[2][:, (ct * 2 + h) * 128:(ct * 2 + h + 1) * 128],
                                                       lhsT=KCm[:, ct, h, :], rhs=c.ident[:], start=True, stop=True),
                  reads=[L + f"KCm{ct}", L + "KCmz", "ident"], writes=["ps2"])
    Sx.op("dve", lambda e: e.tensor_copy(out=KCA[:].rearrange("p h (ct n) -> p ct h n", ct=2),
                                         in_=c.ps[2][:].rearrange("p (ct h n) -> p ct h n", ct=2, h=2)),
          reads=["ps2"], writes=[L + "KCA"])
    sb.reset(m1)
    Sx.mark("phaseA' done")
    NSL = 3
    QA0 = sb.t("QA0", [128, 8, 512], BF16)
    QA = sb.t("QA", [128, 8, 512], BF16)
    QM = sb.t("QM", [128, 4, 8, 128], BF16)
    Xq = [sb.t("Xq", [128, 512], F32) for _ in range(2)]
    Gt = sb.t("Gt", [128, 4, 24], F32)
    CSq = [sb.t("CSq", [128, 64], F32) for _ in range(2)]
    sq8 = [sb.t("sq8", [128, 8, 64], F32) for _ in range(2)]
    tt8 = [sb.t("tt8", [128, 8, 4, 32], F32) for _ in range(2)]
    st8 = [sb.t("st8", [128, 8, 4], F32) for _ in range(2)]
    PT = [sb.t("PT", [128, 512], BF16) for _ in range(NSL)]
    CMf = sb.t("CMf", [128, 2, 512], F32)
    CM = sb.t("CM", [128, 2, 512], BF16)
    VISt = sb.t("VISt", [128, 4, 64], F32)
    ADDt = sb.t("ADDt", [128, 4, 64], F32)
    IMP = sb.t("IMP", [128, 4, 2, 64], F32)
    Y = sb.t("Y", [128, 4, 8, 64], F32)
    tmpo = [sb.t("tmpo", [128, 4, 64], F32) for _ in range(2)]
    den = [sb.t("den", [128, 4, 3], F32) for _ in range(2)]
    sc = sb.t("sc", [128, 64], F32)
    sc2 = sb.t("sc2", [128, 64], F32)
    m8 = sb.t("m8", [128, 16], F32)
    nm = sb.t("nm", [128, 64], BF16)
    for j in range(S // 512):
        Lj = L + f"c{j}"
        Sx.dma(VISt[:], c.vis_d[j * 512:(j + 1) * 512, :].rearrange("(a p) n -> p a n", p=128), writes=[L + "VISt"])
        Sx.dma(ADDt[:], c.add_d[j * 512:(j + 1) * 512, :].rearrange("(a p) n -> p a n", p=128), writes=[L + "ADDt"])
        Sx.dma(CMf[:], c.cm_d[:, :, j * 512:(j + 1) * 512], writes=[L + "CMf"])
        Sx.op("pool", lambda e: e.tensor_copy(out=CM[:], in_=CMf[:]), reads=[L + "CMf"], writes=[L + "CM"])
        Sx.dma(Gt[:], c.proj[j * 512:(j + 1) * 512, N_G:N_G + 24].rearrange("(a p) n -> p a n", p=128),
               reads=[f"proj{4 * j + a}" for a in range(4)], writes=[L + "Gt"])
        Sx.op("act", lambda e: e.activation(out=Gt[:], in_=Gt[:], func=AF.Exp, scale=-1.0),
              reads=[L + "Gt"], writes=[L + "Gt"])
        Sx.op("dve", lambda e: e.tensor_scalar(out=Gt[:], in0=Gt[:], scalar1=1.0, scalar2=None, op0=ALU.add),
              reads=[L + "Gt"], writes=[L + "Gt"])
        Sx.op("dve", lambda e: e.reciprocal(out=Gt[:], in_=Gt[:]), reads=[L + "Gt"], writes=[L + "Gt"])
        Sx.op("pool", lambda e: e.memset(QM[:, :, :, 64:128], 0.0), writes=[L + f"QMm{a}" for a in range(4)])
        Sx.op("pool", lambda e: e.memset(IMP[:], 0.0), writes=[L + "IMP"])
        for a in range(4):
            t = 4 * j + a
            s_ = a % 2
            ve = "dve" if a % 2 == 0 else "pool"
            Sx.dma(Xq[s_][:], c.proj[t * 128:(t + 1) * 128, N_Q:N_Q + 512], reads=[f"proj{t}"], writes=[L + f"Xq{s_}"])
            Sx.dma(CSq[s_][:], c.cs_tab[t * 128:(t + 1) * 128, :], reads=["cs_tab"], writes=[L + f"CSq{s_}"])
            norm_rope(c, {"x": [L + f"Xq{s_}"], "out": [L + f"QMq{a}"], "gn": [L + "gq"],
                          "cs": [L + f"CSq{s_}"], "scr": L + f"nrQ{s_}"},
                      ve, Xq[s_][:].rearrange("p (h d) -> p h d", h=8), gq[:], CSq[s_][:],
                      QM[:, a, :, 0:64], 8, sq8[s_][:], tt8[s_][:], st8[s_][:])

        def transposes(dst, dst_name, with_mask):
            for h in range(8):
                for a in range(4):
                    Sx.op("pe", lambda e, h=h, a=a: e.matmul(
                        out=c.ps[3][:, a * 128:(a + 1) * 128], lhsT=QM[:, a, h, :], rhs=c.ident[:],
                        start=True, stop=True),
                        reads=[L + f"QMq{a}", L + f"QMm{a}", "ident"], writes=["ps3"])
                eng = "act" if h % 2 == 0 else "dve"
                if eng == "act":
                    Sx.op("act", lambda e, h=h: e.copy(out=dst[:, h, :], in_=c.ps[3][:]),
                          reads=["ps3"], writes=[L + f"{dst_name}{h}"])
                else:
                    Sx.op("dve", lambda e, h=h: e.tensor_copy(out=dst[:, h, :], in_=c.ps[3][:]),
                          reads=["ps3"], writes=[L + f"{dst_name}{h}"])

        Sx.mark(f"c{j} q prep done")
        transposes(QA0, "QA0_", False)
        Sx.mark(f"c{j} T0 done")

        units = []

        def add_unit(qsrc, qname, h, lhsT, lres, a_lo, a_hi, masks, vrhs, vres, acc, vw, first, last, fin):
            units.append(dict(qsrc=qsrc, qname=qname, h=h, lhsT=lhsT, lres=lres, a_lo=a_lo, a_hi=a_hi, masks=masks,
                              vrhs=vrhs, vres=vres, acc=acc, vw=vw, first=first, last=last, fin=fin))

        def run_units():
            n = len(units)
            LOOK = 2
            for i in range(n + LOOK):
                if i < n:
                    u = units[i]
                    sl = i % NSL
                    ncol = (u["a_hi"] - u["a_lo"] + 1) * 128
                    q0 = u["a_lo"] * 128
                    nmask = len(u["masks"])
                    Sx.op("pe", lambda e, u=u, sl=sl, ncol=ncol, q0=q0, nmask=nmask: e.matmul(
                        out=c.ps[sl][:, 0:ncol], lhsT=u["lhsT"], rhs=u["qsrc"][:, u["h"], q0:q0 + ncol],
                        start=True, stop=(nmask == 0), skip_group_check=True),
                        reads=u["lres"] + [L + f"{u['qname']}{u['h']}"], writes=[f"ps{sl}"])
                    for mi, (a, mrhs, mres) in enumerate(u["masks"]):
                        o0 = (a - u["a_lo"]) * 128 if a is not None else 0
                        wd = 128 if a is not None else ncol
                        Sx.op("pe", lambda e, sl=sl, o0=o0, wd=wd, mrhs=mrhs, mi=mi, nmask=nmask: e.matmul(
                            out=c.ps[sl][:, o0:o0 + wd], lhsT=c.ident[:], rhs=mrhs, start=False,
                            stop=(mi == nmask - 1), skip_group_check=True),
                            reads=["ident"] + mres, writes=[f"ps{sl}"])
                k = i - LOOK
                if k >= 0:
                    u = units[k]
                    sl = k % NSL
                    ncol = (u["a_hi"] - u["a_lo"] + 1) * 128
                    Sx.op("act", lambda e, sl=sl, ncol=ncol: e.activation(
                        out=PT[sl][:, 0:ncol], in_=c.ps[sl][:, 0:ncol], func=AF.Exp, scale=SCALE),
                        reads=[f"ps{sl}"], writes=[L + f"PT{sl}"])
                    vw = u["vw"]
                    started = set()
                    for a in range(u["a_lo"], u["a_hi"] + 1):
                        bank, col = u["acc"](a)
                        st_flag = u["first"] and (bank not in started) and a == u["first_a"].get(bank, -1)
                        started.add(bank)
                        Sx.op("pe", lambda e, u=u, sl=sl, a=a, bank=bank, col=col, vw=vw, st_flag=st_flag: e.matmul(
                            out=c.ps[bank][:, col:col + vw],
                            lhsT=PT[sl][:, (a - u["a_lo"]) * 128:(a - u["a_lo"] + 1) * 128], rhs=u["vrhs"],
                            start=st_flag, stop=False, skip_group_check=True),
                            reads=[L + f"PT{sl}"] + u["vres"], writes=[f"ps{bank}"])
                    if u["fin"] is not None:
                        u["fin"]()
            units.clear()

        def evac(h, br, banks_cols, vw, cmp_branch):
            g = h // 4
            ds = h % 2
            for a in range(4):
                bank, col = banks_cols(a)
                Sx.op("dve", lambda e, a=a, bank=bank, col=col, ds=ds: e.tensor_scalar(
                    out=den[ds][:, a, 0:1], in0=c.ps[bank][:, col + 64:col + 65], scalar1=1e-30, scalar2=None,
                    op0=ALU.max), reads=[f"ps{bank}"], writes=[L + f"den{ds}"])
            Sx.op("dve", lambda e, ds=ds: e.reciprocal(out=den[ds][:, :, 1], in_=den[ds][:, :, 0]),
                  reads=[L + f"den{ds}"], writes=[L + f"rden{ds}"])
            Sx.op("dve", lambda e, ds=ds, h=h, br=br: e.tensor_tensor(
                out=den[ds][:, :, 2], in0=den[ds][:, :, 1], in1=Gt[:, :, h * 3 + br], op=ALU.mult),
                reads=[L + f"rden{ds}", L + "Gt"], writes=[L + f"coef{ds}"])
            for a in range(4):
                bank, col = banks_cols(a)
                if cmp_branch:
                    Sx.op("dve", lambda e, a=a, bank=bank, col=col, ds=ds, h=h: e.tensor_scalar(
                        out=Y[:, a, h, :], in0=c.ps[bank][:, col:col + 64], scalar1=den[ds][:, a, 2:3], scalar2=None,
                        op0=ALU.mult), reads=[f"ps{bank}", L + f"coef{ds}"], writes=[L + f"Y{h}"])
                    Sx.op("dve", lambda e, a=a, bank=bank, col=col, ds=ds, g=g: e.scalar_tensor_tensor(
                        out=IMP[:, a, g, :], in0=c.ps[bank][:, col + 65:col + 129], scalar=den[ds][:, a, 1:2],
                        in1=IMP[:, a, g, :], op0=ALU.mult, op1=ALU.add),
                        reads=[f"ps{bank}", L + f"rden{ds}"], writes=[L + "IMP"])
                else:
                    Sx.op("dve", lambda e, a=a, bank=bank, col=col, ds=ds, h=h: e.scalar_tensor_tensor(
                        out=Y[:, a, h, :], in0=c.ps[bank][:, col:col + 64], scalar=den[ds][:, a, 2:3],
                        in1=Y[:, a, h, :], op0=ALU.mult, op1=ALU.add),
                        reads=[f"ps{bank}", L + f"coef{ds}"], writes=[L + f"Y{h}"])

        cts = [0] if j <= 3 else [0, 1]
        for h in range(8):
            g = h // 4
            bb = 4 + 2 * (h % 2)
            acc = (lambda a, bb=bb: (bb + a // 2, (a % 2) * 129))
            for ci, ct in enumerate(cts):
                add_unit(QA0, "QA0_", h, KCA[:, g, ct * 128:(ct + 1) * 128], [L + "KCA"], 0, 3,
                         [(None, CM[:, ct, :], [L + "CM"])], VCa[:, ct, g, :],
                         [L + f"VCav{g}", L + f"VCam{g}", L + "VCa1"], acc, 129, ci == 0, ci == len(cts) - 1,
                         (lambda h=h, acc=acc: evac(h, 0, acc, 129, True)) if ci == len(cts) - 1 else None)
                units[-1]["first_a"] = {bb: 0, bb + 1: 2}
        run_units()
        Sx.mark(f"c{j} cmp done")
        if j == 0:
            dbg(c, "Ycmp", Y[:].rearrange("p a h d -> p (a h d)"), [L + f"Y{h}" for h in range(8)])
            dbg(c, "IMP", IMP[:].rearrange("p a g n -> p (a g n)"), [L + "IMP"])
            dbg(c, "Gt", Gt[:].rearrange("p a n -> p (a n)"), [L + "Gt"])
            dbg(c, "QA0", QA0[:].rearrange("p h t -> p (h t)"), [L + f"QA0_{h}" for h in range(8)])
            dbg(c, "KCA", KCA[:].rearrange("p h t -> p (h t)"), [L + "KCA"])
            dbg(c, "VCa", VCa[:].rearrange("p a h t -> p (a h t)"), [L + "VCav0", L + "VCav1", L + "VCam0", L + "VCam1", L + "VCa1"])
            dbg(c, "KWA", KWA[:, 0, 0:512], [L + f"KWA{t}" for t in range(4)])
            dbg(c, "KAs", KAs[:, 0, 0:512], [L + f"KAs{t}" for t in range(4)])
        for h in range(8):
            g = h // 4
            bb = 4 + (h % 4)
            acc = (lambda a, bb=bb: (bb, a * 65))
            kts = list(range(max(0, 4 * j - 4), 4 * j + 4))
            seen = set()
            for ki, kt in enumerate(kts):
                a_lo = max(0, kt - 4 * j)
                a_hi = min(3, kt + 4 - 4 * j)
                masks = []
                if kt >= 4 * j:
                    masks.append((kt - 4 * j, madd[:, 0, :], [L + "madd"]))
                if kt + 4 - 4 * j <= 3:
                    masks.append((kt + 4 - 4 * j, madd[:, 1, :], [L + "madd"]))
                add_unit(QA0, "QA0_", h, KWA[:, g, kt * 128:(kt + 1) * 128], [L + f"KWA{kt}"], a_lo, a_hi, masks,
                         VW[:, kt, g, :], [L + f"VW{kt}", L + "VWone"], acc, 65, ki == 0, ki == len(kts) - 1,
                         (lambda h=h, acc=acc: evac(h, 2, acc, 65, False)) if ki == len(kts) - 1 else None)
                units[-1]["first_a"] = {bb: a_lo}
        run_units()
        Sx.mark(f"c{j} win done")
        if j == 0:
            dbg(c, "Ywin", Y[:].rearrange("p a h d -> p (a h d)"), [L + f"Y{h}" for h in range(8)])
        for a in range(4):
            for g in range(2):
                Sx.op("dve", lambda e, a=a, g=g: e.tensor_tensor(out=sc[:], in0=IMP[:, a, g, :], in1=VISt[:, a, :],
                                                                op=ALU.mult),
                      reads=[L + "IMP", L + "VISt"], writes=[L + "sc"])
                Sx.op("dve", lambda e, a=a: e.tensor_tensor(out=sc[:], in0=sc[:], in1=ADDt[:, a, :], op=ALU.add),
                      reads=[L + "sc", L + "ADDt"], writes=[L + "sc"])
                Sx.op("dve", lambda e: e.max(out=m8[:, 0:8], in_=sc[:]), reads=[L + "sc"], writes=[L + "m8a"])
                Sx.op("dve", lambda e: e.match_replace(out=sc2[:], in_to_replace=m8[:, 0:8], in_values=sc[:],
                                                       imm_value=-1e9),
                      reads=[L + "sc", L + "m8a"], writes=[L + "sc2"])
                Sx.op("dve", lambda e: e.max(out=m8[:, 8:16], in_=sc2[:]), reads=[L + "sc2"], writes=[L + "m8b"])
                Sx.op("dve", lambda e: e.tensor_reduce(out=m8[:, 0:1], in_=m8[:, 8:16], axis=AX.X, op=ALU.min),
                      reads=[L + "m8b"], writes=[L + "thr"])
                Sx.op("dve", lambda e: e.tensor_scalar(out=nm[:], in0=sc[:], scalar1=m8[:, 0:1], scalar2=NEG,
                                                       op0=ALU.is_lt, op1=ALU.mult),
                      reads=[L + "sc", L + "thr"], writes=[L + "nm"])
                Sx.op("pool", lambda e, a=a, g=g: e.tensor_copy(out=QM[:, a, 4 * g:4 * g + 4, 64:128],
                                                               in_=bc(nm[:], [128, 4, 64], 1)),
                      reads=[L + "nm"], writes=[L + f"QMm{a}"])
        Sx.mark(f"c{j} topk done")
        if j == 0:
            dbg(c, "QM", QM[:].rearrange("p a h d -> p (a h d)"), [L + f"QMm{a}" for a in range(4)] + [L + f"QMq{a}" for a in range(4)])
        transposes(QA, "QA_", True)
        for h in range(8):
            g = h // 4
            bb = 4 + (h % 4)
            acc = (lambda a, bb=bb: (bb, a * 65))
            kts = list(range(0, 4 * j + 4))
            for ki, kt in enumerate(kts):
                a_lo = max(0, kt - 4 * j)
                masks = []
                if kt >= 4 * j:
                    masks.append((kt - 4 * j, madd[:, 0, :], [L + "madd"]))
                add_unit(QA, "QA_", h, KAs[:, g, kt * 128:(kt + 1) * 128], [L + f"KAs{kt}"], a_lo, 3, masks,
                         VS[:, kt, g, :], [L + f"VS{kt}", L + "VSone"], acc, 65, ki == 0, ki == len(kts) - 1,
                         (lambda h=h, acc=acc: evac(h, 1, acc, 65, False)) if ki == len(kts) - 1 else None)
                units[-1]["first_a"] = {bb: a_lo}
        run_units()
        Sx.mark(f"c{j} sel done")
        for a in range(4):
            t = 4 * j + a
            Sx.dma(c.mix[t * 128:(t + 1) * 128, 256:768], Y[:, a, :, :].rearrange("p h d -> p (h d)"),
                   reads=[L + f"Y{h}" for h in range(8)], writes=[f"mixB{t}"])
    sb.reset(m)


def _prep_inputs(inputs, names, extra=None):
    ident = np.eye(128, dtype=np.float32)
    maps = []
    for b in range(8):
        m = {}
        for n in names:
            if n == "x":
                m[n] = np.ascontiguousarray(inputs["x"][b])
            elif n == "positions":
                m[n] = np.ascontiguousarray(inputs["positions"][b].reshape(S, 1))
            elif n == "c_ident":
                m[n] = ident
            elif n == "c_invf":
                invf = (10000.0 ** (-(np.arange(32, dtype=np.float32) / np.float32(32)))).astype(np.float32)
                m[n] = np.ascontiguousarray(np.broadcast_to(invf[None, :], (128, 32)))
            elif n in ("c_esel", "c_mmap", "c_cm", "c_vis", "c_add"):
                m[n] = _nsa_consts()[n]
            elif n == "c_masks":
                p = np.arange(128)[:, None]
                f = np.arange(128)[None, :]
                m[n] = np.ascontiguousarray(np.stack([(p >= f), (p <= f), np.zeros((128, 128), bool), (p > f)],
                                                     axis=1).astype(np.float32))
            elif n in WSHAPES:
                m[n] = np.ascontiguousarray(inputs[n])
            elif extra is not None and n in extra:
                m[n] = extra[n][b] if isinstance(extra[n], (list, tuple)) else extra[n]
        maps.append(m)
    return maps


_NC_CACHE = {}
_CONSTS = {}


def _nsa_consts():
    if _CONSTS:
        return _CONSTS
    t = np.arange(S)
    n = np.arange(64)
    cur = (t // 64)[:, None]
    nn = n[None, :]
    _CONSTS["c_esel"] = (cur == nn).astype(np.float32)
    forced = (nn == 0) | (nn == cur) | (nn == cur - 1)
    visible = nn <= cur
    _CONSTS["c_vis"] = (visible & ~forced).astype(np.float32)
    add = np.zeros((S, 64), np.float32)
    add = np.where(nn == cur - 1, 10000.0, add)
    add = np.where(nn == cur, 20000.0, add)
    add = np.where(nn == 0, 30000.0, add)
    add = np.where(~visible, -1.0 - nn, add)
    _CONSTS["c_add"] = add.astype(np.float32)
    cidx = np.arange(256)
    cs_ = cidx[:, None] * 16
    ss_ = n[None, :] * 64
    ov = np.clip(np.minimum(cs_ + 32, ss_ + 64) - np.maximum(cs_, ss_), 0, None) / 32.0
    ov[255] = 0.0
    _CONSTS["c_mmap"] = ov.astype(np.float32)
    cm = np.full((128, 2, S), NEG, np.float32)
    for ct in range(2):
        cc = ct * 128 + np.arange(128)
        vis = ((16 * cc + 31)[:, None] <= t[None, :]) & (cc < 255)[:, None]
        cm[:, ct, :] = np.where(vis, 0.0, NEG)
    _CONSTS["c_cm"] = cm
    return _CONSTS


def kernel(**inputs):
    inputs = {k: np.asarray(v) for k, v in inputs.items()}
    if "full" not in _NC_CACHE:
        _NC_CACHE["full"] = build_program()
    nc = _NC_CACHE["full"]
    res = run_bass_kernel_spmd(nc, _prep_inputs(inputs, nc._mk_inputs), core_ids=list(range(8)))
    return np.stack([r["out"] for r in res.results], axis=0).astype(np.float32)
```
